# Optimizing a Trainium2 kernel written in Bass

```python
import math
import jax, jax.numpy as jnp
from jax import lax
import numpy as np

D_MODEL = 1024
BATCH = 4
SEQ = 4096
DEPTH = 2

MEM_LEN = 256
D_MIX = D_MODEL
SSD_DIM = D_MIX // 2
SSD_HEAD_DIM = 64
SSD_HEADS = SSD_DIM // SSD_HEAD_DIM
SSD_GROUPS = 2
SSD_STATE = 64
CONV_K = 4
CHUNK = 128
SB_DIM = D_MIX - SSD_DIM
SB_HEAD_DIM = 64
SB_HEADS = SB_DIM // SB_HEAD_DIM
Q_BLOCK = 128
XA_HEADS = 4
XA_HEAD_DIM = 128
XA_DIM = XA_HEADS * XA_HEAD_DIM
D_FF = 4 * D_MODEL
EPS = 1e-5
GN = SSD_GROUPS * SSD_STATE
CONV_DIM = SSD_DIM + 2 * GN
IN_DIM = SSD_DIM + CONV_DIM + SSD_HEADS + 3 * SB_DIM

kernel_name = "hymba_ssd_stickbreaking_memxattn_trunk"


def rmsnorm(x, g):
    xf = x.astype(jnp.float32)
    y = xf * lax.rsqrt(jnp.mean(xf * xf, axis=-1, keepdims=True) + EPS)
    return (y * g.astype(jnp.float32)).astype(x.dtype)


def causal_dwconv(x, w, b):
    y = lax.conv_general_dilated(
        x, w.astype(x.dtype)[:, None, :], window_strides=(1,), padding=[(CONV_K - 1, 0)],
        dimension_numbers=("NWC", "WIO", "NWC"), feature_group_count=x.shape[-1])
    return y + b.astype(x.dtype)


def ssd_chunked(xh, dt, a, bm, cm):
    b, s, h, p = xh.shape
    g, n = bm.shape[-2:]
    e = h // g
    nc = s // CHUNK
    x_c = (xh * dt[..., None]).reshape(b, nc, CHUNK, g, e, p)
    a_c = (dt * a).reshape(b, nc, CHUNK, g, e)
    b_c = bm.reshape(b, nc, CHUNK, g, n)
    c_c = cm.reshape(b, nc, CHUNK, g, n)
    a_cum = jnp.cumsum(a_c, axis=2)
    seg = a_cum[:, :, :, None] - a_cum[:, :, None, :]
    causal = jnp.tril(jnp.ones((CHUNK, CHUNK), dtype=bool))[:, :, None, None]
    decay = jnp.exp(jnp.where(causal, seg, -jnp.inf))
    cb = jnp.einsum("bclgn,bcsgn->bclsg", c_c, b_c)
    y_diag = jnp.einsum("bclsg,bclsge,bcsgep->bclgep", cb, decay, x_c)
    decay_states = jnp.exp(a_cum[:, :, -1:] - a_cum)
    states = jnp.einsum("bclgn,bclge,bclgep->bcgepn", b_c, decay_states, x_c)
    chunk_decay = jnp.exp(a_cum[:, :, -1])

    def step(prev, inp):
        st, dec = inp
        return prev * dec[..., None, None] + st, prev

    init = jnp.zeros((b, g, e, p, n), jnp.float32)
    _, prev_states = lax.scan(step, init, (jnp.swapaxes(states, 0, 1), jnp.swapaxes(chunk_decay, 0, 1)))
    prev_states = jnp.swapaxes(prev_states, 0, 1)
    y_off = jnp.einsum("bclgn,bcgepn,bclge->bclgep", c_c, prev_states, jnp.exp(a_cum))
    return (y_diag + y_off).reshape(b, s, h, p)


def stick_breaking_attention(q, k, v):
    s_len, dh = q.shape[2], q.shape[3]
    scale = 1.0 / math.sqrt(dh)
    outs = []
    for i in range(s_len // Q_BLOCK):
        start = i * Q_BLOCK
        end = start + Q_BLOCK
        qb = q[:, :, start:end].astype(jnp.float32)
        kb = k[:, :, :end].astype(jnp.float32)
        vb = v[:, :, :end].astype(jnp.float32)
        logits = jnp.einsum("bhqd,bhkd->bhqk", qb, kb) * scale
        t_pos = start + jnp.arange(Q_BLOCK)[:, None]
        s_pos = jnp.arange(end)[None, :]
        mask = s_pos < t_pos
        log_stay = jnp.where(mask, jax.nn.log_sigmoid(-logits), 0.0)
        log_beta = jax.nn.log_sigmoid(logits)
        later = lax.cumsum(log_stay, axis=3, reverse=True) - log_stay
        w = jnp.where(mask, jnp.exp(log_beta + later), 0.0)
        outs.append(jnp.einsum("bhqk,bhkd->bhqd", w, vb))
    return jnp.concatenate(outs, axis=2)


def hybrid_mixer(hn, w_in, conv_w, conv_b, dt_bias, a_log, d_skip, ssd_norm_g, sb_norm_g, w_out):
    b, s, _ = hn.shape
    proj = hn @ w_in.astype(hn.dtype)
    z, xbc, dt_raw, qkv = jnp.split(
        proj, [SSD_DIM, SSD_DIM + CONV_DIM, SSD_DIM + CONV_DIM + SSD_HEADS], axis=-1)
    xbc = jax.nn.silu(causal_dwconv(xbc, conv_w, conv_b))
    xs, bm, cm = jnp.split(xbc, [SSD_DIM, SSD_DIM + GN], axis=-1)
    dt = jax.nn.softplus(dt_raw.astype(jnp.float32) + dt_bias.astype(jnp.float32))
    a = -jnp.exp(a_log.astype(jnp.float32))
    xh = xs.astype(jnp.float32).reshape(b, s, SSD_HEADS, SSD_HEAD_DIM)
    y = ssd_chunked(xh, dt, a,
                    bm.astype(jnp.float32).reshape(b, s, SSD_GROUPS, SSD_STATE),
                    cm.astype(jnp.float32).reshape(b, s, SSD_GROUPS, SSD_STATE))
    y = y + d_skip.astype(jnp.float32)[:, None] * xh
    y = y.reshape(b, s, SSD_DIM) * jax.nn.silu(z.astype(jnp.float32))
    y_ssd = rmsnorm(y, ssd_norm_g)
    q, k, v = jnp.split(qkv, 3, axis=-1)
    to_heads = lambda t: t.reshape(b, s, SB_HEADS, SB_HEAD_DIM).transpose(0, 2, 1, 3)
    o = stick_breaking_attention(to_heads(q), to_heads(k), to_heads(v))
    y_sb = rmsnorm(o.transpose(0, 2, 1, 3).reshape(b, s, SB_DIM), sb_norm_g)
    y_all = jnp.concatenate([y_ssd, y_sb], axis=-1).astype(hn.dtype)
    return y_all @ w_out.astype(hn.dtype)


def memory_cross_attention(hn, mem, g_mem, w_q, w_k, w_v, w_o):
    b, s, _ = hn.shape
    m = rmsnorm(mem, g_mem)
    q = (hn @ w_q.astype(hn.dtype)).reshape(b, s, XA_HEADS, XA_HEAD_DIM).astype(jnp.float32)
    k = (m @ w_k.astype(m.dtype)).reshape(b, -1, XA_HEADS, XA_HEAD_DIM).astype(jnp.float32)
    v = (m @ w_v.astype(m.dtype)).reshape(b, -1, XA_HEADS, XA_HEAD_DIM).astype(jnp.float32)
    logits = jnp.einsum("bshd,bmhd->bhsm", q, k) * (1.0 / math.sqrt(XA_HEAD_DIM))
    p = jax.nn.softmax(logits, axis=-1)
    o = jnp.einsum("bhsm,bmhd->bshd", p, v).reshape(b, s, XA_DIM).astype(hn.dtype)
    return o @ w_o.astype(hn.dtype)


def sqrelu_mlp(hn, w1, w2):
    u = hn @ w1.astype(hn.dtype)
    u = jnp.square(jax.nn.relu(u))
    return u @ w2.astype(hn.dtype)


def setup_inputs(seed: int = 0) -> dict:
    key = jax.random.key(seed)
    ks = jax.random.split(key, 24)
    f32 = jnp.float32
    nrm = lambda k, shape, fan_in: jax.random.normal(k, shape, f32) * (fan_in ** -0.5)
    gain = lambda k, shape: 1.0 + 0.02 * jax.random.normal(k, shape, f32)
    u = jax.random.uniform(ks[5], (DEPTH, SSD_HEADS), f32)
    dt0 = jnp.exp(u * (math.log(0.1) - math.log(0.001)) + math.log(0.001))
    dt_bias = dt0 + jnp.log(-jnp.expm1(-dt0))
    a_log = jnp.log(jax.random.uniform(ks[6], (DEPTH, SSD_HEADS), f32, minval=1.0, maxval=16.0))
    return {
        "x": jax.random.normal(ks[0], (BATCH, SEQ, D_MODEL), f32),
        "mem": jax.random.normal(ks[1], (BATCH, MEM_LEN, D_MODEL), f32),
        "norm_mix_g": gain(ks[2], (DEPTH, D_MODEL)),
        "w_in": nrm(ks[3], (DEPTH, D_MODEL, IN_DIM), D_MODEL),
        "conv_w": 0.5 * jax.random.normal(ks[4], (DEPTH, CONV_K, CONV_DIM), f32),
        "conv_b": 0.02 * jax.random.normal(ks[7], (DEPTH, CONV_DIM), f32),
        "dt_bias": dt_bias,
        "a_log": a_log,
        "d_skip": 1.0 + 0.1 * jax.random.normal(ks[8], (DEPTH, SSD_HEADS), f32),
        "ssd_norm_g": gain(ks[9], (DEPTH, SSD_DIM)),
        "sb_norm_g": gain(ks[10], (DEPTH, SB_DIM)),
        "w_out": nrm(ks[11], (DEPTH, D_MIX, D_MODEL), D_MIX),
        "norm_xa_g": gain(ks[12], (DEPTH, D_MODEL)),
        "norm_mem_g": gain(ks[13], (DEPTH, D_MODEL)),
        "w_xq": nrm(ks[14], (DEPTH, D_MODEL, XA_DIM), D_MODEL),
        "w_xk": nrm(ks[15], (DEPTH, D_MODEL, XA_DIM), D_MODEL),
        "w_xv": nrm(ks[16], (DEPTH, D_MODEL, XA_DIM), D_MODEL),
        "w_xo": nrm(ks[17], (DEPTH, XA_DIM, D_MODEL), XA_DIM),
        "norm_ff_g": gain(ks[18], (DEPTH, D_MODEL)),
        "w_ff1": nrm(ks[19], (DEPTH, D_MODEL, D_FF), D_MODEL),
        "w_ff2": nrm(ks[20], (DEPTH, D_FF, D_MODEL), D_FF),
        "final_g": gain(ks[21], (D_MODEL,)),
    }


def reference(x, mem, norm_mix_g, w_in, conv_w, conv_b, dt_bias, a_log, d_skip, ssd_norm_g,
              sb_norm_g, w_out, norm_xa_g, norm_mem_g, w_xq, w_xk, w_xv, w_xo,
              norm_ff_g, w_ff1, w_ff2, final_g):
    h = x
    for l in range(DEPTH):
        h = h + hybrid_mixer(rmsnorm(h, norm_mix_g[l]), w_in[l], conv_w[l], conv_b[l], dt_bias[l],
                             a_log[l], d_skip[l], ssd_norm_g[l], sb_norm_g[l], w_out[l])
        h = h + memory_cross_attention(rmsnorm(h, norm_xa_g[l]), mem, norm_mem_g[l],
                                       w_xq[l], w_xk[l], w_xv[l], w_xo[l])
        h = h + sqrelu_mlp(rmsnorm(h, norm_ff_g[l]), w_ff1[l], w_ff2[l])
    return rmsnorm(h, final_g)
```

```python
import math
import contextlib
import numpy as np
import concourse.bass as bass
import concourse.mybir as mybir
from concourse.bass_utils import run_bass_kernel_spmd

F32 = mybir.dt.float32
BF16 = mybir.dt.bfloat16
AF = mybir.ActivationFunctionType
ALU = mybir.AluOpType

ENGS = ("pe", "act", "dve", "pool", "sp")


class Buf:
    def __init__(self, name):
        self.name = name
        self.st = {"*": [[], []]}
        self.sem = None
        self.ndma = 0
        self.excl = False
        self.last_by_eng = {}

    def k(self, key):
        return (self, key)


class Op:
    __slots__ = ("eng", "fn", "waits", "idx", "is_dma", "dest")


class Prog:
    def __init__(self, nc):
        self.nc = nc
        self.ops = {e: [] for e in ENGS}
        self.dma_bufs = []
        self.nops = 0

    @staticmethod
    def _norm(x):
        if isinstance(x, Buf):
            return (x, None)
        return x

    def _entries(self, buf, key):
        d = buf.st
        if key is None:
            return list(d.values())
        if key not in d:
            d[key] = [list(d["*"][0]), list(d["*"][1])]
        return [d[key]]

    def add(self, eng, fn, reads=(), writes=(), dma_dest=None):
        op = Op()
        op.eng = eng
        op.fn = fn
        op.is_dma = dma_dest is not None
        op.dest = dma_dest
        deps = []
        reads = [self._norm(r) for r in reads if r is not None]
        writes = [self._norm(w) for w in writes if w is not None]
        for (b, key) in reads:
            for ent in self._entries(b, key):
                deps.extend(ent[0])
        for (b, key) in writes:
            for ent in self._entries(b, key):
                samegen = op.is_dma and len(ent[0]) > 0 and all(w.is_dma for w in ent[0]) and len(ent[1]) == 0
                if not samegen:
                    deps.extend(ent[0])
                deps.extend(ent[1])
        for (b, key) in list(reads) + list(writes):
            if b.excl:
                for e2, y in b.last_by_eng.items():
                    if e2 != eng:
                        deps.append(y)
                b.last_by_eng[eng] = op
        for (b, key) in writes:
            for ent in self._entries(b, key):
                samegen = op.is_dma and len(ent[0]) > 0 and all(w.is_dma for w in ent[0]) and len(ent[1]) == 0
                if samegen:
                    ent[0].append(op)
                else:
                    ent[0] = [op]
                    ent[1] = []
            if key is None:
                for kk in list(b.st.keys()):
                    b.st[kk][0] = list(b.st["*"][0])
                    b.st[kk][1] = []
        wset = set((id(b), key) for (b, key) in writes)
        for (b, key) in reads:
            if (id(b), key) in wset:
                continue
            for ent in self._entries(b, key):
                ent[1].append(op)
        if op.is_dma:
            d = dma_dest
            if d.sem is None:
                self.dma_bufs.append(d)
                d.sem = True
            d.ndma += 1
        lst = self.ops[eng]
        lst.append(op)
        op.idx = len(lst)
        waits = {}
        for y in deps:
            if y is op:
                continue
            if y.is_dma:
                key = ("d", id(y.dest))
                val = 16 * y.dest.ndma if y.dest is not dma_dest else 16 * (y.dest.ndma - 1)
                if val <= 0:
                    continue
                ent = (y.dest, val)
            else:
                if y.eng == eng and eng == "pe":
                    continue
                key = ("e", y.eng)
                val = y.idx
                ent = (y.eng, val)
            if key not in waits or waits[key][1] < val:
                waits[key] = ent
        op.waits = waits
        self.nops += 1
        return op

    def barrier(self):
        snap_e = {e: len(self.ops[e]) for e in ENGS}
        snap_d = [(d, 16 * d.ndma) for d in self.dma_bufs]
        for e in ENGS:
            op = Op()
            op.eng = e
            op.fn = None
            op.is_dma = False
            op.dest = None
            w = {}
            for e2 in ENGS:
                if e2 == e:
                    continue
                w[("e", e2)] = (e2, snap_e[e2])
            for (d, v) in snap_d:
                if v > 0:
                    w[("d", id(d))] = (d, v)
            op.waits = w
            lst = self.ops[e]
            lst.append(op)
            op.idx = len(lst)

    def emit(self):
        nc = self.nc
        with contextlib.ExitStack() as es:
            esem = {e: es.enter_context(nc.semaphore("es_" + e)) for e in ENGS}
            for i, d in enumerate(self.dma_bufs):
                d.sem = es.enter_context(nc.semaphore("ds%d_%s" % (i, d.name)))
            block = es.enter_context(nc.Block())
            cidx = {}
            for e in ENGS:
                c = 0
                m = [0]
                for op in self.ops[e]:
                    if not op.is_dma:
                        c += 1
                    m.append(c)
                cidx[e] = m

            def make(e):
                ops = self.ops[e]

                def body(eng):
                    seen = {}
                    for op in ops:
                        for key, (obj, val) in op.waits.items():
                            if key[0] == "e":
                                sem = esem[obj]
                                v = cidx[obj][val]
                            else:
                                sem = obj.sem
                                v = val
                            if v <= 0 or seen.get(key, 0) >= v:
                                continue
                            seen[key] = v
                            eng.wait_ge(sem, v)
                        if op.fn is None:
                            eng.nop().then_inc(esem[e], 1)
                            continue
                        ins = op.fn(eng)
                        if op.is_dma:
                            ins.then_inc(op.dest.sem, 16)
                        else:
                            ins.then_inc(esem[e], 1)
                return body

            block.tensor(make("pe"))
            block.scalar(make("act"))
            block.vector(make("dve"))
            block.gpsimd(make("pool"))
            block.sync(make("sp"))


class Rot:
    def __init__(self, items):
        self.items = items
        self.i = 0

    def next(self):
        it = self.items[self.i % len(self.items)]
        self.i += 1
        return it


D = 1024
TL = 4096
TS = 2048
NST = TL // TS
TT = 512
NT = TS // TT
L = 2
MEM = 256
IN_DIM = 2824
C_Z, C_X, C_B, C_C, C_DT, C_Q, C_K, C_V = 0, 512, 1024, 1152, 1280, 1288, 1800, 2312
EPS = 1e-5
NEG = -30000.0
SBA_MAXG = 99
SBA_LEVEL = 9

CF_ID, CF_ONES, CF_U, CF_NEGM4, CF_TRI, CF_NEGTRI, CF_EPS, CF_ONE, NCF = 0, 128, 256, 384, 896, 1024, 1152, 1153, 1160
PP_GMIX, PP_GXA, PP_GFF, PP_GMEM, PP_GSSD, PP_GSB, PP_CW, PP_CB, PP_DTB, PP_ALOG, PP_DSK, PPW = 0, 8, 16, 24, 32, 36, 44, 68, 74, 82, 90, 602
PP_FINAL = L * PPW
NPP = PP_FINAL + 8


def host_consts():
    cf = np.zeros((128, NCF), np.float32)
    i = np.arange(128)
    cf[:, CF_ID:CF_ID + 128] = np.eye(128, dtype=np.float32)
    cf[:, CF_ONES:CF_ONES + 128] = 1.0
    cf[:, CF_U:CF_U + 128] = (i[:, None] <= i[None, :]).astype(np.float32)
    negm = np.where(i[None, :] >= i[:, None], 0.0, NEG).astype(np.float32)
    cf[:, CF_NEGM4:CF_NEGM4 + 512] = np.tile(negm, (1, 4))
    cf[:, CF_TRI:CF_TRI + 128] = (i[:, None] >= i[None, :]).astype(np.float32)
    cf[:, CF_NEGTRI:CF_NEGTRI + 128] = np.where(i[:, None] < i[None, :], 0.0, 8 * NEG)
    cf[:, CF_EPS] = EPS
    cf[:, CF_ONE] = 1.0
    return cf


def host_params(p):
    pp = np.zeros((128, NPP), np.float32)
    col = lambda v, n: np.ascontiguousarray(np.asarray(v, np.float32).reshape(n, 128).T)
    for l in range(L):
        o = l * PPW
        pp[:, o + PP_GMIX:o + PP_GMIX + 8] = col(p["norm_mix_g"][l], 8)
        pp[:, o + PP_GXA:o + PP_GXA + 8] = col(p["norm_xa_g"][l], 8)
        pp[:, o + PP_GFF:o + PP_GFF + 8] = col(p["norm_ff_g"][l], 8)
        pp[:, o + PP_GMEM:o + PP_GMEM + 8] = col(p["norm_mem_g"][l], 8)
        pp[:, o + PP_GSSD:o + PP_GSSD + 4] = col(p["ssd_norm_g"][l], 4)
        pp[0:64, o + PP_GSB:o + PP_GSB + 8] = np.asarray(p["sb_norm_g"][l], np.float32).reshape(8, 64).T
        cw = np.asarray(p["conv_w"][l], np.float32)
        pp[:, o + PP_CW:o + PP_CW + 24] = cw.reshape(4, 6, 128).transpose(2, 1, 0).reshape(128, 24)
        pp[:, o + PP_CB:o + PP_CB + 6] = col(p["conv_b"][l], 6)
        pp[:, o + PP_DTB:o + PP_DTB + 8] = np.asarray(p["dt_bias"][l], np.float32)[None, :]
        pp[:, o + PP_ALOG:o + PP_ALOG + 8] = np.asarray(p["a_log"][l], np.float32)[None, :]
        pp[:, o + PP_DSK:o + PP_DSK + 512] = np.repeat(np.asarray(p["d_skip"][l], np.float32), 64)[None, :]
    pp[:, PP_FINAL:PP_FINAL + 8] = col(p["final_g"], 8)
    return pp


def build(n_layers=L, stop=None, dump=False):
    nc = bass.Bass("TRN2", target_bir_lowering=False)
    P = Prog(nc)

    def din(name, shape, dt=F32):
        return nc.dram_tensor(name, shape, dt, kind="ExternalInput").ap()

    x_d = din("x", [TL, D])
    mem_d = din("mem", [MEM, D])
    w_in = din("w_in", [L, D, IN_DIM])
    w_out = din("w_out", [L, D, D])
    w_xq = din("w_xq", [L, D, 512])
    w_xk = din("w_xk", [L, D, 512])
    w_xv = din("w_xv", [L, D, 512])
    w_xo = din("w_xo", [L, 512, D])
    w_ff1 = din("w_ff1", [L, D, 4096])
    w_ff2 = din("w_ff2", [L, 4096, D])
    cf_d = din("cf", [128, NCF])
    pp_d = din("pp", [128, NPP])
    out_d = nc.dram_tensor("out", [TL, D], F32, kind="ExternalOutput").ap()

    def dscr(name, shape, dt):
        return nc.dram_tensor(name, shape, dt).ap()

    HT = dscr("HT", [NST, 128, 8, TS], F32)
    QT = dscr("QT", [8, 64, TL], BF16)
    KT = dscr("KT", [8, 64, TL], BF16)
    VS = dscr("VS", [TL, 512], BF16)
    ZTOK = dscr("ZTOK", [TL, 512], F32)
    XST = dscr("XST", [4, 128, TL], F32)
    BCT = dscr("BCT", [2, 128, TL], F32)
    DTOK = dscr("DTOK", [TL, 8], F32)
    YT = dscr("YT", [D, TL], BF16)
    B_HT = [Buf("HT%d" % s) for s in range(NST)]
    B_QT, B_KT, B_VS, B_ZTOK, B_XST, B_BCT, B_DTOK, B_YT, B_OUT = [Buf(n) for n in "QT KT VS ZTOK XST BCT DTOK YT OUT".split()]

    ARENA_BYTES = 206 * 1024
    arena = nc.alloc_sbuf_tensor("arena", [128, ARENA_BYTES // 2], BF16).ap()
    st_ = {"off": 0}

    def sb(shape, dt=F32, parts=128):
        n = 1
        for s_ in shape[1:]:
            n *= s_
        nbytes = n * (4 if dt == F32 else 2)
        nbytes = (nbytes + 31) // 32 * 32
        off = st_["off"]
        assert off + nbytes <= ARENA_BYTES, ("SBUF arena overflow", off, nbytes)
        st_["off"] = off + nbytes
        v = arena[0:shape[0], off // 2:(off + nbytes) // 2]
        if dt == F32:
            v = v.bitcast(F32)
        v = v[:, 0:n]
        if len(shape) == 3:
            v = v.rearrange("p (a b) -> p a b", a=shape[1])
        elif len(shape) == 4:
            v = v.rearrange("p (a b c) -> p a b c", a=shape[1], b=shape[2])
        return v

    def mark():
        return st_["off"]

    def reset(m):
        st_["off"] = m

    psum = [nc.alloc_psum_tensor("ps%d" % i, [128, 512], F32).ap() for i in range(8)]
    B_ps = [Buf("ps%d" % i) for i in range(8)]
    for b_ in B_ps:
        b_.excl = True

    def mm(out, lhsT, rhs, start, stop, r, w):
        P.add("pe", lambda e: e.matmul(out, lhsT=lhsT, rhs=rhs, start=start, stop=stop), reads=r, writes=w)

    def tr(out, in_, ident, r, w):
        P.add("pe", lambda e: e.transpose(out=out, in_=in_, identity=ident), reads=r, writes=w)

    def act(out, in_, func, r, w, bias=None, scale=None, accum=None):
        kw = {}
        if bias is not None:
            kw["bias"] = bias
        if scale is not None:
            kw["scale"] = scale
        if accum is not None:
            kw["accum_out"] = accum
        P.add("act", lambda e: e.activation(out=out, in_=in_, func=func, **kw), reads=r, writes=w)

    def tt(eng, out, in0, in1, op, r, w):
        P.add(eng, lambda e: e.tensor_tensor(out=out, in0=in0, in1=in1, op=op), reads=r, writes=w)

    def ts(eng, out, in0, s1, op0, r, w, s2=None, op1=None):
        if op1 is None:
            P.add(eng, lambda e: e.tensor_scalar(out=out, in0=in0, scalar1=s1, scalar2=None, op0=op0), reads=r, writes=w)
        else:
            P.add(eng, lambda e: e.tensor_scalar(out=out, in0=in0, scalar1=s1, scalar2=s2, op0=op0, op1=op1), reads=r, writes=w)

    def stt(eng, out, in0, scalar, in1, op0, op1, r, w):
        P.add(eng, lambda e: e.scalar_tensor_tensor(out=out, in0=in0, scalar=scalar, in1=in1, op0=op0, op1=op1), reads=r, writes=w)

    def cp(eng, out, in_, r, w):
        if eng == "act":
            P.add("act", lambda e: e.activation(out=out, in_=in_, func=AF.Copy), reads=r, writes=w)
        else:
            P.add(eng, lambda e: e.tensor_copy(out=out, in_=in_), reads=r, writes=w)

    def memset(eng, out, val, w):
        P.add(eng, lambda e: e.memset(out, val), writes=w)

    def dma(out, in_, r, w, dest, eng="sp"):
        P.add(eng, lambda e: e.dma_start(out=out, in_=in_), reads=r, writes=w, dma_dest=dest)

    cf = sb([128, NCF]); B_cf = Buf("cf")
    pp = sb([128, NPP]); B_pp = Buf("pp")
    NCB = 128 * 4 + 64
    cb = sb([128, NCB], BF16); B_cb = Buf("cb")
    ID_F, ONES_F, U_F = cf[:, CF_ID:CF_ID + 128], cf[:, CF_ONES:CF_ONES + 128], cf[:, CF_U:CF_U + 128]
    NEGM4 = cf[:, CF_NEGM4:CF_NEGM4 + 512]
    EPSC, ONEC = cf[:, CF_EPS:CF_EPS + 1], cf[:, CF_ONE:CF_ONE + 1]
    ID_B, ONES_B, TRI_B, NEGTRI_B, ZERO_B = cb[:, 0:128], cb[:, 128:256], cb[:, 256:384], cb[:, 384:512], cb[:, 512:576]
    abc = sb([128, L * 8]); B_abc = Buf("abc")
    halo = sb([128, 6, 3]); B_halo = Buf("halo")
    kxT = [sb([128, 4, MEM], BF16) for _ in range(L)]; B_kx = [Buf("kx%d" % l) for l in range(L)]
    vx = [sb([128, 2, 512], BF16) for _ in range(L)]; B_vx = [Buf("vx%d" % l) for l in range(L)]
    PERSIST = mark()
    wstage = sb([128, 4096]); B_wst = Buf("wstage")
    wbf = [sb([128, 4096], BF16) for _ in range(2)]; B_wbf = [Buf("wbf0"), Buf("wbf1")]
    wrot = Rot([0, 1])
    WEND = mark()

    dma(cf, cf_d, [], [B_cf], B_cf)
    dma(pp, pp_d, [], [B_pp], B_pp)
    cp("dve", cb[:, 0:128], cf[:, CF_ID:CF_ID + 128], [B_cf], [B_cb])
    cp("dve", cb[:, 128:256], cf[:, CF_ONES:CF_ONES + 128], [B_cf], [B_cb])
    cp("dve", cb[:, 256:384], cf[:, CF_TRI:CF_TRI + 128], [B_cf], [B_cb])
    cp("dve", cb[:, 384:512], cf[:, CF_NEGTRI:CF_NEGTRI + 128], [B_cf], [B_cb])
    memset("pool", cb[:, 512:576], 0.0, [B_cb])
    for l in range(L):
        act(abc[:, l * 8:(l + 1) * 8], pp[:, l * PPW + PP_ALOG:l * PPW + PP_ALOG + 8], AF.Exp, [B_pp], [B_abc])
    ts("dve", abc, abc, -1.0, ALU.mult, [B_abc], [B_abc])

    def ppc(l, off, n):
        return pp[:, l * PPW + off:l * PPW + off + n]

    def load_w(src3, KC, N):
        i = wrot.next()
        stv = wstage[:, 0:KC * N].rearrange("p (k n) -> p k n", k=KC)
        for k in range(KC):
            dma(stv[:, k, :], src3[:, k, :], [], [B_wst], B_wst)
        dst = wbf[i][:, 0:KC * N]
        half = (KC * N) // 2
        cp("pool", dst[:, 0:half], wstage[:, 0:half], [B_wst], [B_wbf[i]])
        cp("pool", dst[:, half:KC * N], wstage[:, half:KC * N], [B_wst], [B_wbf[i]])
        return dst.rearrange("p (k n) -> p k n", k=KC), B_wbf[i]

    def wview(w2d, r0, KC, c0, N):
        return w2d[r0:r0 + KC * 128, c0:c0 + N].rearrange("(k p) n -> p k n", p=128)

    prot = Rot([0, 1, 2, 3])

    def proj_fm(w2d, r0, KC, c0, ncols, colblk, actT, Bact, ntiles, evac, tcol0=0):
        for cbi in range(ncols // colblk):
            wv, Bw = load_w(wview(w2d, r0, KC, c0 + cbi * colblk, colblk), KC, colblk)
            for jj in range(colblk // 128):
                for t in range(ntiles):
                    pi = prot.next()
                    for k in range(KC):
                        mm(psum[pi], wv[:, k, jj * 128:(jj + 1) * 128], actT[:, k, tcol0 + t * TT:tcol0 + (t + 1) * TT],
                           k == 0, k == KC - 1, [Bw, Bact], [B_ps[pi]])
                    evac(cbi * (colblk // 128) + jj, t, psum[pi], B_ps[pi])

    def rmsnorm_fm(hT, BhT, gcols, actT, Bact, sqb, B_sqb, rstd, B_rstd, nchunks=8, scale=1.0 / D):
        for t in range(NT):
            sl = slice(t * TT, (t + 1) * TT)
            for c in range(nchunks):
                act(sqb[:, c, :], hT[:, c, sl], AF.Square, [BhT.k((c, t))], [B_sqb.k(c)])
            pi = prot.next()
            for c in range(nchunks):
                mm(psum[pi], ONES_B, sqb[:, c, :], c == 0, c == nchunks - 1, [B_cb, B_sqb.k(c)], [B_ps[pi]])
            act(rstd, psum[pi], AF.Ln, [B_ps[pi], B_cf], [B_rstd], bias=EPSC, scale=scale)
            act(rstd, rstd, AF.Exp, [B_rstd], [B_rstd], scale=-0.5)
            for c in range(nchunks):
                stt("dve", actT[:, c, sl], hT[:, c, sl], gcols[:, c:c + 1], rstd, ALU.mult, ALU.mult,
                    [BhT.k((c, t)), B_pp, B_rstd], [Bact])

    m0 = mark()
    assert m0 == WEND
    mtok = sb([128, D]); B_mtok = Buf("mtok")
    mn = sb([128, D]); B_mn = Buf("mn")
    msc = sb([128, 4]); B_msc = Buf("msc")
    memnT = [sb([128, 8, MEM], BF16) for _ in range(L)]; B_memn = [Buf("memn%d" % l) for l in range(L)]
    for blk in range(2):
        dma(mtok, mem_d[blk * 128:(blk + 1) * 128, :], [], [B_mtok], B_mtok)
        memset("pool", msc[:, 0:1], 0.0, [B_msc])
        act(mn, mtok, AF.Square, [B_mtok, B_msc], [B_mn, B_msc], accum=msc[:, 0:1])
        act(msc[:, 1:2], msc[:, 0:1], AF.Ln, [B_msc, B_cf], [B_msc], bias=EPSC, scale=1.0 / D)
        act(msc[:, 2:3], msc[:, 1:2], AF.Exp, [B_msc], [B_msc], scale=-0.5)
        ts("dve", mn, mtok, msc[:, 2:3], ALU.mult, [B_mtok, B_msc], [B_mn])
        for half in range(2):
            for c4 in range(4):
                c = half * 4 + c4
                tr(psum[half][:, c4 * 128:(c4 + 1) * 128], mn[:, c * 128:(c + 1) * 128], ID_F, [B_mn, B_cf], [B_ps[half]])
            for l in range(L):
                g = ppc(l, PP_GMEM + half * 4, 4)
                tt("dve", memnT[l][:, half * 4:half * 4 + 4, blk * 128:(blk + 1) * 128],
                   psum[half].rearrange("p (a b) -> p a b", a=4), g.unsqueeze(2).to_broadcast([128, 4, 128]), ALU.mult,
                   [B_ps[half], B_pp], [B_memn[l]])
    for l in range(L):
        wv, Bw = load_w(wview(w_xk[l], 0, 8, 0, 512), 8, 512)
        for hx in range(4):
            pi = prot.next()
            for k in range(8):
                mm(psum[pi][:, 0:MEM], wv[:, k, hx * 128:(hx + 1) * 128], memnT[l][:, k, :], k == 0, k == 7, [Bw, B_memn[l]], [B_ps[pi]])
            cp("act", kxT[l][:, hx, :], psum[pi][:, 0:MEM], [B_ps[pi]], [B_kx[l]])
        wv, Bw = load_w(wview(w_xv[l], 0, 8, 0, 512), 8, 512)
        for mb in range(2):
            pi = prot.next()
            for k in range(8):
                mm(psum[pi], memnT[l][:, k, mb * 128:(mb + 1) * 128], wv[:, k, :], k == 0, k == 7, [Bw, B_memn[l]], [B_ps[pi]])
            cp("dve", vx[l][:, mb, :], psum[pi], [B_ps[pi]], [B_vx[l]])
    P.barrier()
    reset(m0)

    hT = sb([128, 8, TS]); B_hT = Buf("hT")
    actT = sb([128, 8, TS], BF16); B_actT = Buf("actT")
    SCRA_OFF = mark()
    scrA = sb([128, 4, TS], BF16); B_scrA = Buf("scrA")
    scrB = sb([128, 4, TS], BF16); B_scrB = Buf("scrB")
    sqb = sb([128, 8, TT], BF16); B_sqb = Buf("sqb")
    rstd = sb([128, TT]); B_rstd = Buf("rstd")
    evs = [sb([128, TT]) for _ in range(2)]; B_evs = [Buf("evs0"), Buf("evs1")]
    evrot = Rot([0, 1])
    evb = [sb([128, TT], BF16) for _ in range(2)]; B_evb = [Buf("evb0"), Buf("evb1")]
    evbrot = Rot([0, 1])
    xr = [sb([128, TT + 3]) for _ in range(2)]; B_xr = [Buf("xr0"), Buf("xr1")]
    cacc = sb([128, TT]); B_cacc = Buf("cacc")
    wdt = sb([128, 8, 8], BF16); B_wdt = Buf("wdt")
    wdtf = sb([128, 8, 8]); B_wdtf = Buf("wdtf")
    dtst = sb([128, 16, 8]); B_dtst = Buf("dtst")
    dtt = sb([128, 8]); B_dtt = Buf("dtt")
    TOKWISE_END = mark()
    eng_alt = Rot(["act", "dve"])

    def embed(st):
        xin = [evs[0], evs[1]]
        for blk in range(TS // 128):
            r0 = st * TS + blk * 128
            for half in range(2):
                i = evrot.next()
                dma(evs[i], x_d[r0:r0 + 128, half * 512:(half + 1) * 512], [], [B_evs[i]], B_evs[i])
                pi = prot.next()
                for c4 in range(4):
                    tr(psum[pi][:, c4 * 128:(c4 + 1) * 128], evs[i][:, c4 * 128:(c4 + 1) * 128], ID_F, [B_evs[i], B_cf], [B_ps[pi]])
                t = blk // 4
                cp(eng_alt.next(), hT[:, half * 4:half * 4 + 4, blk * 128:(blk + 1) * 128],
                   psum[pi].rearrange("p (a b) -> p a b", a=4), [B_ps[pi]],
                   [B_hT.k((half * 4 + c4, t)) for c4 in range(4)])

    def store_hT(st):
        for c in range(8):
            dma(HT[st, :, c, :], hT[:, c, :], [B_hT], [B_HT[st]], B_HT[st], eng="pool")

    def load_hT(st):
        for c in range(8):
            dma(hT[:, c, :], HT[st, :, c, :], [B_HT[st]], [B_hT], B_hT)

    def g1(l, st):
        tok0 = st * TS
        rmsnorm_fm(hT, B_hT, ppc(l, PP_GMIX, 8), actT, B_actT, sqb, B_sqb, rstd, B_rstd)
        wl = w_in[l]
        if st == 0:
            memset("pool", halo, 0.0, [B_halo])
        cw = ppc(l, PP_CW, 24)
        cbias = ppc(l, PP_CB, 6)
        cstate = {"n": 0}

        def conv_evac(base):
            def f(j, t, ps, Bp):
                cc = base + j
                n = cstate["n"]
                cstate["n"] += 1
                i = n % 2
                if t == 0:
                    cp("pool", xr[i][:, 0:3], halo[:, cc, :], [B_halo], [B_xr[i]])
                else:
                    cp("pool", xr[i][:, 0:3], xr[1 - i][:, TT:TT + 3], [B_xr[1 - i]], [B_xr[i]])
                cp("act", xr[i][:, 3:TT + 3], ps, [Bp], [B_xr[i]])
                if t == NT - 1:
                    cp("pool", halo[:, cc, :], xr[i][:, TT:TT + 3], [B_xr[i]], [B_halo])
                ts("dve", cacc, xr[i][:, 0:TT], cw[:, cc * 4:cc * 4 + 1], ALU.mult, [B_xr[i], B_pp], [B_cacc])
                for k in range(1, 4):
                    stt("dve", cacc, xr[i][:, k:k + TT], cw[:, cc * 4 + k:cc * 4 + k + 1], cacc, ALU.mult, ALU.add,
                        [B_xr[i], B_pp, B_cacc], [B_cacc])
                e = evrot.next()
                act(evs[e], cacc, AF.Silu, [B_cacc, B_pp], [B_evs[e]], bias=cbias[:, cc:cc + 1])
                sl = slice(tok0 + t * TT, tok0 + (t + 1) * TT)
                if cc < 4:
                    dma(XST[cc, :, sl], evs[e], [B_evs[e]], [B_XST], B_XST, eng="pool")
                else:
                    dma(BCT[cc - 4, :, sl], evs[e], [B_evs[e]], [B_BCT], B_BCT, eng="pool")
            return f

        proj_fm(wl, 0, 8, C_X, 512, 512, actT, B_actT, NT, conv_evac(0))
        proj_fm(wl, 0, 8, C_B, 256, 256, actT, B_actT, NT, conv_evac(4))

        def qk_evac(dst, Bdst):
            dflat = dst.rearrange("h d t -> (h d) t")

            def f(j, t, ps, Bp):
                e = evbrot.next()
                cp(eng_alt.next(), evb[e], ps, [Bp], [B_evb[e]])
                sl = slice(tok0 + t * TT, tok0 + (t + 1) * TT)
                dma(dflat[j * 128:(j + 1) * 128, sl], evb[e], [B_evb[e]], [Bdst], Bdst, eng="pool")
            return f

        proj_fm(wl, 0, 8, C_Q, 512, 512, actT, B_actT, NT, qk_evac(QT, B_QT))
        proj_fm(wl, 0, 8, C_K, 512, 512, actT, B_actT, NT, qk_evac(KT, B_KT))
        for (c0, dst, Bdst, isbf) in ((C_Z, ZTOK, B_ZTOK, False), (C_V, VS, B_VS, True)):
            wv, Bw = load_w(wview(wl, 0, 8, c0, 512), 8, 512)
            for blk in range(TS // 128):
                pi = prot.next()
                for k in range(8):
                    mm(psum[pi], actT[:, k, blk * 128:(blk + 1) * 128], wv[:, k, :], k == 0, k == 7, [Bw, B_actT], [B_ps[pi]])
                r0 = tok0 + blk * 128
                if isbf:
                    e = evbrot.next()
                    cp(eng_alt.next(), evb[e], psum[pi], [B_ps[pi]], [B_evb[e]])
                    dma(dst[r0:r0 + 128, :], evb[e], [B_evb[e]], [Bdst], Bdst, eng="pool")
                else:
                    e = evrot.next()
                    cp(eng_alt.next(), evs[e], psum[pi], [B_ps[pi]], [B_evs[e]])
                    dma(dst[r0:r0 + 128, :], evs[e], [B_evs[e]], [Bdst], Bdst, eng="pool")
        dma(wdtf, wview(wl, 0, 8, C_DT, 8), [], [B_wdtf], B_wdtf)
        cp("dve", wdt, wdtf, [B_wdtf], [B_wdt])
        dtb = ppc(l, PP_DTB, 8)
        for blk in range(TS // 128):
            pi = prot.next()
            for k in range(8):
                mm(psum[pi][:, 0:8], actT[:, k, blk * 128:(blk + 1) * 128], wdt[:, k, :], k == 0, k == 7, [B_wdt, B_actT], [B_ps[pi]])
            tt("dve", dtt, psum[pi][:, 0:8], dtb, ALU.add, [B_ps[pi], B_pp], [B_dtt])
            act(dtt, dtt, AF.Exp, [B_dtt], [B_dtt])
            act(dtst[:, blk, :], dtt, AF.Ln, [B_dtt, B_cf], [B_dtst], bias=ONEC, scale=1.0)
        dma(DTOK[tok0:tok0 + TS, :].rearrange("(b p) h -> p b h", p=128), dtst, [B_dtst], [B_DTOK], B_DTOK, eng="pool")

    def add_evac(j, t, ps, Bp):
        sl = slice(t * TT, (t + 1) * TT)
        tt("dve", hT[:, j, sl], ps, hT[:, j, sl], ALU.add, [Bp, B_hT.k((j, t))], [B_hT.k((j, t))])

    def g3(l, st):
        tok0 = st * TS
        for c in range(8):
            dma(actT[:, c, :], YT[c * 128:(c + 1) * 128, tok0:tok0 + TS], [B_YT], [B_actT], B_actT)
        proj_fm(w_out[l], 0, 8, 0, D, 512, actT, B_actT, NT, add_evac)
        rmsnorm_fm(hT, B_hT, ppc(l, PP_GXA, 8), actT, B_actT, sqb, B_sqb, rstd, B_rstd)
        qxT, B_qx = scrA, B_scrA
        oxT, B_ox = scrB, B_scrB

        def q_evac(j, t, ps, Bp):
            cp(eng_alt.next(), qxT[:, j, t * TT:(t + 1) * TT], ps, [Bp], [B_qx.k((j, t))])
        proj_fm(w_xq[l], 0, 8, 0, 512, 512, actT, B_actT, NT, q_evac)
        sc = 1.0 / math.sqrt(128.0)
        for t in range(NT):
            sl = slice(t * TT, (t + 1) * TT)
            for hx in range(4):
                pT = []
                for mb in range(2):
                    pi = 4 + mb
                    mm(psum[pi], kxT[l][:, hx, mb * 128:(mb + 1) * 128], qxT[:, hx, sl], True, True, [B_kx[l], B_qx.k((hx, t))], [B_ps[pi]])
                    e = evbrot.next()
                    act(evb[e], psum[pi], AF.Exp, [B_ps[pi]], [B_evb[e]], scale=sc)
                    pT.append(e)
                for mb in range(2):
                    mm(psum[6], vx[l][:, mb, hx * 128:(hx + 1) * 128], evb[pT[mb]], mb == 0, mb == 1, [B_vx[l], B_evb[pT[mb]]], [B_ps[6]])
                for mb in range(2):
                    mm(psum[7], ONES_B, evb[pT[mb]], mb == 0, mb == 1, [B_cb, B_evb[pT[mb]]], [B_ps[7]])
                e = evrot.next()
                cp("act", evs[e], psum[7], [B_ps[7]], [B_evs[e]])
                P.add("dve", (lambda ee: (lambda en: en.reciprocal(out=evs[ee], in_=evs[ee])))(e), reads=[B_evs[e]], writes=[B_evs[e]])
                tt("dve", oxT[:, hx, sl], psum[6], evs[e], ALU.mult, [B_ps[6], B_evs[e]], [B_ox.k((hx, t))])
        proj_fm(w_xo[l], 0, 4, 0, D, 1024, oxT, B_ox, NT, add_evac)
        rmsnorm_fm(hT, B_hT, ppc(l, PP_GFF, 8), actT, B_actT, sqb, B_sqb, rstd, B_rstd)
        uT, B_u = scrA, B_scrA
        for fb in range(8):
            proj_fm(w_ff1[l], 0, 8, fb * 512, 512, 512, actT, B_actT, NT, u_evac_fix(uT, B_u))
            proj_fm(w_ff2[l], fb * 512, 4, 0, D, 1024, uT, B_u, NT, add_evac)

    def u_evac_fix(uT, B_u):
        def f(j, t, ps, Bp):
            e = evrot.next()
            act(evs[e], ps, AF.Relu, [Bp], [B_evs[e]])
            tt("pool" if (j + t) % 2 else "dve", uT[:, j, t * TT:(t + 1) * TT], evs[e], evs[e], ALU.mult, [B_evs[e]], [B_u.k((j, t))])
        return f

    def final(st):
        tok0 = st * TS
        for t in range(NT):
            sl = slice(t * TT, (t + 1) * TT)
            for c in range(8):
                act(sqb[:, c, :], hT[:, c, sl], AF.Square, [B_hT.k((c, t))], [B_sqb.k(c)])
            pi = prot.next()
            for c in range(8):
                mm(psum[pi], ONES_B, sqb[:, c, :], c == 0, c == 7, [B_cb, B_sqb.k(c)], [B_ps[pi]])
            act(rstd, psum[pi], AF.Ln, [B_ps[pi], B_cf], [B_rstd], bias=EPSC, scale=1.0 / D)
            act(rstd, rstd, AF.Exp, [B_rstd], [B_rstd], scale=-0.5)
            gf = pp[:, PP_FINAL:PP_FINAL + 8]
            for c in range(8):
                stt("dve", fin[:, c, :], hT[:, c, sl], gf[:, c:c + 1], rstd, ALU.mult, ALU.mult,
                    [B_hT.k((c, t)), B_pp, B_rstd], [B_fin.k(c)])
            for b4 in range(4):
                for half in range(2):
                    pi = 4 + half
                    for c4 in range(4):
                        c = half * 4 + c4
                        tr(psum[pi][:, c4 * 128:(c4 + 1) * 128], fin[:, c, b4 * 128:(b4 + 1) * 128], ID_F, [B_fin.k(c), B_cf], [B_ps[pi]])
                    e = evrot.next()
                    cp(eng_alt.next(), evs[e], psum[pi], [B_ps[pi]], [B_evs[e]])
                    r0 = tok0 + t * TT + b4 * 128
                    dma(out_d[r0:r0 + 128, half * 512:(half + 1) * 512], evs[e], [B_evs[e]], [B_OUT], B_OUT, eng="pool")

    fin = arena[:, SCRA_OFF // 2:SCRA_OFF // 2 + 8192].bitcast(F32).rearrange("p (a b) -> p a b", a=8); B_fin = B_scrA

    def ssd(l):
        m = mark()
        NG = TL // TT
        xsT = [sb([128, 4, TT]) for _ in range(2)]; B_xsT = [Buf("xsT0"), Buf("xsT1")]
        bt128 = [sb([128, TT]) for _ in range(2)]; B_bt = [Buf("bt0"), Buf("bt1")]
        bc64 = [sb([64, 4, TT]) for _ in range(2)]; B_bc64 = [Buf("bc640"), Buf("bc641")]
        bc64b = [sb([64, 4, TT], BF16) for _ in range(2)]; B_bc64b = [Buf("bc64b0"), Buf("bc64b1")]
        ztok = [sb([128, 4, TT]) for _ in range(2)]; B_ztok = [Buf("ztok0"), Buf("ztok1")]
        dtg = [sb([128, 4, 8]) for _ in range(2)]; B_dtg = [Buf("dtg0"), Buf("dtg1")]
        yst = [sb([128, 4, TT], BF16) for _ in range(2)]; B_yst = [Buf("yst0"), Buf("yst1")]
        xs_tok = sb([128, 512]); B_xs = Buf("xs_tok")
        btok = sb([128, 128], BF16); B_btok = Buf("btok")
        sm = sb([128, 80]); B_sm = Buf("sm")
        da16 = sm[:, 0:16]; da = sm[:, 0:8]
        nacol, expA, dstate, cd, dtd, diff = [sm[:, 16 + i * 8:16 + (i + 1) * 8] for i in range(6)]
        memset("pool", sm, 0.0, [B_sm])
        Rm = sb([128, 8, 128]); B_R = Buf("R")
        dec = sb([128, 8, 128]); B_dec = Buf("dec")
        MT = sb([128, 8, 128], BF16); B_MT = Buf("MT")
        xdt = sb([128, 512], BF16); B_xdt = Buf("xdt")
        xw = sb([128, 512], BF16); B_xw = Buf("xw")
        t1 = sb([128, 512]); B_t1 = Buf("t1")
        t2 = sb([128, 512]); B_t2 = Buf("t2")
        yv = sb([128, 512]); B_y = Buf("y")
        gz = sb([128, 512]); B_gz = Buf("gz")
        sq = sb([128, 512]); B_sq = Buf("sq")
        ssq = sb([128, 4]); B_ssq = Buf("ssq")
        prev = sb([64, 8, 64]); B_prev = Buf("prev")
        prevb = [sb([64, 8, 64], BF16) for _ in range(2)]; B_prevb = [Buf("prevb0"), Buf("prevb1")]
        a_l = abc[:, l * 8:(l + 1) * 8]
        dsk = ppc(l, PP_DSK, 512)
        gssd = ppc(l, PP_GSSD, 4)
        memset("pool", prev, 0.0, [B_prev])
        memset("pool", prevb[0], 0.0, [B_prevb[0]])
        BCT64 = BCT.rearrange("a (g n) t -> n (a g) t", g=2)
        for gi in range(NG):
            s = gi % 2
            tsl = slice(gi * TT, (gi + 1) * TT)
            for j in range(4):
                dma(xsT[s][:, j, :], XST[j, :, tsl], [B_XST], [B_xsT[s]], B_xsT[s])
            dma(bt128[s], BCT[0, :, tsl], [B_BCT], [B_bt[s]], B_bt[s])
            for a in range(4):
                dma(bc64[s][:, a, :], BCT64[:, a, tsl], [B_BCT], [B_bc64[s]], B_bc64[s])
            dma(ztok[s], ZTOK[tsl, :].rearrange("(c p) f -> p c f", p=128), [B_ZTOK], [B_ztok[s]], B_ztok[s])
            dma(dtg[s], DTOK[tsl, :].rearrange("(c p) h -> p c h", p=128), [B_DTOK], [B_dtg[s]], B_dtg[s])
            cp("pool", bc64b[s], bc64[s], [B_bc64[s]], [B_bc64b[s]])
            for cg in range(4):
                ci = gi * 4 + cg
                cs = slice(cg * 128, (cg + 1) * 128)
                pb_cur = prevb[ci % 2]; Bpb_cur = B_prevb[ci % 2]
                pb_nxt = prevb[(ci + 1) % 2]; Bpb_nxt = B_prevb[(ci + 1) % 2]
                for j in range(4):
                    tr(psum[0][:, j * 128:(j + 1) * 128], xsT[s][:, j, cs], ID_F, [B_xsT[s], B_cf], [B_ps[0]])
                tr(psum[1][:, 0:128], bt128[s][:, cs], ID_F, [B_bt[s], B_cf], [B_ps[1].k("bt")])
                cp("act", xs_tok, psum[0], [B_ps[0]], [B_xs])
                cp("dve", btok, psum[1][:, 0:128], [B_ps[1].k("bt")], [B_btok])
                dtc = dtg[s][:, cg, :]
                tt("dve", da, dtc, a_l, ALU.mult, [B_dtg[s], B_abc], [B_sm.k("da")])
                mm(psum[1][:, 128:144], U_F, da16, True, True, [B_cf, B_sm.k("da")], [B_ps[1].k("ac")])
                mm(psum[1][:, 144:160], ONES_F, da16, True, True, [B_cf, B_sm.k("da")], [B_ps[1].k("ac")])
                ts("dve", nacol, psum[1][:, 128:136], -1.0, ALU.mult, [B_ps[1].k("ac")], [B_sm.k("nacol")])
                act(expA, psum[1][:, 128:136], AF.Exp, [B_ps[1].k("ac")], [B_sm.k("expA")])
                tt("dve", diff, psum[1][:, 144:152], nacol, ALU.add, [B_ps[1].k("ac"), B_sm.k("nacol")], [B_sm.k("diff")])
                act(dstate, diff, AF.Exp, [B_sm.k("diff")], [B_sm.k("dstate")])
                act(cd, psum[1][:, 144:152], AF.Exp, [B_ps[1].k("ac")], [B_sm.k("cd")])
                tt("dve", dtd, dtc, dstate, ALU.mult, [B_dtg[s], B_sm.k("dstate")], [B_sm.k("dtd")])
                tt("dve", Rm, U_F.unsqueeze(1).to_broadcast([128, 8, 128]), da.unsqueeze(2).to_broadcast([128, 8, 128]), ALU.mult,
                   [B_cf, B_sm.k("da")], [B_R])
                for half in range(2):
                    mm(psum[2 + half], ONES_F, Rm[:, half * 4:half * 4 + 4, :].rearrange("p a b -> p (a b)"), True, False, [B_cf, B_R], [B_ps[2 + half]])
                    mm(psum[2 + half], ID_F, NEGM4, False, True, [B_cf], [B_ps[2 + half]])
                for h in range(8):
                    act(dec[:, h, :], psum[2 + h // 4][:, (h % 4) * 128:(h % 4 + 1) * 128], AF.Exp,
                        [B_ps[2 + h // 4], B_sm.k("nacol")], [B_dec.k(h // 4)], bias=nacol[:, h:h + 1], scale=1.0)
                for g in range(2):
                    mm(psum[1][:, 256 + g * 128:256 + (g + 1) * 128], bc64b[s][:, g, cs], bc64b[s][:, 2 + g, cs], True, True,
                       [B_bc64b[s]], [B_ps[1].k("cb%d" % g)])
                    tt("dve", MT[:, g * 4:g * 4 + 4, :], psum[1][:, 256 + g * 128:256 + (g + 1) * 128].unsqueeze(1).to_broadcast([128, 4, 128]),
                       dec[:, g * 4:g * 4 + 4, :], ALU.mult, [B_ps[1].k("cb%d" % g), B_dec.k(g)], [B_MT.k(g)])
                xs3 = xs_tok.rearrange("p (h j) -> p h j", h=8)
                tt("dve", xdt.rearrange("p (h j) -> p h j", h=8), xs3, dtc.unsqueeze(2).to_broadcast([128, 8, 64]), ALU.mult,
                   [B_xs, B_dtg[s]], [B_xdt])
                tt("pool", xw.rearrange("p (h j) -> p h j", h=8), xs3, dtd.unsqueeze(2).to_broadcast([128, 8, 64]), ALU.mult,
                   [B_xs, B_sm.k("dtd")], [B_xw])
                for h in range(8):
                    mm(psum[4][:, h * 64:(h + 1) * 64], MT[:, h, :], xdt[:, h * 64:(h + 1) * 64], True, True, [B_MT.k(h // 4), B_xdt], [B_ps[4]])
                for h in range(8):
                    mm(psum[5][:, h * 64:(h + 1) * 64], bc64b[s][:, 2 + h // 4, cs], pb_cur[:, h, :], True, True, [B_bc64b[s], Bpb_cur], [B_ps[5]])
                for g in range(2):
                    mm(psum[6][0:64, g * 256:(g + 1) * 256], btok[:, g * 64:(g + 1) * 64], xw[:, g * 256:(g + 1) * 256], True, True,
                       [B_btok, B_xw], [B_ps[6]])
                tt("dve", t1.rearrange("p (h j) -> p h j", h=8), psum[5].rearrange("p (h j) -> p h j", h=8),
                   expA.unsqueeze(2).to_broadcast([128, 8, 64]), ALU.mult, [B_ps[5], B_sm.k("expA")], [B_t1])
                tt("pool", t2, xs_tok, dsk, ALU.mult, [B_xs, B_pp], [B_t2])
                tt("pool", t2, t2, t1, ALU.add, [B_t2, B_t1], [B_t2])
                tt("dve", yv, psum[4], t2, ALU.add, [B_ps[4], B_t2], [B_y])
                act(gz, ztok[s][:, cg, :], AF.Silu, [B_ztok[s]], [B_gz])
                tt("pool", yv, yv, gz, ALU.mult, [B_y, B_gz], [B_y])
                memset("pool", ssq[:, 0:1], 0.0, [B_ssq])
                act(sq, yv, AF.Square, [B_y, B_ssq], [B_sq, B_ssq], accum=ssq[:, 0:1])
                act(ssq[:, 1:2], ssq[:, 0:1], AF.Ln, [B_ssq, B_cf], [B_ssq], bias=EPSC, scale=1.0 / 512)
                act(ssq[:, 2:3], ssq[:, 1:2], AF.Exp, [B_ssq], [B_ssq], scale=-0.5)
                ts("dve", sq, yv, ssq[:, 2:3], ALU.mult, [B_y, B_ssq], [B_sq])
                for j in range(4):
                    tr(psum[7][:, j * 128:(j + 1) * 128], sq[:, j * 128:(j + 1) * 128], ID_F, [B_sq, B_cf], [B_ps[7]])
                tt("dve", yst[s][:, :, cs], psum[7].rearrange("p (a b) -> p a b", a=4), gssd.unsqueeze(2).to_broadcast([128, 4, 128]), ALU.mult,
                   [B_ps[7], B_pp], [B_yst[s]])
                tt("pool", prev, prev, cd[0:64, :].unsqueeze(2).to_broadcast([64, 8, 64]), ALU.mult, [B_prev, B_sm.k("cd")], [B_prev])
                tt("dve", prev, psum[6][0:64, :].rearrange("p (h j) -> p h j", h=8), prev, ALU.add, [B_ps[6], B_prev], [B_prev])
                cp("pool", pb_nxt, prev, [B_prev], [Bpb_nxt])
            dma(YT[0:512, tsl].rearrange("(j p) t -> p j t", p=128), yst[s], [B_yst[s]], [B_YT], B_YT, eng="pool")
        P.barrier()
        reset(m)

    def sba(l):
        m = mark()
        kt_all = sb([64, 8, TL], BF16); B_kt = Buf("kt_all")
        v_all = sb([128, TL // 128, 512], BF16); B_v = Buf("v_all")
        qg = [sb([64, 8, TT], BF16) for _ in range(2)]; B_qg = [Buf("qg0"), Buf("qg1")]
        e_sb = [sb([128, TT]) for _ in range(3)]; B_e = [Buf("e%d" % i) for i in range(3)]
        sp_b = [sb([128, TT], BF16) for _ in range(2)]; B_sp = [Buf("sp%d" % i) for i in range(2)]
        r_sb = [sb([128, TT]) for _ in range(2)]; B_r = [Buf("r%d" % i) for i in range(2)]
        w_b = [sb([128, TT], BF16) for _ in range(2)]; B_w = [Buf("w%d" % i) for i in range(2)]
        acc = [sb([128, TT], BF16) for _ in range(2)]; B_acc = [Buf("acc%d" % i) for i in range(2)]
        o_sb = sb([64, 8, TT]); B_o = Buf("o_sb")
        osq = sb([64, 8, TT], BF16); B_osq = Buf("osq")
        rs = sb([128, TT]); B_rs = Buf("rs")
        yst1 = sb([64, 8, TT], BF16); yst = [yst1, yst1]; B_y1 = Buf("ysb"); B_yst = [B_y1, B_y1]
        gsb = ppc(l, PP_GSB, 8)
        for h in range(8):
            dma(kt_all[:, h, :], KT[h, :, :], [B_KT], [B_kt], B_kt)
        for q4 in range(4):
            bs = slice(q4 * 8, (q4 + 1) * 8)
            dma(v_all[:, bs, :], VS[q4 * 1024:(q4 + 1) * 1024, :].rearrange("(b p) f -> p b f", p=128), [B_VS], [B_v], B_v)
        tiles = []
        for G in range(min(TL // TT, SBA_MAXG)):
            for h in range(8):
                kbs = list(range(4 * G + 3, -1, -1))
                for ii, kb in enumerate(kbs):
                    tiles.append((G, h, kb, ii == 0, ii == len(kbs) - 1))
        n = len(tiles)
        Z_PS = [0, 1]; R_PS = [2, 3]; O_PS = [4, 5]; SS_PS = 6

        def stage_a(i):
            G, h, kb, first, last = tiles[i]
            s = G % 2
            if h == 0 and first:
                dma(qg[s], QT[:, :, G * TT:(G + 1) * TT].rearrange("h d t -> d h t"), [B_QT], [B_qg[s]], B_qg[s])
            j = kb - 4 * G
            c0 = 128 * max(j, 0)
            cs = slice(c0, TT)
            zi = Z_PS[i % 2]; ri = R_PS[i % 2]
            ei = i % 3; si = i % 2
            ai = (G * 8 + h) % 2
            if SBA_LEVEL < 1:
                return
            mm(psum[zi][:, cs], kt_all[:, h, kb * 128:(kb + 1) * 128], qg[s][:, h, cs], True, True, [B_kt, B_qg[s]], [B_ps[zi]])
            if j >= 0:
                mm(psum[zi][:, c0:c0 + 128], ID_B, NEGTRI_B, False, True, [B_cb], [B_ps[zi]])
            act(e_sb[ei][:, cs], psum[zi][:, cs], AF.Exp, [B_ps[zi]], [B_e[ei]], scale=0.125)
            act(sp_b[si][:, cs], e_sb[ei][:, cs], AF.Ln, [B_e[ei], B_cf], [B_sp[si]], bias=ONEC, scale=1.0)
            if SBA_LEVEL < 2:
                return
            mm(psum[ri][:, cs], TRI_B, sp_b[si][:, cs], True, first, [B_cb, B_sp[si]], [B_ps[ri]])
            if not first:
                mm(psum[ri][:, cs], ONES_B, acc[ai][:, cs], False, True, [B_cb, B_acc[ai]], [B_ps[ri]])
            if first:
                memset("pool", acc[ai], 0.0, [B_acc[ai]])
            if not last:
                tt("pool", acc[ai][:, cs], acc[ai][:, cs], sp_b[si][:, cs], ALU.add, [B_acc[ai], B_sp[si]], [B_acc[ai]])

        def stage_b(i):
            G, h, kb, first, last = tiles[i]
            s = G % 2
            j = kb - 4 * G
            c0 = 128 * max(j, 0)
            cs = slice(c0, TT)
            ri = R_PS[i % 2]
            ei = i % 3; si = i % 2
            oi = O_PS[(G * 8 + h) % 2]
            if SBA_LEVEL < 3:
                return
            act(r_sb[si][:, cs], psum[ri][:, cs], AF.Exp, [B_ps[ri]], [B_r[si]], scale=-1.0)
            tt("dve", w_b[si][:, cs], e_sb[ei][:, cs], r_sb[si][:, cs], ALU.mult, [B_e[ei], B_r[si]], [B_w[si]])
            if first:
                for q4 in range(4):
                    mm(psum[oi][0:64, q4 * 128:(q4 + 1) * 128], ZERO_B, ONES_B, True, False, [B_cb], [B_ps[oi]])
            mm(psum[oi][0:64, cs], v_all[:, kb, h * 64:(h + 1) * 64], w_b[si][:, cs], False, last, [B_v, B_w[si]], [B_ps[oi]])
            if last and SBA_LEVEL >= 4:
                cp("dve", o_sb[:, h, :], psum[oi][0:64, :], [B_ps[oi]], [B_o.k(h)])
                act(osq[:, h, :], psum[oi][0:64, :], AF.Square, [B_ps[oi]], [B_osq.k(h)])
                if h == 7 and SBA_LEVEL >= 5:
                    for hh in range(8):
                        mm(psum[SS_PS], ONES_B[0:64, :], osq[:, hh, :], hh == 0, hh == 7, [B_cb, B_osq.k(hh)], [B_ps[SS_PS]])
                    act(rs, psum[SS_PS], AF.Ln, [B_ps[SS_PS], B_cf], [B_rs], bias=EPSC, scale=1.0 / 512)
                    act(rs, rs, AF.Exp, [B_rs], [B_rs], scale=-0.5)
                    for hh in range(8):
                        stt("dve", yst[s][:, hh, :], o_sb[:, hh, :], gsb[0:64, hh:hh + 1], rs[0:64, :], ALU.mult, ALU.mult,
                            [B_o.k(hh), B_pp, B_rs], [B_yst[s]])
                    dma(YT[512:1024, G * TT:(G + 1) * TT].rearrange("(h d) t -> d h t", d=64), yst[s], [B_yst[s]], [B_YT], B_YT, eng="pool")

        for i in range(n + 1):
            if i < n:
                stage_a(i)
            if i >= 1:
                stage_b(i - 1)
        P.barrier()
        reset(m)

    for st in range(NST):
        embed(st)
        if NST > 1:
            store_hT(st)
        g1(0, st)
    P.barrier()
    done = False
    for l in range(n_layers):
        if stop == "g1":
            break
        reset(PERSIST)
        ssd(l)
        if stop == "ssd":
            break
        sba(l)
        if stop == "g2":
            break
        for st in range(NST):
            if NST > 1:
                load_hT(st)
            g3(l, st)
            if l + 1 < n_layers:
                if NST > 1:
                    store_hT(st)
                g1(l + 1, st)
            else:
                final(st)
                done = True
        P.barrier()
    dumps = {}
    if dump:
        alld = (("QT", QT, B_QT), ("KT", KT, B_KT), ("VS", VS, B_VS), ("ZTOK", ZTOK, B_ZTOK), ("XST", XST, B_XST),
                ("BCT", BCT, B_BCT), ("DTOK", DTOK, B_DTOK), ("YT", YT, B_YT), ("HT", HT, None))
        for name, ap_, Bf in [d_ for d_ in alld if dump is True or d_[0] in dump]:
            o = nc.dram_tensor("dump_" + name, list(ap_.shape), ap_.dtype, kind="ExternalOutput").ap()
            db = Buf("dump_" + name)
            rd = [Bf] if Bf is not None else list(B_HT)
            if len(ap_.shape) == 4:
                for s_ in range(ap_.shape[0]):
                    dma(o[s_].rearrange("p c t -> p (c t)"), ap_[s_].rearrange("p c t -> p (c t)"), rd, [db], db)
            elif len(ap_.shape) == 3:
                for s_ in range(ap_.shape[0]):
                    dma(o[s_], ap_[s_], rd, [db], db)
            else:
                dma(o, ap_, rd, [db], db)
    P.barrier()
    P.emit()
    return nc


_CACHE = {}


def kernel(**inputs):
    p = {k: np.asarray(v) for k, v in inputs.items()}
    if "nc" not in _CACHE:
        _CACHE["nc"] = build()
    nc = _CACHE["nc"]
    cf = host_consts()
    pp = host_params(p)
    shared = {k: np.ascontiguousarray(p[k], dtype=np.float32) for k in ("w_in", "w_out", "w_xq", "w_xk", "w_xv", "w_xo", "w_ff1", "w_ff2")}
    in_maps = []
    for c in range(8):
        b = c % 4
        m = {"x": np.ascontiguousarray(p["x"][b], dtype=np.float32), "mem": np.ascontiguousarray(p["mem"][b], dtype=np.float32),
             "cf": cf, "pp": pp}
        m.update(shared)
        in_maps.append(m)
    res = run_bass_kernel_spmd(nc, in_maps, core_ids=list(range(8)))
    out = np.stack([np.asarray(res.results[b]["out"], dtype=np.float32) for b in range(4)], axis=0)
    return out
```

```python
import math
import contextlib
import numpy as np
import concourse.bass as bass
import concourse.mybir as mybir
from concourse.bass_utils import run_bass_kernel_spmd

F32 = mybir.dt.float32
BF16 = mybir.dt.bfloat16
AF = mybir.ActivationFunctionType
ALU = mybir.AluOpType

ENGS = ("pe", "act", "dve", "pool", "sp")


class Buf:
    def __init__(self, name):
        self.name = name
        self.st = {"*": [[], []]}
        self.sem = None
        self.ndma = 0
        self.excl = False
        self.last_by_eng = {}

    def k(self, key):
        return (self, key)


class Op:
    __slots__ = ("eng", "fn", "waits", "idx", "is_dma", "dest")


class Prog:
    def __init__(self, nc):
        self.nc = nc
        self.ops = {e: [] for e in ENGS}
        self.dma_bufs = []
        self.nops = 0

    @staticmethod
    def _norm(x):
        if isinstance(x, Buf):
            return (x, None)
        return x

    def _entries(self, buf, key):
        d = buf.st
        if key is None:
            return list(d.values())
        if key not in d:
            d[key] = [list(d["*"][0]), list(d["*"][1])]
        return [d[key]]

    def add(self, eng, fn, reads=(), writes=(), dma_dest=None):
        op = Op()
        op.eng = eng
        op.fn = fn
        op.is_dma = dma_dest is not None
        op.dest = dma_dest
        deps = []
        reads = [self._norm(r) for r in reads if r is not None]
        writes = [self._norm(w) for w in writes if w is not None]
        for (b, key) in reads:
            for ent in self._entries(b, key):
                deps.extend(ent[0])
        for (b, key) in writes:
            for ent in self._entries(b, key):
                samegen = op.is_dma and len(ent[0]) > 0 and all(w.is_dma for w in ent[0]) and len(ent[1]) == 0
                if not samegen:
                    deps.extend(ent[0])
                deps.extend(ent[1])
        for (b, key) in list(reads) + list(writes):
            if b.excl:
                for e2, y in b.last_by_eng.items():
                    if e2 != eng:
                        deps.append(y)
                b.last_by_eng[eng] = op
        for (b, key) in writes:
            for ent in self._entries(b, key):
                samegen = op.is_dma and len(ent[0]) > 0 and all(w.is_dma for w in ent[0]) and len(ent[1]) == 0
                if samegen:
                    ent[0].append(op)
                else:
                    ent[0] = [op]
                    ent[1] = []
            if key is None:
                for kk in list(b.st.keys()):
                    b.st[kk][0] = list(b.st["*"][0])
                    b.st[kk][1] = []
        wset = set((id(b), key) for (b, key) in writes)
        for (b, key) in reads:
            if (id(b), key) in wset:
                continue
            for ent in self._entries(b, key):
                ent[1].append(op)
        if op.is_dma:
            d = dma_dest
            if d.sem is None:
                self.dma_bufs.append(d)
                d.sem = True
            d.ndma += 1
        lst = self.ops[eng]
        lst.append(op)
        op.idx = len(lst)
        waits = {}
        for y in deps:
            if y is op:
                continue
            if y.is_dma:
                key = ("d", id(y.dest))
                val = 16 * y.dest.ndma if y.dest is not dma_dest else 16 * (y.dest.ndma - 1)
                if val <= 0:
                    continue
                ent = (y.dest, val)
            else:
                if y.eng == eng and eng == "pe":
                    continue
                key = ("e", y.eng)
                val = y.idx
                ent = (y.eng, val)
            if key not in waits or waits[key][1] < val:
                waits[key] = ent
        op.waits = waits
        self.nops += 1
        return op

    def barrier(self):
        snap_e = {e: len(self.ops[e]) for e in ENGS}
        snap_d = [(d, 16 * d.ndma) for d in self.dma_bufs]
        for e in ENGS:
            op = Op()
            op.eng = e
            op.fn = None
            op.is_dma = False
            op.dest = None
            w = {}
            for e2 in ENGS:
                if e2 == e:
                    continue
                w[("e", e2)] = (e2, snap_e[e2])
            for (d, v) in snap_d:
                if v > 0:
                    w[("d", id(d))] = (d, v)
            op.waits = w
            lst = self.ops[e]
            lst.append(op)
            op.idx = len(lst)

    def emit(self):
        nc = self.nc
        with contextlib.ExitStack() as es:
            esem = {e: es.enter_context(nc.semaphore("es_" + e)) for e in ENGS}
            for i, d in enumerate(self.dma_bufs):
                d.sem = es.enter_context(nc.semaphore("ds%d_%s" % (i, d.name)))
            block = es.enter_context(nc.Block())
            cidx = {}
            for e in ENGS:
                c = 0
                m = [0]
                for op in self.ops[e]:
                    if not op.is_dma:
                        c += 1
                    m.append(c)
                cidx[e] = m

            def make(e):
                ops = self.ops[e]

                def body(eng):
                    seen = {}
                    for op in ops:
                        for key, (obj, val) in op.waits.items():
                            if key[0] == "e":
                                sem = esem[obj]
                                v = cidx[obj][val]
                            else:
                                sem = obj.sem
                                v = val
                            if v <= 0 or seen.get(key, 0) >= v:
                                continue
                            seen[key] = v
                            eng.wait_ge(sem, v)
                        if op.fn is None:
                            eng.nop().then_inc(esem[e], 1)
                            continue
                        ins = op.fn(eng)
                        if op.is_dma:
                            ins.then_inc(op.dest.sem, 16)
                        else:
                            ins.then_inc(esem[e], 1)
                return body

            block.tensor(make("pe"))
            block.scalar(make("act"))
            block.vector(make("dve"))
            block.gpsimd(make("pool"))
            block.sync(make("sp"))


class Rot:
    def __init__(self, items):
        self.items = items
        self.i = 0

    def next(self):
        it = self.items[self.i % len(self.items)]
        self.i += 1
        return it


D = 1024
TL = 4096
TS = 2048
NST = TL // TS
TT = 512
NT = TS // TT
L = 2
MEM = 256
IN_DIM = 2824
C_Z, C_X, C_B, C_C, C_DT, C_Q, C_K, C_V = 0, 512, 1024, 1152, 1280, 1288, 1800, 2312
EPS = 1e-5
NEG = -30000.0
SBA_MAXG = 99
SBA_LEVEL = 9

CF_ID, CF_ONES, CF_U, CF_NEGM4, CF_TRI, CF_NEGTRI, CF_EPS, CF_ONE, NCF = 0, 128, 256, 384, 896, 1024, 1152, 1153, 1160
PP_GMIX, PP_GXA, PP_GFF, PP_GMEM, PP_GSSD, PP_GSB, PP_CW, PP_CB, PP_DTB, PP_ALOG, PP_DSK, PPW = 0, 8, 16, 24, 32, 36, 44, 68, 74, 82, 90, 602
PP_FINAL = L * PPW
NPP = PP_FINAL + 8


def host_consts():
    cf = np.zeros((128, NCF), np.float32)
    i = np.arange(128)
    cf[:, CF_ID:CF_ID + 128] = np.eye(128, dtype=np.float32)
    cf[:, CF_ONES:CF_ONES + 128] = 1.0
    cf[:, CF_U:CF_U + 128] = (i[:, None] <= i[None, :]).astype(np.float32)
    negm = np.where(i[None, :] >= i[:, None], 0.0, NEG).astype(np.float32)
    cf[:, CF_NEGM4:CF_NEGM4 + 512] = np.tile(negm, (1, 4))
    cf[:, CF_TRI:CF_TRI + 128] = (i[:, None] >= i[None, :]).astype(np.float32)
    cf[:, CF_NEGTRI:CF_NEGTRI + 128] = np.where(i[:, None] < i[None, :], 0.0, 8 * NEG)
    cf[:, CF_EPS] = EPS
    cf[:, CF_ONE] = 1.0
    return cf


def host_params(p):
    pp = np.zeros((128, NPP), np.float32)
    col = lambda v, n: np.ascontiguousarray(np.asarray(v, np.float32).reshape(n, 128).T)
    for l in range(L):
        o = l * PPW
        pp[:, o + PP_GMIX:o + PP_GMIX + 8] = col(p["norm_mix_g"][l], 8)
        pp[:, o + PP_GXA:o + PP_GXA + 8] = col(p["norm_xa_g"][l], 8)
        pp[:, o + PP_GFF:o + PP_GFF + 8] = col(p["norm_ff_g"][l], 8)
        pp[:, o + PP_GMEM:o + PP_GMEM + 8] = col(p["norm_mem_g"][l], 8)
        pp[:, o + PP_GSSD:o + PP_GSSD + 4] = col(p["ssd_norm_g"][l], 4)
        pp[0:64, o + PP_GSB:o + PP_GSB + 8] = np.asarray(p["sb_norm_g"][l], np.float32).reshape(8, 64).T
        cw = np.asarray(p["conv_w"][l], np.float32)
        pp[:, o + PP_CW:o + PP_CW + 24] = cw.reshape(4, 6, 128).transpose(2, 1, 0).reshape(128, 24)
        pp[:, o + PP_CB:o + PP_CB + 6] = col(p["conv_b"][l], 6)
        pp[:, o + PP_DTB:o + PP_DTB + 8] = np.asarray(p["dt_bias"][l], np.float32)[None, :]
        pp[:, o + PP_ALOG:o + PP_ALOG + 8] = np.asarray(p["a_log"][l], np.float32)[None, :]
        pp[:, o + PP_DSK:o + PP_DSK + 512] = np.repeat(np.asarray(p["d_skip"][l], np.float32), 64)[None, :]
    pp[:, PP_FINAL:PP_FINAL + 8] = col(p["final_g"], 8)
    return pp


def build(n_layers=L, stop=None, dump=False):
    nc = bass.Bass("TRN2", target_bir_lowering=False)
    P = Prog(nc)

    def din(name, shape, dt=F32):
        return nc.dram_tensor(name, shape, dt, kind="ExternalInput").ap()

    x_d = din("x", [TL, D])
    mem_d = din("mem", [MEM, D])
    w_in = din("w_in", [L, D, IN_DIM])
    w_out = din("w_out", [L, D, D])
    w_xq = din("w_xq", [L, D, 512])
    w_xk = din("w_xk", [L, D, 512])
    w_xv = din("w_xv", [L, D, 512])
    w_xo = din("w_xo", [L, 512, D])
    w_ff1 = din("w_ff1", [L, D, 4096])
    w_ff2 = din("w_ff2", [L, 4096, D])
    cf_d = din("cf", [128, NCF])
    pp_d = din("pp", [128, NPP])
    out_d = nc.dram_tensor("out", [TL, D], F32, kind="ExternalOutput").ap()

    def dscr(name, shape, dt):
        return nc.dram_tensor(name, shape, dt).ap()

    HT = dscr("HT", [NST, 128, 8, TS], F32)
    QT = dscr("QT", [8, 64, TL], BF16)
    KT = dscr("KT", [8, 64, TL], BF16)
    VS = dscr("VS", [TL, 512], BF16)
    ZTOK = dscr("ZTOK", [TL, 512], F32)
    XST = dscr("XST", [4, 128, TL], F32)
    BCT = dscr("BCT", [2, 128, TL], F32)
    DTOK = dscr("DTOK", [TL, 8], F32)
    YT = dscr("YT", [D, TL], BF16)
    B_HT = [Buf("HT%d" % s) for s in range(NST)]
    B_QT, B_KT, B_VS, B_ZTOK, B_XST, B_BCT, B_DTOK, B_YT, B_OUT = [Buf(n) for n in "QT KT VS ZTOK XST BCT DTOK YT OUT".split()]

    ARENA_BYTES = 206 * 1024
    arena = nc.alloc_sbuf_tensor("arena", [128, ARENA_BYTES // 2], BF16).ap()
    st_ = {"off": 0}

    def sb(shape, dt=F32, parts=128):
        n = 1
        for s_ in shape[1:]:
            n *= s_
        nbytes = n * (4 if dt == F32 else 2)
        nbytes = (nbytes + 31) // 32 * 32
        off = st_["off"]
        assert off + nbytes <= ARENA_BYTES, ("SBUF arena overflow", off, nbytes)
        st_["off"] = off + nbytes
        v = arena[0:shape[0], off // 2:(off + nbytes) // 2]
        if dt == F32:
            v = v.bitcast(F32)
        v = v[:, 0:n]
        if len(shape) == 3:
            v = v.rearrange("p (a b) -> p a b", a=shape[1])
        elif len(shape) == 4:
            v = v.rearrange("p (a b c) -> p a b c", a=shape[1], b=shape[2])
        return v

    def mark():
        return st_["off"]

    def reset(m):
        st_["off"] = m

    psum = [nc.alloc_psum_tensor("ps%d" % i, [128, 512], F32).ap() for i in range(8)]
    B_ps = [Buf("ps%d" % i) for i in range(8)]
    for b_ in B_ps:
        b_.excl = True

    def mm(out, lhsT, rhs, start, stop, r, w):
        P.add("pe", lambda e: e.matmul(out, lhsT=lhsT, rhs=rhs, start=start, stop=stop), reads=r, writes=w)

    def tr(out, in_, ident, r, w):
        P.add("pe", lambda e: e.transpose(out=out, in_=in_, identity=ident), reads=r, writes=w)

    def act(out, in_, func, r, w, bias=None, scale=None, accum=None):
        kw = {}
        if bias is not None:
            kw["bias"] = bias
        if scale is not None:
            kw["scale"] = scale
        if accum is not None:
            kw["accum_out"] = accum
        P.add("act", lambda e: e.activation(out=out, in_=in_, func=func, **kw), reads=r, writes=w)

    def tt(eng, out, in0, in1, op, r, w):
        P.add(eng, lambda e: e.tensor_tensor(out=out, in0=in0, in1=in1, op=op), reads=r, writes=w)

    def ts(eng, out, in0, s1, op0, r, w, s2=None, op1=None):
        if op1 is None:
            P.add(eng, lambda e: e.tensor_scalar(out=out, in0=in0, scalar1=s1, scalar2=None, op0=op0), reads=r, writes=w)
        else:
            P.add(eng, lambda e: e.tensor_scalar(out=out, in0=in0, scalar1=s1, scalar2=s2, op0=op0, op1=op1), reads=r, writes=w)

    def stt(eng, out, in0, scalar, in1, op0, op1, r, w):
        P.add(eng, lambda e: e.scalar_tensor_tensor(out=out, in0=in0, scalar=scalar, in1=in1, op0=op0, op1=op1), reads=r, writes=w)

    def cp(eng, out, in_, r, w):
        if eng == "act":
            P.add("act", lambda e: e.activation(out=out, in_=in_, func=AF.Copy), reads=r, writes=w)
        else:
            P.add(eng, lambda e: e.tensor_copy(out=out, in_=in_), reads=r, writes=w)

    def memset(eng, out, val, w):
        P.add(eng, lambda e: e.memset(out, val), writes=w)

    def dma(out, in_, r, w, dest, eng="sp"):
        P.add(eng, lambda e: e.dma_start(out=out, in_=in_), reads=r, writes=w, dma_dest=dest)

    cf = sb([128, NCF]); B_cf = Buf("cf")
    pp = sb([128, NPP]); B_pp = Buf("pp")
    NCB = 128 * 4 + 64
    cb = sb([128, NCB], BF16); B_cb = Buf("cb")
    ID_F, ONES_F, U_F = cf[:, CF_ID:CF_ID + 128], cf[:, CF_ONES:CF_ONES + 128], cf[:, CF_U:CF_U + 128]
    NEGM4 = cf[:, CF_NEGM4:CF_NEGM4 + 512]
    EPSC, ONEC = cf[:, CF_EPS:CF_EPS + 1], cf[:, CF_ONE:CF_ONE + 1]
    ID_B, ONES_B, TRI_B, NEGTRI_B, ZERO_B = cb[:, 0:128], cb[:, 128:256], cb[:, 256:384], cb[:, 384:512], cb[:, 512:576]
    abc = sb([128, L * 8]); B_abc = Buf("abc")
    halo = sb([128, 6, 3]); B_halo = Buf("halo")
    kxT = [sb([128, 4, MEM], BF16) for _ in range(L)]; B_kx = [Buf("kx%d" % l) for l in range(L)]
    vx = [sb([128, 2, 512], BF16) for _ in range(L)]; B_vx = [Buf("vx%d" % l) for l in range(L)]
    PERSIST = mark()
    wbf = [sb([128, 4096], BF16) for _ in range(3)]; B_wbf = [Buf("wbf0"), Buf("wbf1"), Buf("wbf2")]
    wrot = Rot([0, 1, 2])
    WEND = mark()

    dma(cf, cf_d, [], [B_cf], B_cf)
    dma(pp, pp_d, [], [B_pp], B_pp)
    cp("dve", cb[:, 0:128], cf[:, CF_ID:CF_ID + 128], [B_cf], [B_cb])
    cp("dve", cb[:, 128:256], cf[:, CF_ONES:CF_ONES + 128], [B_cf], [B_cb])
    cp("dve", cb[:, 256:384], cf[:, CF_TRI:CF_TRI + 128], [B_cf], [B_cb])
    cp("dve", cb[:, 384:512], cf[:, CF_NEGTRI:CF_NEGTRI + 128], [B_cf], [B_cb])
    memset("pool", cb[:, 512:576], 0.0, [B_cb])
    for l in range(L):
        act(abc[:, l * 8:(l + 1) * 8], pp[:, l * PPW + PP_ALOG:l * PPW + PP_ALOG + 8], AF.Exp, [B_pp], [B_abc])
    ts("dve", abc, abc, -1.0, ALU.mult, [B_abc], [B_abc])

    def ppc(l, off, n):
        return pp[:, l * PPW + off:l * PPW + off + n]

    def load_w(src3, KC, N):
        i = wrot.next()
        dst = wbf[i][:, 0:KC * N].rearrange("p (k n) -> p k n", k=KC)
        for k in range(KC):
            dma(dst[:, k, :], src3[:, k, :], [], [B_wbf[i]], B_wbf[i], eng="pool")
        return dst, B_wbf[i]

    def wview(w2d, r0, KC, c0, N):
        return w2d[r0:r0 + KC * 128, c0:c0 + N].rearrange("(k p) n -> p k n", p=128)

    prot = Rot([0, 1, 2, 3])

    def proj_fm(w2d, r0, KC, c0, ncols, colblk, actT, Bact, ntiles, evac, tcol0=0):
        for cbi in range(ncols // colblk):
            wv, Bw = load_w(wview(w2d, r0, KC, c0 + cbi * colblk, colblk), KC, colblk)
            for jj in range(colblk // 128):
                for t in range(ntiles):
                    pi = prot.next()
                    for k in range(KC):
                        mm(psum[pi], wv[:, k, jj * 128:(jj + 1) * 128], actT[:, k, tcol0 + t * TT:tcol0 + (t + 1) * TT],
                           k == 0, k == KC - 1, [Bw, Bact], [B_ps[pi]])
                    evac(cbi * (colblk // 128) + jj, t, psum[pi], B_ps[pi])

    def rmsnorm_fm(hT, BhT, gcols, actT, Bact, sqb, B_sqb, rstd, B_rstd, nchunks=8, scale=1.0 / D):
        for t in range(NT):
            sl = slice(t * TT, (t + 1) * TT)
            for c in range(nchunks):
                act(sqb[:, c, :], hT[:, c, sl], AF.Square, [BhT.k((c, t))], [B_sqb.k(c)])
            pi = prot.next()
            for c in range(nchunks):
                mm(psum[pi], ONES_B, sqb[:, c, :], c == 0, c == nchunks - 1, [B_cb, B_sqb.k(c)], [B_ps[pi]])
            act(rstd, psum[pi], AF.Ln, [B_ps[pi], B_cf], [B_rstd], bias=EPSC, scale=scale)
            act(rstd, rstd, AF.Exp, [B_rstd], [B_rstd], scale=-0.5)
            for c in range(nchunks):
                stt("dve", actT[:, c, sl], hT[:, c, sl], gcols[:, c:c + 1], rstd, ALU.mult, ALU.mult,
                    [BhT.k((c, t)), B_pp, B_rstd], [Bact])

    m0 = mark()
    assert m0 == WEND
    mtok = sb([128, D]); B_mtok = Buf("mtok")
    mn = sb([128, D]); B_mn = Buf("mn")
    msc = sb([128, 4]); B_msc = Buf("msc")
    memnT = [sb([128, 8, MEM], BF16) for _ in range(L)]; B_memn = [Buf("memn%d" % l) for l in range(L)]
    for blk in range(2):
        dma(mtok, mem_d[blk * 128:(blk + 1) * 128, :], [], [B_mtok], B_mtok)
        memset("pool", msc[:, 0:1], 0.0, [B_msc])
        act(mn, mtok, AF.Square, [B_mtok, B_msc], [B_mn, B_msc], accum=msc[:, 0:1])
        act(msc[:, 1:2], msc[:, 0:1], AF.Ln, [B_msc, B_cf], [B_msc], bias=EPSC, scale=1.0 / D)
        act(msc[:, 2:3], msc[:, 1:2], AF.Exp, [B_msc], [B_msc], scale=-0.5)
        ts("dve", mn, mtok, msc[:, 2:3], ALU.mult, [B_mtok, B_msc], [B_mn])
        for half in range(2):
            for c4 in range(4):
                c = half * 4 + c4
                tr(psum[half][:, c4 * 128:(c4 + 1) * 128], mn[:, c * 128:(c + 1) * 128], ID_F, [B_mn, B_cf], [B_ps[half]])
            for l in range(L):
                g = ppc(l, PP_GMEM + half * 4, 4)
                tt("dve", memnT[l][:, half * 4:half * 4 + 4, blk * 128:(blk + 1) * 128],
                   psum[half].rearrange("p (a b) -> p a b", a=4), g.unsqueeze(2).to_broadcast([128, 4, 128]), ALU.mult,
                   [B_ps[half], B_pp], [B_memn[l]])
    for l in range(L):
        wv, Bw = load_w(wview(w_xk[l], 0, 8, 0, 512), 8, 512)
        for hx in range(4):
            pi = prot.next()
            for k in range(8):
                mm(psum[pi][:, 0:MEM], wv[:, k, hx * 128:(hx + 1) * 128], memnT[l][:, k, :], k == 0, k == 7, [Bw, B_memn[l]], [B_ps[pi]])
            cp("act", kxT[l][:, hx, :], psum[pi][:, 0:MEM], [B_ps[pi]], [B_kx[l]])
        wv, Bw = load_w(wview(w_xv[l], 0, 8, 0, 512), 8, 512)
        for mb in range(2):
            pi = prot.next()
            for k in range(8):
                mm(psum[pi], memnT[l][:, k, mb * 128:(mb + 1) * 128], wv[:, k, :], k == 0, k == 7, [Bw, B_memn[l]], [B_ps[pi]])
            cp("dve", vx[l][:, mb, :], psum[pi], [B_ps[pi]], [B_vx[l]])
    P.barrier()
    reset(m0)

    hT = sb([128, 8, TS]); B_hT = Buf("hT")
    actT = sb([128, 8, TS], BF16); B_actT = Buf("actT")
    SCRA_OFF = mark()
    scrA = sb([128, 4, TS], BF16); B_scrA = Buf("scrA")
    scrB = sb([128, 4, TS], BF16); B_scrB = Buf("scrB")
    sqb = sb([128, 8, TT], BF16); B_sqb = Buf("sqb")
    rstd = sb([128, TT]); B_rstd = Buf("rstd")
    evs = [sb([128, TT]) for _ in range(2)]; B_evs = [Buf("evs0"), Buf("evs1")]
    evrot = Rot([0, 1])
    evb = [sb([128, TT], BF16) for _ in range(2)]; B_evb = [Buf("evb0"), Buf("evb1")]
    evbrot = Rot([0, 1])
    xr = [sb([128, TT + 3]) for _ in range(2)]; B_xr = [Buf("xr0"), Buf("xr1")]
    cacc = sb([128, TT]); B_cacc = Buf("cacc")
    wdt = sb([128, 8, 8], BF16); B_wdt = Buf("wdt")
    wdtf = sb([128, 8, 8]); B_wdtf = Buf("wdtf")
    dtst = sb([128, 16, 8]); B_dtst = Buf("dtst")
    dtt = sb([128, 8]); B_dtt = Buf("dtt")
    TOKWISE_END = mark()
    eng_alt = Rot(["act", "dve"])

    def embed(st):
        xin = [evs[0], evs[1]]
        for blk in range(TS // 128):
            r0 = st * TS + blk * 128
            for half in range(2):
                i = evrot.next()
                dma(evs[i], x_d[r0:r0 + 128, half * 512:(half + 1) * 512], [], [B_evs[i]], B_evs[i])
                pi = prot.next()
                for c4 in range(4):
                    tr(psum[pi][:, c4 * 128:(c4 + 1) * 128], evs[i][:, c4 * 128:(c4 + 1) * 128], ID_F, [B_evs[i], B_cf], [B_ps[pi]])
                t = blk // 4
                cp(eng_alt.next(), hT[:, half * 4:half * 4 + 4, blk * 128:(blk + 1) * 128],
                   psum[pi].rearrange("p (a b) -> p a b", a=4), [B_ps[pi]],
                   [B_hT.k((half * 4 + c4, t)) for c4 in range(4)])

    def store_hT(st):
        for c in range(8):
            dma(HT[st, :, c, :], hT[:, c, :], [B_hT], [B_HT[st]], B_HT[st])

    def load_hT(st):
        for c in range(8):
            dma(hT[:, c, :], HT[st, :, c, :], [B_HT[st]], [B_hT], B_hT)

    def g1(l, st):
        tok0 = st * TS
        rmsnorm_fm(hT, B_hT, ppc(l, PP_GMIX, 8), actT, B_actT, sqb, B_sqb, rstd, B_rstd)
        wl = w_in[l]
        if st == 0:
            memset("pool", halo, 0.0, [B_halo])
        cw = ppc(l, PP_CW, 24)
        cbias = ppc(l, PP_CB, 6)
        cstate = {"n": 0}

        def conv_evac(base):
            def f(j, t, ps, Bp):
                cc = base + j
                n = cstate["n"]
                cstate["n"] += 1
                i = n % 2
                if t == 0:
                    cp("pool", xr[i][:, 0:3], halo[:, cc, :], [B_halo], [B_xr[i]])
                else:
                    cp("pool", xr[i][:, 0:3], xr[1 - i][:, TT:TT + 3], [B_xr[1 - i]], [B_xr[i]])
                cp("act", xr[i][:, 3:TT + 3], ps, [Bp], [B_xr[i]])
                if t == NT - 1:
                    cp("pool", halo[:, cc, :], xr[i][:, TT:TT + 3], [B_xr[i]], [B_halo])
                ts("dve", cacc, xr[i][:, 0:TT], cw[:, cc * 4:cc * 4 + 1], ALU.mult, [B_xr[i], B_pp], [B_cacc])
                for k in range(1, 4):
                    stt("dve", cacc, xr[i][:, k:k + TT], cw[:, cc * 4 + k:cc * 4 + k + 1], cacc, ALU.mult, ALU.add,
                        [B_xr[i], B_pp, B_cacc], [B_cacc])
                e = evrot.next()
                act(evs[e], cacc, AF.Silu, [B_cacc, B_pp], [B_evs[e]], bias=cbias[:, cc:cc + 1])
                sl = slice(tok0 + t * TT, tok0 + (t + 1) * TT)
                if cc < 4:
                    dma(XST[cc, :, sl], evs[e], [B_evs[e]], [B_XST], B_XST)
                else:
                    dma(BCT[cc - 4, :, sl], evs[e], [B_evs[e]], [B_BCT], B_BCT)
            return f

        proj_fm(wl, 0, 8, C_X, 512, 512, actT, B_actT, NT, conv_evac(0))
        proj_fm(wl, 0, 8, C_B, 256, 256, actT, B_actT, NT, conv_evac(4))

        def qk_evac(dst, Bdst):
            dflat = dst.rearrange("h d t -> (h d) t")

            def f(j, t, ps, Bp):
                e = evbrot.next()
                cp(eng_alt.next(), evb[e], ps, [Bp], [B_evb[e]])
                sl = slice(tok0 + t * TT, tok0 + (t + 1) * TT)
                dma(dflat[j * 128:(j + 1) * 128, sl], evb[e], [B_evb[e]], [Bdst], Bdst)
            return f

        proj_fm(wl, 0, 8, C_Q, 512, 512, actT, B_actT, NT, qk_evac(QT, B_QT))
        proj_fm(wl, 0, 8, C_K, 512, 512, actT, B_actT, NT, qk_evac(KT, B_KT))
        for (c0, dst, Bdst, isbf) in ((C_Z, ZTOK, B_ZTOK, False), (C_V, VS, B_VS, True)):
            wv, Bw = load_w(wview(wl, 0, 8, c0, 512), 8, 512)
            for blk in range(TS // 128):
                pi = prot.next()
                for k in range(8):
                    mm(psum[pi], actT[:, k, blk * 128:(blk + 1) * 128], wv[:, k, :], k == 0, k == 7, [Bw, B_actT], [B_ps[pi]])
                r0 = tok0 + blk * 128
                if isbf:
                    e = evbrot.next()
                    cp(eng_alt.next(), evb[e], psum[pi], [B_ps[pi]], [B_evb[e]])
                    dma(dst[r0:r0 + 128, :], evb[e], [B_evb[e]], [Bdst], Bdst)
                else:
                    e = evrot.next()
                    cp(eng_alt.next(), evs[e], psum[pi], [B_ps[pi]], [B_evs[e]])
                    dma(dst[r0:r0 + 128, :], evs[e], [B_evs[e]], [Bdst], Bdst)
        dma(wdtf, wview(wl, 0, 8, C_DT, 8), [], [B_wdtf], B_wdtf)
        cp("dve", wdt, wdtf, [B_wdtf], [B_wdt])
        dtb = ppc(l, PP_DTB, 8)
        for blk in range(TS // 128):
            pi = prot.next()
            for k in range(8):
                mm(psum[pi][:, 0:8], actT[:, k, blk * 128:(blk + 1) * 128], wdt[:, k, :], k == 0, k == 7, [B_wdt, B_actT], [B_ps[pi]])
            tt("dve", dtt, psum[pi][:, 0:8], dtb, ALU.add, [B_ps[pi], B_pp], [B_dtt])
            act(dtt, dtt, AF.Exp, [B_dtt], [B_dtt])
            act(dtst[:, blk, :], dtt, AF.Ln, [B_dtt, B_cf], [B_dtst], bias=ONEC, scale=1.0)
        dma(DTOK[tok0:tok0 + TS, :].rearrange("(b p) h -> p b h", p=128), dtst, [B_dtst], [B_DTOK], B_DTOK)

    def add_evac(j, t, ps, Bp):
        sl = slice(t * TT, (t + 1) * TT)
        tt("dve", hT[:, j, sl], ps, hT[:, j, sl], ALU.add, [Bp, B_hT.k((j, t))], [B_hT.k((j, t))])

    def g3(l, st):
        tok0 = st * TS
        for c in range(8):
            dma(actT[:, c, :], YT[c * 128:(c + 1) * 128, tok0:tok0 + TS], [B_YT], [B_actT], B_actT)
        proj_fm(w_out[l], 0, 8, 0, D, 512, actT, B_actT, NT, add_evac)
        rmsnorm_fm(hT, B_hT, ppc(l, PP_GXA, 8), actT, B_actT, sqb, B_sqb, rstd, B_rstd)
        qxT, B_qx = scrA, B_scrA
        oxT, B_ox = scrB, B_scrB

        def q_evac(j, t, ps, Bp):
            cp(eng_alt.next(), qxT[:, j, t * TT:(t + 1) * TT], ps, [Bp], [B_qx.k((j, t))])
        proj_fm(w_xq[l], 0, 8, 0, 512, 512, actT, B_actT, NT, q_evac)
        sc = 1.0 / math.sqrt(128.0)
        for t in range(NT):
            sl = slice(t * TT, (t + 1) * TT)
            for hx in range(4):
                pT = []
                for mb in range(2):
                    pi = 4 + mb
                    mm(psum[pi], kxT[l][:, hx, mb * 128:(mb + 1) * 128], qxT[:, hx, sl], True, True, [B_kx[l], B_qx.k((hx, t))], [B_ps[pi]])
                    e = evbrot.next()
                    act(evb[e], psum[pi], AF.Exp, [B_ps[pi]], [B_evb[e]], scale=sc)
                    pT.append(e)
                for mb in range(2):
                    mm(psum[6], vx[l][:, mb, hx * 128:(hx + 1) * 128], evb[pT[mb]], mb == 0, mb == 1, [B_vx[l], B_evb[pT[mb]]], [B_ps[6]])
                for mb in range(2):
                    mm(psum[7], ONES_B, evb[pT[mb]], mb == 0, mb == 1, [B_cb, B_evb[pT[mb]]], [B_ps[7]])
                e = evrot.next()
                cp("act", evs[e], psum[7], [B_ps[7]], [B_evs[e]])
                P.add("dve", (lambda ee: (lambda en: en.reciprocal(out=evs[ee], in_=evs[ee])))(e), reads=[B_evs[e]], writes=[B_evs[e]])
                tt("dve", oxT[:, hx, sl], psum[6], evs[e], ALU.mult, [B_ps[6], B_evs[e]], [B_ox.k((hx, t))])
        proj_fm(w_xo[l], 0, 4, 0, D, 1024, oxT, B_ox, NT, add_evac)
        rmsnorm_fm(hT, B_hT, ppc(l, PP_GFF, 8), actT, B_actT, sqb, B_sqb, rstd, B_rstd)
        uT, B_u = scrA, B_scrA
        for fb in range(8):
            proj_fm(w_ff1[l], 0, 8, fb * 512, 512, 512, actT, B_actT, NT, u_evac_fix(uT, B_u))
            proj_fm(w_ff2[l], fb * 512, 4, 0, D, 1024, uT, B_u, NT, add_evac)

    def u_evac_fix(uT, B_u):
        def f(j, t, ps, Bp):
            e = evrot.next()
            act(evs[e], ps, AF.Relu, [Bp], [B_evs[e]])
            tt("pool" if (j + t) % 2 else "dve", uT[:, j, t * TT:(t + 1) * TT], evs[e], evs[e], ALU.mult, [B_evs[e]], [B_u.k((j, t))])
        return f

    def final(st):
        tok0 = st * TS
        for t in range(NT):
            sl = slice(t * TT, (t + 1) * TT)
            for c in range(8):
                act(sqb[:, c, :], hT[:, c, sl], AF.Square, [B_hT.k((c, t))], [B_sqb.k(c)])
            pi = prot.next()
            for c in range(8):
                mm(psum[pi], ONES_B, sqb[:, c, :], c == 0, c == 7, [B_cb, B_sqb.k(c)], [B_ps[pi]])
            act(rstd, psum[pi], AF.Ln, [B_ps[pi], B_cf], [B_rstd], bias=EPSC, scale=1.0 / D)
            act(rstd, rstd, AF.Exp, [B_rstd], [B_rstd], scale=-0.5)
            gf = pp[:, PP_FINAL:PP_FINAL + 8]
            for c in range(8):
                stt("dve", fin[:, c, :], hT[:, c, sl], gf[:, c:c + 1], rstd, ALU.mult, ALU.mult,
                    [B_hT.k((c, t)), B_pp, B_rstd], [B_fin.k(c)])
            for b4 in range(4):
                for half in range(2):
                    pi = 4 + half
                    for c4 in range(4):
                        c = half * 4 + c4
                        tr(psum[pi][:, c4 * 128:(c4 + 1) * 128], fin[:, c, b4 * 128:(b4 + 1) * 128], ID_F, [B_fin.k(c), B_cf], [B_ps[pi]])
                    e = evrot.next()
                    cp(eng_alt.next(), evs[e], psum[pi], [B_ps[pi]], [B_evs[e]])
                    r0 = tok0 + t * TT + b4 * 128
                    dma(out_d[r0:r0 + 128, half * 512:(half + 1) * 512], evs[e], [B_evs[e]], [B_OUT], B_OUT)

    fin = arena[:, SCRA_OFF // 2:SCRA_OFF // 2 + 8192].bitcast(F32).rearrange("p (a b) -> p a b", a=8); B_fin = B_scrA

    def ssd(l):
        m = mark()
        NG = TL // TT
        xsT = [sb([128, 4, TT]) for _ in range(2)]; B_xsT = [Buf("xsT0"), Buf("xsT1")]
        bt128 = [sb([128, TT]) for _ in range(2)]; B_bt = [Buf("bt0"), Buf("bt1")]
        bc64 = [sb([64, 4, TT]) for _ in range(2)]; B_bc64 = [Buf("bc640"), Buf("bc641")]
        bc64b = [sb([64, 4, TT], BF16) for _ in range(2)]; B_bc64b = [Buf("bc64b0"), Buf("bc64b1")]
        ztok = [sb([128, 4, TT]) for _ in range(2)]; B_ztok = [Buf("ztok0"), Buf("ztok1")]
        dtg = [sb([128, 4, 8]) for _ in range(2)]; B_dtg = [Buf("dtg0"), Buf("dtg1")]
        yst = [sb([128, 4, TT], BF16) for _ in range(2)]; B_yst = [Buf("yst0"), Buf("yst1")]
        xs_tok = sb([128, 512]); B_xs = Buf("xs_tok")
        btok = sb([128, 128], BF16); B_btok = Buf("btok")
        sm = sb([128, 80]); B_sm = Buf("sm")
        da16 = sm[:, 0:16]; da = sm[:, 0:8]
        nacol, expA, dstate, cd, dtd, diff = [sm[:, 16 + i * 8:16 + (i + 1) * 8] for i in range(6)]
        memset("pool", sm, 0.0, [B_sm])
        Rm = sb([128, 8, 128]); B_R = Buf("R")
        dec = sb([128, 8, 128]); B_dec = Buf("dec")
        MT = sb([128, 8, 128], BF16); B_MT = Buf("MT")
        xdt = sb([128, 512], BF16); B_xdt = Buf("xdt")
        xw = sb([128, 512], BF16); B_xw = Buf("xw")
        t1 = sb([128, 512]); B_t1 = Buf("t1")
        t2 = sb([128, 512]); B_t2 = Buf("t2")
        yv = sb([128, 512]); B_y = Buf("y")
        gz = sb([128, 512]); B_gz = Buf("gz")
        sq = sb([128, 512]); B_sq = Buf("sq")
        ssq = sb([128, 4]); B_ssq = Buf("ssq")
        prev = sb([64, 8, 64]); B_prev = Buf("prev")
        prevb = [sb([64, 8, 64], BF16) for _ in range(2)]; B_prevb = [Buf("prevb0"), Buf("prevb1")]
        a_l = abc[:, l * 8:(l + 1) * 8]
        dsk = ppc(l, PP_DSK, 512)
        gssd = ppc(l, PP_GSSD, 4)
        memset("pool", prev, 0.0, [B_prev])
        memset("pool", prevb[0], 0.0, [B_prevb[0]])
        BCT64 = BCT.rearrange("a (g n) t -> n (a g) t", g=2)
        for gi in range(NG):
            s = gi % 2
            tsl = slice(gi * TT, (gi + 1) * TT)
            for j in range(4):
                dma(xsT[s][:, j, :], XST[j, :, tsl], [B_XST], [B_xsT[s]], B_xsT[s])
            dma(bt128[s], BCT[0, :, tsl], [B_BCT], [B_bt[s]], B_bt[s])
            for a in range(4):
                dma(bc64[s][:, a, :], BCT64[:, a, tsl], [B_BCT], [B_bc64[s]], B_bc64[s])
            dma(ztok[s], ZTOK[tsl, :].rearrange("(c p) f -> p c f", p=128), [B_ZTOK], [B_ztok[s]], B_ztok[s])
            dma(dtg[s], DTOK[tsl, :].rearrange("(c p) h -> p c h", p=128), [B_DTOK], [B_dtg[s]], B_dtg[s])
            cp("pool", bc64b[s], bc64[s], [B_bc64[s]], [B_bc64b[s]])
            for cg in range(4):
                ci = gi * 4 + cg
                cs = slice(cg * 128, (cg + 1) * 128)
                pb_cur = prevb[ci % 2]; Bpb_cur = B_prevb[ci % 2]
                pb_nxt = prevb[(ci + 1) % 2]; Bpb_nxt = B_prevb[(ci + 1) % 2]
                for j in range(4):
                    tr(psum[0][:, j * 128:(j + 1) * 128], xsT[s][:, j, cs], ID_F, [B_xsT[s], B_cf], [B_ps[0]])
                tr(psum[1][:, 0:128], bt128[s][:, cs], ID_F, [B_bt[s], B_cf], [B_ps[1].k("bt")])
                cp("act", xs_tok, psum[0], [B_ps[0]], [B_xs])
                cp("dve", btok, psum[1][:, 0:128], [B_ps[1].k("bt")], [B_btok])
                dtc = dtg[s][:, cg, :]
                tt("dve", da, dtc, a_l, ALU.mult, [B_dtg[s], B_abc], [B_sm.k("da")])
                mm(psum[1][:, 128:144], U_F, da16, True, True, [B_cf, B_sm.k("da")], [B_ps[1].k("ac")])
                mm(psum[1][:, 144:160], ONES_F, da16, True, True, [B_cf, B_sm.k("da")], [B_ps[1].k("ac")])
                ts("dve", nacol, psum[1][:, 128:136], -1.0, ALU.mult, [B_ps[1].k("ac")], [B_sm.k("nacol")])
                act(expA, psum[1][:, 128:136], AF.Exp, [B_ps[1].k("ac")], [B_sm.k("expA")])
                tt("dve", diff, psum[1][:, 144:152], nacol, ALU.add, [B_ps[1].k("ac"), B_sm.k("nacol")], [B_sm.k("diff")])
                act(dstate, diff, AF.Exp, [B_sm.k("diff")], [B_sm.k("dstate")])
                act(cd, psum[1][:, 144:152], AF.Exp, [B_ps[1].k("ac")], [B_sm.k("cd")])
                tt("dve", dtd, dtc, dstate, ALU.mult, [B_dtg[s], B_sm.k("dstate")], [B_sm.k("dtd")])
                tt("dve", Rm, U_F.unsqueeze(1).to_broadcast([128, 8, 128]), da.unsqueeze(2).to_broadcast([128, 8, 128]), ALU.mult,
                   [B_cf, B_sm.k("da")], [B_R])
                for half in range(2):
                    mm(psum[2 + half], ONES_F, Rm[:, half * 4:half * 4 + 4, :].rearrange("p a b -> p (a b)"), True, False, [B_cf, B_R], [B_ps[2 + half]])
                    mm(psum[2 + half], ID_F, NEGM4, False, True, [B_cf], [B_ps[2 + half]])
                for h in range(8):
                    act(dec[:, h, :], psum[2 + h // 4][:, (h % 4) * 128:(h % 4 + 1) * 128], AF.Exp,
                        [B_ps[2 + h // 4], B_sm.k("nacol")], [B_dec.k(h // 4)], bias=nacol[:, h:h + 1], scale=1.0)
                for g in range(2):
                    mm(psum[1][:, 256 + g * 128:256 + (g + 1) * 128], bc64b[s][:, g, cs], bc64b[s][:, 2 + g, cs], True, True,
                       [B_bc64b[s]], [B_ps[1].k("cb%d" % g)])
                    tt("dve", MT[:, g * 4:g * 4 + 4, :], psum[1][:, 256 + g * 128:256 + (g + 1) * 128].unsqueeze(1).to_broadcast([128, 4, 128]),
                       dec[:, g * 4:g * 4 + 4, :], ALU.mult, [B_ps[1].k("cb%d" % g), B_dec.k(g)], [B_MT.k(g)])
                xs3 = xs_tok.rearrange("p (h j) -> p h j", h=8)
                tt("dve", xdt.rearrange("p (h j) -> p h j", h=8), xs3, dtc.unsqueeze(2).to_broadcast([128, 8, 64]), ALU.mult,
                   [B_xs, B_dtg[s]], [B_xdt])
                tt("pool", xw.rearrange("p (h j) -> p h j", h=8), xs3, dtd.unsqueeze(2).to_broadcast([128, 8, 64]), ALU.mult,
                   [B_xs, B_sm.k("dtd")], [B_xw])
                for h in range(8):
                    mm(psum[4][:, h * 64:(h + 1) * 64], MT[:, h, :], xdt[:, h * 64:(h + 1) * 64], True, True, [B_MT.k(h // 4), B_xdt], [B_ps[4]])
                for h in range(8):
                    mm(psum[5][:, h * 64:(h + 1) * 64], bc64b[s][:, 2 + h // 4, cs], pb_cur[:, h, :], True, True, [B_bc64b[s], Bpb_cur], [B_ps[5]])
                for g in range(2):
                    mm(psum[6][0:64, g * 256:(g + 1) * 256], btok[:, g * 64:(g + 1) * 64], xw[:, g * 256:(g + 1) * 256], True, True,
                       [B_btok, B_xw], [B_ps[6]])
                tt("dve", t1.rearrange("p (h j) -> p h j", h=8), psum[5].rearrange("p (h j) -> p h j", h=8),
                   expA.unsqueeze(2).to_broadcast([128, 8, 64]), ALU.mult, [B_ps[5], B_sm.k("expA")], [B_t1])
                tt("pool", t2, xs_tok, dsk, ALU.mult, [B_xs, B_pp], [B_t2])
                tt("pool", t2, t2, t1, ALU.add, [B_t2, B_t1], [B_t2])
                tt("dve", yv, psum[4], t2, ALU.add, [B_ps[4], B_t2], [B_y])
                act(gz, ztok[s][:, cg, :], AF.Silu, [B_ztok[s]], [B_gz])
                tt("pool", yv, yv, gz, ALU.mult, [B_y, B_gz], [B_y])
                memset("pool", ssq[:, 0:1], 0.0, [B_ssq])
                act(sq, yv, AF.Square, [B_y, B_ssq], [B_sq, B_ssq], accum=ssq[:, 0:1])
                act(ssq[:, 1:2], ssq[:, 0:1], AF.Ln, [B_ssq, B_cf], [B_ssq], bias=EPSC, scale=1.0 / 512)
                act(ssq[:, 2:3], ssq[:, 1:2], AF.Exp, [B_ssq], [B_ssq], scale=-0.5)
                ts("dve", sq, yv, ssq[:, 2:3], ALU.mult, [B_y, B_ssq], [B_sq])
                for j in range(4):
                    tr(psum[7][:, j * 128:(j + 1) * 128], sq[:, j * 128:(j + 1) * 128], ID_F, [B_sq, B_cf], [B_ps[7]])
                tt("dve", yst[s][:, :, cs], psum[7].rearrange("p (a b) -> p a b", a=4), gssd.unsqueeze(2).to_broadcast([128, 4, 128]), ALU.mult,
                   [B_ps[7], B_pp], [B_yst[s]])
                tt("pool", prev, prev, cd[0:64, :].unsqueeze(2).to_broadcast([64, 8, 64]), ALU.mult, [B_prev, B_sm.k("cd")], [B_prev])
                tt("dve", prev, psum[6][0:64, :].rearrange("p (h j) -> p h j", h=8), prev, ALU.add, [B_ps[6], B_prev], [B_prev])
                cp("pool", pb_nxt, prev, [B_prev], [Bpb_nxt])
            dma(YT[0:512, tsl].rearrange("(j p) t -> p j t", p=128), yst[s], [B_yst[s]], [B_YT], B_YT)
        P.barrier()
        reset(m)

    def sba(l):
        m = mark()
        kt_all = sb([64, 8, TL], BF16); B_kt = Buf("kt_all")
        v_all = sb([128, TL // 128, 512], BF16); B_v = Buf("v_all")
        qg = [sb([64, 8, TT], BF16) for _ in range(2)]; B_qg = [Buf("qg0"), Buf("qg1")]
        e_sb = [sb([128, TT]) for _ in range(3)]; B_e = [Buf("e%d" % i) for i in range(3)]
        sp_b = [sb([128, TT], BF16) for _ in range(2)]; B_sp = [Buf("sp%d" % i) for i in range(2)]
        r_sb = [sb([128, TT]) for _ in range(2)]; B_r = [Buf("r%d" % i) for i in range(2)]
        w_b = [sb([128, TT], BF16) for _ in range(2)]; B_w = [Buf("w%d" % i) for i in range(2)]
        acc = [sb([128, TT], BF16) for _ in range(2)]; B_acc = [Buf("acc%d" % i) for i in range(2)]
        o_sb = sb([64, 8, TT]); B_o = Buf("o_sb")
        osq = sb([64, 8, TT], BF16); B_osq = Buf("osq")
        rs = sb([128, TT]); B_rs = Buf("rs")
        yst1 = sb([64, 8, TT], BF16); yst = [yst1, yst1]; B_y1 = Buf("ysb"); B_yst = [B_y1, B_y1]
        gsb = ppc(l, PP_GSB, 8)
        for h in range(8):
            dma(kt_all[:, h, :], KT[h, :, :], [B_KT], [B_kt], B_kt)
        for q4 in range(4):
            bs = slice(q4 * 8, (q4 + 1) * 8)
            dma(v_all[:, bs, :], VS[q4 * 1024:(q4 + 1) * 1024, :].rearrange("(b p) f -> p b f", p=128), [B_VS], [B_v], B_v)
        tiles = []
        for G in range(min(TL // TT, SBA_MAXG)):
            for h in range(8):
                kbs = list(range(4 * G + 3, -1, -1))
                for ii, kb in enumerate(kbs):
                    tiles.append((G, h, kb, ii == 0, ii == len(kbs) - 1))
        n = len(tiles)
        Z_PS = [0, 1]; R_PS = [2, 3]; O_PS = [4, 5]; SS_PS = 6

        def stage_q(i):
            G, h, kb, first, last = tiles[i]
            s = G % 2
            if h == 0 and first:
                dma(qg[s], QT[:, :, G * TT:(G + 1) * TT].rearrange("h d t -> d h t"), [B_QT], [B_qg[s]], B_qg[s])
            j = kb - 4 * G
            c0 = 128 * max(j, 0)
            cs = slice(c0, TT)
            zi = Z_PS[i % 2]
            mm(psum[zi][:, cs], kt_all[:, h, kb * 128:(kb + 1) * 128], qg[s][:, h, cs], True, True, [B_kt, B_qg[s]], [B_ps[zi]])
            if j >= 0:
                mm(psum[zi][:, c0:c0 + 128], ID_B, NEGTRI_B, False, True, [B_cb], [B_ps[zi]])

        def stage_a(i):
            G, h, kb, first, last = tiles[i]
            s = G % 2
            j = kb - 4 * G
            c0 = 128 * max(j, 0)
            cs = slice(c0, TT)
            zi = Z_PS[i % 2]; ri = R_PS[i % 2]
            ei = i % 3; si = i % 2
            ai = (G * 8 + h) % 2
            act(e_sb[ei][:, cs], psum[zi][:, cs], AF.Exp, [B_ps[zi]], [B_e[ei]], scale=0.125)
            act(sp_b[si][:, cs], e_sb[ei][:, cs], AF.Ln, [B_e[ei], B_cf], [B_sp[si]], bias=ONEC, scale=1.0)
            mm(psum[ri][:, cs], TRI_B, sp_b[si][:, cs], True, first, [B_cb, B_sp[si]], [B_ps[ri]])
            if not first:
                mm(psum[ri][:, cs], ONES_B, acc[ai][:, cs], False, True, [B_cb, B_acc[ai]], [B_ps[ri]])
            if first:
                memset("pool", acc[ai], 0.0, [B_acc[ai]])
            if not last:
                tt("pool", acc[ai][:, cs], acc[ai][:, cs], sp_b[si][:, cs], ALU.add, [B_acc[ai], B_sp[si]], [B_acc[ai]])

        def stage_b(i):
            G, h, kb, first, last = tiles[i]
            s = G % 2
            j = kb - 4 * G
            c0 = 128 * max(j, 0)
            cs = slice(c0, TT)
            ri = R_PS[i % 2]
            ei = i % 3; si = i % 2
            oi = O_PS[(G * 8 + h) % 2]
            if SBA_LEVEL < 3:
                return
            act(r_sb[si][:, cs], psum[ri][:, cs], AF.Exp, [B_ps[ri]], [B_r[si]], scale=-1.0)
            tt("dve", w_b[si][:, cs], e_sb[ei][:, cs], r_sb[si][:, cs], ALU.mult, [B_e[ei], B_r[si]], [B_w[si]])
            if first:
                for q4 in range(4):
                    mm(psum[oi][0:64, q4 * 128:(q4 + 1) * 128], ZERO_B, ONES_B, True, False, [B_cb], [B_ps[oi]])
            mm(psum[oi][0:64, cs], v_all[:, kb, h * 64:(h + 1) * 64], w_b[si][:, cs], False, last, [B_v, B_w[si]], [B_ps[oi]])
            if last and SBA_LEVEL >= 4:
                cp("dve", o_sb[:, h, :], psum[oi][0:64, :], [B_ps[oi]], [B_o.k(h)])
                act(osq[:, h, :], psum[oi][0:64, :], AF.Square, [B_ps[oi]], [B_osq.k(h)])
                if h == 7 and SBA_LEVEL >= 5:
                    for hh in range(8):
                        mm(psum[SS_PS], ONES_B[0:64, :], osq[:, hh, :], hh == 0, hh == 7, [B_cb, B_osq.k(hh)], [B_ps[SS_PS]])
                    act(rs, psum[SS_PS], AF.Ln, [B_ps[SS_PS], B_cf], [B_rs], bias=EPSC, scale=1.0 / 512)
                    act(rs, rs, AF.Exp, [B_rs], [B_rs], scale=-0.5)
                    for hh in range(8):
                        stt("dve", yst[s][:, hh, :], o_sb[:, hh, :], gsb[0:64, hh:hh + 1], rs[0:64, :], ALU.mult, ALU.mult,
                            [B_o.k(hh), B_pp, B_rs], [B_yst[s]])
                    dma(YT[512:1024, G * TT:(G + 1) * TT].rearrange("(h d) t -> d h t", d=64), yst[s], [B_yst[s]], [B_YT], B_YT)

        for i in range(n + 2):
            if i < n:
                stage_q(i)
            if 1 <= i <= n:
                stage_a(i - 1)
            if i >= 2:
                stage_b(i - 2)
        P.barrier()
        reset(m)

    for st in range(NST):
        embed(st)
        if NST > 1:
            store_hT(st)
        g1(0, st)
    P.barrier()
    done = False
    for l in range(n_layers):
        if stop == "g1":
            break
        reset(PERSIST)
        ssd(l)
        if stop == "ssd":
            break
        sba(l)
        if stop == "g2":
            break
        for st in range(NST):
            if NST > 1:
                load_hT(st)
            g3(l, st)
            if l + 1 < n_layers:
                if NST > 1:
                    store_hT(st)
                g1(l + 1, st)
            else:
                final(st)
                done = True
        P.barrier()
    dumps = {}
    if dump:
        alld = (("QT", QT, B_QT), ("KT", KT, B_KT), ("VS", VS, B_VS), ("ZTOK", ZTOK, B_ZTOK), ("XST", XST, B_XST),
                ("BCT", BCT, B_BCT), ("DTOK", DTOK, B_DTOK), ("YT", YT, B_YT), ("HT", HT, None))
        for name, ap_, Bf in [d_ for d_ in alld if dump is True or d_[0] in dump]:
            o = nc.dram_tensor("dump_" + name, list(ap_.shape), ap_.dtype, kind="ExternalOutput").ap()
            db = Buf("dump_" + name)
            rd = [Bf] if Bf is not None else list(B_HT)
            if len(ap_.shape) == 4:
                for s_ in range(ap_.shape[0]):
                    dma(o[s_].rearrange("p c t -> p (c t)"), ap_[s_].rearrange("p c t -> p (c t)"), rd, [db], db)
            elif len(ap_.shape) == 3:
                for s_ in range(ap_.shape[0]):
                    dma(o[s_], ap_[s_], rd, [db], db)
            else:
                dma(o, ap_, rd, [db], db)
    P.barrier()
    P.emit()
    return nc


_CACHE = {}


def kernel(**inputs):
    p = {k: np.asarray(v) for k, v in inputs.items()}
    if "nc" not in _CACHE:
        _CACHE["nc"] = build()
    nc = _CACHE["nc"]
    cf = host_consts()
    pp = host_params(p)
    shared = {k: np.ascontiguousarray(p[k], dtype=np.float32) for k in ("w_in", "w_out", "w_xq", "w_xk", "w_xv", "w_xo", "w_ff1", "w_ff2")}
    in_maps = []
    for c in range(8):
        b = c % 4
        m = {"x": np.ascontiguousarray(p["x"][b], dtype=np.float32), "mem": np.ascontiguousarray(p["mem"][b], dtype=np.float32),
             "cf": cf, "pp": pp}
        m.update(shared)
        in_maps.append(m)
    res = run_bass_kernel_spmd(nc, in_maps, core_ids=list(range(8)))
    out = np.stack([np.asarray(res.results[b]["out"], dtype=np.float32) for b in range(4)], axis=0)
    return out
```

```python
import math
import contextlib
import numpy as np
import concourse.bass as bass
import concourse.mybir as mybir
from concourse.bass_utils import run_bass_kernel_spmd

F32 = mybir.dt.float32
BF16 = mybir.dt.bfloat16
AF = mybir.ActivationFunctionType
ALU = mybir.AluOpType

ENGS = ("pe", "act", "dve", "pool", "sp")


class Buf:
    def __init__(self, name):
        self.name = name
        self.st = {"*": [[], []]}
        self.sem = None
        self.ndma = 0
        self.excl = False
        self.last_by_eng = {}

    def k(self, key):
        return (self, key)


class Op:
    __slots__ = ("eng", "fn", "waits", "idx", "is_dma", "dest")


class Prog:
    def __init__(self, nc):
        self.nc = nc
        self.ops = {e: [] for e in ENGS}
        self.dma_bufs = []
        self.nops = 0

    @staticmethod
    def _norm(x):
        if isinstance(x, Buf):
            return (x, None)
        return x

    def _entries(self, buf, key):
        d = buf.st
        if key is None:
            return list(d.values())
        if key not in d:
            d[key] = [list(d["*"][0]), list(d["*"][1])]
        return [d[key]]

    def add(self, eng, fn, reads=(), writes=(), dma_dest=None):
        if getattr(self, "capture", None) is not None:
            self.capture.append((eng, fn, reads, writes, dma_dest))
            return None
        op = Op()
        op.eng = eng
        op.fn = fn
        op.is_dma = dma_dest is not None
        op.dest = dma_dest
        deps = []
        reads = [self._norm(r) for r in reads if r is not None]
        writes = [self._norm(w) for w in writes if w is not None]
        for (b, key) in reads:
            for ent in self._entries(b, key):
                deps.extend(ent[0])
        for (b, key) in writes:
            for ent in self._entries(b, key):
                samegen = op.is_dma and len(ent[0]) > 0 and all(w.is_dma for w in ent[0]) and len(ent[1]) == 0
                if not samegen:
                    deps.extend(ent[0])
                deps.extend(ent[1])
        for (b, key) in list(reads) + list(writes):
            if b.excl:
                for e2, y in b.last_by_eng.items():
                    if e2 != eng:
                        deps.append(y)
                b.last_by_eng[eng] = op
        for (b, key) in writes:
            for ent in self._entries(b, key):
                samegen = op.is_dma and len(ent[0]) > 0 and all(w.is_dma for w in ent[0]) and len(ent[1]) == 0
                if samegen:
                    ent[0].append(op)
                else:
                    ent[0] = [op]
                    ent[1] = []
            if key is None:
                for kk in list(b.st.keys()):
                    b.st[kk][0] = list(b.st["*"][0])
                    b.st[kk][1] = []
        wset = set((id(b), key) for (b, key) in writes)
        for (b, key) in reads:
            if (id(b), key) in wset:
                continue
            for ent in self._entries(b, key):
                ent[1].append(op)
        if op.is_dma:
            d = dma_dest
            if d.sem is None:
                self.dma_bufs.append(d)
                d.sem = True
            d.ndma += 1
        lst = self.ops[eng]
        lst.append(op)
        op.idx = len(lst)
        waits = {}
        for y in deps:
            if y is op:
                continue
            if y.is_dma:
                key = ("d", id(y.dest))
                val = 16 * y.dest.ndma if y.dest is not dma_dest else 16 * (y.dest.ndma - 1)
                if val <= 0:
                    continue
                ent = (y.dest, val)
            else:
                if y.eng == eng and eng == "pe":
                    continue
                key = ("e", y.eng)
                val = y.idx
                ent = (y.eng, val)
            if key not in waits or waits[key][1] < val:
                waits[key] = ent
        op.waits = waits
        self.nops += 1
        return op

    def barrier(self):
        snap_e = {e: len(self.ops[e]) for e in ENGS}
        snap_d = [(d, 16 * d.ndma) for d in self.dma_bufs]
        for e in ENGS:
            op = Op()
            op.eng = e
            op.fn = None
            op.is_dma = False
            op.dest = None
            w = {}
            for e2 in ENGS:
                if e2 == e:
                    continue
                w[("e", e2)] = (e2, snap_e[e2])
            for (d, v) in snap_d:
                if v > 0:
                    w[("d", id(d))] = (d, v)
            op.waits = w
            lst = self.ops[e]
            lst.append(op)
            op.idx = len(lst)

    def emit(self):
        nc = self.nc
        with contextlib.ExitStack() as es:
            esem = {e: es.enter_context(nc.semaphore("es_" + e)) for e in ENGS}
            for i, d in enumerate(self.dma_bufs):
                d.sem = es.enter_context(nc.semaphore("ds%d_%s" % (i, d.name)))
            block = es.enter_context(nc.Block())
            cidx = {}
            for e in ENGS:
                c = 0
                m = [0]
                for op in self.ops[e]:
                    if not op.is_dma:
                        c += 1
                    m.append(c)
                cidx[e] = m

            def make(e):
                ops = self.ops[e]

                def body(eng):
                    seen = {}
                    for op in ops:
                        for key, (obj, val) in op.waits.items():
                            if key[0] == "e":
                                sem = esem[obj]
                                v = cidx[obj][val]
                            else:
                                sem = obj.sem
                                v = val
                            if v <= 0 or seen.get(key, 0) >= v:
                                continue
                            seen[key] = v
                            eng.wait_ge(sem, v)
                        if op.fn is None:
                            eng.nop().then_inc(esem[e], 1)
                            continue
                        ins = op.fn(eng)
                        if op.is_dma:
                            ins.then_inc(op.dest.sem, 16)
                        else:
                            ins.then_inc(esem[e], 1)
                return body

            block.tensor(make("pe"))
            block.scalar(make("act"))
            block.vector(make("dve"))
            block.gpsimd(make("pool"))
            block.sync(make("sp"))


def interleave(P, fa, fb):
    P.capture = la = []
    fa()
    P.capture = lb = []
    fb()
    P.capture = None
    na, nb = len(la), len(lb)
    i = j = 0
    while i < na or j < nb:
        if j >= nb or (i < na and i * nb <= j * na):
            P.add(*la[i]); i += 1
        else:
            P.add(*lb[j]); j += 1


class Rot:
    def __init__(self, items):
        self.items = items
        self.i = 0

    def next(self):
        it = self.items[self.i % len(self.items)]
        self.i += 1
        return it


D = 1024
TL = 4096
TS = 2048
NST = TL // TS
TT = 512
NT = TS // TT
L = 2
MEM = 256
IN_DIM = 2824
C_Z, C_X, C_B, C_C, C_DT, C_Q, C_K, C_V = 0, 512, 1024, 1152, 1280, 1288, 1800, 2312
EPS = 1e-5
NEG = -30000.0
SBA_MAXG = 99
SBA_LEVEL = 9

CF_ID, CF_ONES, CF_U, CF_NEGM4, CF_TRI, CF_NEGTRI, CF_EPS, CF_ONE, NCF = 0, 128, 256, 384, 896, 1024, 1152, 1153, 1160
PP_GMIX, PP_GXA, PP_GFF, PP_GMEM, PP_GSSD, PP_GSB, PP_CW, PP_CB, PP_DTB, PP_ALOG, PP_DSK, PPW = 0, 8, 16, 24, 32, 36, 44, 68, 74, 82, 90, 602
PP_FINAL = L * PPW
NPP = PP_FINAL + 8


def host_consts():
    cf = np.zeros((128, NCF), np.float32)
    i = np.arange(128)
    cf[:, CF_ID:CF_ID + 128] = np.eye(128, dtype=np.float32)
    cf[:, CF_ONES:CF_ONES + 128] = 1.0
    cf[:, CF_U:CF_U + 128] = (i[:, None] <= i[None, :]).astype(np.float32)
    negm = np.where(i[None, :] >= i[:, None], 0.0, NEG).astype(np.float32)
    cf[:, CF_NEGM4:CF_NEGM4 + 512] = np.tile(negm, (1, 4))
    cf[:, CF_TRI:CF_TRI + 128] = (i[:, None] >= i[None, :]).astype(np.float32)
    cf[:, CF_NEGTRI:CF_NEGTRI + 128] = np.where(i[:, None] < i[None, :], 0.0, 8 * NEG)
    cf[:, CF_EPS] = EPS
    cf[:, CF_ONE] = 1.0
    return cf


def host_params(p):
    pp = np.zeros((128, NPP), np.float32)
    col = lambda v, n: np.ascontiguousarray(np.asarray(v, np.float32).reshape(n, 128).T)
    for l in range(L):
        o = l * PPW
        pp[:, o + PP_GMIX:o + PP_GMIX + 8] = col(p["norm_mix_g"][l], 8)
        pp[:, o + PP_GXA:o + PP_GXA + 8] = col(p["norm_xa_g"][l], 8)
        pp[:, o + PP_GFF:o + PP_GFF + 8] = col(p["norm_ff_g"][l], 8)
        pp[:, o + PP_GMEM:o + PP_GMEM + 8] = col(p["norm_mem_g"][l], 8)
        pp[:, o + PP_GSSD:o + PP_GSSD + 4] = col(p["ssd_norm_g"][l], 4)
        pp[0:64, o + PP_GSB:o + PP_GSB + 8] = np.asarray(p["sb_norm_g"][l], np.float32).reshape(8, 64).T
        cw = np.asarray(p["conv_w"][l], np.float32)
        pp[:, o + PP_CW:o + PP_CW + 24] = cw.reshape(4, 6, 128).transpose(2, 1, 0).reshape(128, 24)
        pp[:, o + PP_CB:o + PP_CB + 6] = col(p["conv_b"][l], 6)
        pp[:, o + PP_DTB:o + PP_DTB + 8] = np.asarray(p["dt_bias"][l], np.float32)[None, :]
        pp[:, o + PP_ALOG:o + PP_ALOG + 8] = np.asarray(p["a_log"][l], np.float32)[None, :]
        pp[:, o + PP_DSK:o + PP_DSK + 512] = np.repeat(np.asarray(p["d_skip"][l], np.float32), 64)[None, :]
    pp[:, PP_FINAL:PP_FINAL + 8] = col(p["final_g"], 8)
    return pp


def build(n_layers=L, stop=None, dump=False):
    nc = bass.Bass("TRN2", target_bir_lowering=False)
    P = Prog(nc)

    def din(name, shape, dt=F32):
        return nc.dram_tensor(name, shape, dt, kind="ExternalInput").ap()

    x_d = din("x", [TL, D])
    mem_d = din("mem", [MEM, D])
    w_in = din("w_in", [L, D, IN_DIM])
    w_out = din("w_out", [L, D, D])
    w_xq = din("w_xq", [L, D, 512])
    w_xk = din("w_xk", [L, D, 512])
    w_xv = din("w_xv", [L, D, 512])
    w_xo = din("w_xo", [L, 512, D])
    w_ff1 = din("w_ff1", [L, D, 4096])
    w_ff2 = din("w_ff2", [L, 4096, D])
    cf_d = din("cf", [128, NCF])
    pp_d = din("pp", [128, NPP])
    out_d = nc.dram_tensor("out", [TL, D], F32, kind="ExternalOutput").ap()

    def dscr(name, shape, dt):
        return nc.dram_tensor(name, shape, dt).ap()

    HT = dscr("HT", [NST, 128, 8, TS], F32)
    QT = dscr("QT", [8, 64, TL], BF16)
    KT = dscr("KT", [8, 64, TL], BF16)
    VS = dscr("VS", [TL, 512], BF16)
    ZTOK = dscr("ZTOK", [TL, 512], F32)
    XST = dscr("XST", [4, 128, TL], F32)
    BCT = dscr("BCT", [2, 128, TL], F32)
    DTOK = dscr("DTOK", [TL, 8], F32)
    YT = dscr("YT", [D, TL], BF16)
    B_HT = [Buf("HT%d" % s) for s in range(NST)]
    B_QT, B_KT, B_VS, B_ZTOK, B_XST, B_BCT, B_DTOK, B_YT, B_OUT = [Buf(n) for n in "QT KT VS ZTOK XST BCT DTOK YT OUT".split()]

    ARENA_BYTES = 206 * 1024
    arena = nc.alloc_sbuf_tensor("arena", [128, ARENA_BYTES // 2], BF16).ap()
    st_ = {"off": 0}

    def sb(shape, dt=F32, parts=128):
        n = 1
        for s_ in shape[1:]:
            n *= s_
        nbytes = n * (4 if dt == F32 else 2)
        nbytes = (nbytes + 31) // 32 * 32
        off = st_["off"]
        assert off + nbytes <= ARENA_BYTES, ("SBUF arena overflow", off, nbytes)
        st_["off"] = off + nbytes
        v = arena[0:shape[0], off // 2:(off + nbytes) // 2]
        if dt == F32:
            v = v.bitcast(F32)
        v = v[:, 0:n]
        if len(shape) == 3:
            v = v.rearrange("p (a b) -> p a b", a=shape[1])
        elif len(shape) == 4:
            v = v.rearrange("p (a b c) -> p a b c", a=shape[1], b=shape[2])
        return v

    def mark():
        return st_["off"]

    def reset(m):
        st_["off"] = m

    psum = [nc.alloc_psum_tensor("ps%d" % i, [128, 512], F32).ap() for i in range(8)]
    B_ps = [Buf("ps%d" % i) for i in range(8)]
    for b_ in B_ps:
        b_.excl = True

    def mm(out, lhsT, rhs, start, stop, r, w):
        P.add("pe", lambda e: e.matmul(out, lhsT=lhsT, rhs=rhs, start=start, stop=stop), reads=r, writes=w)

    def tr(out, in_, ident, r, w):
        P.add("pe", lambda e: e.transpose(out=out, in_=in_, identity=ident), reads=r, writes=w)

    def act(out, in_, func, r, w, bias=None, scale=None, accum=None):
        kw = {}
        if bias is not None:
            kw["bias"] = bias
        if scale is not None:
            kw["scale"] = scale
        if accum is not None:
            kw["accum_out"] = accum
        P.add("act", lambda e: e.activation(out=out, in_=in_, func=func, **kw), reads=r, writes=w)

    def tt(eng, out, in0, in1, op, r, w):
        P.add(eng, lambda e: e.tensor_tensor(out=out, in0=in0, in1=in1, op=op), reads=r, writes=w)

    def ts(eng, out, in0, s1, op0, r, w, s2=None, op1=None):
        if op1 is None:
            P.add(eng, lambda e: e.tensor_scalar(out=out, in0=in0, scalar1=s1, scalar2=None, op0=op0), reads=r, writes=w)
        else:
            P.add(eng, lambda e: e.tensor_scalar(out=out, in0=in0, scalar1=s1, scalar2=s2, op0=op0, op1=op1), reads=r, writes=w)

    def stt(eng, out, in0, scalar, in1, op0, op1, r, w):
        P.add(eng, lambda e: e.scalar_tensor_tensor(out=out, in0=in0, scalar=scalar, in1=in1, op0=op0, op1=op1), reads=r, writes=w)

    def cp(eng, out, in_, r, w):
        if eng == "act":
            P.add("act", lambda e: e.activation(out=out, in_=in_, func=AF.Copy), reads=r, writes=w)
        else:
            P.add(eng, lambda e: e.tensor_copy(out=out, in_=in_), reads=r, writes=w)

    def memset(eng, out, val, w):
        P.add(eng, lambda e: e.memset(out, val), writes=w)

    def dma(out, in_, r, w, dest, eng="sp"):
        P.add(eng, lambda e: e.dma_start(out=out, in_=in_), reads=r, writes=w, dma_dest=dest)

    cf = sb([128, NCF]); B_cf = Buf("cf")
    pp = sb([128, NPP]); B_pp = Buf("pp")
    NCB = 128 * 5
    cb = sb([128, NCB], BF16); B_cb = Buf("cb")
    ID_F, ONES_F, U_F = cf[:, CF_ID:CF_ID + 128], cf[:, CF_ONES:CF_ONES + 128], cf[:, CF_U:CF_U + 128]
    NEGM4 = cf[:, CF_NEGM4:CF_NEGM4 + 512]
    EPSC, ONEC = cf[:, CF_EPS:CF_EPS + 1], cf[:, CF_ONE:CF_ONE + 1]
    ID_B, ONES_B, TRI_B, NEGTRI_B, ZERO_B = cb[:, 0:128], cb[:, 128:256], cb[:, 256:384], cb[:, 384:512], cb[:, 512:640]
    abc = sb([128, L * 8]); B_abc = Buf("abc")
    halo = sb([128, 6, 3]); B_halo = Buf("halo")
    kxT = [sb([128, 4, MEM], BF16) for _ in range(L)]; B_kx = [Buf("kx%d" % l) for l in range(L)]
    vx = [sb([128, 2, 512], BF16) for _ in range(L)]; B_vx = [Buf("vx%d" % l) for l in range(L)]
    PERSIST = mark()
    wbf = [sb([128, 4096], BF16) for _ in range(3)]; B_wbf = [Buf("wbf0"), Buf("wbf1"), Buf("wbf2")]
    wrot = Rot([0, 1, 2])
    WEND = mark()

    dma(cf, cf_d, [], [B_cf], B_cf)
    dma(pp, pp_d, [], [B_pp], B_pp)
    cp("dve", cb[:, 0:128], cf[:, CF_ID:CF_ID + 128], [B_cf], [B_cb])
    cp("dve", cb[:, 128:256], cf[:, CF_ONES:CF_ONES + 128], [B_cf], [B_cb])
    cp("dve", cb[:, 256:384], cf[:, CF_TRI:CF_TRI + 128], [B_cf], [B_cb])
    cp("dve", cb[:, 384:512], cf[:, CF_NEGTRI:CF_NEGTRI + 128], [B_cf], [B_cb])
    memset("pool", cb[:, 512:640], 0.0, [B_cb])
    for l in range(L):
        act(abc[:, l * 8:(l + 1) * 8], pp[:, l * PPW + PP_ALOG:l * PPW + PP_ALOG + 8], AF.Exp, [B_pp], [B_abc])
    ts("dve", abc, abc, -1.0, ALU.mult, [B_abc], [B_abc])

    def ppc(l, off, n):
        return pp[:, l * PPW + off:l * PPW + off + n]

    def load_w(src3, KC, N):
        i = wrot.next()
        dst = wbf[i][:, 0:KC * N].rearrange("p (k n) -> p k n", k=KC)
        for k in range(KC):
            dma(dst[:, k, :], src3[:, k, :], [], [B_wbf[i]], B_wbf[i], eng="pool")
        return dst, B_wbf[i]

    def wview(w2d, r0, KC, c0, N):
        return w2d[r0:r0 + KC * 128, c0:c0 + N].rearrange("(k p) n -> p k n", p=128)

    prot = Rot([0, 1, 2, 3])

    def proj_fm(w2d, r0, KC, c0, ncols, colblk, actT, Bact, ntiles, evac, tcol0=0):
        for cbi in range(ncols // colblk):
            wv, Bw = load_w(wview(w2d, r0, KC, c0 + cbi * colblk, colblk), KC, colblk)
            for jj in range(colblk // 128):
                for t in range(ntiles):
                    pi = prot.next()
                    for k in range(KC):
                        mm(psum[pi], wv[:, k, jj * 128:(jj + 1) * 128], actT[:, k, tcol0 + t * TT:tcol0 + (t + 1) * TT],
                           k == 0, k == KC - 1, [Bw, Bact], [B_ps[pi]])
                    evac(cbi * (colblk // 128) + jj, t, psum[pi], B_ps[pi])

    def rmsnorm_fm(hT, BhT, gcols, actT, Bact, sqb, B_sqb, rstd, B_rstd, nchunks=8, scale=1.0 / D):
        for t in range(NT):
            sl = slice(t * TT, (t + 1) * TT)
            for c in range(nchunks):
                act(sqb[:, c, :], hT[:, c, sl], AF.Square, [BhT.k((c, t))], [B_sqb.k(c)])
            pi = prot.next()
            for c in range(nchunks):
                mm(psum[pi], ONES_B, sqb[:, c, :], c == 0, c == nchunks - 1, [B_cb, B_sqb.k(c)], [B_ps[pi]])
            act(rstd, psum[pi], AF.Ln, [B_ps[pi], B_cf], [B_rstd], bias=EPSC, scale=scale)
            act(rstd, rstd, AF.Exp, [B_rstd], [B_rstd], scale=-0.5)
            for c in range(nchunks):
                stt("dve", actT[:, c, sl], hT[:, c, sl], gcols[:, c:c + 1], rstd, ALU.mult, ALU.mult,
                    [BhT.k((c, t)), B_pp, B_rstd], [Bact])

    m0 = mark()
    assert m0 == WEND
    mtok = sb([128, D]); B_mtok = Buf("mtok")
    mn = sb([128, D]); B_mn = Buf("mn")
    msc = sb([128, 4]); B_msc = Buf("msc")
    memnT = [sb([128, 8, MEM], BF16) for _ in range(L)]; B_memn = [Buf("memn%d" % l) for l in range(L)]
    for blk in range(2):
        dma(mtok, mem_d[blk * 128:(blk + 1) * 128, :], [], [B_mtok], B_mtok)
        memset("pool", msc[:, 0:1], 0.0, [B_msc])
        act(mn, mtok, AF.Square, [B_mtok, B_msc], [B_mn, B_msc], accum=msc[:, 0:1])
        act(msc[:, 1:2], msc[:, 0:1], AF.Ln, [B_msc, B_cf], [B_msc], bias=EPSC, scale=1.0 / D)
        act(msc[:, 2:3], msc[:, 1:2], AF.Exp, [B_msc], [B_msc], scale=-0.5)
        ts("dve", mn, mtok, msc[:, 2:3], ALU.mult, [B_mtok, B_msc], [B_mn])
        for half in range(2):
            for c4 in range(4):
                c = half * 4 + c4
                tr(psum[half][:, c4 * 128:(c4 + 1) * 128], mn[:, c * 128:(c + 1) * 128], ID_F, [B_mn, B_cf], [B_ps[half]])
            for l in range(L):
                g = ppc(l, PP_GMEM + half * 4, 4)
                tt("dve", memnT[l][:, half * 4:half * 4 + 4, blk * 128:(blk + 1) * 128],
                   psum[half].rearrange("p (a b) -> p a b", a=4), g.unsqueeze(2).to_broadcast([128, 4, 128]), ALU.mult,
                   [B_ps[half], B_pp], [B_memn[l]])
    for l in range(L):
        wv, Bw = load_w(wview(w_xk[l], 0, 8, 0, 512), 8, 512)
        for hx in range(4):
            pi = prot.next()
            for k in range(8):
                mm(psum[pi][:, 0:MEM], wv[:, k, hx * 128:(hx + 1) * 128], memnT[l][:, k, :], k == 0, k == 7, [Bw, B_memn[l]], [B_ps[pi]])
            cp("act", kxT[l][:, hx, :], psum[pi][:, 0:MEM], [B_ps[pi]], [B_kx[l]])
        wv, Bw = load_w(wview(w_xv[l], 0, 8, 0, 512), 8, 512)
        for mb in range(2):
            pi = prot.next()
            for k in range(8):
                mm(psum[pi], memnT[l][:, k, mb * 128:(mb + 1) * 128], wv[:, k, :], k == 0, k == 7, [Bw, B_memn[l]], [B_ps[pi]])
            cp("dve", vx[l][:, mb, :], psum[pi], [B_ps[pi]], [B_vx[l]])
    P.barrier()
    reset(m0)

    hT = sb([128, 8, TS]); B_hT = Buf("hT")
    actT = sb([128, 8, TS], BF16); B_actT = Buf("actT")
    SCRA_OFF = mark()
    scrA = sb([128, 4, TS], BF16); B_scrA = Buf("scrA")
    scrB = sb([128, 4, TS], BF16); B_scrB = Buf("scrB")
    sqb = sb([128, 8, TT], BF16); B_sqb = Buf("sqb")
    rstd = sb([128, TT]); B_rstd = Buf("rstd")
    evs = [sb([128, TT]) for _ in range(2)]; B_evs = [Buf("evs0"), Buf("evs1")]
    evrot = Rot([0, 1])
    evb = [sb([128, TT], BF16) for _ in range(2)]; B_evb = [Buf("evb0"), Buf("evb1")]
    evbrot = Rot([0, 1])
    xr = [sb([128, TT + 3]) for _ in range(2)]; B_xr = [Buf("xr0"), Buf("xr1")]
    cacc = sb([128, TT]); B_cacc = Buf("cacc")
    wdt = sb([128, 8, 8], BF16); B_wdt = Buf("wdt")
    wdtf = sb([128, 8, 8]); B_wdtf = Buf("wdtf")
    dtst = sb([128, 16, 8]); B_dtst = Buf("dtst")
    dtt = sb([128, 8]); B_dtt = Buf("dtt")
    TOKWISE_END = mark()
    eng_alt = Rot(["act", "dve"])

    def embed(st):
        xin = [evs[0], evs[1]]
        for blk in range(TS // 128):
            r0 = st * TS + blk * 128
            for half in range(2):
                i = evrot.next()
                dma(evs[i], x_d[r0:r0 + 128, half * 512:(half + 1) * 512], [], [B_evs[i]], B_evs[i])
                pi = prot.next()
                for c4 in range(4):
                    tr(psum[pi][:, c4 * 128:(c4 + 1) * 128], evs[i][:, c4 * 128:(c4 + 1) * 128], ID_F, [B_evs[i], B_cf], [B_ps[pi]])
                t = blk // 4
                cp(eng_alt.next(), hT[:, half * 4:half * 4 + 4, blk * 128:(blk + 1) * 128],
                   psum[pi].rearrange("p (a b) -> p a b", a=4), [B_ps[pi]],
                   [B_hT.k((half * 4 + c4, t)) for c4 in range(4)])

    def store_hT(st):
        for c in range(8):
            dma(HT[st, :, c, :], hT[:, c, :], [B_hT], [B_HT[st]], B_HT[st])

    def load_hT(st):
        for c in range(8):
            dma(hT[:, c, :], HT[st, :, c, :], [B_HT[st]], [B_hT], B_hT)

    def g1(l, st):
        tok0 = st * TS
        rmsnorm_fm(hT, B_hT, ppc(l, PP_GMIX, 8), actT, B_actT, sqb, B_sqb, rstd, B_rstd)
        wl = w_in[l]
        if st == 0:
            memset("pool", halo, 0.0, [B_halo])
        cw = ppc(l, PP_CW, 24)
        cbias = ppc(l, PP_CB, 6)
        cstate = {"n": 0}

        def conv_evac(base):
            def f(j, t, ps, Bp):
                cc = base + j
                n = cstate["n"]
                cstate["n"] += 1
                i = n % 2
                if t == 0:
                    cp("pool", xr[i][:, 0:3], halo[:, cc, :], [B_halo], [B_xr[i]])
                else:
                    cp("pool", xr[i][:, 0:3], xr[1 - i][:, TT:TT + 3], [B_xr[1 - i]], [B_xr[i]])
                cp("act", xr[i][:, 3:TT + 3], ps, [Bp], [B_xr[i]])
                if t == NT - 1:
                    cp("pool", halo[:, cc, :], xr[i][:, TT:TT + 3], [B_xr[i]], [B_halo])
                ts("dve", cacc, xr[i][:, 0:TT], cw[:, cc * 4:cc * 4 + 1], ALU.mult, [B_xr[i], B_pp], [B_cacc])
                for k in range(1, 4):
                    stt("dve", cacc, xr[i][:, k:k + TT], cw[:, cc * 4 + k:cc * 4 + k + 1], cacc, ALU.mult, ALU.add,
                        [B_xr[i], B_pp, B_cacc], [B_cacc])
                e = evrot.next()
                act(evs[e], cacc, AF.Silu, [B_cacc, B_pp], [B_evs[e]], bias=cbias[:, cc:cc + 1])
                sl = slice(tok0 + t * TT, tok0 + (t + 1) * TT)
                if cc < 4:
                    dma(XST[cc, :, sl], evs[e], [B_evs[e]], [B_XST], B_XST)
                else:
                    dma(BCT[cc - 4, :, sl], evs[e], [B_evs[e]], [B_BCT], B_BCT)
            return f

        proj_fm(wl, 0, 8, C_X, 512, 512, actT, B_actT, NT, conv_evac(0))
        proj_fm(wl, 0, 8, C_B, 256, 256, actT, B_actT, NT, conv_evac(4))

        def qk_evac(dst, Bdst):
            dflat = dst.rearrange("h d t -> (h d) t")

            def f(j, t, ps, Bp):
                e = evbrot.next()
                cp(eng_alt.next(), evb[e], ps, [Bp], [B_evb[e]])
                sl = slice(tok0 + t * TT, tok0 + (t + 1) * TT)
                dma(dflat[j * 128:(j + 1) * 128, sl], evb[e], [B_evb[e]], [Bdst], Bdst)
            return f

        proj_fm(wl, 0, 8, C_Q, 512, 512, actT, B_actT, NT, qk_evac(QT, B_QT))
        proj_fm(wl, 0, 8, C_K, 512, 512, actT, B_actT, NT, qk_evac(KT, B_KT))
        for (c0, dst, Bdst, isbf) in ((C_Z, ZTOK, B_ZTOK, False), (C_V, VS, B_VS, True)):
            wv, Bw = load_w(wview(wl, 0, 8, c0, 512), 8, 512)
            for blk in range(TS // 128):
                pi = prot.next()
                for k in range(8):
                    mm(psum[pi], actT[:, k, blk * 128:(blk + 1) * 128], wv[:, k, :], k == 0, k == 7, [Bw, B_actT], [B_ps[pi]])
                r0 = tok0 + blk * 128
                if isbf:
                    e = evbrot.next()
                    cp(eng_alt.next(), evb[e], psum[pi], [B_ps[pi]], [B_evb[e]])
                    dma(dst[r0:r0 + 128, :], evb[e], [B_evb[e]], [Bdst], Bdst)
                else:
                    e = evrot.next()
                    cp(eng_alt.next(), evs[e], psum[pi], [B_ps[pi]], [B_evs[e]])
                    dma(dst[r0:r0 + 128, :], evs[e], [B_evs[e]], [Bdst], Bdst)
        dma(wdtf, wview(wl, 0, 8, C_DT, 8), [], [B_wdtf], B_wdtf)
        cp("dve", wdt, wdtf, [B_wdtf], [B_wdt])
        dtb = ppc(l, PP_DTB, 8)
        for blk in range(TS // 128):
            pi = prot.next()
            for k in range(8):
                mm(psum[pi][:, 0:8], actT[:, k, blk * 128:(blk + 1) * 128], wdt[:, k, :], k == 0, k == 7, [B_wdt, B_actT], [B_ps[pi]])
            tt("dve", dtt, psum[pi][:, 0:8], dtb, ALU.add, [B_ps[pi], B_pp], [B_dtt])
            act(dtt, dtt, AF.Exp, [B_dtt], [B_dtt])
            act(dtst[:, blk, :], dtt, AF.Ln, [B_dtt, B_cf], [B_dtst], bias=ONEC, scale=1.0)
        dma(DTOK[tok0:tok0 + TS, :].rearrange("(b p) h -> p b h", p=128), dtst, [B_dtst], [B_DTOK], B_DTOK)

    def add_evac(j, t, ps, Bp):
        sl = slice(t * TT, (t + 1) * TT)
        tt("dve", hT[:, j, sl], ps, hT[:, j, sl], ALU.add, [Bp, B_hT.k((j, t))], [B_hT.k((j, t))])

    def g3(l, st):
        tok0 = st * TS
        for c in range(8):
            dma(actT[:, c, :], YT[c * 128:(c + 1) * 128, tok0:tok0 + TS], [B_YT], [B_actT], B_actT)
        proj_fm(w_out[l], 0, 8, 0, D, 512, actT, B_actT, NT, add_evac)
        rmsnorm_fm(hT, B_hT, ppc(l, PP_GXA, 8), actT, B_actT, sqb, B_sqb, rstd, B_rstd)
        qxT, B_qx = scrA, B_scrA
        oxT, B_ox = scrB, B_scrB

        def q_evac(j, t, ps, Bp):
            cp(eng_alt.next(), qxT[:, j, t * TT:(t + 1) * TT], ps, [Bp], [B_qx.k((j, t))])
        proj_fm(w_xq[l], 0, 8, 0, 512, 512, actT, B_actT, NT, q_evac)
        sc = 1.0 / math.sqrt(128.0)
        for t in range(NT):
            sl = slice(t * TT, (t + 1) * TT)
            for hx in range(4):
                pT = []
                for mb in range(2):
                    pi = 4 + mb
                    mm(psum[pi], kxT[l][:, hx, mb * 128:(mb + 1) * 128], qxT[:, hx, sl], True, True, [B_kx[l], B_qx.k((hx, t))], [B_ps[pi]])
                    e = evbrot.next()
                    act(evb[e], psum[pi], AF.Exp, [B_ps[pi]], [B_evb[e]], scale=sc)
                    pT.append(e)
                for mb in range(2):
                    mm(psum[6], vx[l][:, mb, hx * 128:(hx + 1) * 128], evb[pT[mb]], mb == 0, mb == 1, [B_vx[l], B_evb[pT[mb]]], [B_ps[6]])
                for mb in range(2):
                    mm(psum[7], ONES_B, evb[pT[mb]], mb == 0, mb == 1, [B_cb, B_evb[pT[mb]]], [B_ps[7]])
                e = evrot.next()
                cp("act", evs[e], psum[7], [B_ps[7]], [B_evs[e]])
                P.add("dve", (lambda ee: (lambda en: en.reciprocal(out=evs[ee], in_=evs[ee])))(e), reads=[B_evs[e]], writes=[B_evs[e]])
                tt("dve", oxT[:, hx, sl], psum[6], evs[e], ALU.mult, [B_ps[6], B_evs[e]], [B_ox.k((hx, t))])
        proj_fm(w_xo[l], 0, 4, 0, D, 1024, oxT, B_ox, NT, add_evac)
        rmsnorm_fm(hT, B_hT, ppc(l, PP_GFF, 8), actT, B_actT, sqb, B_sqb, rstd, B_rstd)
        uT, B_u = scrA, B_scrA
        for fb in range(8):
            proj_fm(w_ff1[l], 0, 8, fb * 512, 512, 512, actT, B_actT, NT, u_evac_fix(uT, B_u))
            proj_fm(w_ff2[l], fb * 512, 4, 0, D, 1024, uT, B_u, NT, add_evac)

    def u_evac_fix(uT, B_u):
        def f(j, t, ps, Bp):
            e = evrot.next()
            act(evs[e], ps, AF.Relu, [Bp], [B_evs[e]])
            tt("pool" if (j + t) % 2 else "dve", uT[:, j, t * TT:(t + 1) * TT], evs[e], evs[e], ALU.mult, [B_evs[e]], [B_u.k((j, t))])
        return f

    def final(st):
        tok0 = st * TS
        for t in range(NT):
            sl = slice(t * TT, (t + 1) * TT)
            for c in range(8):
                act(sqb[:, c, :], hT[:, c, sl], AF.Square, [B_hT.k((c, t))], [B_sqb.k(c)])
            pi = prot.next()
            for c in range(8):
                mm(psum[pi], ONES_B, sqb[:, c, :], c == 0, c == 7, [B_cb, B_sqb.k(c)], [B_ps[pi]])
            act(rstd, psum[pi], AF.Ln, [B_ps[pi], B_cf], [B_rstd], bias=EPSC, scale=1.0 / D)
            act(rstd, rstd, AF.Exp, [B_rstd], [B_rstd], scale=-0.5)
            gf = pp[:, PP_FINAL:PP_FINAL + 8]
            for c in range(8):
                stt("dve", fin[:, c, :], hT[:, c, sl], gf[:, c:c + 1], rstd, ALU.mult, ALU.mult,
                    [B_hT.k((c, t)), B_pp, B_rstd], [B_fin.k(c)])
            for b4 in range(4):
                for half in range(2):
                    pi = 4 + half
                    for c4 in range(4):
                        c = half * 4 + c4
                        tr(psum[pi][:, c4 * 128:(c4 + 1) * 128], fin[:, c, b4 * 128:(b4 + 1) * 128], ID_F, [B_fin.k(c), B_cf], [B_ps[pi]])
                    e = evrot.next()
                    cp(eng_alt.next(), evs[e], psum[pi], [B_ps[pi]], [B_evs[e]])
                    r0 = tok0 + t * TT + b4 * 128
                    dma(out_d[r0:r0 + 128, half * 512:(half + 1) * 512], evs[e], [B_evs[e]], [B_OUT], B_OUT)

    fin = arena[:, SCRA_OFF // 2:SCRA_OFF // 2 + 8192].bitcast(F32).rearrange("p (a b) -> p a b", a=8); B_fin = B_scrA

    def ssd(l):
        m = mark()
        NG = TL // TT
        NCH = TL // 128
        xsT = [sb([128, 4, TT]) for _ in range(2)]; B_xsT = [Buf("xsT0"), Buf("xsT1")]
        bt128 = [sb([128, TT]) for _ in range(2)]; B_bt = [Buf("bt0"), Buf("bt1")]
        bc64 = [sb([64, 4, TT]) for _ in range(2)]; B_bc64 = [Buf("bc640"), Buf("bc641")]
        bc64b = [sb([64, 4, TT], BF16) for _ in range(2)]; B_bc64b = [Buf("bc64b0"), Buf("bc64b1")]
        ztok = [sb([128, 4, TT]) for _ in range(2)]; B_ztok = [Buf("ztok0"), Buf("ztok1")]
        dtg = [sb([128, 4, 8]) for _ in range(2)]; B_dtg = [Buf("dtg0"), Buf("dtg1")]
        yst = [sb([128, 4, TT], BF16) for _ in range(2)]; B_yst = [Buf("yst0"), Buf("yst1")]
        xs_tok2 = [sb([128, 512]) for _ in range(2)]; B_xs2 = [Buf("xs_tok0"), Buf("xs_tok1")]
        btok2 = [sb([128, 128], BF16) for _ in range(2)]; B_btok2 = [Buf("btok0"), Buf("btok1")]
        sm2 = [sb([128, 80]) for _ in range(2)]; B_sm2 = [Buf("sm0"), Buf("sm1")]
        MT2 = [sb([128, 8, 128], BF16) for _ in range(2)]; B_MT2 = [Buf("MT0"), Buf("MT1")]
        xdt2 = [sb([128, 512], BF16) for _ in range(2)]; B_xdt2 = [Buf("xdt0"), Buf("xdt1")]
        xw2 = [sb([128, 512], BF16) for _ in range(2)]; B_xw2 = [Buf("xw0"), Buf("xw1")]
        for i in range(2):
            memset("pool", sm2[i], 0.0, [B_sm2[i]])
        Rm = sb([128, 8, 128]); B_R = Buf("R")
        dec = sb([128, 8, 128]); B_dec = Buf("dec")
        t1 = sb([128, 512]); B_t1 = Buf("t1")
        t2 = sb([128, 512]); B_t2 = Buf("t2")
        yv = sb([128, 512]); B_y = Buf("y")
        gz = sb([128, 512]); B_gz = Buf("gz")
        sq = sb([128, 512]); B_sq = Buf("sq")
        ssq = sb([128, 4]); B_ssq = Buf("ssq")
        prev = sb([64, 8, 64]); B_prev = Buf("prev")
        prevb = [sb([64, 8, 64], BF16) for _ in range(2)]; B_prevb = [Buf("prevb0"), Buf("prevb1")]
        a_l = abc[:, l * 8:(l + 1) * 8]
        dsk = ppc(l, PP_DSK, 512)
        gssd = ppc(l, PP_GSSD, 4)
        memset("pool", prev, 0.0, [B_prev])
        memset("pool", prevb[0], 0.0, [B_prevb[0]])
        BCT64 = BCT.rearrange("a (g n) t -> n (a g) t", g=2)

        def loads(gi):
            s = gi % 2
            tsl = slice(gi * TT, (gi + 1) * TT)
            for j in range(4):
                dma(xsT[s][:, j, :], XST[j, :, tsl], [B_XST], [B_xsT[s]], B_xsT[s])
            dma(bt128[s], BCT[0, :, tsl], [B_BCT], [B_bt[s]], B_bt[s])
            for a_ in range(4):
                dma(bc64[s][:, a_, :], BCT64[:, a_, tsl], [B_BCT], [B_bc64[s]], B_bc64[s])
            dma(ztok[s], ZTOK[tsl, :].rearrange("(c p) f -> p c f", p=128), [B_ZTOK], [B_ztok[s]], B_ztok[s])
            dma(dtg[s], DTOK[tsl, :].rearrange("(c p) h -> p c h", p=128), [B_DTOK], [B_dtg[s]], B_dtg[s])
            cp("pool", bc64b[s], bc64[s], [B_bc64[s]], [B_bc64b[s]])

        def front(ci):
            gi, cg = ci // 4, ci % 4
            s = gi % 2
            p = ci % 2
            cs = slice(cg * 128, (cg + 1) * 128)
            xs_tok, B_xs = xs_tok2[p], B_xs2[p]
            btok, B_btok = btok2[p], B_btok2[p]
            sm, B_sm = sm2[p], B_sm2[p]
            MT, B_MT = MT2[p], B_MT2[p]
            xdt, B_xdt = xdt2[p], B_xdt2[p]
            xw, B_xw = xw2[p], B_xw2[p]
            da16 = sm[:, 0:16]; da = sm[:, 0:8]
            nacol, expA, dstate, cd, dtd, diff = [sm[:, 16 + i * 8:16 + (i + 1) * 8] for i in range(6)]
            for j in range(4):
                tr(psum[0][:, j * 128:(j + 1) * 128], xsT[s][:, j, cs], ID_F, [B_xsT[s], B_cf], [B_ps[0]])
            tr(psum[1][:, 0:128], bt128[s][:, cs], ID_F, [B_bt[s], B_cf], [B_ps[1].k("bt")])
            cp("act", xs_tok, psum[0], [B_ps[0]], [B_xs])
            cp("dve", btok, psum[1][:, 0:128], [B_ps[1].k("bt")], [B_btok])
            dtc = dtg[s][:, cg, :]
            tt("dve", da, dtc, a_l, ALU.mult, [B_dtg[s], B_abc], [B_sm.k("da")])
            mm(psum[1][:, 128:144], U_F, da16, True, True, [B_cf, B_sm.k("da")], [B_ps[1].k("ac")])
            mm(psum[1][:, 144:160], ONES_F, da16, True, True, [B_cf, B_sm.k("da")], [B_ps[1].k("ac")])
            ts("dve", nacol, psum[1][:, 128:136], -1.0, ALU.mult, [B_ps[1].k("ac")], [B_sm.k("nacol")])
            act(expA, psum[1][:, 128:136], AF.Exp, [B_ps[1].k("ac")], [B_sm.k("expA")])
            tt("dve", diff, psum[1][:, 144:152], nacol, ALU.add, [B_ps[1].k("ac"), B_sm.k("nacol")], [B_sm.k("diff")])
            act(dstate, diff, AF.Exp, [B_sm.k("diff")], [B_sm.k("dstate")])
            act(cd, psum[1][:, 144:152], AF.Exp, [B_ps[1].k("ac")], [B_sm.k("cd")])
            tt("dve", dtd, dtc, dstate, ALU.mult, [B_dtg[s], B_sm.k("dstate")], [B_sm.k("dtd")])
            tt("dve", Rm, U_F.unsqueeze(1).to_broadcast([128, 8, 128]), da.unsqueeze(2).to_broadcast([128, 8, 128]), ALU.mult,
               [B_cf, B_sm.k("da")], [B_R])
            for half in range(2):
                mm(psum[2 + half], ONES_F, Rm[:, half * 4:half * 4 + 4, :].rearrange("p a b -> p (a b)"), True, False, [B_cf, B_R], [B_ps[2 + half]])
                mm(psum[2 + half], ID_F, NEGM4, False, True, [B_cf], [B_ps[2 + half]])
            for h in range(8):
                act(dec[:, h, :], psum[2 + h // 4][:, (h % 4) * 128:(h % 4 + 1) * 128], AF.Exp,
                    [B_ps[2 + h // 4], B_sm.k("nacol")], [B_dec.k(h // 4)], bias=nacol[:, h:h + 1], scale=1.0)
            for g in range(2):
                mm(psum[1][:, 256 + g * 128:256 + (g + 1) * 128], bc64b[s][:, g, cs], bc64b[s][:, 2 + g, cs], True, True,
                   [B_bc64b[s]], [B_ps[1].k("cb%d" % g)])
                tt("dve", MT[:, g * 4:g * 4 + 4, :], psum[1][:, 256 + g * 128:256 + (g + 1) * 128].unsqueeze(1).to_broadcast([128, 4, 128]),
                   dec[:, g * 4:g * 4 + 4, :], ALU.mult, [B_ps[1].k("cb%d" % g), B_dec.k(g)], [B_MT.k(g)])
            xs3 = xs_tok.rearrange("p (h j) -> p h j", h=8)
            tt("dve", xdt.rearrange("p (h j) -> p h j", h=8), xs3, dtc.unsqueeze(2).to_broadcast([128, 8, 64]), ALU.mult,
               [B_xs, B_dtg[s]], [B_xdt])
            tt("pool", xw.rearrange("p (h j) -> p h j", h=8), xs3, dtd.unsqueeze(2).to_broadcast([128, 8, 64]), ALU.mult,
               [B_xs, B_sm.k("dtd")], [B_xw])

        def back(ci):
            gi, cg = ci // 4, ci % 4
            s = gi % 2
            p = ci % 2
            cs = slice(cg * 128, (cg + 1) * 128)
            tsl = slice(gi * TT, (gi + 1) * TT)
            xs_tok, B_xs = xs_tok2[p], B_xs2[p]
            btok, B_btok = btok2[p], B_btok2[p]
            sm, B_sm = sm2[p], B_sm2[p]
            MT, B_MT = MT2[p], B_MT2[p]
            xdt, B_xdt = xdt2[p], B_xdt2[p]
            xw, B_xw = xw2[p], B_xw2[p]
            nacol, expA, dstate, cd, dtd, diff = [sm[:, 16 + i * 8:16 + (i + 1) * 8] for i in range(6)]
            pb_cur = prevb[ci % 2]; Bpb_cur = B_prevb[ci % 2]
            pb_nxt = prevb[(ci + 1) % 2]; Bpb_nxt = B_prevb[(ci + 1) % 2]
            for h in range(8):
                mm(psum[4][:, h * 64:(h + 1) * 64], MT[:, h, :], xdt[:, h * 64:(h + 1) * 64], True, True, [B_MT.k(h // 4), B_xdt], [B_ps[4]])
            for h in range(8):
                mm(psum[5][:, h * 64:(h + 1) * 64], bc64b[s][:, 2 + h // 4, cs], pb_cur[:, h, :], True, True, [B_bc64b[s], Bpb_cur], [B_ps[5]])
            for g in range(2):
                mm(psum[6][0:64, g * 256:(g + 1) * 256], btok[:, g * 64:(g + 1) * 64], xw[:, g * 256:(g + 1) * 256], True, True,
                   [B_btok, B_xw], [B_ps[6]])
            tt("pool", prev, prev, cd[0:64, :].unsqueeze(2).to_broadcast([64, 8, 64]), ALU.mult, [B_prev, B_sm.k("cd")], [B_prev])
            tt("dve", prev, psum[6][0:64, :].rearrange("p (h j) -> p h j", h=8), prev, ALU.add, [B_ps[6], B_prev], [B_prev])
            cp("pool", pb_nxt, prev, [B_prev], [Bpb_nxt])
            tt("dve", t1.rearrange("p (h j) -> p h j", h=8), psum[5].rearrange("p (h j) -> p h j", h=8),
               expA.unsqueeze(2).to_broadcast([128, 8, 64]), ALU.mult, [B_ps[5], B_sm.k("expA")], [B_t1])
            tt("pool", t2, xs_tok, dsk, ALU.mult, [B_xs, B_pp], [B_t2])
            tt("pool", t2, t2, t1, ALU.add, [B_t2, B_t1], [B_t2])
            tt("dve", yv, psum[4], t2, ALU.add, [B_ps[4], B_t2], [B_y])
            act(gz, ztok[s][:, cg, :], AF.Silu, [B_ztok[s]], [B_gz])
            tt("pool", yv, yv, gz, ALU.mult, [B_y, B_gz], [B_y])
            memset("pool", ssq[:, 0:1], 0.0, [B_ssq])
            act(sq, yv, AF.Square, [B_y, B_ssq], [B_sq, B_ssq], accum=ssq[:, 0:1])
            act(ssq[:, 1:2], ssq[:, 0:1], AF.Ln, [B_ssq, B_cf], [B_ssq], bias=EPSC, scale=1.0 / 512)
            act(ssq[:, 2:3], ssq[:, 1:2], AF.Exp, [B_ssq], [B_ssq], scale=-0.5)
            ts("dve", sq, yv, ssq[:, 2:3], ALU.mult, [B_y, B_ssq], [B_sq])
            for j in range(4):
                tr(psum[7][:, j * 128:(j + 1) * 128], sq[:, j * 128:(j + 1) * 128], ID_F, [B_sq, B_cf], [B_ps[7]])
            tt("dve", yst[s][:, :, cs], psum[7].rearrange("p (a b) -> p a b", a=4), gssd.unsqueeze(2).to_broadcast([128, 4, 128]), ALU.mult,
               [B_ps[7], B_pp], [B_yst[s]])
            if cg == 3:
                dma(YT[0:512, tsl].rearrange("(j p) t -> p j t", p=128), yst[s], [B_yst[s]], [B_YT], B_YT)

        for ci in range(NCH + 1):
            if ci < NCH and ci % 4 == 0:
                loads(ci // 4)
            if ci == 0:
                front(0)
            elif ci == NCH:
                back(NCH - 1)
            else:
                interleave(P, (lambda c: (lambda: front(c)))(ci), (lambda c: (lambda: back(c)))(ci - 1))
        P.barrier()
        reset(m)

    def sba(l):
        m = mark()
        kt_all = sb([128, 8, TL], BF16); B_kt = Buf("kt_all")
        v_all = sb([128, TL // 128, 576], BF16); B_v = Buf("v_all")
        qg = [sb([128, 8, TT], BF16) for _ in range(2)]; B_qg = [Buf("qg0"), Buf("qg1")]
        e_sb = [sb([128, TT]) for _ in range(3)]; B_e = [Buf("e%d" % i) for i in range(3)]
        sp_b = [sb([128, TT], BF16) for _ in range(2)]; B_sp = [Buf("sp%d" % i) for i in range(2)]
        r_sb = [sb([128, TT]) for _ in range(2)]; B_r = [Buf("r%d" % i) for i in range(2)]
        w_b = [sb([128, TT], BF16) for _ in range(2)]; B_w = [Buf("w%d" % i) for i in range(2)]
        acc = [sb([128, TT], BF16) for _ in range(2)]; B_acc = [Buf("acc%d" % i) for i in range(2)]
        o_sb = sb([64, 8, TT]); B_o = Buf("o_sb")
        osq = sb([64, 8, TT], BF16); B_osq = Buf("osq")
        rs = sb([128, TT]); B_rs = Buf("rs")
        yst1 = sb([64, 8, TT], BF16); yst = [yst1, yst1]; B_y1 = Buf("ysb"); B_yst = [B_y1, B_y1]
        gsb = ppc(l, PP_GSB, 8)
        memset("pool", v_all[:, :, 512:576], 0.0, [B_v])
        for h in range(8):
            dma(kt_all[0:64, h, :], KT[h, :, :], [B_KT], [B_kt], B_kt)
            dma(kt_all[64:128, h, :], KT[h, :, :], [B_KT], [B_kt], B_kt)
        for q4 in range(TL // 1024):
            bs = slice(q4 * 8, (q4 + 1) * 8)
            dma(v_all[:, bs, 0:512], VS[q4 * 1024:(q4 + 1) * 1024, :].rearrange("(b p) f -> p b f", p=128), [B_VS], [B_v], B_v)
        tiles = []
        for G in range(min(TL // TT, SBA_MAXG)):
            for h in range(8):
                kbs = list(range(4 * G + 3, -1, -1))
                for ii, kb in enumerate(kbs):
                    tiles.append((G, h, kb, ii == 0, ii == len(kbs) - 1))
        n = len(tiles)
        Z_PS = [0, 1]; R_PS = [2, 3]; O_PS = [4, 5]; SS_PS = 6

        def stage_q(i):
            G, h, kb, first, last = tiles[i]
            s = G % 2
            if h == 0 and first:
                dma(qg[s][0:64], QT[:, :, G * TT:(G + 1) * TT].rearrange("h d t -> d h t"), [B_QT], [B_qg[s]], B_qg[s])
                dma(qg[s][64:128], QT[:, :, G * TT:(G + 1) * TT].rearrange("h d t -> d h t"), [B_QT], [B_qg[s]], B_qg[s])
            j = kb - 4 * G
            c0 = 128 * max(j, 0)
            cs = slice(c0, TT)
            zi = Z_PS[i % 2]
            mm(psum[zi][:, cs], kt_all[:, h, kb * 128:(kb + 1) * 128], qg[s][:, h, cs], True, True, [B_kt, B_qg[s]], [B_ps[zi]])
            if j >= 0:
                mm(psum[zi][:, c0:c0 + 128], ID_B, NEGTRI_B, False, True, [B_cb], [B_ps[zi]])

        def stage_a(i):
            G, h, kb, first, last = tiles[i]
            s = G % 2
            j = kb - 4 * G
            c0 = 128 * max(j, 0)
            cs = slice(c0, TT)
            zi = Z_PS[i % 2]; ri = R_PS[i % 2]
            ei = i % 3; si = i % 2
            ai = (G * 8 + h) % 2
            act(e_sb[ei][:, cs], psum[zi][:, cs], AF.Exp, [B_ps[zi]], [B_e[ei]], scale=0.0625)
            act(sp_b[si][:, cs], e_sb[ei][:, cs], AF.Ln, [B_e[ei], B_cf], [B_sp[si]], bias=ONEC, scale=1.0)
            mm(psum[ri][:, cs], TRI_B, sp_b[si][:, cs], True, first, [B_cb, B_sp[si]], [B_ps[ri]])
            if not first:
                mm(psum[ri][:, cs], ONES_B, acc[ai][:, cs], False, True, [B_cb, B_acc[ai]], [B_ps[ri]])
            if first:
                memset("pool", acc[ai], 0.0, [B_acc[ai]])
            if not last:
                tt("pool", acc[ai][:, cs], acc[ai][:, cs], sp_b[si][:, cs], ALU.add, [B_acc[ai], B_sp[si]], [B_acc[ai]])

        def stage_b(i):
            G, h, kb, first, last = tiles[i]
            s = G % 2
            j = kb - 4 * G
            c0 = 128 * max(j, 0)
            cs = slice(c0, TT)
            ri = R_PS[i % 2]
            ei = i % 3; si = i % 2
            oi = O_PS[(G * 8 + h) % 2]
            if SBA_LEVEL < 3:
                return
            act(r_sb[si][:, cs], psum[ri][:, cs], AF.Exp, [B_ps[ri]], [B_r[si]], scale=-1.0)
            tt("dve", w_b[si][:, cs], e_sb[ei][:, cs], r_sb[si][:, cs], ALU.mult, [B_e[ei], B_r[si]], [B_w[si]])
            if first:
                for q4 in range(4):
                    mm(psum[oi][:, q4 * 128:(q4 + 1) * 128], ZERO_B, ONES_B, True, False, [B_cb], [B_ps[oi]])
            mm(psum[oi][:, cs], v_all[:, kb, h * 64:h * 64 + 128], w_b[si][:, cs], False, last, [B_v, B_w[si]], [B_ps[oi]])
            if last and SBA_LEVEL >= 4:
                cp("dve", o_sb[:, h, :], psum[oi][0:64, :], [B_ps[oi]], [B_o.k(h)])
                act(osq[:, h, :], psum[oi][0:64, :], AF.Square, [B_ps[oi]], [B_osq.k(h)])
                if h == 7 and SBA_LEVEL >= 5:
                    for hh in range(8):
                        mm(psum[SS_PS], ONES_B[0:64, :], osq[:, hh, :], hh == 0, hh == 7, [B_cb, B_osq.k(hh)], [B_ps[SS_PS]])
                    act(rs, psum[SS_PS], AF.Ln, [B_ps[SS_PS], B_cf], [B_rs], bias=EPSC, scale=1.0 / 512)
                    act(rs, rs, AF.Exp, [B_rs], [B_rs], scale=-0.5)
                    for hh in range(8):
                        stt("dve", yst[s][:, hh, :], o_sb[:, hh, :], gsb[0:64, hh:hh + 1], rs[0:64, :], ALU.mult, ALU.mult,
                            [B_o.k(hh), B_pp, B_rs], [B_yst[s]])
                    dma(YT[512:1024, G * TT:(G + 1) * TT].rearrange("(h d) t -> d h t", d=64), yst[s], [B_yst[s]], [B_YT], B_YT)

        for i in range(n + 2):
            if i < n:
                stage_q(i)
            if 1 <= i <= n:
                stage_a(i - 1)
            if i >= 2:
                stage_b(i - 2)
        P.barrier()
        reset(m)

    for st in range(NST):
        embed(st)
        if NST > 1:
            store_hT(st)
        g1(0, st)
    P.barrier()
    done = False
    for l in range(n_layers):
        if stop == "g1":
            break
        reset(PERSIST)
        ssd(l)
        if stop == "ssd":
            break
        sba(l)
        if stop == "g2":
            break
        for st in range(NST):
            if NST > 1:
                load_hT(st)
            g3(l, st)
            if l + 1 < n_layers:
                if NST > 1:
                    store_hT(st)
                g1(l + 1, st)
            else:
                final(st)
                done = True
        P.barrier()
    dumps = {}
    if dump:
        alld = (("QT", QT, B_QT), ("KT", KT, B_KT), ("VS", VS, B_VS), ("ZTOK", ZTOK, B_ZTOK), ("XST", XST, B_XST),
                ("BCT", BCT, B_BCT), ("DTOK", DTOK, B_DTOK), ("YT", YT, B_YT), ("HT", HT, None))
        for name, ap_, Bf in [d_ for d_ in alld if dump is True or d_[0] in dump]:
            o = nc.dram_tensor("dump_" + name, list(ap_.shape), ap_.dtype, kind="ExternalOutput").ap()
            db = Buf("dump_" + name)
            rd = [Bf] if Bf is not None else list(B_HT)
            if len(ap_.shape) == 4:
                for s_ in range(ap_.shape[0]):
                    dma(o[s_].rearrange("p c t -> p (c t)"), ap_[s_].rearrange("p c t -> p (c t)"), rd, [db], db)
            elif len(ap_.shape) == 3:
                for s_ in range(ap_.shape[0]):
                    dma(o[s_], ap_[s_], rd, [db], db)
            else:
                dma(o, ap_, rd, [db], db)
    P.barrier()
    P.emit()
    return nc


_CACHE = {}


def kernel(**inputs):
    p = {k: np.asarray(v) for k, v in inputs.items()}
    if "nc" not in _CACHE:
        _CACHE["nc"] = build()
    nc = _CACHE["nc"]
    cf = host_consts()
    pp = host_params(p)
    shared = {k: np.ascontiguousarray(p[k], dtype=np.float32) for k in ("w_in", "w_out", "w_xq", "w_xk", "w_xv", "w_xo", "w_ff1", "w_ff2")}
    in_maps = []
    for c in range(8):
        b = c % 4
        m = {"x": np.ascontiguousarray(p["x"][b], dtype=np.float32), "mem": np.ascontiguousarray(p["mem"][b], dtype=np.float32),
             "cf": cf, "pp": pp}
        m.update(shared)
        in_maps.append(m)
    res = run_bass_kernel_spmd(nc, in_maps, core_ids=list(range(8)))
    out = np.stack([np.asarray(res.results[b]["out"], dtype=np.float32) for b in range(4)], axis=0)
    return out
```

```python
import math
import contextlib
import numpy as np
import concourse.bass as bass
import concourse.mybir as mybir
from concourse.bass_utils import run_bass_kernel_spmd

F32 = mybir.dt.float32
BF16 = mybir.dt.bfloat16
AF = mybir.ActivationFunctionType
ALU = mybir.AluOpType

ENGS = ("pe", "act", "dve", "pool", "sp")


class Buf:
    def __init__(self, name):
        self.name = name
        self.st = {"*": [[], []]}
        self.sem = None
        self.ndma = 0
        self.excl = False
        self.last_by_eng = {}

    def k(self, key):
        return (self, key)


class Op:
    __slots__ = ("eng", "fn", "waits", "idx", "is_dma", "dest")


class Prog:
    def __init__(self, nc):
        self.nc = nc
        self.ops = {e: [] for e in ENGS}
        self.dma_bufs = []
        self.nops = 0

    @staticmethod
    def _norm(x):
        if isinstance(x, Buf):
            return (x, None)
        return x

    def _entries(self, buf, key):
        d = buf.st
        if key is None:
            return list(d.values())
        if key not in d:
            d[key] = [list(d["*"][0]), list(d["*"][1])]
        return [d[key]]

    def add(self, eng, fn, reads=(), writes=(), dma_dest=None):
        if getattr(self, "capture", None) is not None:
            self.capture.append((eng, fn, reads, writes, dma_dest))
            return None
        op = Op()
        op.eng = eng
        op.fn = fn
        op.is_dma = dma_dest is not None
        op.dest = dma_dest
        deps = []
        reads = [self._norm(r) for r in reads if r is not None]
        writes = [self._norm(w) for w in writes if w is not None]
        for (b, key) in reads:
            for ent in self._entries(b, key):
                deps.extend(ent[0])
        for (b, key) in writes:
            for ent in self._entries(b, key):
                samegen = op.is_dma and len(ent[0]) > 0 and all(w.is_dma for w in ent[0]) and len(ent[1]) == 0
                if not samegen:
                    deps.extend(ent[0])
                deps.extend(ent[1])
        for (b, key) in list(reads) + list(writes):
            if b.excl:
                for e2, y in b.last_by_eng.items():
                    if e2 != eng:
                        deps.append(y)
                b.last_by_eng[eng] = op
        for (b, key) in writes:
            for ent in self._entries(b, key):
                samegen = op.is_dma and len(ent[0]) > 0 and all(w.is_dma for w in ent[0]) and len(ent[1]) == 0
                if samegen:
                    ent[0].append(op)
                else:
                    ent[0] = [op]
                    ent[1] = []
            if key is None:
                for kk in list(b.st.keys()):
                    b.st[kk][0] = list(b.st["*"][0])
                    b.st[kk][1] = []
        wset = set((id(b), key) for (b, key) in writes)
        for (b, key) in reads:
            if (id(b), key) in wset:
                continue
            for ent in self._entries(b, key):
                ent[1].append(op)
        if op.is_dma:
            d = dma_dest
            if d.sem is None:
                self.dma_bufs.append(d)
                d.sem = True
            d.ndma += 1
        lst = self.ops[eng]
        lst.append(op)
        op.idx = len(lst)
        waits = {}
        for y in deps:
            if y is op:
                continue
            if y.is_dma:
                key = ("d", id(y.dest))
                val = 16 * y.dest.ndma if y.dest is not dma_dest else 16 * (y.dest.ndma - 1)
                if val <= 0:
                    continue
                ent = (y.dest, val)
            else:
                if y.eng == eng and eng == "pe":
                    continue
                key = ("e", y.eng)
                val = y.idx
                ent = (y.eng, val)
            if key not in waits or waits[key][1] < val:
                waits[key] = ent
        op.waits = waits
        self.nops += 1
        return op

    def barrier(self):
        snap_e = {e: len(self.ops[e]) for e in ENGS}
        snap_d = [(d, 16 * d.ndma) for d in self.dma_bufs]
        for e in ENGS:
            op = Op()
            op.eng = e
            op.fn = None
            op.is_dma = False
            op.dest = None
            w = {}
            for e2 in ENGS:
                if e2 == e:
                    continue
                w[("e", e2)] = (e2, snap_e[e2])
            for (d, v) in snap_d:
                if v > 0:
                    w[("d", id(d))] = (d, v)
            op.waits = w
            lst = self.ops[e]
            lst.append(op)
            op.idx = len(lst)

    def emit(self):
        nc = self.nc
        with contextlib.ExitStack() as es:
            esem = {e: es.enter_context(nc.semaphore("es_" + e)) for e in ENGS}
            for i, d in enumerate(self.dma_bufs):
                d.sem = es.enter_context(nc.semaphore("ds%d_%s" % (i, d.name)))
            block = es.enter_context(nc.Block())
            cidx = {}
            for e in ENGS:
                c = 0
                m = [0]
                for op in self.ops[e]:
                    if not op.is_dma:
                        c += 1
                    m.append(c)
                cidx[e] = m

            def make(e):
                ops = self.ops[e]

                def body(eng):
                    seen = {}
                    for op in ops:
                        for key, (obj, val) in op.waits.items():
                            if key[0] == "e":
                                sem = esem[obj]
                                v = cidx[obj][val]
                            else:
                                sem = obj.sem
                                v = val
                            if v <= 0 or seen.get(key, 0) >= v:
                                continue
                            seen[key] = v
                            eng.wait_ge(sem, v)
                        if op.fn is None:
                            eng.nop().then_inc(esem[e], 1)
                            continue
                        ins = op.fn(eng)
                        if op.is_dma:
                            ins.then_inc(op.dest.sem, 16)
                        else:
                            ins.then_inc(esem[e], 1)
                return body

            block.tensor(make("pe"))
            block.scalar(make("act"))
            block.vector(make("dve"))
            block.gpsimd(make("pool"))
            block.sync(make("sp"))


def interleave(P, *fns):
    lists = []
    for f in fns:
        P.capture = lst = []
        f()
        lists.append(lst)
    P.capture = None
    pos = [0] * len(lists)
    total = sum(len(l) for l in lists)
    for _ in range(total):
        best, bi = None, -1
        for i, l in enumerate(lists):
            if pos[i] < len(l):
                frac = pos[i] / len(l)
                if best is None or frac < best:
                    best, bi = frac, i
        P.add(*lists[bi][pos[bi]])
        pos[bi] += 1


class Rot:
    def __init__(self, items):
        self.items = items
        self.i = 0

    def next(self):
        it = self.items[self.i % len(self.items)]
        self.i += 1
        return it


D = 1024
TL = 4096
TS = 2048
NST = TL // TS
TT = 512
NT = TS // TT
L = 2
MEM = 256
IN_DIM = 2824
C_Z, C_X, C_B, C_C, C_DT, C_Q, C_K, C_V = 0, 512, 1024, 1152, 1280, 1288, 1800, 2312
EPS = 1e-5
NEG = -30000.0
SBA_MAXG = 99
SBA_LEVEL = 9

CF_ID, CF_ONES, CF_U, CF_NEGM4, CF_TRI, CF_NEGTRI, CF_EPS, CF_ONE, NCF = 0, 128, 256, 384, 896, 1024, 1152, 1153, 1160
PP_GMIX, PP_GXA, PP_GFF, PP_GMEM, PP_GSSD, PP_GSB, PP_CW, PP_CB, PP_DTB, PP_ALOG, PP_DSK, PPW = 0, 8, 16, 24, 32, 36, 44, 68, 74, 82, 90, 602
PP_FINAL = L * PPW
NPP = PP_FINAL + 8


def host_consts():
    cf = np.zeros((128, NCF), np.float32)
    i = np.arange(128)
    cf[:, CF_ID:CF_ID + 128] = np.eye(128, dtype=np.float32)
    cf[:, CF_ONES:CF_ONES + 128] = 1.0
    cf[:, CF_U:CF_U + 128] = (i[:, None] <= i[None, :]).astype(np.float32)
    negm = np.where(i[None, :] >= i[:, None], 0.0, NEG).astype(np.float32)
    cf[:, CF_NEGM4:CF_NEGM4 + 512] = np.tile(negm, (1, 4))
    cf[:, CF_TRI:CF_TRI + 128] = (i[:, None] >= i[None, :]).astype(np.float32)
    cf[:, CF_NEGTRI:CF_NEGTRI + 128] = np.where(i[:, None] < i[None, :], 0.0, 8 * NEG)
    cf[:, CF_EPS] = EPS
    cf[:, CF_ONE] = 1.0
    return cf


def host_params(p):
    pp = np.zeros((128, NPP), np.float32)
    col = lambda v, n: np.ascontiguousarray(np.asarray(v, np.float32).reshape(n, 128).T)
    for l in range(L):
        o = l * PPW
        pp[:, o + PP_GMIX:o + PP_GMIX + 8] = col(p["norm_mix_g"][l], 8)
        pp[:, o + PP_GXA:o + PP_GXA + 8] = col(p["norm_xa_g"][l], 8)
        pp[:, o + PP_GFF:o + PP_GFF + 8] = col(p["norm_ff_g"][l], 8)
        pp[:, o + PP_GMEM:o + PP_GMEM + 8] = col(p["norm_mem_g"][l], 8)
        pp[:, o + PP_GSSD:o + PP_GSSD + 4] = col(p["ssd_norm_g"][l], 4)
        pp[0:64, o + PP_GSB:o + PP_GSB + 8] = np.asarray(p["sb_norm_g"][l], np.float32).reshape(8, 64).T
        cw = np.asarray(p["conv_w"][l], np.float32)
        pp[:, o + PP_CW:o + PP_CW + 24] = cw.reshape(4, 6, 128).transpose(2, 1, 0).reshape(128, 24)
        pp[:, o + PP_CB:o + PP_CB + 6] = col(p["conv_b"][l], 6)
        pp[:, o + PP_DTB:o + PP_DTB + 8] = np.asarray(p["dt_bias"][l], np.float32)[None, :]
        pp[:, o + PP_ALOG:o + PP_ALOG + 8] = np.asarray(p["a_log"][l], np.float32)[None, :]
        pp[:, o + PP_DSK:o + PP_DSK + 512] = np.repeat(np.asarray(p["d_skip"][l], np.float32), 64)[None, :]
    pp[:, PP_FINAL:PP_FINAL + 8] = col(p["final_g"], 8)
    return pp


def build(n_layers=L, stop=None, dump=False):
    nc = bass.Bass("TRN2", target_bir_lowering=False)
    P = Prog(nc)

    def din(name, shape, dt=F32):
        return nc.dram_tensor(name, shape, dt, kind="ExternalInput").ap()

    x_d = din("x", [TL, D])
    mem_d = din("mem", [MEM, D])
    w_in = din("w_in", [L, D, IN_DIM])
    w_out = din("w_out", [L, D, D])
    w_xq = din("w_xq", [L, D, 512])
    w_xk = din("w_xk", [L, D, 512])
    w_xv = din("w_xv", [L, D, 512])
    w_xo = din("w_xo", [L, 512, D])
    w_ff1 = din("w_ff1", [L, D, 4096])
    w_ff2 = din("w_ff2", [L, 4096, D])
    cf_d = din("cf", [128, NCF])
    pp_d = din("pp", [128, NPP])
    out_d = nc.dram_tensor("out", [TL, D], F32, kind="ExternalOutput").ap()

    def dscr(name, shape, dt):
        return nc.dram_tensor(name, shape, dt).ap()

    HT = dscr("HT", [NST, 128, 8, TS], F32)
    QT = dscr("QT", [8, 64, TL], BF16)
    KT = dscr("KT", [8, 64, TL], BF16)
    VS = dscr("VS", [TL, 512], BF16)
    ZTOK = dscr("ZTOK", [TL, 512], F32)
    XST = dscr("XST", [4, 128, TL], F32)
    BCT = dscr("BCT", [2, 128, TL], F32)
    DTOK = dscr("DTOK", [TL, 8], F32)
    YT = dscr("YT", [D, TL], BF16)
    B_HT = [Buf("HT%d" % s) for s in range(NST)]
    B_QT, B_KT, B_VS, B_ZTOK, B_XST, B_BCT, B_DTOK, B_YT, B_OUT = [Buf(n) for n in "QT KT VS ZTOK XST BCT DTOK YT OUT".split()]

    ARENA_BYTES = 206 * 1024
    arena = nc.alloc_sbuf_tensor("arena", [128, ARENA_BYTES // 2], BF16).ap()
    st_ = {"off": 0}

    def sb(shape, dt=F32, parts=128):
        n = 1
        for s_ in shape[1:]:
            n *= s_
        nbytes = n * (4 if dt == F32 else 2)
        nbytes = (nbytes + 31) // 32 * 32
        off = st_["off"]
        assert off + nbytes <= ARENA_BYTES, ("SBUF arena overflow", off, nbytes)
        st_["off"] = off + nbytes
        v = arena[0:shape[0], off // 2:(off + nbytes) // 2]
        if dt == F32:
            v = v.bitcast(F32)
        v = v[:, 0:n]
        if len(shape) == 3:
            v = v.rearrange("p (a b) -> p a b", a=shape[1])
        elif len(shape) == 4:
            v = v.rearrange("p (a b c) -> p a b c", a=shape[1], b=shape[2])
        return v

    def mark():
        return st_["off"]

    def reset(m):
        st_["off"] = m

    psum = [nc.alloc_psum_tensor("ps%d" % i, [128, 512], F32).ap() for i in range(8)]
    B_ps = [Buf("ps%d" % i) for i in range(8)]
    for b_ in B_ps:
        b_.excl = True

    def mm(out, lhsT, rhs, start, stop, r, w):
        P.add("pe", lambda e: e.matmul(out, lhsT=lhsT, rhs=rhs, start=start, stop=stop), reads=r, writes=w)

    def tr(out, in_, ident, r, w):
        P.add("pe", lambda e: e.transpose(out=out, in_=in_, identity=ident), reads=r, writes=w)

    def act(out, in_, func, r, w, bias=None, scale=None, accum=None):
        kw = {}
        if bias is not None:
            kw["bias"] = bias
        if scale is not None:
            kw["scale"] = scale
        if accum is not None:
            kw["accum_out"] = accum
        P.add("act", lambda e: e.activation(out=out, in_=in_, func=func, **kw), reads=r, writes=w)

    def tt(eng, out, in0, in1, op, r, w):
        P.add(eng, lambda e: e.tensor_tensor(out=out, in0=in0, in1=in1, op=op), reads=r, writes=w)

    def ts(eng, out, in0, s1, op0, r, w, s2=None, op1=None):
        if op1 is None:
            P.add(eng, lambda e: e.tensor_scalar(out=out, in0=in0, scalar1=s1, scalar2=None, op0=op0), reads=r, writes=w)
        else:
            P.add(eng, lambda e: e.tensor_scalar(out=out, in0=in0, scalar1=s1, scalar2=s2, op0=op0, op1=op1), reads=r, writes=w)

    def stt(eng, out, in0, scalar, in1, op0, op1, r, w):
        P.add(eng, lambda e: e.scalar_tensor_tensor(out=out, in0=in0, scalar=scalar, in1=in1, op0=op0, op1=op1), reads=r, writes=w)

    def cp(eng, out, in_, r, w):
        if eng == "act":
            P.add("act", lambda e: e.activation(out=out, in_=in_, func=AF.Copy), reads=r, writes=w)
        else:
            P.add(eng, lambda e: e.tensor_copy(out=out, in_=in_), reads=r, writes=w)

    def memset(eng, out, val, w):
        P.add(eng, lambda e: e.memset(out, val), writes=w)

    def dma(out, in_, r, w, dest, eng="sp"):
        P.add(eng, lambda e: e.dma_start(out=out, in_=in_), reads=r, writes=w, dma_dest=dest)

    cf = sb([128, NCF]); B_cf = Buf("cf")
    pp = sb([128, NPP]); B_pp = Buf("pp")
    NCB = 128 * 5
    cb = sb([128, NCB], BF16); B_cb = Buf("cb")
    ID_F, ONES_F, U_F = cf[:, CF_ID:CF_ID + 128], cf[:, CF_ONES:CF_ONES + 128], cf[:, CF_U:CF_U + 128]
    NEGM4 = cf[:, CF_NEGM4:CF_NEGM4 + 512]
    EPSC, ONEC = cf[:, CF_EPS:CF_EPS + 1], cf[:, CF_ONE:CF_ONE + 1]
    ID_B, ONES_B, TRI_B, NEGTRI_B, ZERO_B = cb[:, 0:128], cb[:, 128:256], cb[:, 256:384], cb[:, 384:512], cb[:, 512:640]
    abc = sb([128, L * 8]); B_abc = Buf("abc")
    halo = sb([128, 6, 3]); B_halo = Buf("halo")
    kxT = [sb([128, 4, MEM], BF16) for _ in range(L)]; B_kx = [Buf("kx%d" % l) for l in range(L)]
    vx = [sb([128, 2, 512], BF16) for _ in range(L)]; B_vx = [Buf("vx%d" % l) for l in range(L)]
    PERSIST = mark()
    wbf = [sb([128, 4096], BF16) for _ in range(3)]; B_wbf = [Buf("wbf0"), Buf("wbf1"), Buf("wbf2")]
    wrot = Rot([0, 1, 2])
    WEND = mark()

    dma(cf, cf_d, [], [B_cf], B_cf)
    dma(pp, pp_d, [], [B_pp], B_pp)
    cp("dve", cb[:, 0:128], cf[:, CF_ID:CF_ID + 128], [B_cf], [B_cb])
    cp("dve", cb[:, 128:256], cf[:, CF_ONES:CF_ONES + 128], [B_cf], [B_cb])
    cp("dve", cb[:, 256:384], cf[:, CF_TRI:CF_TRI + 128], [B_cf], [B_cb])
    cp("dve", cb[:, 384:512], cf[:, CF_NEGTRI:CF_NEGTRI + 128], [B_cf], [B_cb])
    memset("pool", cb[:, 512:640], 0.0, [B_cb])
    for l in range(L):
        act(abc[:, l * 8:(l + 1) * 8], pp[:, l * PPW + PP_ALOG:l * PPW + PP_ALOG + 8], AF.Exp, [B_pp], [B_abc])
    ts("dve", abc, abc, -1.0, ALU.mult, [B_abc], [B_abc])

    def ppc(l, off, n):
        return pp[:, l * PPW + off:l * PPW + off + n]

    def load_w(src3, KC, N):
        i = wrot.next()
        dst = wbf[i][:, 0:KC * N].rearrange("p (k n) -> p k n", k=KC)
        for k in range(KC):
            dma(dst[:, k, :], src3[:, k, :], [], [B_wbf[i]], B_wbf[i], eng="pool")
        return dst, B_wbf[i]

    def wview(w2d, r0, KC, c0, N):
        return w2d[r0:r0 + KC * 128, c0:c0 + N].rearrange("(k p) n -> p k n", p=128)

    prot = Rot([0, 1, 2, 3])

    def proj_fm(w2d, r0, KC, c0, ncols, colblk, actT, Bact, ntiles, evac, tcol0=0):
        for cbi in range(ncols // colblk):
            wv, Bw = load_w(wview(w2d, r0, KC, c0 + cbi * colblk, colblk), KC, colblk)
            for jj in range(colblk // 128):
                for t in range(ntiles):
                    pi = prot.next()
                    for k in range(KC):
                        mm(psum[pi], wv[:, k, jj * 128:(jj + 1) * 128], actT[:, k, tcol0 + t * TT:tcol0 + (t + 1) * TT],
                           k == 0, k == KC - 1, [Bw, Bact], [B_ps[pi]])
                    evac(cbi * (colblk // 128) + jj, t, psum[pi], B_ps[pi])

    def rmsnorm_fm(hT, BhT, gcols, actT, Bact, sqb, B_sqb, rstd, B_rstd, nchunks=8, scale=1.0 / D):
        for t in range(NT):
            sl = slice(t * TT, (t + 1) * TT)
            for c in range(nchunks):
                act(sqb[:, c, :], hT[:, c, sl], AF.Square, [BhT.k((c, t))], [B_sqb.k(c)])
            pi = prot.next()
            for c in range(nchunks):
                mm(psum[pi], ONES_B, sqb[:, c, :], c == 0, c == nchunks - 1, [B_cb, B_sqb.k(c)], [B_ps[pi]])
            act(rstd, psum[pi], AF.Ln, [B_ps[pi], B_cf], [B_rstd], bias=EPSC, scale=scale)
            act(rstd, rstd, AF.Exp, [B_rstd], [B_rstd], scale=-0.5)
            for c in range(nchunks):
                stt("dve", actT[:, c, sl], hT[:, c, sl], gcols[:, c:c + 1], rstd, ALU.mult, ALU.mult,
                    [BhT.k((c, t)), B_pp, B_rstd], [Bact])

    m0 = mark()
    assert m0 == WEND
    mtok = sb([128, D]); B_mtok = Buf("mtok")
    mn = sb([128, D]); B_mn = Buf("mn")
    msc = sb([128, 4]); B_msc = Buf("msc")
    memnT = [sb([128, 8, MEM], BF16) for _ in range(L)]; B_memn = [Buf("memn%d" % l) for l in range(L)]
    for blk in range(2):
        dma(mtok, mem_d[blk * 128:(blk + 1) * 128, :], [], [B_mtok], B_mtok)
        memset("pool", msc[:, 0:1], 0.0, [B_msc])
        act(mn, mtok, AF.Square, [B_mtok, B_msc], [B_mn, B_msc], accum=msc[:, 0:1])
        act(msc[:, 1:2], msc[:, 0:1], AF.Ln, [B_msc, B_cf], [B_msc], bias=EPSC, scale=1.0 / D)
        act(msc[:, 2:3], msc[:, 1:2], AF.Exp, [B_msc], [B_msc], scale=-0.5)
        ts("dve", mn, mtok, msc[:, 2:3], ALU.mult, [B_mtok, B_msc], [B_mn])
        for half in range(2):
            for c4 in range(4):
                c = half * 4 + c4
                tr(psum[half][:, c4 * 128:(c4 + 1) * 128], mn[:, c * 128:(c + 1) * 128], ID_F, [B_mn, B_cf], [B_ps[half]])
            for l in range(L):
                g = ppc(l, PP_GMEM + half * 4, 4)
                tt("dve", memnT[l][:, half * 4:half * 4 + 4, blk * 128:(blk + 1) * 128],
                   psum[half].rearrange("p (a b) -> p a b", a=4), g.unsqueeze(2).to_broadcast([128, 4, 128]), ALU.mult,
                   [B_ps[half], B_pp], [B_memn[l]])
    for l in range(L):
        wv, Bw = load_w(wview(w_xk[l], 0, 8, 0, 512), 8, 512)
        for hx in range(4):
            pi = prot.next()
            for k in range(8):
                mm(psum[pi][:, 0:MEM], wv[:, k, hx * 128:(hx + 1) * 128], memnT[l][:, k, :], k == 0, k == 7, [Bw, B_memn[l]], [B_ps[pi]])
            cp("act", kxT[l][:, hx, :], psum[pi][:, 0:MEM], [B_ps[pi]], [B_kx[l]])
        wv, Bw = load_w(wview(w_xv[l], 0, 8, 0, 512), 8, 512)
        for mb in range(2):
            pi = prot.next()
            for k in range(8):
                mm(psum[pi], memnT[l][:, k, mb * 128:(mb + 1) * 128], wv[:, k, :], k == 0, k == 7, [Bw, B_memn[l]], [B_ps[pi]])
            cp("dve", vx[l][:, mb, :], psum[pi], [B_ps[pi]], [B_vx[l]])
    P.barrier()
    reset(m0)

    hT = sb([128, 8, TS]); B_hT = Buf("hT")
    actT = sb([128, 8, TS], BF16); B_actT = Buf("actT")
    SCRA_OFF = mark()
    scrA = sb([128, 4, TS], BF16); B_scrA = Buf("scrA")
    scrB = sb([128, 4, TS], BF16); B_scrB = Buf("scrB")
    sqb = sb([128, 8, TT], BF16); B_sqb = Buf("sqb")
    rstd = sb([128, TT]); B_rstd = Buf("rstd")
    evs = [sb([128, TT]) for _ in range(2)]; B_evs = [Buf("evs0"), Buf("evs1")]
    evrot = Rot([0, 1])
    evb = [sb([128, TT], BF16) for _ in range(2)]; B_evb = [Buf("evb0"), Buf("evb1")]
    evbrot = Rot([0, 1])
    xr = [sb([128, TT + 3]) for _ in range(2)]; B_xr = [Buf("xr0"), Buf("xr1")]
    cacc = sb([128, TT]); B_cacc = Buf("cacc")
    wdt = sb([128, 8, 8], BF16); B_wdt = Buf("wdt")
    wdtf = sb([128, 8, 8]); B_wdtf = Buf("wdtf")
    dtst = sb([128, 16, 8]); B_dtst = Buf("dtst")
    dtt = sb([128, 8]); B_dtt = Buf("dtt")
    TOKWISE_END = mark()
    eng_alt = Rot(["act", "dve"])

    def embed(st):
        xin = [evs[0], evs[1]]
        for blk in range(TS // 128):
            r0 = st * TS + blk * 128
            for half in range(2):
                i = evrot.next()
                dma(evs[i], x_d[r0:r0 + 128, half * 512:(half + 1) * 512], [], [B_evs[i]], B_evs[i])
                pi = prot.next()
                for c4 in range(4):
                    tr(psum[pi][:, c4 * 128:(c4 + 1) * 128], evs[i][:, c4 * 128:(c4 + 1) * 128], ID_F, [B_evs[i], B_cf], [B_ps[pi]])
                t = blk // 4
                cp(eng_alt.next(), hT[:, half * 4:half * 4 + 4, blk * 128:(blk + 1) * 128],
                   psum[pi].rearrange("p (a b) -> p a b", a=4), [B_ps[pi]],
                   [B_hT.k((half * 4 + c4, t)) for c4 in range(4)])

    def store_hT(st):
        for c in range(8):
            dma(HT[st, :, c, :], hT[:, c, :], [B_hT], [B_HT[st]], B_HT[st])

    def load_hT(st):
        for c in range(8):
            dma(hT[:, c, :], HT[st, :, c, :], [B_HT[st]], [B_hT], B_hT)

    def g1(l, st):
        tok0 = st * TS
        rmsnorm_fm(hT, B_hT, ppc(l, PP_GMIX, 8), actT, B_actT, sqb, B_sqb, rstd, B_rstd)
        wl = w_in[l]
        if st == 0:
            memset("pool", halo, 0.0, [B_halo])
        cw = ppc(l, PP_CW, 24)
        cbias = ppc(l, PP_CB, 6)
        cstate = {"n": 0}

        def conv_evac(base):
            def f(j, t, ps, Bp):
                cc = base + j
                n = cstate["n"]
                cstate["n"] += 1
                i = n % 2
                if t == 0:
                    cp("pool", xr[i][:, 0:3], halo[:, cc, :], [B_halo], [B_xr[i]])
                else:
                    cp("pool", xr[i][:, 0:3], xr[1 - i][:, TT:TT + 3], [B_xr[1 - i]], [B_xr[i]])
                cp("act", xr[i][:, 3:TT + 3], ps, [Bp], [B_xr[i]])
                if t == NT - 1:
                    cp("pool", halo[:, cc, :], xr[i][:, TT:TT + 3], [B_xr[i]], [B_halo])
                ts("dve", cacc, xr[i][:, 0:TT], cw[:, cc * 4:cc * 4 + 1], ALU.mult, [B_xr[i], B_pp], [B_cacc])
                for k in range(1, 4):
                    stt("dve", cacc, xr[i][:, k:k + TT], cw[:, cc * 4 + k:cc * 4 + k + 1], cacc, ALU.mult, ALU.add,
                        [B_xr[i], B_pp, B_cacc], [B_cacc])
                e = evrot.next()
                act(evs[e], cacc, AF.Silu, [B_cacc, B_pp], [B_evs[e]], bias=cbias[:, cc:cc + 1])
                sl = slice(tok0 + t * TT, tok0 + (t + 1) * TT)
                if cc < 4:
                    dma(XST[cc, :, sl], evs[e], [B_evs[e]], [B_XST], B_XST)
                else:
                    dma(BCT[cc - 4, :, sl], evs[e], [B_evs[e]], [B_BCT], B_BCT)
            return f

        proj_fm(wl, 0, 8, C_X, 512, 512, actT, B_actT, NT, conv_evac(0))
        proj_fm(wl, 0, 8, C_B, 256, 256, actT, B_actT, NT, conv_evac(4))

        def qk_evac(dst, Bdst):
            dflat = dst.rearrange("h d t -> (h d) t")

            def f(j, t, ps, Bp):
                e = evbrot.next()
                cp(eng_alt.next(), evb[e], ps, [Bp], [B_evb[e]])
                sl = slice(tok0 + t * TT, tok0 + (t + 1) * TT)
                dma(dflat[j * 128:(j + 1) * 128, sl], evb[e], [B_evb[e]], [Bdst], Bdst)
            return f

        proj_fm(wl, 0, 8, C_Q, 512, 512, actT, B_actT, NT, qk_evac(QT, B_QT))
        proj_fm(wl, 0, 8, C_K, 512, 512, actT, B_actT, NT, qk_evac(KT, B_KT))
        for (c0, dst, Bdst, isbf) in ((C_Z, ZTOK, B_ZTOK, False), (C_V, VS, B_VS, True)):
            wv, Bw = load_w(wview(wl, 0, 8, c0, 512), 8, 512)
            for blk in range(TS // 128):
                pi = prot.next()
                for k in range(8):
                    mm(psum[pi], actT[:, k, blk * 128:(blk + 1) * 128], wv[:, k, :], k == 0, k == 7, [Bw, B_actT], [B_ps[pi]])
                r0 = tok0 + blk * 128
                if isbf:
                    e = evbrot.next()
                    cp(eng_alt.next(), evb[e], psum[pi], [B_ps[pi]], [B_evb[e]])
                    dma(dst[r0:r0 + 128, :], evb[e], [B_evb[e]], [Bdst], Bdst)
                else:
                    e = evrot.next()
                    cp(eng_alt.next(), evs[e], psum[pi], [B_ps[pi]], [B_evs[e]])
                    dma(dst[r0:r0 + 128, :], evs[e], [B_evs[e]], [Bdst], Bdst)
        dma(wdtf, wview(wl, 0, 8, C_DT, 8), [], [B_wdtf], B_wdtf)
        cp("dve", wdt, wdtf, [B_wdtf], [B_wdt])
        dtb = ppc(l, PP_DTB, 8)
        pi = prot.next()
        NBLK = TS // 128
        for blk in range(NBLK):
            for k in range(8):
                mm(psum[pi][:, blk * 8:(blk + 1) * 8], actT[:, k, blk * 128:(blk + 1) * 128], wdt[:, k, :], k == 0, k == 7, [B_wdt, B_actT], [B_ps[pi]])
        tt("dve", dtst, psum[pi][:, 0:NBLK * 8].rearrange("p (b h) -> p b h", h=8), dtb.unsqueeze(1).to_broadcast([128, NBLK, 8]), ALU.add,
           [B_ps[pi], B_pp], [B_dtst])
        act(dtst, dtst, AF.Exp, [B_dtst], [B_dtst])
        act(dtst, dtst, AF.Ln, [B_dtst, B_cf], [B_dtst], bias=ONEC, scale=1.0)
        dma(DTOK[tok0:tok0 + TS, :].rearrange("(b p) h -> p b h", p=128), dtst, [B_dtst], [B_DTOK], B_DTOK)

    def add_evac(j, t, ps, Bp):
        sl = slice(t * TT, (t + 1) * TT)
        tt("dve", hT[:, j, sl], ps, hT[:, j, sl], ALU.add, [Bp, B_hT.k((j, t))], [B_hT.k((j, t))])

    def g3(l, st):
        tok0 = st * TS
        for c in range(8):
            dma(actT[:, c, :], YT[c * 128:(c + 1) * 128, tok0:tok0 + TS], [B_YT], [B_actT], B_actT)
        proj_fm(w_out[l], 0, 8, 0, D, 512, actT, B_actT, NT, add_evac)
        rmsnorm_fm(hT, B_hT, ppc(l, PP_GXA, 8), actT, B_actT, sqb, B_sqb, rstd, B_rstd)
        qxT, B_qx = scrA, B_scrA
        oxT, B_ox = scrB, B_scrB

        def q_evac(j, t, ps, Bp):
            cp(eng_alt.next(), qxT[:, j, t * TT:(t + 1) * TT], ps, [Bp], [B_qx.k((j, t))])
        proj_fm(w_xq[l], 0, 8, 0, 512, 512, actT, B_actT, NT, q_evac)
        sc = 1.0 / math.sqrt(128.0)
        for t in range(NT):
            sl = slice(t * TT, (t + 1) * TT)
            for hx in range(4):
                pT = []
                bk = (4, 5, 6, 7) if (t * 4 + hx) % 2 == 0 else (0, 1, 2, 3)
                for mb in range(2):
                    pi = bk[mb]
                    mm(psum[pi], kxT[l][:, hx, mb * 128:(mb + 1) * 128], qxT[:, hx, sl], True, True, [B_kx[l], B_qx.k((hx, t))], [B_ps[pi]])
                    e = evbrot.next()
                    act(evb[e], psum[pi], AF.Exp, [B_ps[pi]], [B_evb[e]], scale=sc)
                    pT.append(e)
                po, pd = bk[2], bk[3]
                for mb in range(2):
                    mm(psum[po], vx[l][:, mb, hx * 128:(hx + 1) * 128], evb[pT[mb]], mb == 0, mb == 1, [B_vx[l], B_evb[pT[mb]]], [B_ps[po]])
                for mb in range(2):
                    mm(psum[pd], ONES_B, evb[pT[mb]], mb == 0, mb == 1, [B_cb, B_evb[pT[mb]]], [B_ps[pd]])
                e = evrot.next()
                cp("act", evs[e], psum[pd], [B_ps[pd]], [B_evs[e]])
                P.add("dve", (lambda ee: (lambda en: en.reciprocal(out=evs[ee], in_=evs[ee])))(e), reads=[B_evs[e]], writes=[B_evs[e]])
                tt("dve", oxT[:, hx, sl], psum[po], evs[e], ALU.mult, [B_ps[po], B_evs[e]], [B_ox.k((hx, t))])
        proj_fm(w_xo[l], 0, 4, 0, D, 1024, oxT, B_ox, NT, add_evac)
        rmsnorm_fm(hT, B_hT, ppc(l, PP_GFF, 8), actT, B_actT, sqb, B_sqb, rstd, B_rstd)
        uT, B_u = scrA, B_scrA
        for fb in range(8):
            proj_fm(w_ff1[l], 0, 8, fb * 512, 512, 512, actT, B_actT, NT, u_evac_fix(uT, B_u))
            proj_fm(w_ff2[l], fb * 512, 4, 0, D, 1024, uT, B_u, NT, add_evac)

    def u_evac_fix(uT, B_u):
        def f(j, t, ps, Bp):
            e = evrot.next()
            act(evs[e], ps, AF.Relu, [Bp], [B_evs[e]])
            tt("pool" if (j + t) % 2 else "dve", uT[:, j, t * TT:(t + 1) * TT], evs[e], evs[e], ALU.mult, [B_evs[e]], [B_u.k((j, t))])
        return f

    def final(st):
        tok0 = st * TS
        for t in range(NT):
            sl = slice(t * TT, (t + 1) * TT)
            for c in range(8):
                act(sqb[:, c, :], hT[:, c, sl], AF.Square, [B_hT.k((c, t))], [B_sqb.k(c)])
            pi = prot.next()
            for c in range(8):
                mm(psum[pi], ONES_B, sqb[:, c, :], c == 0, c == 7, [B_cb, B_sqb.k(c)], [B_ps[pi]])
            act(rstd, psum[pi], AF.Ln, [B_ps[pi], B_cf], [B_rstd], bias=EPSC, scale=1.0 / D)
            act(rstd, rstd, AF.Exp, [B_rstd], [B_rstd], scale=-0.5)
            gf = pp[:, PP_FINAL:PP_FINAL + 8]
            for c in range(8):
                stt("dve", fin[:, c, :], hT[:, c, sl], gf[:, c:c + 1], rstd, ALU.mult, ALU.mult,
                    [B_hT.k((c, t)), B_pp, B_rstd], [B_fin.k(c)])
            for b4 in range(4):
                for half in range(2):
                    pi = 4 + half
                    for c4 in range(4):
                        c = half * 4 + c4
                        tr(psum[pi][:, c4 * 128:(c4 + 1) * 128], fin[:, c, b4 * 128:(b4 + 1) * 128], ID_F, [B_fin.k(c), B_cf], [B_ps[pi]])
                    e = evrot.next()
                    cp(eng_alt.next(), evs[e], psum[pi], [B_ps[pi]], [B_evs[e]])
                    r0 = tok0 + t * TT + b4 * 128
                    dma(out_d[r0:r0 + 128, half * 512:(half + 1) * 512], evs[e], [B_evs[e]], [B_OUT], B_OUT)

    fin = arena[:, SCRA_OFF // 2:SCRA_OFF // 2 + 8192].bitcast(F32).rearrange("p (a b) -> p a b", a=8); B_fin = B_scrA

    def ssd(l):
        m = mark()
        NG = TL // TT
        NCH = TL // 128
        xsT = [sb([128, 4, TT]) for _ in range(2)]; B_xsT = [Buf("xsT0"), Buf("xsT1")]
        bt128 = [sb([128, TT]) for _ in range(2)]; B_bt = [Buf("bt0"), Buf("bt1")]
        bc64 = [sb([64, 4, TT]) for _ in range(2)]; B_bc64 = [Buf("bc640"), Buf("bc641")]
        bc64b = [sb([64, 4, TT], BF16) for _ in range(2)]; B_bc64b = [Buf("bc64b0"), Buf("bc64b1")]
        ztok = [sb([128, 4, TT]) for _ in range(2)]; B_ztok = [Buf("ztok0"), Buf("ztok1")]
        dtg = [sb([128, 4, 8]) for _ in range(2)]; B_dtg = [Buf("dtg0"), Buf("dtg1")]
        yst = [sb([128, 4, TT], BF16) for _ in range(2)]; B_yst = [Buf("yst0"), Buf("yst1")]
        xs_tok2 = [sb([128, 512]) for _ in range(2)]; B_xs2 = [Buf("xs_tok0"), Buf("xs_tok1")]
        btok2 = [sb([128, 128], BF16) for _ in range(2)]; B_btok2 = [Buf("btok0"), Buf("btok1")]
        sm2 = [sb([128, 80]) for _ in range(2)]; B_sm2 = [Buf("sm0"), Buf("sm1")]
        MT2 = [sb([128, 8, 128], BF16) for _ in range(2)]; B_MT2 = [Buf("MT0"), Buf("MT1")]
        xdt2 = [sb([128, 512], BF16) for _ in range(2)]; B_xdt2 = [Buf("xdt0"), Buf("xdt1")]
        xw2 = [sb([128, 512], BF16) for _ in range(2)]; B_xw2 = [Buf("xw0"), Buf("xw1")]
        for i in range(2):
            memset("pool", sm2[i], 0.0, [B_sm2[i]])
        Rm = sb([128, 8, 128]); B_R = Buf("R")
        dec = sb([128, 8, 128]); B_dec = Buf("dec")
        t1 = sb([128, 512]); B_t1 = Buf("t1")
        t2 = sb([128, 512]); B_t2 = Buf("t2")
        yv2 = [sb([128, 512]) for _ in range(2)]; B_y2 = [Buf("y0"), Buf("y1")]
        gz = sb([128, 512]); B_gz = Buf("gz")
        sq = sb([128, 512]); B_sq = Buf("sq")
        ssq = sb([128, 4]); B_ssq = Buf("ssq")
        prev = sb([64, 8, 64]); B_prev = Buf("prev")
        prevb = [sb([64, 8, 64], BF16) for _ in range(2)]; B_prevb = [Buf("prevb0"), Buf("prevb1")]
        a_l = abc[:, l * 8:(l + 1) * 8]
        dsk = ppc(l, PP_DSK, 512)
        gssd = ppc(l, PP_GSSD, 4)
        memset("pool", prev, 0.0, [B_prev])
        memset("pool", prevb[0], 0.0, [B_prevb[0]])
        BCT64 = BCT.rearrange("a (g n) t -> n (a g) t", g=2)

        def loads(gi):
            s = gi % 2
            tsl = slice(gi * TT, (gi + 1) * TT)
            for j in range(4):
                dma(xsT[s][:, j, :], XST[j, :, tsl], [B_XST], [B_xsT[s]], B_xsT[s])
            dma(bt128[s], BCT[0, :, tsl], [B_BCT], [B_bt[s]], B_bt[s])
            for a_ in range(4):
                dma(bc64[s][:, a_, :], BCT64[:, a_, tsl], [B_BCT], [B_bc64[s]], B_bc64[s])
            dma(ztok[s], ZTOK[tsl, :].rearrange("(c p) f -> p c f", p=128), [B_ZTOK], [B_ztok[s]], B_ztok[s])
            dma(dtg[s], DTOK[tsl, :].rearrange("(c p) h -> p c h", p=128), [B_DTOK], [B_dtg[s]], B_dtg[s])
            cp("pool", bc64b[s], bc64[s], [B_bc64[s]], [B_bc64b[s]])

        def front(ci):
            gi, cg = ci // 4, ci % 4
            s = gi % 2
            p = ci % 2
            cs = slice(cg * 128, (cg + 1) * 128)
            xs_tok, B_xs = xs_tok2[p], B_xs2[p]
            btok, B_btok = btok2[p], B_btok2[p]
            sm, B_sm = sm2[p], B_sm2[p]
            MT, B_MT = MT2[p], B_MT2[p]
            xdt, B_xdt = xdt2[p], B_xdt2[p]
            xw, B_xw = xw2[p], B_xw2[p]
            da16 = sm[:, 0:16]; da = sm[:, 0:8]
            nacol, expA, dstate, cd, dtd, diff = [sm[:, 16 + i * 8:16 + (i + 1) * 8] for i in range(6)]
            for j in range(4):
                tr(psum[0][:, j * 128:(j + 1) * 128], xsT[s][:, j, cs], ID_F, [B_xsT[s], B_cf], [B_ps[0]])
            tr(psum[1][:, 0:128], bt128[s][:, cs], ID_F, [B_bt[s], B_cf], [B_ps[1].k("bt")])
            cp("act", xs_tok, psum[0], [B_ps[0]], [B_xs])
            cp("dve", btok, psum[1][:, 0:128], [B_ps[1].k("bt")], [B_btok])
            dtc = dtg[s][:, cg, :]
            tt("dve", da, dtc, a_l, ALU.mult, [B_dtg[s], B_abc], [B_sm.k("da")])
            mm(psum[1][:, 128:144], U_F, da16, True, True, [B_cf, B_sm.k("da")], [B_ps[1].k("ac")])
            mm(psum[1][:, 144:160], ONES_F, da16, True, True, [B_cf, B_sm.k("da")], [B_ps[1].k("ac")])
            ts("dve", nacol, psum[1][:, 128:136], -1.0, ALU.mult, [B_ps[1].k("ac")], [B_sm.k("nacol")])
            act(expA, psum[1][:, 128:136], AF.Exp, [B_ps[1].k("ac")], [B_sm.k("expA")])
            tt("dve", diff, psum[1][:, 144:152], nacol, ALU.add, [B_ps[1].k("ac"), B_sm.k("nacol")], [B_sm.k("diff")])
            act(dstate, diff, AF.Exp, [B_sm.k("diff")], [B_sm.k("dstate")])
            act(cd, psum[1][:, 144:152], AF.Exp, [B_ps[1].k("ac")], [B_sm.k("cd")])
            tt("dve", dtd, dtc, dstate, ALU.mult, [B_dtg[s], B_sm.k("dstate")], [B_sm.k("dtd")])
            tt("dve", Rm, U_F.unsqueeze(1).to_broadcast([128, 8, 128]), da.unsqueeze(2).to_broadcast([128, 8, 128]), ALU.mult,
               [B_cf, B_sm.k("da")], [B_R])
            for half in range(2):
                mm(psum[2 + half], ONES_F, Rm[:, half * 4:half * 4 + 4, :].rearrange("p a b -> p (a b)"), True, False, [B_cf, B_R], [B_ps[2 + half]])
                mm(psum[2 + half], ID_F, NEGM4, False, True, [B_cf], [B_ps[2 + half]])
            for h in range(8):
                act(dec[:, h, :], psum[2 + h // 4][:, (h % 4) * 128:(h % 4 + 1) * 128], AF.Exp,
                    [B_ps[2 + h // 4], B_sm.k("nacol")], [B_dec.k(h // 4)], bias=nacol[:, h:h + 1], scale=1.0)
            for g in range(2):
                mm(psum[1][:, 256 + g * 128:256 + (g + 1) * 128], bc64b[s][:, g, cs], bc64b[s][:, 2 + g, cs], True, True,
                   [B_bc64b[s]], [B_ps[1].k("cb%d" % g)])
                tt("dve", MT[:, g * 4:g * 4 + 4, :], psum[1][:, 256 + g * 128:256 + (g + 1) * 128].unsqueeze(1).to_broadcast([128, 4, 128]),
                   dec[:, g * 4:g * 4 + 4, :], ALU.mult, [B_ps[1].k("cb%d" % g), B_dec.k(g)], [B_MT.k(g)])
            xs3 = xs_tok.rearrange("p (h j) -> p h j", h=8)
            tt("dve", xdt.rearrange("p (h j) -> p h j", h=8), xs3, dtc.unsqueeze(2).to_broadcast([128, 8, 64]), ALU.mult,
               [B_xs, B_dtg[s]], [B_xdt])
            tt("pool", xw.rearrange("p (h j) -> p h j", h=8), xs3, dtd.unsqueeze(2).to_broadcast([128, 8, 64]), ALU.mult,
               [B_xs, B_sm.k("dtd")], [B_xw])

        def back(ci):
            gi, cg = ci // 4, ci % 4
            s = gi % 2
            p = ci % 2
            cs = slice(cg * 128, (cg + 1) * 128)
            tsl = slice(gi * TT, (gi + 1) * TT)
            xs_tok, B_xs = xs_tok2[p], B_xs2[p]
            btok, B_btok = btok2[p], B_btok2[p]
            sm, B_sm = sm2[p], B_sm2[p]
            MT, B_MT = MT2[p], B_MT2[p]
            xdt, B_xdt = xdt2[p], B_xdt2[p]
            xw, B_xw = xw2[p], B_xw2[p]
            nacol, expA, dstate, cd, dtd, diff = [sm[:, 16 + i * 8:16 + (i + 1) * 8] for i in range(6)]
            pb_cur = prevb[ci % 2]; Bpb_cur = B_prevb[ci % 2]
            pb_nxt = prevb[(ci + 1) % 2]; Bpb_nxt = B_prevb[(ci + 1) % 2]
            yv, B_y = yv2[p], B_y2[p]
            for h in range(8):
                mm(psum[4][:, h * 64:(h + 1) * 64], MT[:, h, :], xdt[:, h * 64:(h + 1) * 64], True, True, [B_MT.k(h // 4), B_xdt], [B_ps[4]])
            for h in range(8):
                mm(psum[5][:, h * 64:(h + 1) * 64], bc64b[s][:, 2 + h // 4, cs], pb_cur[:, h, :], True, True, [B_bc64b[s], Bpb_cur], [B_ps[5]])
            for g in range(2):
                mm(psum[6][0:64, g * 256:(g + 1) * 256], btok[:, g * 64:(g + 1) * 64], xw[:, g * 256:(g + 1) * 256], True, True,
                   [B_btok, B_xw], [B_ps[6]])
            tt("pool", prev, prev, cd[0:64, :].unsqueeze(2).to_broadcast([64, 8, 64]), ALU.mult, [B_prev, B_sm.k("cd")], [B_prev])
            tt("dve", prev, psum[6][0:64, :].rearrange("p (h j) -> p h j", h=8), prev, ALU.add, [B_ps[6], B_prev], [B_prev])
            cp("pool", pb_nxt, prev, [B_prev], [Bpb_nxt])
            tt("dve", t1.rearrange("p (h j) -> p h j", h=8), psum[5].rearrange("p (h j) -> p h j", h=8),
               expA.unsqueeze(2).to_broadcast([128, 8, 64]), ALU.mult, [B_ps[5], B_sm.k("expA")], [B_t1])
            tt("pool", t2, xs_tok, dsk, ALU.mult, [B_xs, B_pp], [B_t2])
            tt("pool", t2, t2, t1, ALU.add, [B_t2, B_t1], [B_t2])
            tt("dve", yv, psum[4], t2, ALU.add, [B_ps[4], B_t2], [B_y])

        def back2(ci):
            gi, cg = ci // 4, ci % 4
            s = gi % 2
            p = ci % 2
            cs = slice(cg * 128, (cg + 1) * 128)
            tsl = slice(gi * TT, (gi + 1) * TT)
            yv, B_y = yv2[p], B_y2[p]
            act(gz, ztok[s][:, cg, :], AF.Silu, [B_ztok[s]], [B_gz])
            tt("pool", yv, yv, gz, ALU.mult, [B_y, B_gz], [B_y])
            memset("pool", ssq[:, 0:1], 0.0, [B_ssq])
            act(sq, yv, AF.Square, [B_y, B_ssq], [B_sq, B_ssq], accum=ssq[:, 0:1])
            act(ssq[:, 1:2], ssq[:, 0:1], AF.Ln, [B_ssq, B_cf], [B_ssq], bias=EPSC, scale=1.0 / 512)
            act(ssq[:, 2:3], ssq[:, 1:2], AF.Exp, [B_ssq], [B_ssq], scale=-0.5)
            ts("dve", sq, yv, ssq[:, 2:3], ALU.mult, [B_y, B_ssq], [B_sq])
            for j in range(4):
                tr(psum[7][:, j * 128:(j + 1) * 128], sq[:, j * 128:(j + 1) * 128], ID_F, [B_sq, B_cf], [B_ps[7]])
            tt("dve", yst[s][:, :, cs], psum[7].rearrange("p (a b) -> p a b", a=4), gssd.unsqueeze(2).to_broadcast([128, 4, 128]), ALU.mult,
               [B_ps[7], B_pp], [B_yst[s]])
            if cg == 3:
                dma(YT[0:512, tsl].rearrange("(j p) t -> p j t", p=128), yst[s], [B_yst[s]], [B_YT], B_YT)

        for ci in range(NCH + 1):
            if ci < NCH and ci % 4 == 0:
                loads(ci // 4)
            fns = []
            if ci < NCH:
                fns.append((lambda c: (lambda: front(c)))(ci))
            if ci >= 1:
                fns.append((lambda c: (lambda: (back(c), back2(c))))(ci - 1))
            interleave(P, *fns)
        P.barrier()
        reset(m)

    def sba(l):
        m = mark()
        kt_all = sb([128, 8, TL], BF16); B_kt = Buf("kt_all")
        v_all = sb([128, TL // 128, 576], BF16); B_v = Buf("v_all")
        qg = [sb([128, 8, TT], BF16) for _ in range(2)]; B_qg = [Buf("qg0"), Buf("qg1")]
        e_sb = [sb([128, TT]) for _ in range(3)]; B_e = [Buf("e%d" % i) for i in range(3)]
        sp_b = [sb([128, TT], BF16) for _ in range(2)]; B_sp = [Buf("sp%d" % i) for i in range(2)]
        r_sb = [sb([128, TT]) for _ in range(2)]; B_r = [Buf("r%d" % i) for i in range(2)]
        w_b = [sb([128, TT], BF16) for _ in range(2)]; B_w = [Buf("w%d" % i) for i in range(2)]
        acc = [sb([128, TT], BF16) for _ in range(2)]; B_acc = [Buf("acc%d" % i) for i in range(2)]
        o_sb = sb([64, 8, TT]); B_o = Buf("o_sb")
        osq = sb([64, 8, TT], BF16); B_osq = Buf("osq")
        rs = sb([128, TT]); B_rs = Buf("rs")
        yst1 = sb([64, 8, TT], BF16); yst = [yst1, yst1]; B_y1 = Buf("ysb"); B_yst = [B_y1, B_y1]
        gsb = ppc(l, PP_GSB, 8)
        memset("pool", v_all[:, :, 512:576], 0.0, [B_v])
        for h in range(8):
            dma(kt_all[0:64, h, :], KT[h, :, :], [B_KT], [B_kt], B_kt)
            dma(kt_all[64:128, h, :], KT[h, :, :], [B_KT], [B_kt], B_kt)
        for q4 in range(TL // 1024):
            bs = slice(q4 * 8, (q4 + 1) * 8)
            dma(v_all[:, bs, 0:512], VS[q4 * 1024:(q4 + 1) * 1024, :].rearrange("(b p) f -> p b f", p=128), [B_VS], [B_v], B_v)
        tiles = []
        for G in range(min(TL // TT, SBA_MAXG)):
            for h in range(8):
                kbs = list(range(4 * G + 3, -1, -1))
                for ii, kb in enumerate(kbs):
                    tiles.append((G, h, kb, ii == 0, ii == len(kbs) - 1))
        n = len(tiles)
        Z_PS = [0, 1]; R_PS = [2, 3]; O_PS = [4, 5]; SS_PS = 6

        def stage_q(i):
            G, h, kb, first, last = tiles[i]
            s = G % 2
            if h == 0 and first:
                dma(qg[s][0:64], QT[:, :, G * TT:(G + 1) * TT].rearrange("h d t -> d h t"), [B_QT], [B_qg[s]], B_qg[s])
                dma(qg[s][64:128], QT[:, :, G * TT:(G + 1) * TT].rearrange("h d t -> d h t"), [B_QT], [B_qg[s]], B_qg[s])
            j = kb - 4 * G
            c0 = 128 * max(j, 0)
            cs = slice(c0, TT)
            zi = Z_PS[i % 2]
            mm(psum[zi][:, cs], kt_all[:, h, kb * 128:(kb + 1) * 128], qg[s][:, h, cs], True, True, [B_kt, B_qg[s]], [B_ps[zi]])
            if j >= 0:
                mm(psum[zi][:, c0:c0 + 128], ID_B, NEGTRI_B, False, True, [B_cb], [B_ps[zi]])

        def stage_a(i):
            G, h, kb, first, last = tiles[i]
            s = G % 2
            j = kb - 4 * G
            c0 = 128 * max(j, 0)
            cs = slice(c0, TT)
            zi = Z_PS[i % 2]; ri = R_PS[i % 2]
            ei = i % 3; si = i % 2
            ai = (G * 8 + h) % 2
            act(e_sb[ei][:, cs], psum[zi][:, cs], AF.Exp, [B_ps[zi]], [B_e[ei]], scale=0.0625)
            act(sp_b[si][:, cs], e_sb[ei][:, cs], AF.Ln, [B_e[ei], B_cf], [B_sp[si]], bias=ONEC, scale=1.0)
            mm(psum[ri][:, cs], TRI_B, sp_b[si][:, cs], True, first, [B_cb, B_sp[si]], [B_ps[ri]])
            if not first:
                mm(psum[ri][:, cs], ONES_B, acc[ai][:, cs], False, True, [B_cb, B_acc[ai]], [B_ps[ri]])
            if first:
                memset("pool", acc[ai], 0.0, [B_acc[ai]])
            if not last:
                tt("pool", acc[ai][:, cs], acc[ai][:, cs], sp_b[si][:, cs], ALU.add, [B_acc[ai], B_sp[si]], [B_acc[ai]])

        def stage_b(i):
            G, h, kb, first, last = tiles[i]
            s = G % 2
            j = kb - 4 * G
            c0 = 128 * max(j, 0)
            cs = slice(c0, TT)
            ri = R_PS[i % 2]
            ei = i % 3; si = i % 2
            oi = O_PS[(G * 8 + h) % 2]
            if SBA_LEVEL < 3:
                return
            act(r_sb[si][:, cs], psum[ri][:, cs], AF.Exp, [B_ps[ri]], [B_r[si]], scale=-1.0)
            tt("dve", w_b[si][:, cs], e_sb[ei][:, cs], r_sb[si][:, cs], ALU.mult, [B_e[ei], B_r[si]], [B_w[si]])
            if first:
                for q4 in range(4):
                    mm(psum[oi][:, q4 * 128:(q4 + 1) * 128], ZERO_B, ONES_B, True, False, [B_cb], [B_ps[oi]])
            mm(psum[oi][:, cs], v_all[:, kb, h * 64:h * 64 + 128], w_b[si][:, cs], False, last, [B_v, B_w[si]], [B_ps[oi]])
            if last and SBA_LEVEL >= 4:
                cp("dve", o_sb[:, h, :], psum[oi][0:64, :], [B_ps[oi]], [B_o.k(h)])
                act(osq[:, h, :], psum[oi][0:64, :], AF.Square, [B_ps[oi]], [B_osq.k(h)])
                if h == 7 and SBA_LEVEL >= 5:
                    for hh in range(8):
                        mm(psum[SS_PS], ONES_B[0:64, :], osq[:, hh, :], hh == 0, hh == 7, [B_cb, B_osq.k(hh)], [B_ps[SS_PS]])
                    act(rs, psum[SS_PS], AF.Ln, [B_ps[SS_PS], B_cf], [B_rs], bias=EPSC, scale=1.0 / 512)
                    act(rs, rs, AF.Exp, [B_rs], [B_rs], scale=-0.5)
                    for hh in range(8):
                        stt("dve", yst[s][:, hh, :], o_sb[:, hh, :], gsb[0:64, hh:hh + 1], rs[0:64, :], ALU.mult, ALU.mult,
                            [B_o.k(hh), B_pp, B_rs], [B_yst[s]])
                    dma(YT[512:1024, G * TT:(G + 1) * TT].rearrange("(h d) t -> d h t", d=64), yst[s], [B_yst[s]], [B_YT], B_YT)

        for i in range(n + 2):
            if i < n:
                stage_q(i)
            if 1 <= i <= n:
                stage_a(i - 1)
            if i >= 2:
                stage_b(i - 2)
        P.barrier()
        reset(m)

    for st in range(NST):
        embed(st)
        if NST > 1:
            store_hT(st)
        g1(0, st)
    P.barrier()
    done = False
    for l in range(n_layers):
        if stop == "g1":
            break
        reset(PERSIST)
        ssd(l)
        if stop == "ssd":
            break
        sba(l)
        if stop == "g2":
            break
        for st in range(NST):
            if NST > 1:
                load_hT(st)
            g3(l, st)
            if l + 1 < n_layers:
                if NST > 1:
                    store_hT(st)
                g1(l + 1, st)
            else:
                final(st)
                done = True
        P.barrier()
    dumps = {}
    if dump:
        alld = (("QT", QT, B_QT), ("KT", KT, B_KT), ("VS", VS, B_VS), ("ZTOK", ZTOK, B_ZTOK), ("XST", XST, B_XST),
                ("BCT", BCT, B_BCT), ("DTOK", DTOK, B_DTOK), ("YT", YT, B_YT), ("HT", HT, None))
        for name, ap_, Bf in [d_ for d_ in alld if dump is True or d_[0] in dump]:
            o = nc.dram_tensor("dump_" + name, list(ap_.shape), ap_.dtype, kind="ExternalOutput").ap()
            db = Buf("dump_" + name)
            rd = [Bf] if Bf is not None else list(B_HT)
            if len(ap_.shape) == 4:
                for s_ in range(ap_.shape[0]):
                    dma(o[s_].rearrange("p c t -> p (c t)"), ap_[s_].rearrange("p c t -> p (c t)"), rd, [db], db)
            elif len(ap_.shape) == 3:
                for s_ in range(ap_.shape[0]):
                    dma(o[s_], ap_[s_], rd, [db], db)
            else:
                dma(o, ap_, rd, [db], db)
    P.barrier()
    P.emit()
    return nc


_CACHE = {}


def kernel(**inputs):
    p = {k: np.asarray(v) for k, v in inputs.items()}
    if "nc" not in _CACHE:
        _CACHE["nc"] = build()
    nc = _CACHE["nc"]
    cf = host_consts()
    pp = host_params(p)
    shared = {k: np.ascontiguousarray(p[k], dtype=np.float32) for k in ("w_in", "w_out", "w_xq", "w_xk", "w_xv", "w_xo", "w_ff1", "w_ff2")}
    in_maps = []
    for c in range(8):
        b = c % 4
        m = {"x": np.ascontiguousarray(p["x"][b], dtype=np.float32), "mem": np.ascontiguousarray(p["mem"][b], dtype=np.float32),
             "cf": cf, "pp": pp}
        m.update(shared)
        in_maps.append(m)
    res = run_bass_kernel_spmd(nc, in_maps, core_ids=list(range(8)))
    out = np.stack([np.asarray(res.results[b]["out"], dtype=np.float32) for b in range(4)], axis=0)
    return out
```

```python
import math
import contextlib
import numpy as np
import concourse.bass as bass
import concourse.mybir as mybir
from concourse.bass_utils import run_bass_kernel_spmd

F32 = mybir.dt.float32
BF16 = mybir.dt.bfloat16
AF = mybir.ActivationFunctionType
ALU = mybir.AluOpType

ENGS = ("pe", "act", "dve", "pool", "sp")


class Buf:
    def __init__(self, name):
        self.name = name
        self.st = {"*": [[], []]}
        self.sem = None
        self.ndma = 0
        self.excl = False
        self.last_by_eng = {}

    def k(self, key):
        return (self, key)


class Op:
    __slots__ = ("eng", "fn", "waits", "idx", "is_dma", "dest")


class Prog:
    def __init__(self, nc):
        self.nc = nc
        self.ops = {e: [] for e in ENGS}
        self.dma_bufs = []
        self.nops = 0

    @staticmethod
    def _norm(x):
        if isinstance(x, Buf):
            return (x, None)
        return x

    def _entries(self, buf, key):
        d = buf.st
        if key is None:
            return list(d.values())
        if key not in d:
            d[key] = [list(d["*"][0]), list(d["*"][1])]
        return [d[key]]

    def add(self, eng, fn, reads=(), writes=(), dma_dest=None):
        if getattr(self, "capture", None) is not None:
            self.capture.append((eng, fn, reads, writes, dma_dest))
            return None
        op = Op()
        op.eng = eng
        op.fn = fn
        op.is_dma = dma_dest is not None
        op.dest = dma_dest
        deps = []
        reads = [self._norm(r) for r in reads if r is not None]
        writes = [self._norm(w) for w in writes if w is not None]
        for (b, key) in reads:
            for ent in self._entries(b, key):
                deps.extend(ent[0])
        for (b, key) in writes:
            for ent in self._entries(b, key):
                samegen = op.is_dma and len(ent[0]) > 0 and all(w.is_dma for w in ent[0]) and len(ent[1]) == 0
                if not samegen:
                    deps.extend(ent[0])
                deps.extend(ent[1])
        for (b, key) in list(reads) + list(writes):
            if b.excl:
                for e2, y in b.last_by_eng.items():
                    if e2 != eng:
                        deps.append(y)
                b.last_by_eng[eng] = op
        for (b, key) in writes:
            for ent in self._entries(b, key):
                samegen = op.is_dma and len(ent[0]) > 0 and all(w.is_dma for w in ent[0]) and len(ent[1]) == 0
                if samegen:
                    ent[0].append(op)
                else:
                    ent[0] = [op]
                    ent[1] = []
            if key is None:
                for kk in list(b.st.keys()):
                    b.st[kk][0] = list(b.st["*"][0])
                    b.st[kk][1] = []
        wset = set((id(b), key) for (b, key) in writes)
        for (b, key) in reads:
            if (id(b), key) in wset:
                continue
            for ent in self._entries(b, key):
                ent[1].append(op)
        if op.is_dma:
            d = dma_dest
            if d.sem is None:
                self.dma_bufs.append(d)
                d.sem = True
            d.ndma += 1
        lst = self.ops[eng]
        lst.append(op)
        op.idx = len(lst)
        waits = {}
        for y in deps:
            if y is op:
                continue
            if y.is_dma:
                key = ("d", id(y.dest))
                val = 16 * y.dest.ndma if y.dest is not dma_dest else 16 * (y.dest.ndma - 1)
                if val <= 0:
                    continue
                ent = (y.dest, val)
            else:
                if y.eng == eng and eng == "pe":
                    continue
                key = ("e", y.eng)
                val = y.idx
                ent = (y.eng, val)
            if key not in waits or waits[key][1] < val:
                waits[key] = ent
        op.waits = waits
        self.nops += 1
        return op

    def barrier(self):
        snap_e = {e: len(self.ops[e]) for e in ENGS}
        snap_d = [(d, 16 * d.ndma) for d in self.dma_bufs]
        for e in ENGS:
            op = Op()
            op.eng = e
            op.fn = None
            op.is_dma = False
            op.dest = None
            w = {}
            for e2 in ENGS:
                if e2 == e:
                    continue
                w[("e", e2)] = (e2, snap_e[e2])
            for (d, v) in snap_d:
                if v > 0:
                    w[("d", id(d))] = (d, v)
            op.waits = w
            lst = self.ops[e]
            lst.append(op)
            op.idx = len(lst)

    def emit(self):
        nc = self.nc
        with contextlib.ExitStack() as es:
            esem = {e: es.enter_context(nc.semaphore("es_" + e)) for e in ENGS}
            for i, d in enumerate(self.dma_bufs):
                d.sem = es.enter_context(nc.semaphore("ds%d_%s" % (i, d.name)))
            block = es.enter_context(nc.Block())
            cidx = {}
            for e in ENGS:
                c = 0
                m = [0]
                for op in self.ops[e]:
                    if not op.is_dma:
                        c += 1
                    m.append(c)
                cidx[e] = m

            def make(e):
                ops = self.ops[e]

                def body(eng):
                    seen = {}
                    for op in ops:
                        for key, (obj, val) in op.waits.items():
                            if key[0] == "e":
                                sem = esem[obj]
                                v = cidx[obj][val]
                            else:
                                sem = obj.sem
                                v = val
                            if v <= 0 or seen.get(key, 0) >= v:
                                continue
                            seen[key] = v
                            eng.wait_ge(sem, v)
                        if op.fn is None:
                            eng.nop().then_inc(esem[e], 1)
                            continue
                        ins = op.fn(eng)
                        if op.is_dma:
                            ins.then_inc(op.dest.sem, 16)
                        else:
                            ins.then_inc(esem[e], 1)
                return body

            block.tensor(make("pe"))
            block.scalar(make("act"))
            block.vector(make("dve"))
            block.gpsimd(make("pool"))
            block.sync(make("sp"))


def interleave(P, *fns):
    lists = []
    for f in fns:
        P.capture = lst = []
        f()
        lists.append(lst)
    P.capture = None
    pos = [0] * len(lists)
    total = sum(len(l) for l in lists)
    for _ in range(total):
        best, bi = None, -1
        for i, l in enumerate(lists):
            if pos[i] < len(l):
                frac = pos[i] / len(l)
                if best is None or frac < best:
                    best, bi = frac, i
        P.add(*lists[bi][pos[bi]])
        pos[bi] += 1


class Rot:
    def __init__(self, items):
        self.items = items
        self.i = 0

    def next(self):
        it = self.items[self.i % len(self.items)]
        self.i += 1
        return it


D = 1024
TL = 4096
TS = 2048
NST = TL // TS
TT = 512
NT = TS // TT
L = 2
MEM = 256
IN_DIM = 2824
C_Z, C_X, C_B, C_C, C_DT, C_Q, C_K, C_V = 0, 512, 1024, 1152, 1280, 1288, 1800, 2312
EPS = 1e-5
NEG = -30000.0
SBA_MAXG = 99
SBA_LEVEL = 9
SBA_DUMMY = 1

CF_ID, CF_ONES, CF_U, CF_NEGM4, CF_TRI, CF_NEGTRI, CF_EPS, CF_ONE, NCF = 0, 128, 256, 384, 896, 1024, 1152, 1153, 1160
PP_GMIX, PP_GXA, PP_GFF, PP_GMEM, PP_GSSD, PP_GSB, PP_CW, PP_CB, PP_DTB, PP_ALOG, PP_DSK, PPW = 0, 8, 16, 24, 32, 36, 44, 68, 74, 82, 90, 602
PP_FINAL = L * PPW
NPP = PP_FINAL + 8


def host_consts():
    cf = np.zeros((128, NCF), np.float32)
    i = np.arange(128)
    cf[:, CF_ID:CF_ID + 128] = np.eye(128, dtype=np.float32)
    cf[:, CF_ONES:CF_ONES + 128] = 1.0
    cf[:, CF_U:CF_U + 128] = (i[:, None] <= i[None, :]).astype(np.float32)
    negm = np.where(i[None, :] >= i[:, None], 0.0, NEG).astype(np.float32)
    cf[:, CF_NEGM4:CF_NEGM4 + 512] = np.tile(negm, (1, 4))
    cf[:, CF_TRI:CF_TRI + 128] = (i[:, None] >= i[None, :]).astype(np.float32)
    cf[:, CF_NEGTRI:CF_NEGTRI + 128] = np.where(i[:, None] < i[None, :], 0.0, 8 * NEG)
    cf[:, CF_EPS] = EPS
    cf[:, CF_ONE] = 1.0
    return cf


def host_params(p):
    pp = np.zeros((128, NPP), np.float32)
    col = lambda v, n: np.ascontiguousarray(np.asarray(v, np.float32).reshape(n, 128).T)
    for l in range(L):
        o = l * PPW
        pp[:, o + PP_GMIX:o + PP_GMIX + 8] = col(p["norm_mix_g"][l], 8)
        pp[:, o + PP_GXA:o + PP_GXA + 8] = col(p["norm_xa_g"][l], 8)
        pp[:, o + PP_GFF:o + PP_GFF + 8] = col(p["norm_ff_g"][l], 8)
        pp[:, o + PP_GMEM:o + PP_GMEM + 8] = col(p["norm_mem_g"][l], 8)
        pp[:, o + PP_GSSD:o + PP_GSSD + 4] = col(p["ssd_norm_g"][l], 4)
        pp[0:64, o + PP_GSB:o + PP_GSB + 8] = np.asarray(p["sb_norm_g"][l], np.float32).reshape(8, 64).T
        cw = np.asarray(p["conv_w"][l], np.float32)
        pp[:, o + PP_CW:o + PP_CW + 24] = cw.reshape(4, 6, 128).transpose(2, 1, 0).reshape(128, 24)
        pp[:, o + PP_CB:o + PP_CB + 6] = col(p["conv_b"][l], 6)
        pp[:, o + PP_DTB:o + PP_DTB + 8] = np.asarray(p["dt_bias"][l], np.float32)[None, :]
        pp[:, o + PP_ALOG:o + PP_ALOG + 8] = np.asarray(p["a_log"][l], np.float32)[None, :]
        pp[:, o + PP_DSK:o + PP_DSK + 512] = np.repeat(np.asarray(p["d_skip"][l], np.float32), 64)[None, :]
    pp[:, PP_FINAL:PP_FINAL + 8] = col(p["final_g"], 8)
    return pp


def build(n_layers=L, stop=None, dump=False):
    nc = bass.Bass("TRN2", target_bir_lowering=False)
    P = Prog(nc)

    def din(name, shape, dt=F32):
        return nc.dram_tensor(name, shape, dt, kind="ExternalInput").ap()

    x_d = din("x", [TL, D])
    mem_d = din("mem", [MEM, D])
    w_in = din("w_in", [L, D, IN_DIM])
    w_out = din("w_out", [L, D, D])
    w_xq = din("w_xq", [L, D, 512])
    w_xk = din("w_xk", [L, D, 512])
    w_xv = din("w_xv", [L, D, 512])
    w_xo = din("w_xo", [L, 512, D])
    w_ff1 = din("w_ff1", [L, D, 4096])
    w_ff2 = din("w_ff2", [L, 4096, D])
    cf_d = din("cf", [128, NCF])
    pp_d = din("pp", [128, NPP])
    out_d = nc.dram_tensor("out", [TL, D], F32, kind="ExternalOutput").ap()

    def dscr(name, shape, dt):
        return nc.dram_tensor(name, shape, dt).ap()

    HT = dscr("HT", [NST, 128, 8, TS], F32)
    QT = dscr("QT", [8, 64, TL], BF16)
    KT = dscr("KT", [8, 64, TL], BF16)
    VS = dscr("VS", [TL, 512], BF16)
    ZTOK = dscr("ZTOK", [TL, 512], F32)
    XST = dscr("XST", [4, 128, TL], F32)
    BCT = dscr("BCT", [2, 128, TL], F32)
    DTOK = dscr("DTOK", [TL, 8], F32)
    YT = dscr("YT", [D, TL], BF16)
    B_HT = [Buf("HT%d" % s) for s in range(NST)]
    B_QT, B_KT, B_VS, B_ZTOK, B_XST, B_BCT, B_DTOK, B_YT, B_OUT = [Buf(n) for n in "QT KT VS ZTOK XST BCT DTOK YT OUT".split()]

    ARENA_BYTES = 206 * 1024
    arena = nc.alloc_sbuf_tensor("arena", [128, ARENA_BYTES // 2], BF16).ap()
    st_ = {"off": 0}

    def sb(shape, dt=F32, parts=128):
        n = 1
        for s_ in shape[1:]:
            n *= s_
        nbytes = n * (4 if dt == F32 else 2)
        nbytes = (nbytes + 31) // 32 * 32
        off = st_["off"]
        assert off + nbytes <= ARENA_BYTES, ("SBUF arena overflow", off, nbytes)
        st_["off"] = off + nbytes
        v = arena[0:shape[0], off // 2:(off + nbytes) // 2]
        if dt == F32:
            v = v.bitcast(F32)
        v = v[:, 0:n]
        if len(shape) == 3:
            v = v.rearrange("p (a b) -> p a b", a=shape[1])
        elif len(shape) == 4:
            v = v.rearrange("p (a b c) -> p a b c", a=shape[1], b=shape[2])
        return v

    def mark():
        return st_["off"]

    def reset(m):
        st_["off"] = m

    psum = [nc.alloc_psum_tensor("ps%d" % i, [128, 512], F32).ap() for i in range(8)]
    B_ps = [Buf("ps%d" % i) for i in range(8)]
    for b_ in B_ps:
        b_.excl = True

    def mm(out, lhsT, rhs, start, stop, r, w):
        P.add("pe", lambda e: e.matmul(out, lhsT=lhsT, rhs=rhs, start=start, stop=stop), reads=r, writes=w)

    def tr(out, in_, ident, r, w):
        P.add("pe", lambda e: e.transpose(out=out, in_=in_, identity=ident), reads=r, writes=w)

    def act(out, in_, func, r, w, bias=None, scale=None, accum=None):
        kw = {}
        if bias is not None:
            kw["bias"] = bias
        if scale is not None:
            kw["scale"] = scale
        if accum is not None:
            kw["accum_out"] = accum
        P.add("act", lambda e: e.activation(out=out, in_=in_, func=func, **kw), reads=r, writes=w)

    def tt(eng, out, in0, in1, op, r, w):
        P.add(eng, lambda e: e.tensor_tensor(out=out, in0=in0, in1=in1, op=op), reads=r, writes=w)

    def ts(eng, out, in0, s1, op0, r, w, s2=None, op1=None):
        if op1 is None:
            P.add(eng, lambda e: e.tensor_scalar(out=out, in0=in0, scalar1=s1, scalar2=None, op0=op0), reads=r, writes=w)
        else:
            P.add(eng, lambda e: e.tensor_scalar(out=out, in0=in0, scalar1=s1, scalar2=s2, op0=op0, op1=op1), reads=r, writes=w)

    def stt(eng, out, in0, scalar, in1, op0, op1, r, w):
        P.add(eng, lambda e: e.scalar_tensor_tensor(out=out, in0=in0, scalar=scalar, in1=in1, op0=op0, op1=op1), reads=r, writes=w)

    def cp(eng, out, in_, r, w):
        if eng == "act":
            P.add("act", lambda e: e.activation(out=out, in_=in_, func=AF.Copy), reads=r, writes=w)
        else:
            P.add(eng, lambda e: e.tensor_copy(out=out, in_=in_), reads=r, writes=w)

    def memset(eng, out, val, w):
        P.add(eng, lambda e: e.memset(out, val), writes=w)

    def dma(out, in_, r, w, dest, eng="sp"):
        P.add(eng, lambda e: e.dma_start(out=out, in_=in_), reads=r, writes=w, dma_dest=dest)

    cf = sb([128, NCF]); B_cf = Buf("cf")
    pp = sb([128, NPP]); B_pp = Buf("pp")
    NCB = 128 * 5
    cb = sb([128, NCB], BF16); B_cb = Buf("cb")
    ID_F, ONES_F, U_F = cf[:, CF_ID:CF_ID + 128], cf[:, CF_ONES:CF_ONES + 128], cf[:, CF_U:CF_U + 128]
    NEGM4 = cf[:, CF_NEGM4:CF_NEGM4 + 512]
    EPSC, ONEC = cf[:, CF_EPS:CF_EPS + 1], cf[:, CF_ONE:CF_ONE + 1]
    ID_B, ONES_B, TRI_B, NEGTRI_B, ZERO_B = cb[:, 0:128], cb[:, 128:256], cb[:, 256:384], cb[:, 384:512], cb[:, 512:640]
    abc = sb([128, L * 8]); B_abc = Buf("abc")
    halo = sb([128, 6, 3]); B_halo = Buf("halo")
    kxT = [sb([128, 4, MEM], BF16) for _ in range(L)]; B_kx = [Buf("kx%d" % l) for l in range(L)]
    vx = [sb([128, 2, 512], BF16) for _ in range(L)]; B_vx = [Buf("vx%d" % l) for l in range(L)]
    PERSIST = mark()
    wbf = [sb([128, 4096], BF16) for _ in range(3)]; B_wbf = [Buf("wbf0"), Buf("wbf1"), Buf("wbf2")]
    wrot = Rot([0, 1, 2])
    WEND = mark()

    dma(cf, cf_d, [], [B_cf], B_cf)
    dma(pp, pp_d, [], [B_pp], B_pp)
    cp("dve", cb[:, 0:128], cf[:, CF_ID:CF_ID + 128], [B_cf], [B_cb])
    cp("dve", cb[:, 128:256], cf[:, CF_ONES:CF_ONES + 128], [B_cf], [B_cb])
    cp("dve", cb[:, 256:384], cf[:, CF_TRI:CF_TRI + 128], [B_cf], [B_cb])
    cp("dve", cb[:, 384:512], cf[:, CF_NEGTRI:CF_NEGTRI + 128], [B_cf], [B_cb])
    memset("pool", cb[:, 512:640], 0.0, [B_cb])
    for l in range(L):
        act(abc[:, l * 8:(l + 1) * 8], pp[:, l * PPW + PP_ALOG:l * PPW + PP_ALOG + 8], AF.Exp, [B_pp], [B_abc])
    ts("dve", abc, abc, -1.0, ALU.mult, [B_abc], [B_abc])

    def ppc(l, off, n):
        return pp[:, l * PPW + off:l * PPW + off + n]

    def load_w(src3, KC, N):
        i = wrot.next()
        dst = wbf[i][:, 0:KC * N].rearrange("p (k n) -> p k n", k=KC)
        for k in range(KC):
            dma(dst[:, k, :], src3[:, k, :], [], [B_wbf[i]], B_wbf[i], eng="pool")
        return dst, B_wbf[i]

    def wview(w2d, r0, KC, c0, N):
        return w2d[r0:r0 + KC * 128, c0:c0 + N].rearrange("(k p) n -> p k n", p=128)

    prot = Rot([0, 1, 2, 3])

    def proj_fm(w2d, r0, KC, c0, ncols, colblk, actT, Bact, ntiles, evac, tcol0=0):
        for cbi in range(ncols // colblk):
            wv, Bw = load_w(wview(w2d, r0, KC, c0 + cbi * colblk, colblk), KC, colblk)
            for jj in range(colblk // 128):
                for t in range(ntiles):
                    pi = prot.next()
                    for k in range(KC):
                        mm(psum[pi], wv[:, k, jj * 128:(jj + 1) * 128], actT[:, k, tcol0 + t * TT:tcol0 + (t + 1) * TT],
                           k == 0, k == KC - 1, [Bw, Bact], [B_ps[pi]])
                    evac(cbi * (colblk // 128) + jj, t, psum[pi], B_ps[pi])

    def rmsnorm_fm(hT, BhT, gcols, actT, Bact, sqb, B_sqb, rstd, B_rstd, nchunks=8, scale=1.0 / D):
        for t in range(NT):
            sl = slice(t * TT, (t + 1) * TT)
            for c in range(nchunks):
                act(sqb[:, c, :], hT[:, c, sl], AF.Square, [BhT.k((c, t))], [B_sqb.k(c)])
            pi = prot.next()
            for c in range(nchunks):
                mm(psum[pi], ONES_B, sqb[:, c, :], c == 0, c == nchunks - 1, [B_cb, B_sqb.k(c)], [B_ps[pi]])
            act(rstd, psum[pi], AF.Ln, [B_ps[pi], B_cf], [B_rstd], bias=EPSC, scale=scale)
            act(rstd, rstd, AF.Exp, [B_rstd], [B_rstd], scale=-0.5)
            for c in range(nchunks):
                stt("dve", actT[:, c, sl], hT[:, c, sl], gcols[:, c:c + 1], rstd, ALU.mult, ALU.mult,
                    [BhT.k((c, t)), B_pp, B_rstd], [Bact])

    m0 = mark()
    assert m0 == WEND
    mtok = sb([128, D]); B_mtok = Buf("mtok")
    mn = sb([128, D]); B_mn = Buf("mn")
    msc = sb([128, 4]); B_msc = Buf("msc")
    memnT = [sb([128, 8, MEM], BF16) for _ in range(L)]; B_memn = [Buf("memn%d" % l) for l in range(L)]
    for blk in range(2):
        dma(mtok, mem_d[blk * 128:(blk + 1) * 128, :], [], [B_mtok], B_mtok)
        memset("pool", msc[:, 0:1], 0.0, [B_msc])
        act(mn, mtok, AF.Square, [B_mtok, B_msc], [B_mn, B_msc], accum=msc[:, 0:1])
        act(msc[:, 1:2], msc[:, 0:1], AF.Ln, [B_msc, B_cf], [B_msc], bias=EPSC, scale=1.0 / D)
        act(msc[:, 2:3], msc[:, 1:2], AF.Exp, [B_msc], [B_msc], scale=-0.5)
        ts("dve", mn, mtok, msc[:, 2:3], ALU.mult, [B_mtok, B_msc], [B_mn])
        for half in range(2):
            for c4 in range(4):
                c = half * 4 + c4
                tr(psum[half][:, c4 * 128:(c4 + 1) * 128], mn[:, c * 128:(c + 1) * 128], ID_F, [B_mn, B_cf], [B_ps[half]])
            for l in range(L):
                g = ppc(l, PP_GMEM + half * 4, 4)
                tt("dve", memnT[l][:, half * 4:half * 4 + 4, blk * 128:(blk + 1) * 128],
                   psum[half].rearrange("p (a b) -> p a b", a=4), g.unsqueeze(2).to_broadcast([128, 4, 128]), ALU.mult,
                   [B_ps[half], B_pp], [B_memn[l]])
    for l in range(L):
        wv, Bw = load_w(wview(w_xk[l], 0, 8, 0, 512), 8, 512)
        for hx in range(4):
            pi = prot.next()
            for k in range(8):
                mm(psum[pi][:, 0:MEM], wv[:, k, hx * 128:(hx + 1) * 128], memnT[l][:, k, :], k == 0, k == 7, [Bw, B_memn[l]], [B_ps[pi]])
            cp("act", kxT[l][:, hx, :], psum[pi][:, 0:MEM], [B_ps[pi]], [B_kx[l]])
        wv, Bw = load_w(wview(w_xv[l], 0, 8, 0, 512), 8, 512)
        for mb in range(2):
            pi = prot.next()
            for k in range(8):
                mm(psum[pi], memnT[l][:, k, mb * 128:(mb + 1) * 128], wv[:, k, :], k == 0, k == 7, [Bw, B_memn[l]], [B_ps[pi]])
            cp("dve", vx[l][:, mb, :], psum[pi], [B_ps[pi]], [B_vx[l]])
    P.barrier()
    reset(m0)

    hT = sb([128, 8, TS]); B_hT = Buf("hT")
    actT = sb([128, 8, TS], BF16); B_actT = Buf("actT")
    SCRA_OFF = mark()
    scrA = sb([128, 4, TS], BF16); B_scrA = Buf("scrA")
    scrB = sb([128, 4, TS], BF16); B_scrB = Buf("scrB")
    sqb = sb([128, 8, TT], BF16); B_sqb = Buf("sqb")
    rstd = sb([128, TT]); B_rstd = Buf("rstd")
    evs = [sb([128, TT]) for _ in range(2)]; B_evs = [Buf("evs0"), Buf("evs1")]
    evrot = Rot([0, 1])
    evb = [sb([128, TT], BF16) for _ in range(2)]; B_evb = [Buf("evb0"), Buf("evb1")]
    evbrot = Rot([0, 1])
    xr = [sb([128, TT + 3]) for _ in range(2)]; B_xr = [Buf("xr0"), Buf("xr1")]
    cacc = sb([128, TT]); B_cacc = Buf("cacc")
    wdt = sb([128, 8, 8], BF16); B_wdt = Buf("wdt")
    wdtf = sb([128, 8, 8]); B_wdtf = Buf("wdtf")
    dtst = sb([128, 16, 8]); B_dtst = Buf("dtst")
    dtt = sb([128, 8]); B_dtt = Buf("dtt")
    TOKWISE_END = mark()
    eng_alt = Rot(["act", "dve"])

    def embed(st):
        xin = [evs[0], evs[1]]
        for blk in range(TS // 128):
            r0 = st * TS + blk * 128
            for half in range(2):
                i = evrot.next()
                dma(evs[i], x_d[r0:r0 + 128, half * 512:(half + 1) * 512], [], [B_evs[i]], B_evs[i])
                pi = prot.next()
                for c4 in range(4):
                    tr(psum[pi][:, c4 * 128:(c4 + 1) * 128], evs[i][:, c4 * 128:(c4 + 1) * 128], ID_F, [B_evs[i], B_cf], [B_ps[pi]])
                t = blk // 4
                cp(eng_alt.next(), hT[:, half * 4:half * 4 + 4, blk * 128:(blk + 1) * 128],
                   psum[pi].rearrange("p (a b) -> p a b", a=4), [B_ps[pi]],
                   [B_hT.k((half * 4 + c4, t)) for c4 in range(4)])

    def store_hT(st):
        for c in range(8):
            dma(HT[st, :, c, :], hT[:, c, :], [B_hT], [B_HT[st]], B_HT[st])

    def load_hT(st):
        for c in range(8):
            dma(hT[:, c, :], HT[st, :, c, :], [B_HT[st]], [B_hT], B_hT)

    def g1(l, st):
        tok0 = st * TS
        rmsnorm_fm(hT, B_hT, ppc(l, PP_GMIX, 8), actT, B_actT, sqb, B_sqb, rstd, B_rstd)
        wl = w_in[l]
        if st == 0:
            memset("pool", halo, 0.0, [B_halo])
        cw = ppc(l, PP_CW, 24)
        cbias = ppc(l, PP_CB, 6)
        cstate = {"n": 0}

        def conv_evac(base):
            def f(j, t, ps, Bp):
                cc = base + j
                n = cstate["n"]
                cstate["n"] += 1
                i = n % 2
                if t == 0:
                    cp("pool", xr[i][:, 0:3], halo[:, cc, :], [B_halo], [B_xr[i]])
                else:
                    cp("pool", xr[i][:, 0:3], xr[1 - i][:, TT:TT + 3], [B_xr[1 - i]], [B_xr[i]])
                cp("act", xr[i][:, 3:TT + 3], ps, [Bp], [B_xr[i]])
                if t == NT - 1:
                    cp("pool", halo[:, cc, :], xr[i][:, TT:TT + 3], [B_xr[i]], [B_halo])
                ts("dve", cacc, xr[i][:, 0:TT], cw[:, cc * 4:cc * 4 + 1], ALU.mult, [B_xr[i], B_pp], [B_cacc])
                for k in range(1, 4):
                    stt("dve", cacc, xr[i][:, k:k + TT], cw[:, cc * 4 + k:cc * 4 + k + 1], cacc, ALU.mult, ALU.add,
                        [B_xr[i], B_pp, B_cacc], [B_cacc])
                e = evrot.next()
                act(evs[e], cacc, AF.Silu, [B_cacc, B_pp], [B_evs[e]], bias=cbias[:, cc:cc + 1])
                sl = slice(tok0 + t * TT, tok0 + (t + 1) * TT)
                if cc < 4:
                    dma(XST[cc, :, sl], evs[e], [B_evs[e]], [B_XST], B_XST)
                else:
                    dma(BCT[cc - 4, :, sl], evs[e], [B_evs[e]], [B_BCT], B_BCT)
            return f

        proj_fm(wl, 0, 8, C_X, 512, 512, actT, B_actT, NT, conv_evac(0))
        proj_fm(wl, 0, 8, C_B, 256, 256, actT, B_actT, NT, conv_evac(4))

        def qk_evac(dst, Bdst):
            dflat = dst.rearrange("h d t -> (h d) t")

            def f(j, t, ps, Bp):
                e = evbrot.next()
                cp(eng_alt.next(), evb[e], ps, [Bp], [B_evb[e]])
                sl = slice(tok0 + t * TT, tok0 + (t + 1) * TT)
                dma(dflat[j * 128:(j + 1) * 128, sl], evb[e], [B_evb[e]], [Bdst], Bdst)
            return f

        proj_fm(wl, 0, 8, C_Q, 512, 512, actT, B_actT, NT, qk_evac(QT, B_QT))
        proj_fm(wl, 0, 8, C_K, 512, 512, actT, B_actT, NT, qk_evac(KT, B_KT))
        for (c0, dst, Bdst, isbf) in ((C_Z, ZTOK, B_ZTOK, False), (C_V, VS, B_VS, True)):
            wv, Bw = load_w(wview(wl, 0, 8, c0, 512), 8, 512)
            for blk in range(TS // 128):
                pi = prot.next()
                for k in range(8):
                    mm(psum[pi], actT[:, k, blk * 128:(blk + 1) * 128], wv[:, k, :], k == 0, k == 7, [Bw, B_actT], [B_ps[pi]])
                r0 = tok0 + blk * 128
                if isbf:
                    e = evbrot.next()
                    cp(eng_alt.next(), evb[e], psum[pi], [B_ps[pi]], [B_evb[e]])
                    dma(dst[r0:r0 + 128, :], evb[e], [B_evb[e]], [Bdst], Bdst)
                else:
                    e = evrot.next()
                    cp(eng_alt.next(), evs[e], psum[pi], [B_ps[pi]], [B_evs[e]])
                    dma(dst[r0:r0 + 128, :], evs[e], [B_evs[e]], [Bdst], Bdst)
        dma(wdtf, wview(wl, 0, 8, C_DT, 8), [], [B_wdtf], B_wdtf)
        cp("dve", wdt, wdtf, [B_wdtf], [B_wdt])
        dtb = ppc(l, PP_DTB, 8)
        pi = prot.next()
        NBLK = TS // 128
        for blk in range(NBLK):
            for k in range(8):
                mm(psum[pi][:, blk * 8:(blk + 1) * 8], actT[:, k, blk * 128:(blk + 1) * 128], wdt[:, k, :], k == 0, k == 7, [B_wdt, B_actT], [B_ps[pi]])
        tt("dve", dtst, psum[pi][:, 0:NBLK * 8].rearrange("p (b h) -> p b h", h=8), dtb.unsqueeze(1).to_broadcast([128, NBLK, 8]), ALU.add,
           [B_ps[pi], B_pp], [B_dtst])
        act(dtst, dtst, AF.Exp, [B_dtst], [B_dtst])
        act(dtst, dtst, AF.Ln, [B_dtst, B_cf], [B_dtst], bias=ONEC, scale=1.0)
        dma(DTOK[tok0:tok0 + TS, :].rearrange("(b p) h -> p b h", p=128), dtst, [B_dtst], [B_DTOK], B_DTOK)

    def add_evac(j, t, ps, Bp):
        sl = slice(t * TT, (t + 1) * TT)
        tt("dve", hT[:, j, sl], ps, hT[:, j, sl], ALU.add, [Bp, B_hT.k((j, t))], [B_hT.k((j, t))])

    def g3(l, st):
        tok0 = st * TS
        for c in range(8):
            dma(actT[:, c, :], YT[c * 128:(c + 1) * 128, tok0:tok0 + TS], [B_YT], [B_actT], B_actT)
        proj_fm(w_out[l], 0, 8, 0, D, 512, actT, B_actT, NT, add_evac)
        rmsnorm_fm(hT, B_hT, ppc(l, PP_GXA, 8), actT, B_actT, sqb, B_sqb, rstd, B_rstd)
        qxT, B_qx = scrA, B_scrA
        oxT, B_ox = scrB, B_scrB

        def q_evac(j, t, ps, Bp):
            cp(eng_alt.next(), qxT[:, j, t * TT:(t + 1) * TT], ps, [Bp], [B_qx.k((j, t))])
        proj_fm(w_xq[l], 0, 8, 0, 512, 512, actT, B_actT, NT, q_evac)
        sc = 1.0 / math.sqrt(128.0)
        for t in range(NT):
            sl = slice(t * TT, (t + 1) * TT)
            for hx in range(4):
                pT = []
                bk = (4, 5, 6, 7) if (t * 4 + hx) % 2 == 0 else (0, 1, 2, 3)
                for mb in range(2):
                    pi = bk[mb]
                    mm(psum[pi], kxT[l][:, hx, mb * 128:(mb + 1) * 128], qxT[:, hx, sl], True, True, [B_kx[l], B_qx.k((hx, t))], [B_ps[pi]])
                    e = evbrot.next()
                    act(evb[e], psum[pi], AF.Exp, [B_ps[pi]], [B_evb[e]], scale=sc)
                    pT.append(e)
                po, pd = bk[2], bk[3]
                for mb in range(2):
                    mm(psum[po], vx[l][:, mb, hx * 128:(hx + 1) * 128], evb[pT[mb]], mb == 0, mb == 1, [B_vx[l], B_evb[pT[mb]]], [B_ps[po]])
                for mb in range(2):
                    mm(psum[pd], ONES_B, evb[pT[mb]], mb == 0, mb == 1, [B_cb, B_evb[pT[mb]]], [B_ps[pd]])
                e = evrot.next()
                cp("act", evs[e], psum[pd], [B_ps[pd]], [B_evs[e]])
                P.add("dve", (lambda ee: (lambda en: en.reciprocal(out=evs[ee], in_=evs[ee])))(e), reads=[B_evs[e]], writes=[B_evs[e]])
                tt("dve", oxT[:, hx, sl], psum[po], evs[e], ALU.mult, [B_ps[po], B_evs[e]], [B_ox.k((hx, t))])
        proj_fm(w_xo[l], 0, 4, 0, D, 1024, oxT, B_ox, NT, add_evac)
        rmsnorm_fm(hT, B_hT, ppc(l, PP_GFF, 8), actT, B_actT, sqb, B_sqb, rstd, B_rstd)
        uT, B_u = scrA, B_scrA
        for fb in range(8):
            proj_fm(w_ff1[l], 0, 8, fb * 512, 512, 512, actT, B_actT, NT, u_evac_fix(uT, B_u))
            proj_fm(w_ff2[l], fb * 512, 4, 0, D, 1024, uT, B_u, NT, add_evac)

    def u_evac_fix(uT, B_u):
        def f(j, t, ps, Bp):
            e = evrot.next()
            act(evs[e], ps, AF.Relu, [Bp], [B_evs[e]])
            tt("pool" if (j + t) % 2 else "dve", uT[:, j, t * TT:(t + 1) * TT], evs[e], evs[e], ALU.mult, [B_evs[e]], [B_u.k((j, t))])
        return f

    def final(st):
        tok0 = st * TS
        for t in range(NT):
            sl = slice(t * TT, (t + 1) * TT)
            for c in range(8):
                act(sqb[:, c, :], hT[:, c, sl], AF.Square, [B_hT.k((c, t))], [B_sqb.k(c)])
            pi = prot.next()
            for c in range(8):
                mm(psum[pi], ONES_B, sqb[:, c, :], c == 0, c == 7, [B_cb, B_sqb.k(c)], [B_ps[pi]])
            act(rstd, psum[pi], AF.Ln, [B_ps[pi], B_cf], [B_rstd], bias=EPSC, scale=1.0 / D)
            act(rstd, rstd, AF.Exp, [B_rstd], [B_rstd], scale=-0.5)
            gf = pp[:, PP_FINAL:PP_FINAL + 8]
            for c in range(8):
                stt("dve", fin[:, c, :], hT[:, c, sl], gf[:, c:c + 1], rstd, ALU.mult, ALU.mult,
                    [B_hT.k((c, t)), B_pp, B_rstd], [B_fin.k(c)])
            for b4 in range(4):
                for half in range(2):
                    pi = 4 + half
                    for c4 in range(4):
                        c = half * 4 + c4
                        tr(psum[pi][:, c4 * 128:(c4 + 1) * 128], fin[:, c, b4 * 128:(b4 + 1) * 128], ID_F, [B_fin.k(c), B_cf], [B_ps[pi]])
                    e = evrot.next()
                    cp(eng_alt.next(), evs[e], psum[pi], [B_ps[pi]], [B_evs[e]])
                    r0 = tok0 + t * TT + b4 * 128
                    dma(out_d[r0:r0 + 128, half * 512:(half + 1) * 512], evs[e], [B_evs[e]], [B_OUT], B_OUT)

    fin = arena[:, SCRA_OFF // 2:SCRA_OFF // 2 + 8192].bitcast(F32).rearrange("p (a b) -> p a b", a=8); B_fin = B_scrA

    def ssd(l):
        m = mark()
        NG = TL // TT
        NCH = TL // 128
        xsT = [sb([128, 4, TT]) for _ in range(2)]; B_xsT = [Buf("xsT0"), Buf("xsT1")]
        bt128 = [sb([128, TT]) for _ in range(2)]; B_bt = [Buf("bt0"), Buf("bt1")]
        bc64 = [sb([64, 4, TT]) for _ in range(2)]; B_bc64 = [Buf("bc640"), Buf("bc641")]
        bc64b = [sb([64, 4, TT], BF16) for _ in range(2)]; B_bc64b = [Buf("bc64b0"), Buf("bc64b1")]
        ztok = [sb([128, 4, TT]) for _ in range(2)]; B_ztok = [Buf("ztok0"), Buf("ztok1")]
        dtg = [sb([128, 4, 8]) for _ in range(2)]; B_dtg = [Buf("dtg0"), Buf("dtg1")]
        yst = [sb([128, 4, TT], BF16) for _ in range(2)]; B_yst = [Buf("yst0"), Buf("yst1")]
        xs_tok2 = [sb([128, 512]) for _ in range(2)]; B_xs2 = [Buf("xs_tok0"), Buf("xs_tok1")]
        btok2 = [sb([128, 128], BF16) for _ in range(2)]; B_btok2 = [Buf("btok0"), Buf("btok1")]
        sm2 = [sb([128, 80]) for _ in range(2)]; B_sm2 = [Buf("sm0"), Buf("sm1")]
        MT2 = [sb([128, 8, 128], BF16) for _ in range(2)]; B_MT2 = [Buf("MT0"), Buf("MT1")]
        xdt2 = [sb([128, 512], BF16) for _ in range(2)]; B_xdt2 = [Buf("xdt0"), Buf("xdt1")]
        xw2 = [sb([128, 512], BF16) for _ in range(2)]; B_xw2 = [Buf("xw0"), Buf("xw1")]
        for i in range(2):
            memset("pool", sm2[i], 0.0, [B_sm2[i]])
        Rm = sb([128, 8, 128]); B_R = Buf("R")
        dec = sb([128, 8, 128]); B_dec = Buf("dec")
        t1 = sb([128, 512]); B_t1 = Buf("t1")
        t2 = sb([128, 512]); B_t2 = Buf("t2")
        yv2 = [sb([128, 512]) for _ in range(2)]; B_y2 = [Buf("y0"), Buf("y1")]
        gz = sb([128, 512]); B_gz = Buf("gz")
        sq = sb([128, 512]); B_sq = Buf("sq")
        ssq = sb([128, 4]); B_ssq = Buf("ssq")
        prev = sb([64, 8, 64]); B_prev = Buf("prev")
        prevb = [sb([64, 8, 64], BF16) for _ in range(2)]; B_prevb = [Buf("prevb0"), Buf("prevb1")]
        a_l = abc[:, l * 8:(l + 1) * 8]
        dsk = ppc(l, PP_DSK, 512)
        gssd = ppc(l, PP_GSSD, 4)
        memset("pool", prev, 0.0, [B_prev])
        memset("pool", prevb[0], 0.0, [B_prevb[0]])
        BCT64 = BCT.rearrange("a (g n) t -> n (a g) t", g=2)

        def loads(gi):
            s = gi % 2
            tsl = slice(gi * TT, (gi + 1) * TT)
            for j in range(4):
                dma(xsT[s][:, j, :], XST[j, :, tsl], [B_XST], [B_xsT[s]], B_xsT[s])
            dma(bt128[s], BCT[0, :, tsl], [B_BCT], [B_bt[s]], B_bt[s])
            for a_ in range(4):
                dma(bc64[s][:, a_, :], BCT64[:, a_, tsl], [B_BCT], [B_bc64[s]], B_bc64[s])
            dma(ztok[s], ZTOK[tsl, :].rearrange("(c p) f -> p c f", p=128), [B_ZTOK], [B_ztok[s]], B_ztok[s])
            dma(dtg[s], DTOK[tsl, :].rearrange("(c p) h -> p c h", p=128), [B_DTOK], [B_dtg[s]], B_dtg[s])
            cp("pool", bc64b[s], bc64[s], [B_bc64[s]], [B_bc64b[s]])

        def front(ci):
            gi, cg = ci // 4, ci % 4
            s = gi % 2
            p = ci % 2
            cs = slice(cg * 128, (cg + 1) * 128)
            xs_tok, B_xs = xs_tok2[p], B_xs2[p]
            btok, B_btok = btok2[p], B_btok2[p]
            sm, B_sm = sm2[p], B_sm2[p]
            MT, B_MT = MT2[p], B_MT2[p]
            xdt, B_xdt = xdt2[p], B_xdt2[p]
            xw, B_xw = xw2[p], B_xw2[p]
            da16 = sm[:, 0:16]; da = sm[:, 0:8]
            nacol, expA, dstate, cd, dtd, diff = [sm[:, 16 + i * 8:16 + (i + 1) * 8] for i in range(6)]
            for j in range(4):
                tr(psum[0][:, j * 128:(j + 1) * 128], xsT[s][:, j, cs], ID_F, [B_xsT[s], B_cf], [B_ps[0]])
            tr(psum[1][:, 0:128], bt128[s][:, cs], ID_F, [B_bt[s], B_cf], [B_ps[1].k("bt")])
            cp("act", xs_tok, psum[0], [B_ps[0]], [B_xs])
            cp("dve", btok, psum[1][:, 0:128], [B_ps[1].k("bt")], [B_btok])
            dtc = dtg[s][:, cg, :]
            tt("dve", da, dtc, a_l, ALU.mult, [B_dtg[s], B_abc], [B_sm.k("da")])
            mm(psum[1][:, 128:144], U_F, da16, True, True, [B_cf, B_sm.k("da")], [B_ps[1].k("ac")])
            mm(psum[1][:, 144:160], ONES_F, da16, True, True, [B_cf, B_sm.k("da")], [B_ps[1].k("ac")])
            ts("dve", nacol, psum[1][:, 128:136], -1.0, ALU.mult, [B_ps[1].k("ac")], [B_sm.k("nacol")])
            act(expA, psum[1][:, 128:136], AF.Exp, [B_ps[1].k("ac")], [B_sm.k("expA")])
            tt("dve", diff, psum[1][:, 144:152], nacol, ALU.add, [B_ps[1].k("ac"), B_sm.k("nacol")], [B_sm.k("diff")])
            act(dstate, diff, AF.Exp, [B_sm.k("diff")], [B_sm.k("dstate")])
            act(cd, psum[1][:, 144:152], AF.Exp, [B_ps[1].k("ac")], [B_sm.k("cd")])
            tt("dve", dtd, dtc, dstate, ALU.mult, [B_dtg[s], B_sm.k("dstate")], [B_sm.k("dtd")])
            tt("dve", Rm, U_F.unsqueeze(1).to_broadcast([128, 8, 128]), da.unsqueeze(2).to_broadcast([128, 8, 128]), ALU.mult,
               [B_cf, B_sm.k("da")], [B_R])
            for half in range(2):
                mm(psum[2 + half], ONES_F, Rm[:, half * 4:half * 4 + 4, :].rearrange("p a b -> p (a b)"), True, False, [B_cf, B_R], [B_ps[2 + half]])
                mm(psum[2 + half], ID_F, NEGM4, False, True, [B_cf], [B_ps[2 + half]])
            for h in range(8):
                act(dec[:, h, :], psum[2 + h // 4][:, (h % 4) * 128:(h % 4 + 1) * 128], AF.Exp,
                    [B_ps[2 + h // 4], B_sm.k("nacol")], [B_dec.k(h // 4)], bias=nacol[:, h:h + 1], scale=1.0)
            for g in range(2):
                mm(psum[1][:, 256 + g * 128:256 + (g + 1) * 128], bc64b[s][:, g, cs], bc64b[s][:, 2 + g, cs], True, True,
                   [B_bc64b[s]], [B_ps[1].k("cb%d" % g)])
                tt("dve", MT[:, g * 4:g * 4 + 4, :], psum[1][:, 256 + g * 128:256 + (g + 1) * 128].unsqueeze(1).to_broadcast([128, 4, 128]),
                   dec[:, g * 4:g * 4 + 4, :], ALU.mult, [B_ps[1].k("cb%d" % g), B_dec.k(g)], [B_MT.k(g)])
            xs3 = xs_tok.rearrange("p (h j) -> p h j", h=8)
            tt("dve", xdt.rearrange("p (h j) -> p h j", h=8), xs3, dtc.unsqueeze(2).to_broadcast([128, 8, 64]), ALU.mult,
               [B_xs, B_dtg[s]], [B_xdt])
            tt("pool", xw.rearrange("p (h j) -> p h j", h=8), xs3, dtd.unsqueeze(2).to_broadcast([128, 8, 64]), ALU.mult,
               [B_xs, B_sm.k("dtd")], [B_xw])

        def back(ci):
            gi, cg = ci // 4, ci % 4
            s = gi % 2
            p = ci % 2
            cs = slice(cg * 128, (cg + 1) * 128)
            tsl = slice(gi * TT, (gi + 1) * TT)
            xs_tok, B_xs = xs_tok2[p], B_xs2[p]
            btok, B_btok = btok2[p], B_btok2[p]
            sm, B_sm = sm2[p], B_sm2[p]
            MT, B_MT = MT2[p], B_MT2[p]
            xdt, B_xdt = xdt2[p], B_xdt2[p]
            xw, B_xw = xw2[p], B_xw2[p]
            nacol, expA, dstate, cd, dtd, diff = [sm[:, 16 + i * 8:16 + (i + 1) * 8] for i in range(6)]
            pb_cur = prevb[ci % 2]; Bpb_cur = B_prevb[ci % 2]
            pb_nxt = prevb[(ci + 1) % 2]; Bpb_nxt = B_prevb[(ci + 1) % 2]
            yv, B_y = yv2[p], B_y2[p]
            for h in range(8):
                mm(psum[4][:, h * 64:(h + 1) * 64], MT[:, h, :], xdt[:, h * 64:(h + 1) * 64], True, True, [B_MT.k(h // 4), B_xdt], [B_ps[4]])
            for h in range(8):
                mm(psum[5][:, h * 64:(h + 1) * 64], bc64b[s][:, 2 + h // 4, cs], pb_cur[:, h, :], True, True, [B_bc64b[s], Bpb_cur], [B_ps[5]])
            for g in range(2):
                mm(psum[6][0:64, g * 256:(g + 1) * 256], btok[:, g * 64:(g + 1) * 64], xw[:, g * 256:(g + 1) * 256], True, True,
                   [B_btok, B_xw], [B_ps[6]])
            tt("pool", prev, prev, cd[0:64, :].unsqueeze(2).to_broadcast([64, 8, 64]), ALU.mult, [B_prev, B_sm.k("cd")], [B_prev])
            tt("dve", prev, psum[6][0:64, :].rearrange("p (h j) -> p h j", h=8), prev, ALU.add, [B_ps[6], B_prev], [B_prev])
            cp("pool", pb_nxt, prev, [B_prev], [Bpb_nxt])
            tt("dve", t1.rearrange("p (h j) -> p h j", h=8), psum[5].rearrange("p (h j) -> p h j", h=8),
               expA.unsqueeze(2).to_broadcast([128, 8, 64]), ALU.mult, [B_ps[5], B_sm.k("expA")], [B_t1])
            tt("pool", t2, xs_tok, dsk, ALU.mult, [B_xs, B_pp], [B_t2])
            tt("pool", t2, t2, t1, ALU.add, [B_t2, B_t1], [B_t2])
            tt("dve", yv, psum[4], t2, ALU.add, [B_ps[4], B_t2], [B_y])

        def back2(ci):
            gi, cg = ci // 4, ci % 4
            s = gi % 2
            p = ci % 2
            cs = slice(cg * 128, (cg + 1) * 128)
            tsl = slice(gi * TT, (gi + 1) * TT)
            yv, B_y = yv2[p], B_y2[p]
            act(gz, ztok[s][:, cg, :], AF.Silu, [B_ztok[s]], [B_gz])
            tt("pool", yv, yv, gz, ALU.mult, [B_y, B_gz], [B_y])
            memset("pool", ssq[:, 0:1], 0.0, [B_ssq])
            act(sq, yv, AF.Square, [B_y, B_ssq], [B_sq, B_ssq], accum=ssq[:, 0:1])
            act(ssq[:, 1:2], ssq[:, 0:1], AF.Ln, [B_ssq, B_cf], [B_ssq], bias=EPSC, scale=1.0 / 512)
            act(ssq[:, 2:3], ssq[:, 1:2], AF.Exp, [B_ssq], [B_ssq], scale=-0.5)
            ts("dve", sq, yv, ssq[:, 2:3], ALU.mult, [B_y, B_ssq], [B_sq])
            for j in range(4):
                tr(psum[7][:, j * 128:(j + 1) * 128], sq[:, j * 128:(j + 1) * 128], ID_F, [B_sq, B_cf], [B_ps[7]])
            tt("dve", yst[s][:, :, cs], psum[7].rearrange("p (a b) -> p a b", a=4), gssd.unsqueeze(2).to_broadcast([128, 4, 128]), ALU.mult,
               [B_ps[7], B_pp], [B_yst[s]])
            if cg == 3:
                dma(YT[0:512, tsl].rearrange("(j p) t -> p j t", p=128), yst[s], [B_yst[s]], [B_YT], B_YT)

        for ci in range(NCH + 1):
            if ci < NCH and ci % 4 == 0:
                loads(ci // 4)
            fns = []
            if ci < NCH:
                fns.append((lambda c: (lambda: front(c)))(ci))
            if ci >= 1:
                fns.append((lambda c: (lambda: (back(c), back2(c))))(ci - 1))
            interleave(P, *fns)
        P.barrier()
        reset(m)

    def sba(l):
        m = mark()
        kt_all = sb([128, 8, TL], BF16); B_kt = Buf("kt_all")
        v_all = sb([128, TL // 128, 576], BF16); B_v = Buf("v_all")
        qg = [sb([128, 8, TT], BF16) for _ in range(2)]; B_qg = [Buf("qg0"), Buf("qg1")]
        e_sb = [sb([128, TT]) for _ in range(3)]; B_e = [Buf("e%d" % i) for i in range(3)]
        sp_b = [sb([128, TT], BF16) for _ in range(2)]; B_sp = [Buf("sp%d" % i) for i in range(2)]
        r_sb = [sb([128, TT]) for _ in range(2)]; B_r = [Buf("r%d" % i) for i in range(2)]
        w_b = [sb([128, TT], BF16) for _ in range(2)]; B_w = [Buf("w%d" % i) for i in range(2)]
        acc = [sb([128, TT], BF16) for _ in range(2)]; B_acc = [Buf("acc%d" % i) for i in range(2)]
        o_sb = sb([64, 8, TT]); B_o = Buf("o_sb")
        osq = sb([64, 8, TT], BF16); B_osq = Buf("osq")
        rs = sb([128, TT]); B_rs = Buf("rs")
        yst1 = sb([64, 8, TT], BF16); yst = [yst1, yst1]; B_y1 = Buf("ysb"); B_yst = [B_y1, B_y1]
        gsb = ppc(l, PP_GSB, 8)
        memset("pool", v_all[:, :, 512:576], 0.0, [B_v])
        for h in range(8):
            dma(kt_all[0:64, h, :], KT[h, :, :], [B_KT], [B_kt], B_kt)
            dma(kt_all[64:128, h, :], KT[h, :, :], [B_KT], [B_kt], B_kt)
        for q4 in range(TL // 1024):
            bs = slice(q4 * 8, (q4 + 1) * 8)
            dma(v_all[:, bs, 0:512], VS[q4 * 1024:(q4 + 1) * 1024, :].rearrange("(b p) f -> p b f", p=128), [B_VS], [B_v], B_v)
        tiles = []
        for G in range(min(TL // TT, SBA_MAXG)):
            for h in range(8):
                kbs = list(range(4 * G + 3, -1, -1))
                for ii, kb in enumerate(kbs):
                    tiles.append((G, h, kb, ii == 0, ii == len(kbs) - 1))
        n = len(tiles)
        Z_PS = [0, 1]; R_PS = [2, 3]; O_PS = [4, 5]; SS_PS = 6

        def stage_q(i):
            G, h, kb, first, last = tiles[i]
            s = G % 2
            if h == 0 and first:
                dma(qg[s][0:64], QT[:, :, G * TT:(G + 1) * TT].rearrange("h d t -> d h t"), [B_QT], [B_qg[s]], B_qg[s])
                dma(qg[s][64:128], QT[:, :, G * TT:(G + 1) * TT].rearrange("h d t -> d h t"), [B_QT], [B_qg[s]], B_qg[s])
            j = kb - 4 * G
            c0 = 128 * max(j, 0)
            cs = slice(c0, TT)
            zi = Z_PS[i % 2]
            mm(psum[zi][:, cs], kt_all[:, h, kb * 128:(kb + 1) * 128], qg[s][:, h, cs], True, True, [B_kt, B_qg[s]], [B_ps[zi]])
            if j >= 0:
                mm(psum[zi][:, c0:c0 + 128], ID_B, NEGTRI_B, False, True, [B_cb], [B_ps[zi]])
            for _ in range(SBA_DUMMY):
                mm(psum[7], ID_B, cb[:, 0:512], True, True, [B_cb], [B_ps[7]])

        def stage_a(i):
            G, h, kb, first, last = tiles[i]
            s = G % 2
            j = kb - 4 * G
            c0 = 128 * max(j, 0)
            cs = slice(c0, TT)
            zi = Z_PS[i % 2]; ri = R_PS[i % 2]
            ei = i % 3; si = i % 2
            ai = (G * 8 + h) % 2
            act(e_sb[ei][:, cs], psum[zi][:, cs], AF.Exp, [B_ps[zi]], [B_e[ei]], scale=0.0625)
            act(sp_b[si][:, cs], e_sb[ei][:, cs], AF.Ln, [B_e[ei], B_cf], [B_sp[si]], bias=ONEC, scale=1.0)
            mm(psum[ri][:, cs], TRI_B, sp_b[si][:, cs], True, first, [B_cb, B_sp[si]], [B_ps[ri]])
            if not first:
                mm(psum[ri][:, cs], ONES_B, acc[ai][:, cs], False, True, [B_cb, B_acc[ai]], [B_ps[ri]])
            if first:
                memset("pool", acc[ai], 0.0, [B_acc[ai]])
            if not last:
                tt("pool", acc[ai][:, cs], acc[ai][:, cs], sp_b[si][:, cs], ALU.add, [B_acc[ai], B_sp[si]], [B_acc[ai]])

        def stage_b(i):
            G, h, kb, first, last = tiles[i]
            s = G % 2
            j = kb - 4 * G
            c0 = 128 * max(j, 0)
            cs = slice(c0, TT)
            ri = R_PS[i % 2]
            ei = i % 3; si = i % 2
            oi = O_PS[(G * 8 + h) % 2]
            if SBA_LEVEL < 3:
                return
            act(r_sb[si][:, cs], psum[ri][:, cs], AF.Exp, [B_ps[ri]], [B_r[si]], scale=-1.0)
            tt("dve", w_b[si][:, cs], e_sb[ei][:, cs], r_sb[si][:, cs], ALU.mult, [B_e[ei], B_r[si]], [B_w[si]])
            if first:
                for q4 in range(4):
                    mm(psum[oi][:, q4 * 128:(q4 + 1) * 128], ZERO_B, ONES_B, True, False, [B_cb], [B_ps[oi]])
            mm(psum[oi][:, cs], v_all[:, kb, h * 64:h * 64 + 128], w_b[si][:, cs], False, last, [B_v, B_w[si]], [B_ps[oi]])
            if last and SBA_LEVEL >= 4:
                cp("dve", o_sb[:, h, :], psum[oi][0:64, :], [B_ps[oi]], [B_o.k(h)])
                act(osq[:, h, :], psum[oi][0:64, :], AF.Square, [B_ps[oi]], [B_osq.k(h)])
                if h == 7 and SBA_LEVEL >= 5:
                    for hh in range(8):
                        mm(psum[SS_PS], ONES_B[0:64, :], osq[:, hh, :], hh == 0, hh == 7, [B_cb, B_osq.k(hh)], [B_ps[SS_PS]])
                    act(rs, psum[SS_PS], AF.Ln, [B_ps[SS_PS], B_cf], [B_rs], bias=EPSC, scale=1.0 / 512)
                    act(rs, rs, AF.Exp, [B_rs], [B_rs], scale=-0.5)
                    for hh in range(8):
                        stt("dve", yst[s][:, hh, :], o_sb[:, hh, :], gsb[0:64, hh:hh + 1], rs[0:64, :], ALU.mult, ALU.mult,
                            [B_o.k(hh), B_pp, B_rs], [B_yst[s]])
                    dma(YT[512:1024, G * TT:(G + 1) * TT].rearrange("(h d) t -> d h t", d=64), yst[s], [B_yst[s]], [B_YT], B_YT)

        for i in range(n + 2):
            if i < n:
                stage_q(i)
            if 1 <= i <= n:
                stage_a(i - 1)
            if i >= 2:
                stage_b(i - 2)
        P.barrier()
        reset(m)

    for st in range(NST):
        embed(st)
        if NST > 1:
            store_hT(st)
        g1(0, st)
    P.barrier()
    done = False
    for l in range(n_layers):
        if stop == "g1":
            break
        reset(PERSIST)
        ssd(l)
        if stop == "ssd":
            break
        sba(l)
        if stop == "g2":
            break
        for st in range(NST):
            if NST > 1:
                load_hT(st)
            g3(l, st)
            if l + 1 < n_layers:
                if NST > 1:
                    store_hT(st)
                g1(l + 1, st)
            else:
                final(st)
                done = True
        P.barrier()
    dumps = {}
    if dump:
        alld = (("QT", QT, B_QT), ("KT", KT, B_KT), ("VS", VS, B_VS), ("ZTOK", ZTOK, B_ZTOK), ("XST", XST, B_XST),
                ("BCT", BCT, B_BCT), ("DTOK", DTOK, B_DTOK), ("YT", YT, B_YT), ("HT", HT, None))
        for name, ap_, Bf in [d_ for d_ in alld if dump is True or d_[0] in dump]:
            o = nc.dram_tensor("dump_" + name, list(ap_.shape), ap_.dtype, kind="ExternalOutput").ap()
            db = Buf("dump_" + name)
            rd = [Bf] if Bf is not None else list(B_HT)
            if len(ap_.shape) == 4:
                for s_ in range(ap_.shape[0]):
                    dma(o[s_].rearrange("p c t -> p (c t)"), ap_[s_].rearrange("p c t -> p (c t)"), rd, [db], db)
            elif len(ap_.shape) == 3:
                for s_ in range(ap_.shape[0]):
                    dma(o[s_], ap_[s_], rd, [db], db)
            else:
                dma(o, ap_, rd, [db], db)
    P.barrier()
    P.emit()
    return nc


_CACHE = {}


def kernel(**inputs):
    p = {k: np.asarray(v) for k, v in inputs.items()}
    if "nc" not in _CACHE:
        _CACHE["nc"] = build()
    nc = _CACHE["nc"]
    cf = host_consts()
    pp = host_params(p)
    shared = {k: np.ascontiguousarray(p[k], dtype=np.float32) for k in ("w_in", "w_out", "w_xq", "w_xk", "w_xv", "w_xo", "w_ff1", "w_ff2")}
    in_maps = []
    for c in range(8):
        b = c % 4
        m = {"x": np.ascontiguousarray(p["x"][b], dtype=np.float32), "mem": np.ascontiguousarray(p["mem"][b], dtype=np.float32),
             "cf": cf, "pp": pp}
        m.update(shared)
        in_maps.append(m)
    res = run_bass_kernel_spmd(nc, in_maps, core_ids=list(range(8)))
    out = np.stack([np.asarray(res.results[b]["out"], dtype=np.float32) for b in range(4)], axis=0)
    return out
```

```python
import math
import contextlib
import numpy as np
import concourse.bass as bass
import concourse.mybir as mybir
from concourse.bass_utils import run_bass_kernel_spmd

F32 = mybir.dt.float32
BF16 = mybir.dt.bfloat16
AF = mybir.ActivationFunctionType
ALU = mybir.AluOpType

ENGS = ("pe", "act", "dve", "pool", "sp")


class Buf:
    def __init__(self, name):
        self.name = name
        self.st = {"*": [[], []]}
        self.sem = None
        self.ndma = 0
        self.excl = False
        self.last_by_eng = {}

    def k(self, key):
        return (self, key)


class Op:
    __slots__ = ("eng", "fn", "waits", "idx", "is_dma", "dest")


class Prog:
    def __init__(self, nc):
        self.nc = nc
        self.ops = {e: [] for e in ENGS}
        self.dma_bufs = []
        self.nops = 0

    @staticmethod
    def _norm(x):
        if isinstance(x, Buf):
            return (x, None)
        return x

    def _entries(self, buf, key):
        d = buf.st
        if key is None:
            return list(d.values())
        if key not in d:
            d[key] = [list(d["*"][0]), list(d["*"][1])]
        return [d[key]]

    def add(self, eng, fn, reads=(), writes=(), dma_dest=None):
        if getattr(self, "capture", None) is not None:
            self.capture.append((eng, fn, reads, writes, dma_dest))
            return None
        op = Op()
        op.eng = eng
        op.fn = fn
        op.is_dma = dma_dest is not None
        op.dest = dma_dest
        deps = []
        reads = [self._norm(r) for r in reads if r is not None]
        writes = [self._norm(w) for w in writes if w is not None]
        for (b, key) in reads:
            for ent in self._entries(b, key):
                deps.extend(ent[0])
        for (b, key) in writes:
            for ent in self._entries(b, key):
                samegen = op.is_dma and len(ent[0]) > 0 and all(w.is_dma for w in ent[0]) and len(ent[1]) == 0
                if not samegen:
                    deps.extend(ent[0])
                deps.extend(ent[1])
        for (b, key) in list(reads) + list(writes):
            if b.excl:
                for e2, y in b.last_by_eng.items():
                    if e2 != eng:
                        deps.append(y)
                b.last_by_eng[eng] = op
        for (b, key) in writes:
            for ent in self._entries(b, key):
                samegen = op.is_dma and len(ent[0]) > 0 and all(w.is_dma for w in ent[0]) and len(ent[1]) == 0
                if samegen:
                    ent[0].append(op)
                else:
                    ent[0] = [op]
                    ent[1] = []
            if key is None:
                for kk in list(b.st.keys()):
                    b.st[kk][0] = list(b.st["*"][0])
                    b.st[kk][1] = []
        wset = set((id(b), key) for (b, key) in writes)
        for (b, key) in reads:
            if (id(b), key) in wset:
                continue
            for ent in self._entries(b, key):
                ent[1].append(op)
        if op.is_dma:
            d = dma_dest
            if d.sem is None:
                self.dma_bufs.append(d)
                d.sem = True
            d.ndma += 1
        lst = self.ops[eng]
        lst.append(op)
        op.idx = len(lst)
        waits = {}
        for y in deps:
            if y is op:
                continue
            if y.is_dma:
                key = ("d", id(y.dest))
                val = 16 * y.dest.ndma if y.dest is not dma_dest else 16 * (y.dest.ndma - 1)
                if val <= 0:
                    continue
                ent = (y.dest, val)
            else:
                if y.eng == eng and eng == "pe":
                    continue
                key = ("e", y.eng)
                val = y.idx
                ent = (y.eng, val)
            if key not in waits or waits[key][1] < val:
                waits[key] = ent
        op.waits = waits
        self.nops += 1
        return op

    def barrier(self):
        snap_e = {e: len(self.ops[e]) for e in ENGS}
        snap_d = [(d, 16 * d.ndma) for d in self.dma_bufs]
        for e in ENGS:
            op = Op()
            op.eng = e
            op.fn = None
            op.is_dma = False
            op.dest = None
            w = {}
            for e2 in ENGS:
                if e2 == e:
                    continue
                w[("e", e2)] = (e2, snap_e[e2])
            for (d, v) in snap_d:
                if v > 0:
                    w[("d", id(d))] = (d, v)
            op.waits = w
            lst = self.ops[e]
            lst.append(op)
            op.idx = len(lst)

    def emit(self):
        nc = self.nc
        with contextlib.ExitStack() as es:
            esem = {e: es.enter_context(nc.semaphore("es_" + e)) for e in ENGS}
            for i, d in enumerate(self.dma_bufs):
                d.sem = es.enter_context(nc.semaphore("ds%d_%s" % (i, d.name)))
            block = es.enter_context(nc.Block())
            cidx = {}
            for e in ENGS:
                c = 0
                m = [0]
                for op in self.ops[e]:
                    if not op.is_dma:
                        c += 1
                    m.append(c)
                cidx[e] = m

            def make(e):
                ops = self.ops[e]

                def body(eng):
                    seen = {}
                    for op in ops:
                        for key, (obj, val) in op.waits.items():
                            if key[0] == "e":
                                sem = esem[obj]
                                v = cidx[obj][val]
                            else:
                                sem = obj.sem
                                v = val
                            if v <= 0 or seen.get(key, 0) >= v:
                                continue
                            seen[key] = v
                            eng.wait_ge(sem, v)
                        if op.fn is None:
                            eng.nop(nofuse=True).then_inc(esem[e], 1)
                            continue
                        ins = op.fn(eng)
                        if op.is_dma:
                            ins.then_inc(op.dest.sem, 16)
                        else:
                            ins.then_inc(esem[e], 1)
                return body

            block.tensor(make("pe"))
            block.scalar(make("act"))
            block.vector(make("dve"))
            block.gpsimd(make("pool"))
            block.sync(make("sp"))


def interleave(P, *fns):
    lists = []
    for f in fns:
        P.capture = lst = []
        f()
        lists.append(lst)
    P.capture = None
    pos = [0] * len(lists)
    total = sum(len(l) for l in lists)
    for _ in range(total):
        best, bi = None, -1
        for i, l in enumerate(lists):
            if pos[i] < len(l):
                frac = pos[i] / len(l)
                if best is None or frac < best:
                    best, bi = frac, i
        P.add(*lists[bi][pos[bi]])
        pos[bi] += 1


class Rot:
    def __init__(self, items):
        self.items = items
        self.i = 0

    def next(self):
        it = self.items[self.i % len(self.items)]
        self.i += 1
        return it


D = 1024
TL = 4096
TS = 2048
NST = TL // TS
TT = 512
NT = TS // TT
L = 2
MEM = 256
IN_DIM = 2824
C_Z, C_X, C_B, C_C, C_DT, C_Q, C_K, C_V = 0, 512, 1024, 1152, 1280, 1288, 1800, 2312
EPS = 1e-5
NEG = -30000.0
SBA_MAXG = 99
SBA_LEVEL = 9
SBA_DUMMY = 1

CF_ID, CF_ONES, CF_U, CF_NEGM4, CF_TRI, CF_NEGTRI, CF_EPS, CF_ONE, NCF = 0, 128, 256, 384, 896, 1024, 1152, 1153, 1160
PP_GMIX, PP_GXA, PP_GFF, PP_GMEM, PP_GSSD, PP_GSB, PP_CW, PP_CB, PP_DTB, PP_ALOG, PP_DSK, PPW = 0, 8, 16, 24, 32, 36, 44, 68, 74, 82, 90, 602
PP_FINAL = L * PPW
NPP = PP_FINAL + 8


def host_consts():
    cf = np.zeros((128, NCF), np.float32)
    i = np.arange(128)
    cf[:, CF_ID:CF_ID + 128] = np.eye(128, dtype=np.float32)
    cf[:, CF_ONES:CF_ONES + 128] = 1.0
    cf[:, CF_U:CF_U + 128] = (i[:, None] <= i[None, :]).astype(np.float32)
    negm = np.where(i[None, :] >= i[:, None], 0.0, NEG).astype(np.float32)
    cf[:, CF_NEGM4:CF_NEGM4 + 512] = np.tile(negm, (1, 4))
    cf[:, CF_TRI:CF_TRI + 128] = (i[:, None] >= i[None, :]).astype(np.float32)
    cf[:, CF_NEGTRI:CF_NEGTRI + 128] = np.where(i[:, None] < i[None, :], 0.0, 8 * NEG)
    cf[:, CF_EPS] = EPS
    cf[:, CF_ONE] = 1.0
    return cf


def host_params(p):
    pp = np.zeros((128, NPP), np.float32)
    col = lambda v, n: np.ascontiguousarray(np.asarray(v, np.float32).reshape(n, 128).T)
    for l in range(L):
        o = l * PPW
        pp[:, o + PP_GMIX:o + PP_GMIX + 8] = col(p["norm_mix_g"][l], 8)
        pp[:, o + PP_GXA:o + PP_GXA + 8] = col(p["norm_xa_g"][l], 8)
        pp[:, o + PP_GFF:o + PP_GFF + 8] = col(p["norm_ff_g"][l], 8)
        pp[:, o + PP_GMEM:o + PP_GMEM + 8] = col(p["norm_mem_g"][l], 8)
        pp[:, o + PP_GSSD:o + PP_GSSD + 4] = col(p["ssd_norm_g"][l], 4)
        pp[0:64, o + PP_GSB:o + PP_GSB + 8] = np.asarray(p["sb_norm_g"][l], np.float32).reshape(8, 64).T
        cw = np.asarray(p["conv_w"][l], np.float32)
        pp[:, o + PP_CW:o + PP_CW + 24] = cw.reshape(4, 6, 128).transpose(2, 1, 0).reshape(128, 24)
        pp[:, o + PP_CB:o + PP_CB + 6] = col(p["conv_b"][l], 6)
        pp[:, o + PP_DTB:o + PP_DTB + 8] = np.asarray(p["dt_bias"][l], np.float32)[None, :]
        pp[:, o + PP_ALOG:o + PP_ALOG + 8] = np.asarray(p["a_log"][l], np.float32)[None, :]
        pp[:, o + PP_DSK:o + PP_DSK + 512] = np.repeat(np.asarray(p["d_skip"][l], np.float32), 64)[None, :]
    pp[:, PP_FINAL:PP_FINAL + 8] = col(p["final_g"], 8)
    return pp


def build(n_layers=L, stop=None, dump=False):
    nc = bass.Bass("TRN2", target_bir_lowering=False)
    P = Prog(nc)

    def din(name, shape, dt=F32):
        return nc.dram_tensor(name, shape, dt, kind="ExternalInput").ap()

    x_d = din("x", [TL, D])
    mem_d = din("mem", [MEM, D])
    w_in = din("w_in", [L, D, IN_DIM])
    w_out = din("w_out", [L, D, D])
    w_xq = din("w_xq", [L, D, 512])
    w_xk = din("w_xk", [L, D, 512])
    w_xv = din("w_xv", [L, D, 512])
    w_xo = din("w_xo", [L, 512, D])
    w_ff1 = din("w_ff1", [L, D, 4096])
    w_ff2 = din("w_ff2", [L, 4096, D])
    cf_d = din("cf", [128, NCF])
    pp_d = din("pp", [128, NPP])
    out_d = nc.dram_tensor("out", [TL, D], F32, kind="ExternalOutput").ap()

    def dscr(name, shape, dt):
        return nc.dram_tensor(name, shape, dt).ap()

    HT = dscr("HT", [NST, 128, 8, TS], F32)
    QT = dscr("QT", [8, 64, TL], BF16)
    KT = dscr("KT", [8, 64, TL], BF16)
    VS = dscr("VS", [TL, 512], BF16)
    ZTOK = dscr("ZTOK", [TL, 512], F32)
    XST = dscr("XST", [4, 128, TL], F32)
    BCT = dscr("BCT", [2, 128, TL], F32)
    DTOK = dscr("DTOK", [TL, 8], F32)
    YT = dscr("YT", [D, TL], BF16)
    B_HT = [Buf("HT%d" % s) for s in range(NST)]
    B_QT, B_KT, B_VS, B_ZTOK, B_XST, B_BCT, B_DTOK, B_YT, B_OUT = [Buf(n) for n in "QT KT VS ZTOK XST BCT DTOK YT OUT".split()]

    ARENA_BYTES = 206 * 1024
    arena = nc.alloc_sbuf_tensor("arena", [128, ARENA_BYTES // 2], BF16).ap()
    st_ = {"off": 0}

    def sb(shape, dt=F32, parts=128):
        n = 1
        for s_ in shape[1:]:
            n *= s_
        nbytes = n * (4 if dt == F32 else 2)
        nbytes = (nbytes + 31) // 32 * 32
        off = st_["off"]
        assert off + nbytes <= ARENA_BYTES, ("SBUF arena overflow", off, nbytes)
        st_["off"] = off + nbytes
        v = arena[0:shape[0], off // 2:(off + nbytes) // 2]
        if dt == F32:
            v = v.bitcast(F32)
        v = v[:, 0:n]
        if len(shape) == 3:
            v = v.rearrange("p (a b) -> p a b", a=shape[1])
        elif len(shape) == 4:
            v = v.rearrange("p (a b c) -> p a b c", a=shape[1], b=shape[2])
        return v

    def mark():
        return st_["off"]

    def reset(m):
        st_["off"] = m

    psum = [nc.alloc_psum_tensor("ps%d" % i, [128, 512], F32).ap() for i in range(8)]
    B_ps = [Buf("ps%d" % i) for i in range(8)]
    for b_ in B_ps:
        b_.excl = True

    def mm(out, lhsT, rhs, start, stop, r, w):
        P.add("pe", lambda e: e.matmul(out, lhsT=lhsT, rhs=rhs, start=start, stop=stop), reads=r, writes=w)

    def tr(out, in_, ident, r, w):
        P.add("pe", lambda e: e.transpose(out=out, in_=in_, identity=ident), reads=r, writes=w)

    def act(out, in_, func, r, w, bias=None, scale=None, accum=None):
        kw = {}
        if bias is not None:
            kw["bias"] = bias
        if scale is not None:
            kw["scale"] = scale
        if accum is not None:
            kw["accum_out"] = accum
        P.add("act", lambda e: e.activation(out=out, in_=in_, func=func, **kw), reads=r, writes=w)

    def tt(eng, out, in0, in1, op, r, w):
        P.add(eng, lambda e: e.tensor_tensor(out=out, in0=in0, in1=in1, op=op), reads=r, writes=w)

    def ts(eng, out, in0, s1, op0, r, w, s2=None, op1=None):
        if op1 is None:
            P.add(eng, lambda e: e.tensor_scalar(out=out, in0=in0, scalar1=s1, scalar2=None, op0=op0), reads=r, writes=w)
        else:
            P.add(eng, lambda e: e.tensor_scalar(out=out, in0=in0, scalar1=s1, scalar2=s2, op0=op0, op1=op1), reads=r, writes=w)

    def stt(eng, out, in0, scalar, in1, op0, op1, r, w):
        P.add(eng, lambda e: e.scalar_tensor_tensor(out=out, in0=in0, scalar=scalar, in1=in1, op0=op0, op1=op1), reads=r, writes=w)

    def cp(eng, out, in_, r, w):
        if eng == "act":
            P.add("act", lambda e: e.activation(out=out, in_=in_, func=AF.Copy), reads=r, writes=w)
        else:
            P.add(eng, lambda e: e.tensor_copy(out=out, in_=in_), reads=r, writes=w)

    def memset(eng, out, val, w):
        P.add(eng, lambda e: e.memset(out, val), writes=w)

    def dma(out, in_, r, w, dest, eng="sp"):
        P.add(eng, lambda e: e.dma_start(out=out, in_=in_), reads=r, writes=w, dma_dest=dest)

    cf = sb([128, NCF]); B_cf = Buf("cf")
    pp = sb([128, NPP]); B_pp = Buf("pp")
    NCB = 128 * 5
    cb = sb([128, NCB], BF16); B_cb = Buf("cb")
    ID_F, ONES_F, U_F = cf[:, CF_ID:CF_ID + 128], cf[:, CF_ONES:CF_ONES + 128], cf[:, CF_U:CF_U + 128]
    NEGM4 = cf[:, CF_NEGM4:CF_NEGM4 + 512]
    EPSC, ONEC = cf[:, CF_EPS:CF_EPS + 1], cf[:, CF_ONE:CF_ONE + 1]
    ID_B, ONES_B, TRI_B, NEGTRI_B, ZERO_B = cb[:, 0:128], cb[:, 128:256], cb[:, 256:384], cb[:, 384:512], cb[:, 512:640]
    abc = sb([128, L * 8]); B_abc = Buf("abc")
    halo = sb([128, 6, 3]); B_halo = Buf("halo")
    kxT = [sb([128, 4, MEM], BF16) for _ in range(L)]; B_kx = [Buf("kx%d" % l) for l in range(L)]
    vx = [sb([128, 2, 512], BF16) for _ in range(L)]; B_vx = [Buf("vx%d" % l) for l in range(L)]
    PERSIST = mark()
    wbf = [sb([128, 4096], BF16) for _ in range(3)]; B_wbf = [Buf("wbf0"), Buf("wbf1"), Buf("wbf2")]
    wrot = Rot([0, 1, 2])
    WEND = mark()

    dma(cf, cf_d, [], [B_cf], B_cf)
    dma(pp, pp_d, [], [B_pp], B_pp)
    cp("dve", cb[:, 0:128], cf[:, CF_ID:CF_ID + 128], [B_cf], [B_cb])
    cp("dve", cb[:, 128:256], cf[:, CF_ONES:CF_ONES + 128], [B_cf], [B_cb])
    cp("dve", cb[:, 256:384], cf[:, CF_TRI:CF_TRI + 128], [B_cf], [B_cb])
    cp("dve", cb[:, 384:512], cf[:, CF_NEGTRI:CF_NEGTRI + 128], [B_cf], [B_cb])
    memset("pool", cb[:, 512:640], 0.0, [B_cb])
    for l in range(L):
        act(abc[:, l * 8:(l + 1) * 8], pp[:, l * PPW + PP_ALOG:l * PPW + PP_ALOG + 8], AF.Exp, [B_pp], [B_abc])
    ts("dve", abc, abc, -1.0, ALU.mult, [B_abc], [B_abc])

    def ppc(l, off, n):
        return pp[:, l * PPW + off:l * PPW + off + n]

    def load_w(src3, KC, N):
        i = wrot.next()
        dst = wbf[i][:, 0:KC * N].rearrange("p (k n) -> p k n", k=KC)
        for k in range(KC):
            dma(dst[:, k, :], src3[:, k, :], [], [B_wbf[i]], B_wbf[i], eng="pool")
        return dst, B_wbf[i]

    def wview(w2d, r0, KC, c0, N):
        return w2d[r0:r0 + KC * 128, c0:c0 + N].rearrange("(k p) n -> p k n", p=128)

    prot = Rot([0, 1, 2, 3])

    def proj_fm(w2d, r0, KC, c0, ncols, colblk, actT, Bact, ntiles, evac, tcol0=0):
        for cbi in range(ncols // colblk):
            wv, Bw = load_w(wview(w2d, r0, KC, c0 + cbi * colblk, colblk), KC, colblk)
            for jj in range(colblk // 128):
                for t in range(ntiles):
                    pi = prot.next()
                    for k in range(KC):
                        mm(psum[pi], wv[:, k, jj * 128:(jj + 1) * 128], actT[:, k, tcol0 + t * TT:tcol0 + (t + 1) * TT],
                           k == 0, k == KC - 1, [Bw, Bact], [B_ps[pi]])
                    evac(cbi * (colblk // 128) + jj, t, psum[pi], B_ps[pi])

    def rmsnorm_fm(hT, BhT, gcols, actT, Bact, sqb, B_sqb, rstd, B_rstd, nchunks=8, scale=1.0 / D):
        for t in range(NT):
            sl = slice(t * TT, (t + 1) * TT)
            for c in range(nchunks):
                act(sqb[:, c, :], hT[:, c, sl], AF.Square, [BhT.k((c, t))], [B_sqb.k(c)])
            pi = prot.next()
            for c in range(nchunks):
                mm(psum[pi], ONES_B, sqb[:, c, :], c == 0, c == nchunks - 1, [B_cb, B_sqb.k(c)], [B_ps[pi]])
            act(rstd, psum[pi], AF.Ln, [B_ps[pi], B_cf], [B_rstd], bias=EPSC, scale=scale)
            act(rstd, rstd, AF.Exp, [B_rstd], [B_rstd], scale=-0.5)
            for c in range(nchunks):
                stt("dve", actT[:, c, sl], hT[:, c, sl], gcols[:, c:c + 1], rstd, ALU.mult, ALU.mult,
                    [BhT.k((c, t)), B_pp, B_rstd], [Bact])

    m0 = mark()
    assert m0 == WEND
    mtok = sb([128, D]); B_mtok = Buf("mtok")
    mn = sb([128, D]); B_mn = Buf("mn")
    msc = sb([128, 4]); B_msc = Buf("msc")
    memnT = [sb([128, 8, MEM], BF16) for _ in range(L)]; B_memn = [Buf("memn%d" % l) for l in range(L)]
    for blk in range(2):
        dma(mtok, mem_d[blk * 128:(blk + 1) * 128, :], [], [B_mtok], B_mtok)
        memset("pool", msc[:, 0:1], 0.0, [B_msc])
        act(mn, mtok, AF.Square, [B_mtok, B_msc], [B_mn, B_msc], accum=msc[:, 0:1])
        act(msc[:, 1:2], msc[:, 0:1], AF.Ln, [B_msc, B_cf], [B_msc], bias=EPSC, scale=1.0 / D)
        act(msc[:, 2:3], msc[:, 1:2], AF.Exp, [B_msc], [B_msc], scale=-0.5)
        ts("dve", mn, mtok, msc[:, 2:3], ALU.mult, [B_mtok, B_msc], [B_mn])
        for half in range(2):
            for c4 in range(4):
                c = half * 4 + c4
                tr(psum[half][:, c4 * 128:(c4 + 1) * 128], mn[:, c * 128:(c + 1) * 128], ID_F, [B_mn, B_cf], [B_ps[half]])
            for l in range(L):
                g = ppc(l, PP_GMEM + half * 4, 4)
                tt("dve", memnT[l][:, half * 4:half * 4 + 4, blk * 128:(blk + 1) * 128],
                   psum[half].rearrange("p (a b) -> p a b", a=4), g.unsqueeze(2).to_broadcast([128, 4, 128]), ALU.mult,
                   [B_ps[half], B_pp], [B_memn[l]])
    for l in range(L):
        wv, Bw = load_w(wview(w_xk[l], 0, 8, 0, 512), 8, 512)
        for hx in range(4):
            pi = prot.next()
            for k in range(8):
                mm(psum[pi][:, 0:MEM], wv[:, k, hx * 128:(hx + 1) * 128], memnT[l][:, k, :], k == 0, k == 7, [Bw, B_memn[l]], [B_ps[pi]])
            cp("act", kxT[l][:, hx, :], psum[pi][:, 0:MEM], [B_ps[pi]], [B_kx[l]])
        wv, Bw = load_w(wview(w_xv[l], 0, 8, 0, 512), 8, 512)
        for mb in range(2):
            pi = prot.next()
            for k in range(8):
                mm(psum[pi], memnT[l][:, k, mb * 128:(mb + 1) * 128], wv[:, k, :], k == 0, k == 7, [Bw, B_memn[l]], [B_ps[pi]])
            cp("dve", vx[l][:, mb, :], psum[pi], [B_ps[pi]], [B_vx[l]])
    P.barrier()
    reset(m0)

    hT = sb([128, 8, TS]); B_hT = Buf("hT")
    actT = sb([128, 8, TS], BF16); B_actT = Buf("actT")
    SCRA_OFF = mark()
    scrA = sb([128, 4, TS], BF16); B_scrA = Buf("scrA")
    scrB = sb([128, 4, TS], BF16); B_scrB = Buf("scrB")
    sqb = sb([128, 8, TT], BF16); B_sqb = Buf("sqb")
    rstd = sb([128, TT]); B_rstd = Buf("rstd")
    evs = [sb([128, TT]) for _ in range(2)]; B_evs = [Buf("evs0"), Buf("evs1")]
    evrot = Rot([0, 1])
    evb = [sb([128, TT], BF16) for _ in range(2)]; B_evb = [Buf("evb0"), Buf("evb1")]
    evbrot = Rot([0, 1])
    xr = [sb([128, TT + 3]) for _ in range(2)]; B_xr = [Buf("xr0"), Buf("xr1")]
    cacc = sb([128, TT]); B_cacc = Buf("cacc")
    wdt = sb([128, 8, 8], BF16); B_wdt = Buf("wdt")
    wdtf = sb([128, 8, 8]); B_wdtf = Buf("wdtf")
    dtst = sb([128, 16, 8]); B_dtst = Buf("dtst")
    dtt = sb([128, 8]); B_dtt = Buf("dtt")
    TOKWISE_END = mark()
    eng_alt = Rot(["act", "dve"])

    def embed(st):
        xin = [evs[0], evs[1]]
        for blk in range(TS // 128):
            r0 = st * TS + blk * 128
            for half in range(2):
                i = evrot.next()
                dma(evs[i], x_d[r0:r0 + 128, half * 512:(half + 1) * 512], [], [B_evs[i]], B_evs[i])
                pi = prot.next()
                for c4 in range(4):
                    tr(psum[pi][:, c4 * 128:(c4 + 1) * 128], evs[i][:, c4 * 128:(c4 + 1) * 128], ID_F, [B_evs[i], B_cf], [B_ps[pi]])
                t = blk // 4
                cp(eng_alt.next(), hT[:, half * 4:half * 4 + 4, blk * 128:(blk + 1) * 128],
                   psum[pi].rearrange("p (a b) -> p a b", a=4), [B_ps[pi]],
                   [B_hT.k((half * 4 + c4, t)) for c4 in range(4)])

    def store_hT(st):
        for c in range(8):
            dma(HT[st, :, c, :], hT[:, c, :], [B_hT], [B_HT[st]], B_HT[st])

    def load_hT(st):
        for c in range(8):
            dma(hT[:, c, :], HT[st, :, c, :], [B_HT[st]], [B_hT], B_hT)

    def g1(l, st):
        tok0 = st * TS
        rmsnorm_fm(hT, B_hT, ppc(l, PP_GMIX, 8), actT, B_actT, sqb, B_sqb, rstd, B_rstd)
        wl = w_in[l]
        if st == 0:
            memset("pool", halo, 0.0, [B_halo])
        cw = ppc(l, PP_CW, 24)
        cbias = ppc(l, PP_CB, 6)
        cstate = {"n": 0}

        def conv_evac(base):
            def f(j, t, ps, Bp):
                cc = base + j
                n = cstate["n"]
                cstate["n"] += 1
                i = n % 2
                if t == 0:
                    cp("pool", xr[i][:, 0:3], halo[:, cc, :], [B_halo], [B_xr[i]])
                else:
                    cp("pool", xr[i][:, 0:3], xr[1 - i][:, TT:TT + 3], [B_xr[1 - i]], [B_xr[i]])
                cp("act", xr[i][:, 3:TT + 3], ps, [Bp], [B_xr[i]])
                if t == NT - 1:
                    cp("pool", halo[:, cc, :], xr[i][:, TT:TT + 3], [B_xr[i]], [B_halo])
                ts("dve", cacc, xr[i][:, 0:TT], cw[:, cc * 4:cc * 4 + 1], ALU.mult, [B_xr[i], B_pp], [B_cacc])
                for k in range(1, 4):
                    stt("dve", cacc, xr[i][:, k:k + TT], cw[:, cc * 4 + k:cc * 4 + k + 1], cacc, ALU.mult, ALU.add,
                        [B_xr[i], B_pp, B_cacc], [B_cacc])
                e = evrot.next()
                act(evs[e], cacc, AF.Silu, [B_cacc, B_pp], [B_evs[e]], bias=cbias[:, cc:cc + 1])
                sl = slice(tok0 + t * TT, tok0 + (t + 1) * TT)
                if cc < 4:
                    dma(XST[cc, :, sl], evs[e], [B_evs[e]], [B_XST], B_XST)
                else:
                    dma(BCT[cc - 4, :, sl], evs[e], [B_evs[e]], [B_BCT], B_BCT)
            return f

        proj_fm(wl, 0, 8, C_X, 512, 512, actT, B_actT, NT, conv_evac(0))
        proj_fm(wl, 0, 8, C_B, 256, 256, actT, B_actT, NT, conv_evac(4))

        def qk_evac(dst, Bdst):
            dflat = dst.rearrange("h d t -> (h d) t")

            def f(j, t, ps, Bp):
                e = evbrot.next()
                cp(eng_alt.next(), evb[e], ps, [Bp], [B_evb[e]])
                sl = slice(tok0 + t * TT, tok0 + (t + 1) * TT)
                dma(dflat[j * 128:(j + 1) * 128, sl], evb[e], [B_evb[e]], [Bdst], Bdst)
            return f

        proj_fm(wl, 0, 8, C_Q, 512, 512, actT, B_actT, NT, qk_evac(QT, B_QT))
        proj_fm(wl, 0, 8, C_K, 512, 512, actT, B_actT, NT, qk_evac(KT, B_KT))
        for (c0, dst, Bdst, isbf) in ((C_Z, ZTOK, B_ZTOK, False), (C_V, VS, B_VS, True)):
            wv, Bw = load_w(wview(wl, 0, 8, c0, 512), 8, 512)
            for blk in range(TS // 128):
                pi = prot.next()
                for k in range(8):
                    mm(psum[pi], actT[:, k, blk * 128:(blk + 1) * 128], wv[:, k, :], k == 0, k == 7, [Bw, B_actT], [B_ps[pi]])
                r0 = tok0 + blk * 128
                if isbf:
                    e = evbrot.next()
                    cp(eng_alt.next(), evb[e], psum[pi], [B_ps[pi]], [B_evb[e]])
                    dma(dst[r0:r0 + 128, :], evb[e], [B_evb[e]], [Bdst], Bdst)
                else:
                    e = evrot.next()
                    cp(eng_alt.next(), evs[e], psum[pi], [B_ps[pi]], [B_evs[e]])
                    dma(dst[r0:r0 + 128, :], evs[e], [B_evs[e]], [Bdst], Bdst)
        dma(wdtf, wview(wl, 0, 8, C_DT, 8), [], [B_wdtf], B_wdtf)
        cp("dve", wdt, wdtf, [B_wdtf], [B_wdt])
        dtb = ppc(l, PP_DTB, 8)
        pi = prot.next()
        NBLK = TS // 128
        for blk in range(NBLK):
            for k in range(8):
                mm(psum[pi][:, blk * 8:(blk + 1) * 8], actT[:, k, blk * 128:(blk + 1) * 128], wdt[:, k, :], k == 0, k == 7, [B_wdt, B_actT], [B_ps[pi]])
        tt("dve", dtst, psum[pi][:, 0:NBLK * 8].rearrange("p (b h) -> p b h", h=8), dtb.unsqueeze(1).to_broadcast([128, NBLK, 8]), ALU.add,
           [B_ps[pi], B_pp], [B_dtst])
        act(dtst, dtst, AF.Exp, [B_dtst], [B_dtst])
        act(dtst, dtst, AF.Ln, [B_dtst, B_cf], [B_dtst], bias=ONEC, scale=1.0)
        dma(DTOK[tok0:tok0 + TS, :].rearrange("(b p) h -> p b h", p=128), dtst, [B_dtst], [B_DTOK], B_DTOK)

    def add_evac(j, t, ps, Bp):
        sl = slice(t * TT, (t + 1) * TT)
        tt("dve", hT[:, j, sl], ps, hT[:, j, sl], ALU.add, [Bp, B_hT.k((j, t))], [B_hT.k((j, t))])

    def g3(l, st):
        tok0 = st * TS
        for c in range(8):
            dma(actT[:, c, :], YT[c * 128:(c + 1) * 128, tok0:tok0 + TS], [B_YT], [B_actT], B_actT)
        proj_fm(w_out[l], 0, 8, 0, D, 512, actT, B_actT, NT, add_evac)
        rmsnorm_fm(hT, B_hT, ppc(l, PP_GXA, 8), actT, B_actT, sqb, B_sqb, rstd, B_rstd)
        qxT, B_qx = scrA, B_scrA
        oxT, B_ox = scrB, B_scrB

        def q_evac(j, t, ps, Bp):
            cp(eng_alt.next(), qxT[:, j, t * TT:(t + 1) * TT], ps, [Bp], [B_qx.k((j, t))])
        proj_fm(w_xq[l], 0, 8, 0, 512, 512, actT, B_actT, NT, q_evac)
        sc = 1.0 / math.sqrt(128.0)
        for t in range(NT):
            sl = slice(t * TT, (t + 1) * TT)
            for hx in range(4):
                pT = []
                bk = (4, 5, 6, 7) if (t * 4 + hx) % 2 == 0 else (0, 1, 2, 3)
                for mb in range(2):
                    pi = bk[mb]
                    mm(psum[pi], kxT[l][:, hx, mb * 128:(mb + 1) * 128], qxT[:, hx, sl], True, True, [B_kx[l], B_qx.k((hx, t))], [B_ps[pi]])
                    e = evbrot.next()
                    act(evb[e], psum[pi], AF.Exp, [B_ps[pi]], [B_evb[e]], scale=sc)
                    pT.append(e)
                po, pd = bk[2], bk[3]
                for mb in range(2):
                    mm(psum[po], vx[l][:, mb, hx * 128:(hx + 1) * 128], evb[pT[mb]], mb == 0, mb == 1, [B_vx[l], B_evb[pT[mb]]], [B_ps[po]])
                for mb in range(2):
                    mm(psum[pd], ONES_B, evb[pT[mb]], mb == 0, mb == 1, [B_cb, B_evb[pT[mb]]], [B_ps[pd]])
                e = evrot.next()
                cp("act", evs[e], psum[pd], [B_ps[pd]], [B_evs[e]])
                P.add("dve", (lambda ee: (lambda en: en.reciprocal(out=evs[ee], in_=evs[ee])))(e), reads=[B_evs[e]], writes=[B_evs[e]])
                tt("dve", oxT[:, hx, sl], psum[po], evs[e], ALU.mult, [B_ps[po], B_evs[e]], [B_ox.k((hx, t))])
        proj_fm(w_xo[l], 0, 4, 0, D, 1024, oxT, B_ox, NT, add_evac)
        rmsnorm_fm(hT, B_hT, ppc(l, PP_GFF, 8), actT, B_actT, sqb, B_sqb, rstd, B_rstd)
        uT, B_u = scrA, B_scrA
        for fb in range(8):
            proj_fm(w_ff1[l], 0, 8, fb * 512, 512, 512, actT, B_actT, NT, u_evac_fix(uT, B_u))
            proj_fm(w_ff2[l], fb * 512, 4, 0, D, 1024, uT, B_u, NT, add_evac)

    def u_evac_fix(uT, B_u):
        def f(j, t, ps, Bp):
            e = evrot.next()
            act(evs[e], ps, AF.Relu, [Bp], [B_evs[e]])
            tt("pool" if (j + t) % 2 else "dve", uT[:, j, t * TT:(t + 1) * TT], evs[e], evs[e], ALU.mult, [B_evs[e]], [B_u.k((j, t))])
        return f

    def final(st):
        tok0 = st * TS
        for t in range(NT):
            sl = slice(t * TT, (t + 1) * TT)
            for c in range(8):
                act(sqb[:, c, :], hT[:, c, sl], AF.Square, [B_hT.k((c, t))], [B_sqb.k(c)])
            pi = prot.next()
            for c in range(8):
                mm(psum[pi], ONES_B, sqb[:, c, :], c == 0, c == 7, [B_cb, B_sqb.k(c)], [B_ps[pi]])
            act(rstd, psum[pi], AF.Ln, [B_ps[pi], B_cf], [B_rstd], bias=EPSC, scale=1.0 / D)
            act(rstd, rstd, AF.Exp, [B_rstd], [B_rstd], scale=-0.5)
            gf = pp[:, PP_FINAL:PP_FINAL + 8]
            for c in range(8):
                stt("dve", fin[:, c, :], hT[:, c, sl], gf[:, c:c + 1], rstd, ALU.mult, ALU.mult,
                    [B_hT.k((c, t)), B_pp, B_rstd], [B_fin.k(c)])
            for b4 in range(4):
                for half in range(2):
                    pi = 4 + half
                    for c4 in range(4):
                        c = half * 4 + c4
                        tr(psum[pi][:, c4 * 128:(c4 + 1) * 128], fin[:, c, b4 * 128:(b4 + 1) * 128], ID_F, [B_fin.k(c), B_cf], [B_ps[pi]])
                    e = evrot.next()
                    cp(eng_alt.next(), evs[e], psum[pi], [B_ps[pi]], [B_evs[e]])
                    r0 = tok0 + t * TT + b4 * 128
                    dma(out_d[r0:r0 + 128, half * 512:(half + 1) * 512], evs[e], [B_evs[e]], [B_OUT], B_OUT)

    fin = arena[:, SCRA_OFF // 2:SCRA_OFF // 2 + 8192].bitcast(F32).rearrange("p (a b) -> p a b", a=8); B_fin = B_scrA

    def ssd(l):
        m = mark()
        NG = TL // TT
        NCH = TL // 128
        xsT = [sb([128, 4, TT]) for _ in range(2)]; B_xsT = [Buf("xsT0"), Buf("xsT1")]
        bt128 = [sb([128, TT]) for _ in range(2)]; B_bt = [Buf("bt0"), Buf("bt1")]
        bc64 = [sb([64, 4, TT]) for _ in range(2)]; B_bc64 = [Buf("bc640"), Buf("bc641")]
        bc64b = [sb([64, 4, TT], BF16) for _ in range(2)]; B_bc64b = [Buf("bc64b0"), Buf("bc64b1")]
        ztok = [sb([128, 4, TT]) for _ in range(2)]; B_ztok = [Buf("ztok0"), Buf("ztok1")]
        dtg = [sb([128, 4, 8]) for _ in range(2)]; B_dtg = [Buf("dtg0"), Buf("dtg1")]
        yst = [sb([128, 4, TT], BF16) for _ in range(2)]; B_yst = [Buf("yst0"), Buf("yst1")]
        xs_tok2 = [sb([128, 512]) for _ in range(2)]; B_xs2 = [Buf("xs_tok0"), Buf("xs_tok1")]
        btok2 = [sb([128, 128], BF16) for _ in range(2)]; B_btok2 = [Buf("btok0"), Buf("btok1")]
        sm2 = [sb([128, 80]) for _ in range(2)]; B_sm2 = [Buf("sm0"), Buf("sm1")]
        MT2 = [sb([128, 8, 128], BF16) for _ in range(2)]; B_MT2 = [Buf("MT0"), Buf("MT1")]
        xdt2 = [sb([128, 512], BF16) for _ in range(2)]; B_xdt2 = [Buf("xdt0"), Buf("xdt1")]
        xw2 = [sb([128, 512], BF16) for _ in range(2)]; B_xw2 = [Buf("xw0"), Buf("xw1")]
        for i in range(2):
            memset("pool", sm2[i], 0.0, [B_sm2[i]])
        Rm = sb([128, 8, 128]); B_R = Buf("R")
        dec = sb([128, 8, 128]); B_dec = Buf("dec")
        t1 = sb([128, 512]); B_t1 = Buf("t1")
        t2 = sb([128, 512]); B_t2 = Buf("t2")
        yv2 = [sb([128, 512]) for _ in range(2)]; B_y2 = [Buf("y0"), Buf("y1")]
        gz = sb([128, 512]); B_gz = Buf("gz")
        sq = sb([128, 512]); B_sq = Buf("sq")
        ssq = sb([128, 4]); B_ssq = Buf("ssq")
        prev = sb([64, 8, 64]); B_prev = Buf("prev")
        prevb = [sb([64, 8, 64], BF16) for _ in range(2)]; B_prevb = [Buf("prevb0"), Buf("prevb1")]
        a_l = abc[:, l * 8:(l + 1) * 8]
        dsk = ppc(l, PP_DSK, 512)
        gssd = ppc(l, PP_GSSD, 4)
        memset("pool", prev, 0.0, [B_prev])
        memset("pool", prevb[0], 0.0, [B_prevb[0]])
        BCT64 = BCT.rearrange("a (g n) t -> n (a g) t", g=2)

        def loads(gi):
            s = gi % 2
            tsl = slice(gi * TT, (gi + 1) * TT)
            for j in range(4):
                dma(xsT[s][:, j, :], XST[j, :, tsl], [B_XST], [B_xsT[s]], B_xsT[s])
            dma(bt128[s], BCT[0, :, tsl], [B_BCT], [B_bt[s]], B_bt[s])
            for a_ in range(4):
                dma(bc64[s][:, a_, :], BCT64[:, a_, tsl], [B_BCT], [B_bc64[s]], B_bc64[s])
            dma(ztok[s], ZTOK[tsl, :].rearrange("(c p) f -> p c f", p=128), [B_ZTOK], [B_ztok[s]], B_ztok[s])
            dma(dtg[s], DTOK[tsl, :].rearrange("(c p) h -> p c h", p=128), [B_DTOK], [B_dtg[s]], B_dtg[s])
            cp("pool", bc64b[s], bc64[s], [B_bc64[s]], [B_bc64b[s]])

        def front(ci):
            gi, cg = ci // 4, ci % 4
            s = gi % 2
            p = ci % 2
            cs = slice(cg * 128, (cg + 1) * 128)
            xs_tok, B_xs = xs_tok2[p], B_xs2[p]
            btok, B_btok = btok2[p], B_btok2[p]
            sm, B_sm = sm2[p], B_sm2[p]
            MT, B_MT = MT2[p], B_MT2[p]
            xdt, B_xdt = xdt2[p], B_xdt2[p]
            xw, B_xw = xw2[p], B_xw2[p]
            da16 = sm[:, 0:16]; da = sm[:, 0:8]
            nacol, expA, dstate, cd, dtd, diff = [sm[:, 16 + i * 8:16 + (i + 1) * 8] for i in range(6)]
            for j in range(4):
                tr(psum[0][:, j * 128:(j + 1) * 128], xsT[s][:, j, cs], ID_F, [B_xsT[s], B_cf], [B_ps[0]])
            tr(psum[1][:, 0:128], bt128[s][:, cs], ID_F, [B_bt[s], B_cf], [B_ps[1].k("bt")])
            cp("act", xs_tok, psum[0], [B_ps[0]], [B_xs])
            cp("dve", btok, psum[1][:, 0:128], [B_ps[1].k("bt")], [B_btok])
            dtc = dtg[s][:, cg, :]
            tt("dve", da, dtc, a_l, ALU.mult, [B_dtg[s], B_abc], [B_sm.k("da")])
            mm(psum[1][:, 128:144], U_F, da16, True, True, [B_cf, B_sm.k("da")], [B_ps[1].k("ac")])
            mm(psum[1][:, 144:160], ONES_F, da16, True, True, [B_cf, B_sm.k("da")], [B_ps[1].k("ac")])
            ts("dve", nacol, psum[1][:, 128:136], -1.0, ALU.mult, [B_ps[1].k("ac")], [B_sm.k("nacol")])
            act(expA, psum[1][:, 128:136], AF.Exp, [B_ps[1].k("ac")], [B_sm.k("expA")])
            tt("dve", diff, psum[1][:, 144:152], nacol, ALU.add, [B_ps[1].k("ac"), B_sm.k("nacol")], [B_sm.k("diff")])
            act(dstate, diff, AF.Exp, [B_sm.k("diff")], [B_sm.k("dstate")])
            act(cd, psum[1][:, 144:152], AF.Exp, [B_ps[1].k("ac")], [B_sm.k("cd")])
            tt("dve", dtd, dtc, dstate, ALU.mult, [B_dtg[s], B_sm.k("dstate")], [B_sm.k("dtd")])
            tt("dve", Rm, U_F.unsqueeze(1).to_broadcast([128, 8, 128]), da.unsqueeze(2).to_broadcast([128, 8, 128]), ALU.mult,
               [B_cf, B_sm.k("da")], [B_R])
            for half in range(2):
                mm(psum[2 + half], ONES_F, Rm[:, half * 4:half * 4 + 4, :].rearrange("p a b -> p (a b)"), True, False, [B_cf, B_R], [B_ps[2 + half]])
                mm(psum[2 + half], ID_F, NEGM4, False, True, [B_cf], [B_ps[2 + half]])
            for h in range(8):
                act(dec[:, h, :], psum[2 + h // 4][:, (h % 4) * 128:(h % 4 + 1) * 128], AF.Exp,
                    [B_ps[2 + h // 4], B_sm.k("nacol")], [B_dec.k(h // 4)], bias=nacol[:, h:h + 1], scale=1.0)
            for g in range(2):
                mm(psum[1][:, 256 + g * 128:256 + (g + 1) * 128], bc64b[s][:, g, cs], bc64b[s][:, 2 + g, cs], True, True,
                   [B_bc64b[s]], [B_ps[1].k("cb%d" % g)])
                tt("dve", MT[:, g * 4:g * 4 + 4, :], psum[1][:, 256 + g * 128:256 + (g + 1) * 128].unsqueeze(1).to_broadcast([128, 4, 128]),
                   dec[:, g * 4:g * 4 + 4, :], ALU.mult, [B_ps[1].k("cb%d" % g), B_dec.k(g)], [B_MT.k(g)])
            xs3 = xs_tok.rearrange("p (h j) -> p h j", h=8)
            tt("dve", xdt.rearrange("p (h j) -> p h j", h=8), xs3, dtc.unsqueeze(2).to_broadcast([128, 8, 64]), ALU.mult,
               [B_xs, B_dtg[s]], [B_xdt])
            tt("pool", xw.rearrange("p (h j) -> p h j", h=8), xs3, dtd.unsqueeze(2).to_broadcast([128, 8, 64]), ALU.mult,
               [B_xs, B_sm.k("dtd")], [B_xw])

        def back(ci):
            gi, cg = ci // 4, ci % 4
            s = gi % 2
            p = ci % 2
            cs = slice(cg * 128, (cg + 1) * 128)
            tsl = slice(gi * TT, (gi + 1) * TT)
            xs_tok, B_xs = xs_tok2[p], B_xs2[p]
            btok, B_btok = btok2[p], B_btok2[p]
            sm, B_sm = sm2[p], B_sm2[p]
            MT, B_MT = MT2[p], B_MT2[p]
            xdt, B_xdt = xdt2[p], B_xdt2[p]
            xw, B_xw = xw2[p], B_xw2[p]
            nacol, expA, dstate, cd, dtd, diff = [sm[:, 16 + i * 8:16 + (i + 1) * 8] for i in range(6)]
            pb_cur = prevb[ci % 2]; Bpb_cur = B_prevb[ci % 2]
            pb_nxt = prevb[(ci + 1) % 2]; Bpb_nxt = B_prevb[(ci + 1) % 2]
            yv, B_y = yv2[p], B_y2[p]
            for h in range(8):
                mm(psum[4][:, h * 64:(h + 1) * 64], MT[:, h, :], xdt[:, h * 64:(h + 1) * 64], True, True, [B_MT.k(h // 4), B_xdt], [B_ps[4]])
            for h in range(8):
                mm(psum[5][:, h * 64:(h + 1) * 64], bc64b[s][:, 2 + h // 4, cs], pb_cur[:, h, :], True, True, [B_bc64b[s], Bpb_cur], [B_ps[5]])
            for g in range(2):
                mm(psum[6][0:64, g * 256:(g + 1) * 256], btok[:, g * 64:(g + 1) * 64], xw[:, g * 256:(g + 1) * 256], True, True,
                   [B_btok, B_xw], [B_ps[6]])
            tt("pool", prev, prev, cd[0:64, :].unsqueeze(2).to_broadcast([64, 8, 64]), ALU.mult, [B_prev, B_sm.k("cd")], [B_prev])
            tt("dve", prev, psum[6][0:64, :].rearrange("p (h j) -> p h j", h=8), prev, ALU.add, [B_ps[6], B_prev], [B_prev])
            cp("pool", pb_nxt, prev, [B_prev], [Bpb_nxt])
            tt("dve", t1.rearrange("p (h j) -> p h j", h=8), psum[5].rearrange("p (h j) -> p h j", h=8),
               expA.unsqueeze(2).to_broadcast([128, 8, 64]), ALU.mult, [B_ps[5], B_sm.k("expA")], [B_t1])
            tt("pool", t2, xs_tok, dsk, ALU.mult, [B_xs, B_pp], [B_t2])
            tt("pool", t2, t2, t1, ALU.add, [B_t2, B_t1], [B_t2])
            tt("dve", yv, psum[4], t2, ALU.add, [B_ps[4], B_t2], [B_y])

        def back2(ci):
            gi, cg = ci // 4, ci % 4
            s = gi % 2
            p = ci % 2
            cs = slice(cg * 128, (cg + 1) * 128)
            tsl = slice(gi * TT, (gi + 1) * TT)
            yv, B_y = yv2[p], B_y2[p]
            act(gz, ztok[s][:, cg, :], AF.Silu, [B_ztok[s]], [B_gz])
            tt("pool", yv, yv, gz, ALU.mult, [B_y, B_gz], [B_y])
            memset("pool", ssq[:, 0:1], 0.0, [B_ssq])
            act(sq, yv, AF.Square, [B_y, B_ssq], [B_sq, B_ssq], accum=ssq[:, 0:1])
            act(ssq[:, 1:2], ssq[:, 0:1], AF.Ln, [B_ssq, B_cf], [B_ssq], bias=EPSC, scale=1.0 / 512)
            act(ssq[:, 2:3], ssq[:, 1:2], AF.Exp, [B_ssq], [B_ssq], scale=-0.5)
            ts("dve", sq, yv, ssq[:, 2:3], ALU.mult, [B_y, B_ssq], [B_sq])
            for j in range(4):
                tr(psum[7][:, j * 128:(j + 1) * 128], sq[:, j * 128:(j + 1) * 128], ID_F, [B_sq, B_cf], [B_ps[7]])
            tt("dve", yst[s][:, :, cs], psum[7].rearrange("p (a b) -> p a b", a=4), gssd.unsqueeze(2).to_broadcast([128, 4, 128]), ALU.mult,
               [B_ps[7], B_pp], [B_yst[s]])
            if cg == 3:
                dma(YT[0:512, tsl].rearrange("(j p) t -> p j t", p=128), yst[s], [B_yst[s]], [B_YT], B_YT)

        for ci in range(NCH + 1):
            if ci < NCH and ci % 4 == 0:
                loads(ci // 4)
            fns = []
            if ci < NCH:
                fns.append((lambda c: (lambda: front(c)))(ci))
            if ci >= 1:
                fns.append((lambda c: (lambda: (back(c), back2(c))))(ci - 1))
            interleave(P, *fns)
        P.barrier()
        reset(m)

    def sba(l):
        m = mark()
        kt_all = sb([128, 8, TL], BF16); B_kt = Buf("kt_all")
        v_all = sb([128, TL // 128, 576], BF16); B_v = Buf("v_all")
        qg = [sb([128, 8, TT], BF16) for _ in range(2)]; B_qg = [Buf("qg0"), Buf("qg1")]
        e_sb = [sb([128, TT]) for _ in range(3)]; B_e = [Buf("e%d" % i) for i in range(3)]
        sp_b = [sb([128, TT], BF16) for _ in range(2)]; B_sp = [Buf("sp%d" % i) for i in range(2)]
        r_sb = [sb([128, TT]) for _ in range(2)]; B_r = [Buf("r%d" % i) for i in range(2)]
        w_b = [sb([128, TT], BF16) for _ in range(2)]; B_w = [Buf("w%d" % i) for i in range(2)]
        acc = [sb([128, TT], BF16) for _ in range(2)]; B_acc = [Buf("acc%d" % i) for i in range(2)]
        o_sb = sb([64, 8, TT]); B_o = Buf("o_sb")
        osq = sb([64, 8, TT], BF16); B_osq = Buf("osq")
        rs = sb([128, TT]); B_rs = Buf("rs")
        yst1 = sb([64, 8, TT], BF16); yst = [yst1, yst1]; B_y1 = Buf("ysb"); B_yst = [B_y1, B_y1]
        gsb = ppc(l, PP_GSB, 8)
        memset("pool", v_all[:, :, 512:576], 0.0, [B_v])
        for h in range(8):
            dma(kt_all[0:64, h, :], KT[h, :, :], [B_KT], [B_kt], B_kt)
            dma(kt_all[64:128, h, :], KT[h, :, :], [B_KT], [B_kt], B_kt)
        for q4 in range(TL // 1024):
            bs = slice(q4 * 8, (q4 + 1) * 8)
            dma(v_all[:, bs, 0:512], VS[q4 * 1024:(q4 + 1) * 1024, :].rearrange("(b p) f -> p b f", p=128), [B_VS], [B_v], B_v)
        tiles = []
        for G in range(min(TL // TT, SBA_MAXG)):
            for h in range(8):
                kbs = list(range(4 * G + 3, -1, -1))
                for ii, kb in enumerate(kbs):
                    tiles.append((G, h, kb, ii == 0, ii == len(kbs) - 1))
        n = len(tiles)
        Z_PS = [0, 1]; R_PS = [2, 3]; O_PS = [4, 5]; SS_PS = 6

        def stage_q(i):
            G, h, kb, first, last = tiles[i]
            s = G % 2
            if h == 0 and first:
                dma(qg[s][0:64], QT[:, :, G * TT:(G + 1) * TT].rearrange("h d t -> d h t"), [B_QT], [B_qg[s]], B_qg[s])
                dma(qg[s][64:128], QT[:, :, G * TT:(G + 1) * TT].rearrange("h d t -> d h t"), [B_QT], [B_qg[s]], B_qg[s])
            j = kb - 4 * G
            c0 = 128 * max(j, 0)
            cs = slice(c0, TT)
            zi = Z_PS[i % 2]
            mm(psum[zi][:, cs], kt_all[:, h, kb * 128:(kb + 1) * 128], qg[s][:, h, cs], True, True, [B_kt, B_qg[s]], [B_ps[zi]])
            if j >= 0:
                mm(psum[zi][:, c0:c0 + 128], ID_B, NEGTRI_B, False, True, [B_cb], [B_ps[zi]])
            for _ in range(SBA_DUMMY):
                mm(psum[7], ID_B, cb[:, 0:512], True, True, [B_cb], [B_ps[7]])

        def stage_a(i):
            G, h, kb, first, last = tiles[i]
            s = G % 2
            j = kb - 4 * G
            c0 = 128 * max(j, 0)
            cs = slice(c0, TT)
            zi = Z_PS[i % 2]; ri = R_PS[i % 2]
            ei = i % 3; si = i % 2
            ai = (G * 8 + h) % 2
            act(e_sb[ei][:, cs], psum[zi][:, cs], AF.Exp, [B_ps[zi]], [B_e[ei]], scale=0.0625)
            act(sp_b[si][:, cs], e_sb[ei][:, cs], AF.Ln, [B_e[ei], B_cf], [B_sp[si]], bias=ONEC, scale=1.0)
            mm(psum[ri][:, cs], TRI_B, sp_b[si][:, cs], True, first, [B_cb, B_sp[si]], [B_ps[ri]])
            if not first:
                mm(psum[ri][:, cs], ONES_B, acc[ai][:, cs], False, True, [B_cb, B_acc[ai]], [B_ps[ri]])
            if first:
                memset("pool", acc[ai], 0.0, [B_acc[ai]])
            if not last:
                tt("pool", acc[ai][:, cs], acc[ai][:, cs], sp_b[si][:, cs], ALU.add, [B_acc[ai], B_sp[si]], [B_acc[ai]])

        def stage_b(i):
            G, h, kb, first, last = tiles[i]
            s = G % 2
            j = kb - 4 * G
            c0 = 128 * max(j, 0)
            cs = slice(c0, TT)
            ri = R_PS[i % 2]
            ei = i % 3; si = i % 2
            oi = O_PS[(G * 8 + h) % 2]
            if SBA_LEVEL < 3:
                return
            act(r_sb[si][:, cs], psum[ri][:, cs], AF.Exp, [B_ps[ri]], [B_r[si]], scale=-1.0)
            tt("dve", w_b[si][:, cs], e_sb[ei][:, cs], r_sb[si][:, cs], ALU.mult, [B_e[ei], B_r[si]], [B_w[si]])
            if first:
                for q4 in range(4):
                    mm(psum[oi][:, q4 * 128:(q4 + 1) * 128], ZERO_B, ONES_B, True, False, [B_cb], [B_ps[oi]])
            mm(psum[oi][:, cs], v_all[:, kb, h * 64:h * 64 + 128], w_b[si][:, cs], False, last, [B_v, B_w[si]], [B_ps[oi]])
            if last and SBA_LEVEL >= 4:
                cp("dve", o_sb[:, h, :], psum[oi][0:64, :], [B_ps[oi]], [B_o.k(h)])
                act(osq[:, h, :], psum[oi][0:64, :], AF.Square, [B_ps[oi]], [B_osq.k(h)])
                if h == 7 and SBA_LEVEL >= 5:
                    for hh in range(8):
                        mm(psum[SS_PS], ONES_B[0:64, :], osq[:, hh, :], hh == 0, hh == 7, [B_cb, B_osq.k(hh)], [B_ps[SS_PS]])
                    act(rs, psum[SS_PS], AF.Ln, [B_ps[SS_PS], B_cf], [B_rs], bias=EPSC, scale=1.0 / 512)
                    act(rs, rs, AF.Exp, [B_rs], [B_rs], scale=-0.5)
                    for hh in range(8):
                        stt("dve", yst[s][:, hh, :], o_sb[:, hh, :], gsb[0:64, hh:hh + 1], rs[0:64, :], ALU.mult, ALU.mult,
                            [B_o.k(hh), B_pp, B_rs], [B_yst[s]])
                    dma(YT[512:1024, G * TT:(G + 1) * TT].rearrange("(h d) t -> d h t", d=64), yst[s], [B_yst[s]], [B_YT], B_YT)

        for i in range(n + 2):
            if i < n:
                stage_q(i)
            if 1 <= i <= n:
                stage_a(i - 1)
            if i >= 2:
                stage_b(i - 2)
        P.barrier()
        reset(m)

    for st in range(NST):
        embed(st)
        if NST > 1:
            store_hT(st)
        g1(0, st)
    P.barrier()
    done = False
    for l in range(n_layers):
        if stop == "g1":
            break
        reset(PERSIST)
        ssd(l)
        if stop == "ssd":
            break
        sba(l)
        if stop == "g2":
            break
        for st in range(NST):
            if NST > 1:
                load_hT(st)
            g3(l, st)
            if l + 1 < n_layers:
                if NST > 1:
                    store_hT(st)
                g1(l + 1, st)
            else:
                final(st)
                done = True
        P.barrier()
    dumps = {}
    if dump:
        alld = (("QT", QT, B_QT), ("KT", KT, B_KT), ("VS", VS, B_VS), ("ZTOK", ZTOK, B_ZTOK), ("XST", XST, B_XST),
                ("BCT", BCT, B_BCT), ("DTOK", DTOK, B_DTOK), ("YT", YT, B_YT), ("HT", HT, None))
        for name, ap_, Bf in [d_ for d_ in alld if dump is True or d_[0] in dump]:
            o = nc.dram_tensor("dump_" + name, list(ap_.shape), ap_.dtype, kind="ExternalOutput").ap()
            db = Buf("dump_" + name)
            rd = [Bf] if Bf is not None else list(B_HT)
            if len(ap_.shape) == 4:
                for s_ in range(ap_.shape[0]):
                    dma(o[s_].rearrange("p c t -> p (c t)"), ap_[s_].rearrange("p c t -> p (c t)"), rd, [db], db)
            elif len(ap_.shape) == 3:
                for s_ in range(ap_.shape[0]):
                    dma(o[s_], ap_[s_], rd, [db], db)
            else:
                dma(o, ap_, rd, [db], db)
    P.barrier()
    P.emit()
    return nc


_CACHE = {}


def kernel(**inputs):
    p = {k: np.asarray(v) for k, v in inputs.items()}
    if "nc" not in _CACHE:
        _CACHE["nc"] = build()
    nc = _CACHE["nc"]
    cf = host_consts()
    pp = host_params(p)
    shared = {k: np.ascontiguousarray(p[k], dtype=np.float32) for k in ("w_in", "w_out", "w_xq", "w_xk", "w_xv", "w_xo", "w_ff1", "w_ff2")}
    in_maps = []
    for c in range(8):
        b = c % 4
        m = {"x": np.ascontiguousarray(p["x"][b], dtype=np.float32), "mem": np.ascontiguousarray(p["mem"][b], dtype=np.float32),
             "cf": cf, "pp": pp}
        m.update(shared)
        in_maps.append(m)
    res = run_bass_kernel_spmd(nc, in_maps, core_ids=list(range(8)))
    out = np.stack([np.asarray(res.results[b]["out"], dtype=np.float32) for b in range(4)], axis=0)
    return out
```

```python
import math
import contextlib
import numpy as np
import concourse.bass as bass
import concourse.mybir as mybir
from concourse.bass_utils import run_bass_kernel_spmd

F32 = mybir.dt.float32
BF16 = mybir.dt.bfloat16
AF = mybir.ActivationFunctionType
ALU = mybir.AluOpType

ENGS = ("pe", "act", "dve", "pool", "sp")


class Buf:
    def __init__(self, name):
        self.name = name
        self.st = {"*": [[], []]}
        self.sem = None
        self.ndma = 0
        self.excl = False
        self.last_by_eng = {}

    def k(self, key):
        return (self, key)


class Op:
    __slots__ = ("eng", "fn", "waits", "idx", "is_dma", "dest")


class Prog:
    def __init__(self, nc):
        self.nc = nc
        self.ops = {e: [] for e in ENGS}
        self.dma_bufs = []
        self.nops = 0

    @staticmethod
    def _norm(x):
        if isinstance(x, Buf):
            return (x, None)
        return x

    def _entries(self, buf, key):
        d = buf.st
        if key is None:
            return list(d.values())
        if key not in d:
            d[key] = [list(d["*"][0]), list(d["*"][1])]
        return [d[key]]

    def add(self, eng, fn, reads=(), writes=(), dma_dest=None):
        if getattr(self, "capture", None) is not None:
            self.capture.append((eng, fn, reads, writes, dma_dest))
            return None
        op = Op()
        op.eng = eng
        op.fn = fn
        op.is_dma = dma_dest is not None
        op.dest = dma_dest
        deps = []
        reads = [self._norm(r) for r in reads if r is not None]
        writes = [self._norm(w) for w in writes if w is not None]
        for (b, key) in reads:
            for ent in self._entries(b, key):
                deps.extend(ent[0])
        for (b, key) in writes:
            for ent in self._entries(b, key):
                samegen = op.is_dma and len(ent[0]) > 0 and all(w.is_dma for w in ent[0]) and len(ent[1]) == 0
                if not samegen:
                    deps.extend(ent[0])
                deps.extend(ent[1])
        for (b, key) in list(reads) + list(writes):
            if b.excl:
                for e2, y in b.last_by_eng.items():
                    if e2 != eng:
                        deps.append(y)
                b.last_by_eng[eng] = op
        for (b, key) in writes:
            for ent in self._entries(b, key):
                samegen = op.is_dma and len(ent[0]) > 0 and all(w.is_dma for w in ent[0]) and len(ent[1]) == 0
                if samegen:
                    ent[0].append(op)
                else:
                    ent[0] = [op]
                    ent[1] = []
            if key is None:
                for kk in list(b.st.keys()):
                    b.st[kk][0] = list(b.st["*"][0])
                    b.st[kk][1] = []
        wset = set((id(b), key) for (b, key) in writes)
        for (b, key) in reads:
            if (id(b), key) in wset:
                continue
            for ent in self._entries(b, key):
                ent[1].append(op)
        if op.is_dma:
            d = dma_dest
            if d.sem is None:
                self.dma_bufs.append(d)
                d.sem = True
            d.ndma += 1
        lst = self.ops[eng]
        lst.append(op)
        op.idx = len(lst)
        waits = {}
        for y in deps:
            if y is op:
                continue
            if y.is_dma:
                key = ("d", id(y.dest))
                val = 16 * y.dest.ndma if y.dest is not dma_dest else 16 * (y.dest.ndma - 1)
                if val <= 0:
                    continue
                ent = (y.dest, val)
            else:
                if y.eng == eng and eng == "pe":
                    continue
                key = ("e", y.eng)
                val = y.idx
                ent = (y.eng, val)
            if key not in waits or waits[key][1] < val:
                waits[key] = ent
        op.waits = waits
        self.nops += 1
        return op

    def barrier(self):
        snap_e = {e: len(self.ops[e]) for e in ENGS}
        snap_d = [(d, 16 * d.ndma) for d in self.dma_bufs]
        for e in ENGS:
            op = Op()
            op.eng = e
            op.fn = None
            op.is_dma = False
            op.dest = None
            w = {}
            for e2 in ENGS:
                if e2 == e:
                    continue
                w[("e", e2)] = (e2, snap_e[e2])
            for (d, v) in snap_d:
                if v > 0:
                    w[("d", id(d))] = (d, v)
            op.waits = w
            lst = self.ops[e]
            lst.append(op)
            op.idx = len(lst)

    def emit(self):
        nc = self.nc
        with contextlib.ExitStack() as es:
            esem = {e: es.enter_context(nc.semaphore("es_" + e)) for e in ENGS}
            for i, d in enumerate(self.dma_bufs):
                d.sem = es.enter_context(nc.semaphore("ds%d_%s" % (i, d.name)))
            block = es.enter_context(nc.Block())
            cidx = {}
            for e in ENGS:
                c = 0
                m = [0]
                for op in self.ops[e]:
                    if not op.is_dma and op.fn is not None:
                        c += 1
                    m.append(c)
                cidx[e] = m

            def make(e):
                ops = self.ops[e]

                def body(eng):
                    seen = {}
                    for op in ops:
                        for key, (obj, val) in op.waits.items():
                            if key[0] == "e":
                                sem = esem[obj]
                                v = cidx[obj][val]
                            else:
                                sem = obj.sem
                                v = val
                            if v <= 0 or seen.get(key, 0) >= v:
                                continue
                            seen[key] = v
                            eng.wait_ge(sem, v)
                        if op.fn is None:
                            continue
                        ins = op.fn(eng)
                        if op.is_dma:
                            ins.then_inc(op.dest.sem, 16)
                        else:
                            ins.then_inc(esem[e], 1)
                return body

            block.tensor(make("pe"))
            block.scalar(make("act"))
            block.vector(make("dve"))
            block.gpsimd(make("pool"))
            block.sync(make("sp"))


def interleave(P, *fns):
    lists = []
    for f in fns:
        P.capture = lst = []
        f()
        lists.append(lst)
    P.capture = None
    pos = [0] * len(lists)
    total = sum(len(l) for l in lists)
    for _ in range(total):
        best, bi = None, -1
        for i, l in enumerate(lists):
            if pos[i] < len(l):
                frac = pos[i] / len(l)
                if best is None or frac < best:
                    best, bi = frac, i
        P.add(*lists[bi][pos[bi]])
        pos[bi] += 1


class Rot:
    def __init__(self, items):
        self.items = items
        self.i = 0

    def next(self):
        it = self.items[self.i % len(self.items)]
        self.i += 1
        return it


D = 1024
TL = 4096
TS = 2048
NST = TL // TS
TT = 512
NT = TS // TT
L = 2
MEM = 256
IN_DIM = 2824
C_Z, C_X, C_B, C_C, C_DT, C_Q, C_K, C_V = 0, 512, 1024, 1152, 1280, 1288, 1800, 2312
EPS = 1e-5
NEG = -30000.0
SBA_MAXG = 99
SBA_LEVEL = 9
SBA_DUMMY = 1

CF_ID, CF_ONES, CF_U, CF_NEGM4, CF_TRI, CF_NEGTRI, CF_EPS, CF_ONE, NCF = 0, 128, 256, 384, 896, 1024, 1152, 1153, 1160
PP_GMIX, PP_GXA, PP_GFF, PP_GMEM, PP_GSSD, PP_GSB, PP_CW, PP_CB, PP_DTB, PP_ALOG, PP_DSK, PPW = 0, 8, 16, 24, 32, 36, 44, 68, 74, 82, 90, 602
PP_FINAL = L * PPW
NPP = PP_FINAL + 8


def host_consts():
    cf = np.zeros((128, NCF), np.float32)
    i = np.arange(128)
    cf[:, CF_ID:CF_ID + 128] = np.eye(128, dtype=np.float32)
    cf[:, CF_ONES:CF_ONES + 128] = 1.0
    cf[:, CF_U:CF_U + 128] = (i[:, None] <= i[None, :]).astype(np.float32)
    negm = np.where(i[None, :] >= i[:, None], 0.0, NEG).astype(np.float32)
    cf[:, CF_NEGM4:CF_NEGM4 + 512] = np.tile(negm, (1, 4))
    cf[:, CF_TRI:CF_TRI + 128] = (i[:, None] >= i[None, :]).astype(np.float32)
    cf[:, CF_NEGTRI:CF_NEGTRI + 128] = np.where(i[:, None] < i[None, :], 0.0, 8 * NEG)
    cf[:, CF_EPS] = EPS
    cf[:, CF_ONE] = 1.0
    return cf


def host_params(p):
    pp = np.zeros((128, NPP), np.float32)
    col = lambda v, n: np.ascontiguousarray(np.asarray(v, np.float32).reshape(n, 128).T)
    for l in range(L):
        o = l * PPW
        pp[:, o + PP_GMIX:o + PP_GMIX + 8] = col(p["norm_mix_g"][l], 8)
        pp[:, o + PP_GXA:o + PP_GXA + 8] = col(p["norm_xa_g"][l], 8)
        pp[:, o + PP_GFF:o + PP_GFF + 8] = col(p["norm_ff_g"][l], 8)
        pp[:, o + PP_GMEM:o + PP_GMEM + 8] = col(p["norm_mem_g"][l], 8)
        pp[:, o + PP_GSSD:o + PP_GSSD + 4] = col(p["ssd_norm_g"][l], 4)
        pp[0:64, o + PP_GSB:o + PP_GSB + 8] = np.asarray(p["sb_norm_g"][l], np.float32).reshape(8, 64).T
        cw = np.asarray(p["conv_w"][l], np.float32)
        pp[:, o + PP_CW:o + PP_CW + 24] = cw.reshape(4, 6, 128).transpose(2, 1, 0).reshape(128, 24)
        pp[:, o + PP_CB:o + PP_CB + 6] = col(p["conv_b"][l], 6)
        pp[:, o + PP_DTB:o + PP_DTB + 8] = np.asarray(p["dt_bias"][l], np.float32)[None, :]
        pp[:, o + PP_ALOG:o + PP_ALOG + 8] = np.asarray(p["a_log"][l], np.float32)[None, :]
        pp[:, o + PP_DSK:o + PP_DSK + 512] = np.repeat(np.asarray(p["d_skip"][l], np.float32), 64)[None, :]
    pp[:, PP_FINAL:PP_FINAL + 8] = col(p["final_g"], 8)
    return pp


def build(n_layers=L, stop=None, dump=False):
    nc = bass.Bass("TRN2", target_bir_lowering=False)
    P = Prog(nc)

    def din(name, shape, dt=F32):
        return nc.dram_tensor(name, shape, dt, kind="ExternalInput").ap()

    x_d = din("x", [TL, D])
    mem_d = din("mem", [MEM, D])
    w_in = din("w_in", [L, D, IN_DIM])
    w_out = din("w_out", [L, D, D])
    w_xq = din("w_xq", [L, D, 512])
    w_xk = din("w_xk", [L, D, 512])
    w_xv = din("w_xv", [L, D, 512])
    w_xo = din("w_xo", [L, 512, D])
    w_ff1 = din("w_ff1", [L, D, 4096])
    w_ff2 = din("w_ff2", [L, 4096, D])
    cf_d = din("cf", [128, NCF])
    pp_d = din("pp", [128, NPP])
    out_d = nc.dram_tensor("out", [TL, D], F32, kind="ExternalOutput").ap()

    def dscr(name, shape, dt):
        return nc.dram_tensor(name, shape, dt).ap()

    HT = dscr("HT", [NST, 128, 8, TS], F32)
    QT = dscr("QT", [8, 64, TL], BF16)
    KT = dscr("KT", [8, 64, TL], BF16)
    VS = dscr("VS", [TL, 512], BF16)
    ZTOK = dscr("ZTOK", [TL, 512], F32)
    XST = dscr("XST", [4, 128, TL], F32)
    BCT = dscr("BCT", [2, 128, TL], F32)
    DTOK = dscr("DTOK", [TL, 8], F32)
    YT = dscr("YT", [D, TL], BF16)
    B_HT = [Buf("HT%d" % s) for s in range(NST)]
    B_QT, B_KT, B_VS, B_ZTOK, B_XST, B_BCT, B_DTOK, B_YT, B_OUT = [Buf(n) for n in "QT KT VS ZTOK XST BCT DTOK YT OUT".split()]

    ARENA_BYTES = 206 * 1024
    arena = nc.alloc_sbuf_tensor("arena", [128, ARENA_BYTES // 2], BF16).ap()
    st_ = {"off": 0}

    def sb(shape, dt=F32, parts=128):
        n = 1
        for s_ in shape[1:]:
            n *= s_
        nbytes = n * (4 if dt == F32 else 2)
        nbytes = (nbytes + 31) // 32 * 32
        off = st_["off"]
        assert off + nbytes <= ARENA_BYTES, ("SBUF arena overflow", off, nbytes)
        st_["off"] = off + nbytes
        v = arena[0:shape[0], off // 2:(off + nbytes) // 2]
        if dt == F32:
            v = v.bitcast(F32)
        v = v[:, 0:n]
        if len(shape) == 3:
            v = v.rearrange("p (a b) -> p a b", a=shape[1])
        elif len(shape) == 4:
            v = v.rearrange("p (a b c) -> p a b c", a=shape[1], b=shape[2])
        return v

    def mark():
        return st_["off"]

    def reset(m):
        st_["off"] = m

    psum = [nc.alloc_psum_tensor("ps%d" % i, [128, 512], F32).ap() for i in range(8)]
    B_ps = [Buf("ps%d" % i) for i in range(8)]
    for b_ in B_ps:
        b_.excl = True

    def mm(out, lhsT, rhs, start, stop, r, w):
        P.add("pe", lambda e: e.matmul(out, lhsT=lhsT, rhs=rhs, start=start, stop=stop), reads=r, writes=w)

    def tr(out, in_, ident, r, w):
        P.add("pe", lambda e: e.transpose(out=out, in_=in_, identity=ident), reads=r, writes=w)

    def act(out, in_, func, r, w, bias=None, scale=None, accum=None):
        kw = {}
        if bias is not None:
            kw["bias"] = bias
        if scale is not None:
            kw["scale"] = scale
        if accum is not None:
            kw["accum_out"] = accum
        P.add("act", lambda e: e.activation(out=out, in_=in_, func=func, **kw), reads=r, writes=w)

    def tt(eng, out, in0, in1, op, r, w):
        P.add(eng, lambda e: e.tensor_tensor(out=out, in0=in0, in1=in1, op=op), reads=r, writes=w)

    def ts(eng, out, in0, s1, op0, r, w, s2=None, op1=None):
        if op1 is None:
            P.add(eng, lambda e: e.tensor_scalar(out=out, in0=in0, scalar1=s1, scalar2=None, op0=op0), reads=r, writes=w)
        else:
            P.add(eng, lambda e: e.tensor_scalar(out=out, in0=in0, scalar1=s1, scalar2=s2, op0=op0, op1=op1), reads=r, writes=w)

    def stt(eng, out, in0, scalar, in1, op0, op1, r, w):
        P.add(eng, lambda e: e.scalar_tensor_tensor(out=out, in0=in0, scalar=scalar, in1=in1, op0=op0, op1=op1), reads=r, writes=w)

    def cp(eng, out, in_, r, w):
        if eng == "act":
            P.add("act", lambda e: e.activation(out=out, in_=in_, func=AF.Copy), reads=r, writes=w)
        else:
            P.add(eng, lambda e: e.tensor_copy(out=out, in_=in_), reads=r, writes=w)

    def memset(eng, out, val, w):
        P.add(eng, lambda e: e.memset(out, val), writes=w)

    def dma(out, in_, r, w, dest, eng="sp"):
        P.add(eng, lambda e: e.dma_start(out=out, in_=in_), reads=r, writes=w, dma_dest=dest)

    cf = sb([128, NCF]); B_cf = Buf("cf")
    pp = sb([128, NPP]); B_pp = Buf("pp")
    NCB = 128 * 5
    cb = sb([128, NCB], BF16); B_cb = Buf("cb")
    ID_F, ONES_F, U_F = cf[:, CF_ID:CF_ID + 128], cf[:, CF_ONES:CF_ONES + 128], cf[:, CF_U:CF_U + 128]
    NEGM4 = cf[:, CF_NEGM4:CF_NEGM4 + 512]
    EPSC, ONEC = cf[:, CF_EPS:CF_EPS + 1], cf[:, CF_ONE:CF_ONE + 1]
    ID_B, ONES_B, TRI_B, NEGTRI_B, ZERO_B = cb[:, 0:128], cb[:, 128:256], cb[:, 256:384], cb[:, 384:512], cb[:, 512:640]
    abc = sb([128, L * 8]); B_abc = Buf("abc")
    halo = sb([128, 6, 3]); B_halo = Buf("halo")
    kxT = [sb([128, 4, MEM], BF16) for _ in range(L)]; B_kx = [Buf("kx%d" % l) for l in range(L)]
    vx = [sb([128, 2, 512], BF16) for _ in range(L)]; B_vx = [Buf("vx%d" % l) for l in range(L)]
    PERSIST = mark()
    wbf = [sb([128, 4096], BF16) for _ in range(3)]; B_wbf = [Buf("wbf0"), Buf("wbf1"), Buf("wbf2")]
    wrot = Rot([0, 1, 2])
    WEND = mark()

    dma(cf, cf_d, [], [B_cf], B_cf)
    dma(pp, pp_d, [], [B_pp], B_pp)
    cp("dve", cb[:, 0:128], cf[:, CF_ID:CF_ID + 128], [B_cf], [B_cb])
    cp("dve", cb[:, 128:256], cf[:, CF_ONES:CF_ONES + 128], [B_cf], [B_cb])
    cp("dve", cb[:, 256:384], cf[:, CF_TRI:CF_TRI + 128], [B_cf], [B_cb])
    cp("dve", cb[:, 384:512], cf[:, CF_NEGTRI:CF_NEGTRI + 128], [B_cf], [B_cb])
    memset("pool", cb[:, 512:640], 0.0, [B_cb])
    for l in range(L):
        act(abc[:, l * 8:(l + 1) * 8], pp[:, l * PPW + PP_ALOG:l * PPW + PP_ALOG + 8], AF.Exp, [B_pp], [B_abc])
    ts("dve", abc, abc, -1.0, ALU.mult, [B_abc], [B_abc])

    def ppc(l, off, n):
        return pp[:, l * PPW + off:l * PPW + off + n]

    def load_w(src3, KC, N):
        i = wrot.next()
        dst = wbf[i][:, 0:KC * N].rearrange("p (k n) -> p k n", k=KC)
        for k in range(KC):
            dma(dst[:, k, :], src3[:, k, :], [], [B_wbf[i]], B_wbf[i], eng="pool")
        return dst, B_wbf[i]

    def wview(w2d, r0, KC, c0, N):
        return w2d[r0:r0 + KC * 128, c0:c0 + N].rearrange("(k p) n -> p k n", p=128)

    prot = Rot([0, 1, 2, 3])

    def proj_fm(w2d, r0, KC, c0, ncols, colblk, actT, Bact, ntiles, evac, tcol0=0):
        for cbi in range(ncols // colblk):
            wv, Bw = load_w(wview(w2d, r0, KC, c0 + cbi * colblk, colblk), KC, colblk)
            for jj in range(colblk // 128):
                for t in range(ntiles):
                    pi = prot.next()
                    for k in range(KC):
                        mm(psum[pi], wv[:, k, jj * 128:(jj + 1) * 128], actT[:, k, tcol0 + t * TT:tcol0 + (t + 1) * TT],
                           k == 0, k == KC - 1, [Bw, Bact], [B_ps[pi]])
                    evac(cbi * (colblk // 128) + jj, t, psum[pi], B_ps[pi])

    def rmsnorm_fm(hT, BhT, gcols, actT, Bact, sqb, B_sqb, rstd, B_rstd, nchunks=8, scale=1.0 / D):
        for t in range(NT):
            sl = slice(t * TT, (t + 1) * TT)
            for c in range(nchunks):
                act(sqb[:, c, :], hT[:, c, sl], AF.Square, [BhT.k((c, t))], [B_sqb.k(c)])
            pi = prot.next()
            for c in range(nchunks):
                mm(psum[pi], ONES_B, sqb[:, c, :], c == 0, c == nchunks - 1, [B_cb, B_sqb.k(c)], [B_ps[pi]])
            act(rstd, psum[pi], AF.Ln, [B_ps[pi], B_cf], [B_rstd], bias=EPSC, scale=scale)
            act(rstd, rstd, AF.Exp, [B_rstd], [B_rstd], scale=-0.5)
            for c in range(nchunks):
                stt("dve", actT[:, c, sl], hT[:, c, sl], gcols[:, c:c + 1], rstd, ALU.mult, ALU.mult,
                    [BhT.k((c, t)), B_pp, B_rstd], [Bact])

    m0 = mark()
    assert m0 == WEND
    mtok = sb([128, D]); B_mtok = Buf("mtok")
    mn = sb([128, D]); B_mn = Buf("mn")
    msc = sb([128, 4]); B_msc = Buf("msc")
    memnT = [sb([128, 8, MEM], BF16) for _ in range(L)]; B_memn = [Buf("memn%d" % l) for l in range(L)]
    for blk in range(2):
        dma(mtok, mem_d[blk * 128:(blk + 1) * 128, :], [], [B_mtok], B_mtok)
        memset("pool", msc[:, 0:1], 0.0, [B_msc])
        act(mn, mtok, AF.Square, [B_mtok, B_msc], [B_mn, B_msc], accum=msc[:, 0:1])
        act(msc[:, 1:2], msc[:, 0:1], AF.Ln, [B_msc, B_cf], [B_msc], bias=EPSC, scale=1.0 / D)
        act(msc[:, 2:3], msc[:, 1:2], AF.Exp, [B_msc], [B_msc], scale=-0.5)
        ts("dve", mn, mtok, msc[:, 2:3], ALU.mult, [B_mtok, B_msc], [B_mn])
        for half in range(2):
            for c4 in range(4):
                c = half * 4 + c4
                tr(psum[half][:, c4 * 128:(c4 + 1) * 128], mn[:, c * 128:(c + 1) * 128], ID_F, [B_mn, B_cf], [B_ps[half]])
            for l in range(L):
                g = ppc(l, PP_GMEM + half * 4, 4)
                tt("dve", memnT[l][:, half * 4:half * 4 + 4, blk * 128:(blk + 1) * 128],
                   psum[half].rearrange("p (a b) -> p a b", a=4), g.unsqueeze(2).to_broadcast([128, 4, 128]), ALU.mult,
                   [B_ps[half], B_pp], [B_memn[l]])
    for l in range(L):
        wv, Bw = load_w(wview(w_xk[l], 0, 8, 0, 512), 8, 512)
        for hx in range(4):
            pi = prot.next()
            for k in range(8):
                mm(psum[pi][:, 0:MEM], wv[:, k, hx * 128:(hx + 1) * 128], memnT[l][:, k, :], k == 0, k == 7, [Bw, B_memn[l]], [B_ps[pi]])
            cp("act", kxT[l][:, hx, :], psum[pi][:, 0:MEM], [B_ps[pi]], [B_kx[l]])
        wv, Bw = load_w(wview(w_xv[l], 0, 8, 0, 512), 8, 512)
        for mb in range(2):
            pi = prot.next()
            for k in range(8):
                mm(psum[pi], memnT[l][:, k, mb * 128:(mb + 1) * 128], wv[:, k, :], k == 0, k == 7, [Bw, B_memn[l]], [B_ps[pi]])
            cp("dve", vx[l][:, mb, :], psum[pi], [B_ps[pi]], [B_vx[l]])
    P.barrier()
    reset(m0)

    hT = sb([128, 8, TS]); B_hT = Buf("hT")
    actT = sb([128, 8, TS], BF16); B_actT = Buf("actT")
    SCRA_OFF = mark()
    scrA = sb([128, 4, TS], BF16); B_scrA = Buf("scrA")
    scrB = sb([128, 4, TS], BF16); B_scrB = Buf("scrB")
    sqb = sb([128, 8, TT], BF16); B_sqb = Buf("sqb")
    rstd = sb([128, TT]); B_rstd = Buf("rstd")
    evs = [sb([128, TT]) for _ in range(2)]; B_evs = [Buf("evs0"), Buf("evs1")]
    evrot = Rot([0, 1])
    evb = [sb([128, TT], BF16) for _ in range(2)]; B_evb = [Buf("evb0"), Buf("evb1")]
    evbrot = Rot([0, 1])
    xr = [sb([128, TT + 3]) for _ in range(2)]; B_xr = [Buf("xr0"), Buf("xr1")]
    cacc = sb([128, TT]); B_cacc = Buf("cacc")
    wdt = sb([128, 8, 8], BF16); B_wdt = Buf("wdt")
    wdtf = sb([128, 8, 8]); B_wdtf = Buf("wdtf")
    dtst = sb([128, 16, 8]); B_dtst = Buf("dtst")
    dtt = sb([128, 8]); B_dtt = Buf("dtt")
    TOKWISE_END = mark()
    eng_alt = Rot(["act", "dve"])

    def embed(st):
        xin = [evs[0], evs[1]]
        for blk in range(TS // 128):
            r0 = st * TS + blk * 128
            for half in range(2):
                i = evrot.next()
                dma(evs[i], x_d[r0:r0 + 128, half * 512:(half + 1) * 512], [], [B_evs[i]], B_evs[i])
                pi = prot.next()
                for c4 in range(4):
                    tr(psum[pi][:, c4 * 128:(c4 + 1) * 128], evs[i][:, c4 * 128:(c4 + 1) * 128], ID_F, [B_evs[i], B_cf], [B_ps[pi]])
                t = blk // 4
                cp(eng_alt.next(), hT[:, half * 4:half * 4 + 4, blk * 128:(blk + 1) * 128],
                   psum[pi].rearrange("p (a b) -> p a b", a=4), [B_ps[pi]],
                   [B_hT.k((half * 4 + c4, t)) for c4 in range(4)])

    def store_hT(st):
        for c in range(8):
            dma(HT[st, :, c, :], hT[:, c, :], [B_hT], [B_HT[st]], B_HT[st])

    def load_hT(st):
        for c in range(8):
            dma(hT[:, c, :], HT[st, :, c, :], [B_HT[st]], [B_hT], B_hT)

    def g1(l, st):
        tok0 = st * TS
        rmsnorm_fm(hT, B_hT, ppc(l, PP_GMIX, 8), actT, B_actT, sqb, B_sqb, rstd, B_rstd)
        wl = w_in[l]
        if st == 0:
            memset("pool", halo, 0.0, [B_halo])
        cw = ppc(l, PP_CW, 24)
        cbias = ppc(l, PP_CB, 6)
        cstate = {"n": 0}

        def conv_evac(base):
            def f(j, t, ps, Bp):
                cc = base + j
                n = cstate["n"]
                cstate["n"] += 1
                i = n % 2
                if t == 0:
                    cp("pool", xr[i][:, 0:3], halo[:, cc, :], [B_halo], [B_xr[i]])
                else:
                    cp("pool", xr[i][:, 0:3], xr[1 - i][:, TT:TT + 3], [B_xr[1 - i]], [B_xr[i]])
                cp("act", xr[i][:, 3:TT + 3], ps, [Bp], [B_xr[i]])
                if t == NT - 1:
                    cp("pool", halo[:, cc, :], xr[i][:, TT:TT + 3], [B_xr[i]], [B_halo])
                ts("dve", cacc, xr[i][:, 0:TT], cw[:, cc * 4:cc * 4 + 1], ALU.mult, [B_xr[i], B_pp], [B_cacc])
                for k in range(1, 4):
                    stt("dve", cacc, xr[i][:, k:k + TT], cw[:, cc * 4 + k:cc * 4 + k + 1], cacc, ALU.mult, ALU.add,
                        [B_xr[i], B_pp, B_cacc], [B_cacc])
                e = evrot.next()
                act(evs[e], cacc, AF.Silu, [B_cacc, B_pp], [B_evs[e]], bias=cbias[:, cc:cc + 1])
                sl = slice(tok0 + t * TT, tok0 + (t + 1) * TT)
                if cc < 4:
                    dma(XST[cc, :, sl], evs[e], [B_evs[e]], [B_XST], B_XST)
                else:
                    dma(BCT[cc - 4, :, sl], evs[e], [B_evs[e]], [B_BCT], B_BCT)
            return f

        proj_fm(wl, 0, 8, C_X, 512, 512, actT, B_actT, NT, conv_evac(0))
        proj_fm(wl, 0, 8, C_B, 256, 256, actT, B_actT, NT, conv_evac(4))

        def qk_evac(dst, Bdst):
            dflat = dst.rearrange("h d t -> (h d) t")

            def f(j, t, ps, Bp):
                e = evbrot.next()
                cp(eng_alt.next(), evb[e], ps, [Bp], [B_evb[e]])
                sl = slice(tok0 + t * TT, tok0 + (t + 1) * TT)
                dma(dflat[j * 128:(j + 1) * 128, sl], evb[e], [B_evb[e]], [Bdst], Bdst)
            return f

        proj_fm(wl, 0, 8, C_Q, 512, 512, actT, B_actT, NT, qk_evac(QT, B_QT))
        proj_fm(wl, 0, 8, C_K, 512, 512, actT, B_actT, NT, qk_evac(KT, B_KT))
        for (c0, dst, Bdst, isbf) in ((C_Z, ZTOK, B_ZTOK, False), (C_V, VS, B_VS, True)):
            wv, Bw = load_w(wview(wl, 0, 8, c0, 512), 8, 512)
            for blk in range(TS // 128):
                pi = prot.next()
                for k in range(8):
                    mm(psum[pi], actT[:, k, blk * 128:(blk + 1) * 128], wv[:, k, :], k == 0, k == 7, [Bw, B_actT], [B_ps[pi]])
                r0 = tok0 + blk * 128
                if isbf:
                    e = evbrot.next()
                    cp(eng_alt.next(), evb[e], psum[pi], [B_ps[pi]], [B_evb[e]])
                    dma(dst[r0:r0 + 128, :], evb[e], [B_evb[e]], [Bdst], Bdst)
                else:
                    e = evrot.next()
                    cp(eng_alt.next(), evs[e], psum[pi], [B_ps[pi]], [B_evs[e]])
                    dma(dst[r0:r0 + 128, :], evs[e], [B_evs[e]], [Bdst], Bdst)
        dma(wdtf, wview(wl, 0, 8, C_DT, 8), [], [B_wdtf], B_wdtf)
        cp("dve", wdt, wdtf, [B_wdtf], [B_wdt])
        dtb = ppc(l, PP_DTB, 8)
        pi = prot.next()
        NBLK = TS // 128
        for blk in range(NBLK):
            for k in range(8):
                mm(psum[pi][:, blk * 8:(blk + 1) * 8], actT[:, k, blk * 128:(blk + 1) * 128], wdt[:, k, :], k == 0, k == 7, [B_wdt, B_actT], [B_ps[pi]])
        tt("dve", dtst, psum[pi][:, 0:NBLK * 8].rearrange("p (b h) -> p b h", h=8), dtb.unsqueeze(1).to_broadcast([128, NBLK, 8]), ALU.add,
           [B_ps[pi], B_pp], [B_dtst])
        act(dtst, dtst, AF.Exp, [B_dtst], [B_dtst])
        act(dtst, dtst, AF.Ln, [B_dtst, B_cf], [B_dtst], bias=ONEC, scale=1.0)
        dma(DTOK[tok0:tok0 + TS, :].rearrange("(b p) h -> p b h", p=128), dtst, [B_dtst], [B_DTOK], B_DTOK)

    def add_evac(j, t, ps, Bp):
        sl = slice(t * TT, (t + 1) * TT)
        tt("dve", hT[:, j, sl], ps, hT[:, j, sl], ALU.add, [Bp, B_hT.k((j, t))], [B_hT.k((j, t))])

    def g3(l, st):
        tok0 = st * TS
        for c in range(8):
            dma(actT[:, c, :], YT[c * 128:(c + 1) * 128, tok0:tok0 + TS], [B_YT], [B_actT], B_actT)
        proj_fm(w_out[l], 0, 8, 0, D, 512, actT, B_actT, NT, add_evac)
        rmsnorm_fm(hT, B_hT, ppc(l, PP_GXA, 8), actT, B_actT, sqb, B_sqb, rstd, B_rstd)
        qxT, B_qx = scrA, B_scrA
        oxT, B_ox = scrB, B_scrB

        def q_evac(j, t, ps, Bp):
            cp(eng_alt.next(), qxT[:, j, t * TT:(t + 1) * TT], ps, [Bp], [B_qx.k((j, t))])
        proj_fm(w_xq[l], 0, 8, 0, 512, 512, actT, B_actT, NT, q_evac)
        sc = 1.0 / math.sqrt(128.0)
        for t in range(NT):
            sl = slice(t * TT, (t + 1) * TT)
            for hx in range(4):
                pT = []
                bk = (4, 5, 6, 7) if (t * 4 + hx) % 2 == 0 else (0, 1, 2, 3)
                for mb in range(2):
                    pi = bk[mb]
                    mm(psum[pi], kxT[l][:, hx, mb * 128:(mb + 1) * 128], qxT[:, hx, sl], True, True, [B_kx[l], B_qx.k((hx, t))], [B_ps[pi]])
                    e = evbrot.next()
                    act(evb[e], psum[pi], AF.Exp, [B_ps[pi]], [B_evb[e]], scale=sc)
                    pT.append(e)
                po, pd = bk[2], bk[3]
                for mb in range(2):
                    mm(psum[po], vx[l][:, mb, hx * 128:(hx + 1) * 128], evb[pT[mb]], mb == 0, mb == 1, [B_vx[l], B_evb[pT[mb]]], [B_ps[po]])
                for mb in range(2):
                    mm(psum[pd], ONES_B, evb[pT[mb]], mb == 0, mb == 1, [B_cb, B_evb[pT[mb]]], [B_ps[pd]])
                e = evrot.next()
                cp("act", evs[e], psum[pd], [B_ps[pd]], [B_evs[e]])
                P.add("dve", (lambda ee: (lambda en: en.reciprocal(out=evs[ee], in_=evs[ee])))(e), reads=[B_evs[e]], writes=[B_evs[e]])
                tt("dve", oxT[:, hx, sl], psum[po], evs[e], ALU.mult, [B_ps[po], B_evs[e]], [B_ox.k((hx, t))])
        proj_fm(w_xo[l], 0, 4, 0, D, 1024, oxT, B_ox, NT, add_evac)
        rmsnorm_fm(hT, B_hT, ppc(l, PP_GFF, 8), actT, B_actT, sqb, B_sqb, rstd, B_rstd)
        uT, B_u = scrA, B_scrA
        for fb in range(8):
            proj_fm(w_ff1[l], 0, 8, fb * 512, 512, 512, actT, B_actT, NT, u_evac_fix(uT, B_u))
            proj_fm(w_ff2[l], fb * 512, 4, 0, D, 1024, uT, B_u, NT, add_evac)

    def u_evac_fix(uT, B_u):
        def f(j, t, ps, Bp):
            e = evrot.next()
            act(evs[e], ps, AF.Relu, [Bp], [B_evs[e]])
            tt("pool" if (j + t) % 2 else "dve", uT[:, j, t * TT:(t + 1) * TT], evs[e], evs[e], ALU.mult, [B_evs[e]], [B_u.k((j, t))])
        return f

    def final(st):
        tok0 = st * TS
        for t in range(NT):
            sl = slice(t * TT, (t + 1) * TT)
            for c in range(8):
                act(sqb[:, c, :], hT[:, c, sl], AF.Square, [B_hT.k((c, t))], [B_sqb.k(c)])
            pi = prot.next()
            for c in range(8):
                mm(psum[pi], ONES_B, sqb[:, c, :], c == 0, c == 7, [B_cb, B_sqb.k(c)], [B_ps[pi]])
            act(rstd, psum[pi], AF.Ln, [B_ps[pi], B_cf], [B_rstd], bias=EPSC, scale=1.0 / D)
            act(rstd, rstd, AF.Exp, [B_rstd], [B_rstd], scale=-0.5)
            gf = pp[:, PP_FINAL:PP_FINAL + 8]
            for c in range(8):
                stt("dve", fin[:, c, :], hT[:, c, sl], gf[:, c:c + 1], rstd, ALU.mult, ALU.mult,
                    [B_hT.k((c, t)), B_pp, B_rstd], [B_fin.k(c)])
            for b4 in range(4):
                for half in range(2):
                    pi = 4 + half
                    for c4 in range(4):
                        c = half * 4 + c4
                        tr(psum[pi][:, c4 * 128:(c4 + 1) * 128], fin[:, c, b4 * 128:(b4 + 1) * 128], ID_F, [B_fin.k(c), B_cf], [B_ps[pi]])
                    e = evrot.next()
                    cp(eng_alt.next(), evs[e], psum[pi], [B_ps[pi]], [B_evs[e]])
                    r0 = tok0 + t * TT + b4 * 128
                    dma(out_d[r0:r0 + 128, half * 512:(half + 1) * 512], evs[e], [B_evs[e]], [B_OUT], B_OUT)

    fin = arena[:, SCRA_OFF // 2:SCRA_OFF // 2 + 8192].bitcast(F32).rearrange("p (a b) -> p a b", a=8); B_fin = B_scrA

    def ssd(l):
        m = mark()
        NG = TL // TT
        NCH = TL // 128
        xsT = [sb([128, 4, TT]) for _ in range(2)]; B_xsT = [Buf("xsT0"), Buf("xsT1")]
        bt128 = [sb([128, TT]) for _ in range(2)]; B_bt = [Buf("bt0"), Buf("bt1")]
        bc64 = [sb([64, 4, TT]) for _ in range(2)]; B_bc64 = [Buf("bc640"), Buf("bc641")]
        bc64b = [sb([64, 4, TT], BF16) for _ in range(2)]; B_bc64b = [Buf("bc64b0"), Buf("bc64b1")]
        ztok = [sb([128, 4, TT]) for _ in range(2)]; B_ztok = [Buf("ztok0"), Buf("ztok1")]
        dtg = [sb([128, 4, 8]) for _ in range(2)]; B_dtg = [Buf("dtg0"), Buf("dtg1")]
        yst = [sb([128, 4, TT], BF16) for _ in range(2)]; B_yst = [Buf("yst0"), Buf("yst1")]
        xs_tok2 = [sb([128, 512]) for _ in range(2)]; B_xs2 = [Buf("xs_tok0"), Buf("xs_tok1")]
        btok2 = [sb([128, 128], BF16) for _ in range(2)]; B_btok2 = [Buf("btok0"), Buf("btok1")]
        sm2 = [sb([128, 80]) for _ in range(2)]; B_sm2 = [Buf("sm0"), Buf("sm1")]
        MT2 = [sb([128, 8, 128], BF16) for _ in range(2)]; B_MT2 = [Buf("MT0"), Buf("MT1")]
        xdt2 = [sb([128, 512], BF16) for _ in range(2)]; B_xdt2 = [Buf("xdt0"), Buf("xdt1")]
        xw2 = [sb([128, 512], BF16) for _ in range(2)]; B_xw2 = [Buf("xw0"), Buf("xw1")]
        for i in range(2):
            memset("pool", sm2[i], 0.0, [B_sm2[i]])
        Rm = sb([128, 8, 128]); B_R = Buf("R")
        dec = sb([128, 8, 128]); B_dec = Buf("dec")
        t1 = sb([128, 512]); B_t1 = Buf("t1")
        t2 = sb([128, 512]); B_t2 = Buf("t2")
        yv2 = [sb([128, 512]) for _ in range(2)]; B_y2 = [Buf("y0"), Buf("y1")]
        gz = sb([128, 512]); B_gz = Buf("gz")
        sq = sb([128, 512]); B_sq = Buf("sq")
        ssq = sb([128, 4]); B_ssq = Buf("ssq")
        prev = sb([64, 8, 64]); B_prev = Buf("prev")
        prevb = [sb([64, 8, 64], BF16) for _ in range(2)]; B_prevb = [Buf("prevb0"), Buf("prevb1")]
        a_l = abc[:, l * 8:(l + 1) * 8]
        dsk = ppc(l, PP_DSK, 512)
        gssd = ppc(l, PP_GSSD, 4)
        memset("pool", prev, 0.0, [B_prev])
        memset("pool", prevb[0], 0.0, [B_prevb[0]])
        BCT64 = BCT.rearrange("a (g n) t -> n (a g) t", g=2)

        def loads(gi):
            s = gi % 2
            tsl = slice(gi * TT, (gi + 1) * TT)
            for j in range(4):
                dma(xsT[s][:, j, :], XST[j, :, tsl], [B_XST], [B_xsT[s]], B_xsT[s])
            dma(bt128[s], BCT[0, :, tsl], [B_BCT], [B_bt[s]], B_bt[s])
            for a_ in range(4):
                dma(bc64[s][:, a_, :], BCT64[:, a_, tsl], [B_BCT], [B_bc64[s]], B_bc64[s])
            dma(ztok[s], ZTOK[tsl, :].rearrange("(c p) f -> p c f", p=128), [B_ZTOK], [B_ztok[s]], B_ztok[s])
            dma(dtg[s], DTOK[tsl, :].rearrange("(c p) h -> p c h", p=128), [B_DTOK], [B_dtg[s]], B_dtg[s])
            cp("pool", bc64b[s], bc64[s], [B_bc64[s]], [B_bc64b[s]])

        def front(ci):
            gi, cg = ci // 4, ci % 4
            s = gi % 2
            p = ci % 2
            cs = slice(cg * 128, (cg + 1) * 128)
            xs_tok, B_xs = xs_tok2[p], B_xs2[p]
            btok, B_btok = btok2[p], B_btok2[p]
            sm, B_sm = sm2[p], B_sm2[p]
            MT, B_MT = MT2[p], B_MT2[p]
            xdt, B_xdt = xdt2[p], B_xdt2[p]
            xw, B_xw = xw2[p], B_xw2[p]
            da16 = sm[:, 0:16]; da = sm[:, 0:8]
            nacol, expA, dstate, cd, dtd, diff = [sm[:, 16 + i * 8:16 + (i + 1) * 8] for i in range(6)]
            for j in range(4):
                tr(psum[0][:, j * 128:(j + 1) * 128], xsT[s][:, j, cs], ID_F, [B_xsT[s], B_cf], [B_ps[0]])
            tr(psum[1][:, 0:128], bt128[s][:, cs], ID_F, [B_bt[s], B_cf], [B_ps[1].k("bt")])
            cp("act", xs_tok, psum[0], [B_ps[0]], [B_xs])
            cp("dve", btok, psum[1][:, 0:128], [B_ps[1].k("bt")], [B_btok])
            dtc = dtg[s][:, cg, :]
            tt("dve", da, dtc, a_l, ALU.mult, [B_dtg[s], B_abc], [B_sm.k("da")])
            mm(psum[1][:, 128:144], U_F, da16, True, True, [B_cf, B_sm.k("da")], [B_ps[1].k("ac")])
            mm(psum[1][:, 144:160], ONES_F, da16, True, True, [B_cf, B_sm.k("da")], [B_ps[1].k("ac")])
            ts("dve", nacol, psum[1][:, 128:136], -1.0, ALU.mult, [B_ps[1].k("ac")], [B_sm.k("nacol")])
            act(expA, psum[1][:, 128:136], AF.Exp, [B_ps[1].k("ac")], [B_sm.k("expA")])
            tt("dve", diff, psum[1][:, 144:152], nacol, ALU.add, [B_ps[1].k("ac"), B_sm.k("nacol")], [B_sm.k("diff")])
            act(dstate, diff, AF.Exp, [B_sm.k("diff")], [B_sm.k("dstate")])
            act(cd, psum[1][:, 144:152], AF.Exp, [B_ps[1].k("ac")], [B_sm.k("cd")])
            tt("dve", dtd, dtc, dstate, ALU.mult, [B_dtg[s], B_sm.k("dstate")], [B_sm.k("dtd")])
            tt("dve", Rm, U_F.unsqueeze(1).to_broadcast([128, 8, 128]), da.unsqueeze(2).to_broadcast([128, 8, 128]), ALU.mult,
               [B_cf, B_sm.k("da")], [B_R])
            for half in range(2):
                mm(psum[2 + half], ONES_F, Rm[:, half * 4:half * 4 + 4, :].rearrange("p a b -> p (a b)"), True, False, [B_cf, B_R], [B_ps[2 + half]])
                mm(psum[2 + half], ID_F, NEGM4, False, True, [B_cf], [B_ps[2 + half]])
            for h in range(8):
                act(dec[:, h, :], psum[2 + h // 4][:, (h % 4) * 128:(h % 4 + 1) * 128], AF.Exp,
                    [B_ps[2 + h // 4], B_sm.k("nacol")], [B_dec.k(h // 4)], bias=nacol[:, h:h + 1], scale=1.0)
            for g in range(2):
                mm(psum[1][:, 256 + g * 128:256 + (g + 1) * 128], bc64b[s][:, g, cs], bc64b[s][:, 2 + g, cs], True, True,
                   [B_bc64b[s]], [B_ps[1].k("cb%d" % g)])
                tt("dve", MT[:, g * 4:g * 4 + 4, :], psum[1][:, 256 + g * 128:256 + (g + 1) * 128].unsqueeze(1).to_broadcast([128, 4, 128]),
                   dec[:, g * 4:g * 4 + 4, :], ALU.mult, [B_ps[1].k("cb%d" % g), B_dec.k(g)], [B_MT.k(g)])
            xs3 = xs_tok.rearrange("p (h j) -> p h j", h=8)
            tt("dve", xdt.rearrange("p (h j) -> p h j", h=8), xs3, dtc.unsqueeze(2).to_broadcast([128, 8, 64]), ALU.mult,
               [B_xs, B_dtg[s]], [B_xdt])
            tt("pool", xw.rearrange("p (h j) -> p h j", h=8), xs3, dtd.unsqueeze(2).to_broadcast([128, 8, 64]), ALU.mult,
               [B_xs, B_sm.k("dtd")], [B_xw])

        def back(ci):
            gi, cg = ci // 4, ci % 4
            s = gi % 2
            p = ci % 2
            cs = slice(cg * 128, (cg + 1) * 128)
            tsl = slice(gi * TT, (gi + 1) * TT)
            xs_tok, B_xs = xs_tok2[p], B_xs2[p]
            btok, B_btok = btok2[p], B_btok2[p]
            sm, B_sm = sm2[p], B_sm2[p]
            MT, B_MT = MT2[p], B_MT2[p]
            xdt, B_xdt = xdt2[p], B_xdt2[p]
            xw, B_xw = xw2[p], B_xw2[p]
            nacol, expA, dstate, cd, dtd, diff = [sm[:, 16 + i * 8:16 + (i + 1) * 8] for i in range(6)]
            pb_cur = prevb[ci % 2]; Bpb_cur = B_prevb[ci % 2]
            pb_nxt = prevb[(ci + 1) % 2]; Bpb_nxt = B_prevb[(ci + 1) % 2]
            yv, B_y = yv2[p], B_y2[p]
            for h in range(8):
                mm(psum[4][:, h * 64:(h + 1) * 64], MT[:, h, :], xdt[:, h * 64:(h + 1) * 64], True, True, [B_MT.k(h // 4), B_xdt], [B_ps[4]])
            for h in range(8):
                mm(psum[5][:, h * 64:(h + 1) * 64], bc64b[s][:, 2 + h // 4, cs], pb_cur[:, h, :], True, True, [B_bc64b[s], Bpb_cur], [B_ps[5]])
            for g in range(2):
                mm(psum[6][0:64, g * 256:(g + 1) * 256], btok[:, g * 64:(g + 1) * 64], xw[:, g * 256:(g + 1) * 256], True, True,
                   [B_btok, B_xw], [B_ps[6]])
            tt("pool", prev, prev, cd[0:64, :].unsqueeze(2).to_broadcast([64, 8, 64]), ALU.mult, [B_prev, B_sm.k("cd")], [B_prev])
            tt("dve", prev, psum[6][0:64, :].rearrange("p (h j) -> p h j", h=8), prev, ALU.add, [B_ps[6], B_prev], [B_prev])
            cp("pool", pb_nxt, prev, [B_prev], [Bpb_nxt])
            tt("dve", t1.rearrange("p (h j) -> p h j", h=8), psum[5].rearrange("p (h j) -> p h j", h=8),
               expA.unsqueeze(2).to_broadcast([128, 8, 64]), ALU.mult, [B_ps[5], B_sm.k("expA")], [B_t1])
            tt("pool", t2, xs_tok, dsk, ALU.mult, [B_xs, B_pp], [B_t2])
            tt("pool", t2, t2, t1, ALU.add, [B_t2, B_t1], [B_t2])
            tt("dve", yv, psum[4], t2, ALU.add, [B_ps[4], B_t2], [B_y])

        def back2(ci):
            gi, cg = ci // 4, ci % 4
            s = gi % 2
            p = ci % 2
            cs = slice(cg * 128, (cg + 1) * 128)
            tsl = slice(gi * TT, (gi + 1) * TT)
            yv, B_y = yv2[p], B_y2[p]
            act(gz, ztok[s][:, cg, :], AF.Silu, [B_ztok[s]], [B_gz])
            tt("pool", yv, yv, gz, ALU.mult, [B_y, B_gz], [B_y])
            memset("pool", ssq[:, 0:1], 0.0, [B_ssq])
            act(sq, yv, AF.Square, [B_y, B_ssq], [B_sq, B_ssq], accum=ssq[:, 0:1])
            act(ssq[:, 1:2], ssq[:, 0:1], AF.Ln, [B_ssq, B_cf], [B_ssq], bias=EPSC, scale=1.0 / 512)
            act(ssq[:, 2:3], ssq[:, 1:2], AF.Exp, [B_ssq], [B_ssq], scale=-0.5)
            ts("dve", sq, yv, ssq[:, 2:3], ALU.mult, [B_y, B_ssq], [B_sq])
            for j in range(4):
                tr(psum[7][:, j * 128:(j + 1) * 128], sq[:, j * 128:(j + 1) * 128], ID_F, [B_sq, B_cf], [B_ps[7]])
            tt("dve", yst[s][:, :, cs], psum[7].rearrange("p (a b) -> p a b", a=4), gssd.unsqueeze(2).to_broadcast([128, 4, 128]), ALU.mult,
               [B_ps[7], B_pp], [B_yst[s]])
            if cg == 3:
                dma(YT[0:512, tsl].rearrange("(j p) t -> p j t", p=128), yst[s], [B_yst[s]], [B_YT], B_YT)

        for ci in range(NCH + 1):
            if ci < NCH and ci % 4 == 0:
                loads(ci // 4)
            fns = []
            if ci < NCH:
                fns.append((lambda c: (lambda: front(c)))(ci))
            if ci >= 1:
                fns.append((lambda c: (lambda: (back(c), back2(c))))(ci - 1))
            interleave(P, *fns)
        P.barrier()
        reset(m)

    def sba(l):
        m = mark()
        kt_all = sb([128, 8, TL], BF16); B_kt = Buf("kt_all")
        v_all = sb([128, TL // 128, 576], BF16); B_v = Buf("v_all")
        qg = [sb([128, 8, TT], BF16) for _ in range(2)]; B_qg = [Buf("qg0"), Buf("qg1")]
        e_sb = [sb([128, TT]) for _ in range(3)]; B_e = [Buf("e%d" % i) for i in range(3)]
        sp_b = [sb([128, TT], BF16) for _ in range(2)]; B_sp = [Buf("sp%d" % i) for i in range(2)]
        r_sb = [sb([128, TT]) for _ in range(2)]; B_r = [Buf("r%d" % i) for i in range(2)]
        w_b = [sb([128, TT], BF16) for _ in range(2)]; B_w = [Buf("w%d" % i) for i in range(2)]
        acc = [sb([128, TT], BF16) for _ in range(2)]; B_acc = [Buf("acc%d" % i) for i in range(2)]
        o_sb = sb([64, 8, TT]); B_o = Buf("o_sb")
        osq = sb([64, 8, TT], BF16); B_osq = Buf("osq")
        rs = sb([128, TT]); B_rs = Buf("rs")
        yst1 = sb([64, 8, TT], BF16); yst = [yst1, yst1]; B_y1 = Buf("ysb"); B_yst = [B_y1, B_y1]
        gsb = ppc(l, PP_GSB, 8)
        memset("pool", v_all[:, :, 512:576], 0.0, [B_v])
        for h in range(8):
            dma(kt_all[0:64, h, :], KT[h, :, :], [B_KT], [B_kt], B_kt)
            dma(kt_all[64:128, h, :], KT[h, :, :], [B_KT], [B_kt], B_kt)
        for q4 in range(TL // 1024):
            bs = slice(q4 * 8, (q4 + 1) * 8)
            dma(v_all[:, bs, 0:512], VS[q4 * 1024:(q4 + 1) * 1024, :].rearrange("(b p) f -> p b f", p=128), [B_VS], [B_v], B_v)
        tiles = []
        for G in range(min(TL // TT, SBA_MAXG)):
            for h in range(8):
                kbs = list(range(4 * G + 3, -1, -1))
                for ii, kb in enumerate(kbs):
                    tiles.append((G, h, kb, ii == 0, ii == len(kbs) - 1))
        n = len(tiles)
        Z_PS = [0, 1]; R_PS = [2, 3]; O_PS = [4, 5]; SS_PS = 6

        def stage_q(i):
            G, h, kb, first, last = tiles[i]
            s = G % 2
            if h == 0 and first:
                dma(qg[s][0:64], QT[:, :, G * TT:(G + 1) * TT].rearrange("h d t -> d h t"), [B_QT], [B_qg[s]], B_qg[s])
                dma(qg[s][64:128], QT[:, :, G * TT:(G + 1) * TT].rearrange("h d t -> d h t"), [B_QT], [B_qg[s]], B_qg[s])
            j = kb - 4 * G
            c0 = 128 * max(j, 0)
            cs = slice(c0, TT)
            zi = Z_PS[i % 2]
            mm(psum[zi][:, cs], kt_all[:, h, kb * 128:(kb + 1) * 128], qg[s][:, h, cs], True, True, [B_kt, B_qg[s]], [B_ps[zi]])
            if j >= 0:
                mm(psum[zi][:, c0:c0 + 128], ID_B, NEGTRI_B, False, True, [B_cb], [B_ps[zi]])
            for _ in range(SBA_DUMMY):
                mm(psum[7], ID_B, cb[:, 0:512], True, True, [B_cb], [B_ps[7]])

        def stage_a(i):
            G, h, kb, first, last = tiles[i]
            s = G % 2
            j = kb - 4 * G
            c0 = 128 * max(j, 0)
            cs = slice(c0, TT)
            zi = Z_PS[i % 2]; ri = R_PS[i % 2]
            ei = i % 3; si = i % 2
            ai = (G * 8 + h) % 2
            act(e_sb[ei][:, cs], psum[zi][:, cs], AF.Exp, [B_ps[zi]], [B_e[ei]], scale=0.0625)
            act(sp_b[si][:, cs], e_sb[ei][:, cs], AF.Ln, [B_e[ei], B_cf], [B_sp[si]], bias=ONEC, scale=1.0)
            mm(psum[ri][:, cs], TRI_B, sp_b[si][:, cs], True, first, [B_cb, B_sp[si]], [B_ps[ri]])
            if not first:
                mm(psum[ri][:, cs], ONES_B, acc[ai][:, cs], False, True, [B_cb, B_acc[ai]], [B_ps[ri]])
            if first:
                memset("pool", acc[ai], 0.0, [B_acc[ai]])
            if not last:
                tt("pool", acc[ai][:, cs], acc[ai][:, cs], sp_b[si][:, cs], ALU.add, [B_acc[ai], B_sp[si]], [B_acc[ai]])

        def stage_b(i):
            G, h, kb, first, last = tiles[i]
            s = G % 2
            j = kb - 4 * G
            c0 = 128 * max(j, 0)
            cs = slice(c0, TT)
            ri = R_PS[i % 2]
            ei = i % 3; si = i % 2
            oi = O_PS[(G * 8 + h) % 2]
            if SBA_LEVEL < 3:
                return
            act(r_sb[si][:, cs], psum[ri][:, cs], AF.Exp, [B_ps[ri]], [B_r[si]], scale=-1.0)
            tt("dve", w_b[si][:, cs], e_sb[ei][:, cs], r_sb[si][:, cs], ALU.mult, [B_e[ei], B_r[si]], [B_w[si]])
            if first:
                for q4 in range(4):
                    mm(psum[oi][:, q4 * 128:(q4 + 1) * 128], ZERO_B, ONES_B, True, False, [B_cb], [B_ps[oi]])
            mm(psum[oi][:, cs], v_all[:, kb, h * 64:h * 64 + 128], w_b[si][:, cs], False, last, [B_v, B_w[si]], [B_ps[oi]])
            if last and SBA_LEVEL >= 4:
                cp("dve", o_sb[:, h, :], psum[oi][0:64, :], [B_ps[oi]], [B_o.k(h)])
                act(osq[:, h, :], psum[oi][0:64, :], AF.Square, [B_ps[oi]], [B_osq.k(h)])
                if h == 7 and SBA_LEVEL >= 5:
                    for hh in range(8):
                        mm(psum[SS_PS], ONES_B[0:64, :], osq[:, hh, :], hh == 0, hh == 7, [B_cb, B_osq.k(hh)], [B_ps[SS_PS]])
                    act(rs, psum[SS_PS], AF.Ln, [B_ps[SS_PS], B_cf], [B_rs], bias=EPSC, scale=1.0 / 512)
                    act(rs, rs, AF.Exp, [B_rs], [B_rs], scale=-0.5)
                    for hh in range(8):
                        stt("dve", yst[s][:, hh, :], o_sb[:, hh, :], gsb[0:64, hh:hh + 1], rs[0:64, :], ALU.mult, ALU.mult,
                            [B_o.k(hh), B_pp, B_rs], [B_yst[s]])
                    dma(YT[512:1024, G * TT:(G + 1) * TT].rearrange("(h d) t -> d h t", d=64), yst[s], [B_yst[s]], [B_YT], B_YT)

        for i in range(n + 2):
            if i < n:
                stage_q(i)
            if 1 <= i <= n:
                stage_a(i - 1)
            if i >= 2:
                stage_b(i - 2)
        P.barrier()
        reset(m)

    for st in range(NST):
        embed(st)
        if NST > 1:
            store_hT(st)
        g1(0, st)
    P.barrier()
    done = False
    for l in range(n_layers):
        if stop == "g1":
            break
        reset(PERSIST)
        ssd(l)
        if stop == "ssd":
            break
        sba(l)
        if stop == "g2":
            break
        for st in range(NST):
            if NST > 1:
                load_hT(st)
            g3(l, st)
            if l + 1 < n_layers:
                if NST > 1:
                    store_hT(st)
                g1(l + 1, st)
            else:
                final(st)
                done = True
        P.barrier()
    dumps = {}
    if dump:
        alld = (("QT", QT, B_QT), ("KT", KT, B_KT), ("VS", VS, B_VS), ("ZTOK", ZTOK, B_ZTOK), ("XST", XST, B_XST),
                ("BCT", BCT, B_BCT), ("DTOK", DTOK, B_DTOK), ("YT", YT, B_YT), ("HT", HT, None))
        for name, ap_, Bf in [d_ for d_ in alld if dump is True or d_[0] in dump]:
            o = nc.dram_tensor("dump_" + name, list(ap_.shape), ap_.dtype, kind="ExternalOutput").ap()
            db = Buf("dump_" + name)
            rd = [Bf] if Bf is not None else list(B_HT)
            if len(ap_.shape) == 4:
                for s_ in range(ap_.shape[0]):
                    dma(o[s_].rearrange("p c t -> p (c t)"), ap_[s_].rearrange("p c t -> p (c t)"), rd, [db], db)
            elif len(ap_.shape) == 3:
                for s_ in range(ap_.shape[0]):
                    dma(o[s_], ap_[s_], rd, [db], db)
            else:
                dma(o, ap_, rd, [db], db)
    P.barrier()
    P.emit()
    return nc


_CACHE = {}


def kernel(**inputs):
    p = {k: np.asarray(v) for k, v in inputs.items()}
    if "nc" not in _CACHE:
        _CACHE["nc"] = build()
    nc = _CACHE["nc"]
    cf = host_consts()
    pp = host_params(p)
    shared = {k: np.ascontiguousarray(p[k], dtype=np.float32) for k in ("w_in", "w_out", "w_xq", "w_xk", "w_xv", "w_xo", "w_ff1", "w_ff2")}
    in_maps = []
    for c in range(8):
        b = c % 4
        m = {"x": np.ascontiguousarray(p["x"][b], dtype=np.float32), "mem": np.ascontiguousarray(p["mem"][b], dtype=np.float32),
             "cf": cf, "pp": pp}
        m.update(shared)
        in_maps.append(m)
    res = run_bass_kernel_spmd(nc, in_maps, core_ids=list(range(8)))
    out = np.stack([np.asarray(res.results[b]["out"], dtype=np.float32) for b in range(4)], axis=0)
    return out
```
